# Optimizing a Trainium2 kernel written in Bass

```python
import math
import jax, jax.numpy as jnp
from jax import lax
import numpy as np

D_MODEL = 1024
BATCH = 4
SEQ = 4096
DEPTH = 1
DEC_BATCH = 128
DEC_SEQ = 4
PAST_LEN = 8192
PAGE_SIZE = 128

HEAD_DIM = 64
A_Q_HEADS = 8
A_KV_HEADS = 2
A_GROUP = A_Q_HEADS // A_KV_HEADS
A_WINDOW = 128
A_DILATION = 1
B_GROUPS = ((128, 1), (512, 4), (2048, 16))
N_B_GROUPS = 3
B_HEADS = 8
A_WIDTH = A_Q_HEADS * HEAD_DIM
A_KV_WIDTH = A_KV_HEADS * HEAD_DIM
B_WIDTH = B_HEADS * HEAD_DIM
B_QKV_WIDTH = N_B_GROUPS * B_HEADS * HEAD_DIM
H_TOTAL = A_Q_HEADS + N_B_GROUPS * B_HEADS
N_BUCKETS = 32
MAX_DISTANCE = 2048
EPS = 1e-6
NEG_INF = -1e30
Q_SCALE = HEAD_DIM ** -0.5
IN_SPLITS = (A_WIDTH, A_KV_WIDTH, A_KV_WIDTH, A_WIDTH, B_QKV_WIDTH, B_QKV_WIDTH, B_QKV_WIDTH, B_WIDTH, D_MODEL, D_MODEL)
C_IN = A_WIDTH + 2 * A_KV_WIDTH + A_WIDTH + 3 * B_QKV_WIDTH + B_WIDTH + 2 * D_MODEL

kernel_name = "hybrid_swa_sink_dilated_gated_merge_step"


def rmsnorm(x, g):
    x32 = x.astype(jnp.float32)
    y = x32 * lax.rsqrt(jnp.mean(x32 * x32, axis=-1, keepdims=True) + EPS)
    return (y * g.astype(jnp.float32)).astype(x.dtype)


def qk_norm(a, g, scale=1.0):
    a32 = a.astype(jnp.float32)
    y = a32 * lax.rsqrt(jnp.mean(a32 * a32, axis=-1, keepdims=True) + EPS)
    return (y * g.astype(jnp.float32) * scale).astype(a.dtype)


def t5_bucket(dist):
    max_exact = N_BUCKETS // 2
    d = jnp.maximum(dist, 0)
    df = jnp.maximum(d, 1).astype(jnp.float32)
    large = max_exact + (jnp.log(df / max_exact) / math.log(MAX_DISTANCE / max_exact)
                         * (N_BUCKETS - max_exact)).astype(jnp.int32)
    large = jnp.minimum(large, N_BUCKETS - 1)
    return jnp.where(d < max_exact, d, large)


def softmax_stats(s, mask, sink):
    s = jnp.where(mask, s, NEG_INF)
    m = jnp.max(s, axis=-1, keepdims=True)
    if sink is not None:
        m = jnp.maximum(m, sink)
    p = jnp.exp(s - m)
    l = jnp.sum(p, axis=-1, keepdims=True)
    if sink is not None:
        l = l + jnp.exp(sink - m)
    return p, l, m


def banded_window_attention(q, k, v, dilation, n_keys, bias_table, sink):
    N, S, Hk, G, Dh = q.shape
    d = dilation
    M = S // d
    blk = n_keys
    nb = -(-M // blk)
    Mp = nb * blk

    def to_blocks(a):
        a = a.reshape((N, M, d) + a.shape[2:])
        a = jnp.moveaxis(a, 2, 1).reshape((N * d, M) + a.shape[3:])
        a = jnp.pad(a, ((0, 0), (0, Mp - M)) + ((0, 0),) * (a.ndim - 2))
        return a.reshape((N * d, nb, blk) + a.shape[2:])

    def with_prev(a):
        prev = jnp.pad(a, ((0, 0), (1, 0)) + ((0, 0),) * (a.ndim - 2))[:, :-1]
        return jnp.concatenate([prev, a], axis=2)

    qb = to_blocks(q)
    kc = with_prev(to_blocks(k))
    vc = with_prev(to_blocks(v))
    s = jnp.einsum('nbqhgd,nbchd->nbhgqc', qb, kc, preferred_element_type=jnp.float32)
    qi = jnp.arange(blk)[:, None]
    ci = jnp.arange(2 * blk)[None, :]
    delta = qi + blk - ci
    band = (delta >= 0) & (delta < n_keys)
    key_pos = (jnp.arange(nb)[:, None, None] - 1) * blk + ci[None]
    mask = band[None] & (key_pos >= 0)
    bias = bias_table[t5_bucket(delta * d)]
    bias = jnp.transpose(bias, (2, 0, 1)).reshape(Hk, G, blk, 2 * blk).astype(jnp.float32)
    p, l, m = softmax_stats(s + bias, mask[None, :, None, None], sink)
    o = jnp.einsum('nbhgqc,nbchd->nbqhgd', p, vc.astype(jnp.float32))
    o = o / jnp.moveaxis(l[..., 0], -1, 2)[..., None]
    lse = jnp.moveaxis((m + jnp.log(l))[..., 0], -1, 2)

    def from_blocks(a):
        a = a.reshape((N * d, Mp) + a.shape[3:])[:, :M]
        a = a.reshape((N, d, M) + a.shape[2:])
        return jnp.moveaxis(a, 1, 2).reshape((N, S) + a.shape[3:])

    return from_blocks(o).astype(q.dtype), from_blocks(lse)


def gathered_window_attention(q, k_buf, v_buf, k_new, v_new, dilation, n_keys, bias_table, sink):
    N, T, Hk, G, Dh = q.shape
    L = k_buf.shape[1]
    k_all = jnp.concatenate([k_buf, k_new], axis=1)
    v_all = jnp.concatenate([v_buf, v_new], axis=1)
    idx = L + jnp.arange(T)[:, None] - jnp.arange(n_keys)[None, :] * dilation
    valid = idx >= 0
    flat = jnp.maximum(idx, 0).reshape(-1)
    kg = jnp.take(k_all, flat, axis=1).reshape(N, T, n_keys, Hk, Dh)
    vg = jnp.take(v_all, flat, axis=1).reshape(N, T, n_keys, Hk, Dh)
    s = jnp.einsum('nthgd,ntjhd->nhgtj', q, kg, preferred_element_type=jnp.float32)
    bias = bias_table[t5_bucket(jnp.arange(n_keys) * dilation)]
    bias = bias.T.reshape(Hk, G, 1, n_keys).astype(jnp.float32)
    p, l, m = softmax_stats(s + bias, valid, sink)
    o = jnp.einsum('nhgtj,ntjhd->nthgd', p, vg.astype(jnp.float32))
    o = o / jnp.transpose(l, (0, 3, 1, 2, 4))
    lse = jnp.transpose((m + jnp.log(l))[..., 0], (0, 3, 1, 2))
    return o.astype(q.dtype), lse


def mixer_inputs(x, norm_g, w_in, q_gain_a, k_gain_a, q_gain_b, k_gain_b):
    Bn, Sn = x.shape[:2]
    h = rmsnorm(x, norm_g)
    proj = jnp.einsum('bsd,dc->bsc', h, w_in)
    offsets = [int(o) for o in np.cumsum(IN_SPLITS)[:-1]]
    qa, ka, va, ga, qb, kb, vb, gb, ma, mb = jnp.split(proj, offsets, axis=-1)
    qa = qk_norm(qa.reshape(Bn, Sn, A_KV_HEADS, A_GROUP, HEAD_DIM), q_gain_a, Q_SCALE)
    ka = qk_norm(ka.reshape(Bn, Sn, A_KV_HEADS, HEAD_DIM), k_gain_a)
    va = va.reshape(Bn, Sn, A_KV_HEADS, HEAD_DIM)
    qb = qk_norm(qb.reshape(Bn, Sn, N_B_GROUPS, B_HEADS, HEAD_DIM), q_gain_b[:, None, :], Q_SCALE)
    kb = qk_norm(kb.reshape(Bn, Sn, N_B_GROUPS, B_HEADS, HEAD_DIM), k_gain_b[:, None, :])
    vb = vb.reshape(Bn, Sn, N_B_GROUPS, B_HEADS, HEAD_DIM)
    return qa, ka, va, ga, qb, kb, vb, gb, ma, mb


def combine_dilations(outs, lses):
    w = jax.nn.softmax(jnp.stack(lses, axis=0), axis=0)[..., None]
    o = jnp.sum(w * jnp.stack(outs, axis=0).astype(jnp.float32), axis=0)
    return o.astype(outs[0].dtype)


def mixer_output(x, o_a, o_b, ga, gb, ma, mb, w_up_a, w_up_b, w_out):
    Bn, Sn = x.shape[:2]
    ya = jnp.einsum('bsc,cd->bsd', o_a.reshape(Bn, Sn, A_WIDTH) * jax.nn.silu(ga), w_up_a)
    yb = jnp.einsum('bsc,cd->bsd', o_b.reshape(Bn, Sn, B_WIDTH) * jax.nn.silu(gb), w_up_b)
    merged = jax.nn.sigmoid(ma) * ya + jax.nn.sigmoid(mb) * yb
    return x + jnp.einsum('bsd,de->bse', merged, w_out)


def prompt_layer(x, rel_bias, norm_g, w_in, q_gain_a, k_gain_a, sinks_a, q_gain_b, k_gain_b, w_up_a, w_up_b, w_out):
    Sn = x.shape[1]
    qa, ka, va, ga, qb, kb, vb, gb, ma, mb = mixer_inputs(x, norm_g, w_in, q_gain_a, k_gain_a, q_gain_b, k_gain_b)
    sink = sinks_a.reshape(A_KV_HEADS, A_GROUP, 1, 1).astype(jnp.float32)
    o_a, _ = banded_window_attention(qa, ka, va, A_DILATION, A_WINDOW // A_DILATION,
                                     rel_bias[:, :A_Q_HEADS], sink)
    n_a = min(A_WINDOW, Sn)
    states = [jnp.stack([ka[:, Sn - n_a:], va[:, Sn - n_a:]], axis=2)]
    outs, lses = [], []
    for gi, (win, dil) in enumerate(B_GROUPS):
        c0 = A_Q_HEADS + gi * B_HEADS
        o, lse = banded_window_attention(qb[:, :, gi, :, None, :], kb[:, :, gi], vb[:, :, gi], dil, win // dil,
                                         rel_bias[:, c0:c0 + B_HEADS], None)
        outs.append(o)
        lses.append(lse)
        n_g = min(win, Sn)
        states.append(jnp.stack([kb[:, Sn - n_g:, gi], vb[:, Sn - n_g:, gi]], axis=2))
    o_b = combine_dilations(outs, lses)
    y = mixer_output(x, o_a, o_b, ga, gb, ma, mb, w_up_a, w_up_b, w_out)
    return y, states


def sample_layer(x, caches, rel_bias, norm_g, w_in, q_gain_a, k_gain_a, sinks_a, q_gain_b, k_gain_b, w_up_a, w_up_b, w_out):
    qa, ka, va, ga, qb, kb, vb, gb, ma, mb = mixer_inputs(x, norm_g, w_in, q_gain_a, k_gain_a, q_gain_b, k_gain_b)
    sink = sinks_a.reshape(A_KV_HEADS, A_GROUP, 1, 1).astype(jnp.float32)
    cache_a = caches[0]
    o_a, _ = gathered_window_attention(qa, cache_a[:, :, 0], cache_a[:, :, 1], ka, va, A_DILATION,
                                       A_WINDOW // A_DILATION, rel_bias[:, :A_Q_HEADS], sink)
    states = [jnp.stack([ka, va], axis=2)]
    outs, lses = [], []
    for gi, (win, dil) in enumerate(B_GROUPS):
        c0 = A_Q_HEADS + gi * B_HEADS
        cache_g = caches[1 + gi]
        o, lse = gathered_window_attention(qb[:, :, gi, :, None, :], cache_g[:, :, 0], cache_g[:, :, 1],
                                           kb[:, :, gi], vb[:, :, gi], dil, win // dil,
                                           rel_bias[:, c0:c0 + B_HEADS], None)
        outs.append(o)
        lses.append(lse)
        states.append(jnp.stack([kb[:, :, gi], vb[:, :, gi]], axis=2))
    o_b = combine_dilations(outs, lses)
    y = mixer_output(x, o_a, o_b, ga, gb, ma, mb, w_up_a, w_up_b, w_out)
    return y, states


def setup_inputs(seed: int = 0) -> dict:
    key = jax.random.key(seed)
    ks = jax.random.split(key, 20)
    f32 = jnp.float32
    la = min(A_WINDOW, PAST_LEN)
    l1 = min(B_GROUPS[0][0], PAST_LEN)
    l2 = min(B_GROUPS[1][0], PAST_LEN)
    l3 = min(B_GROUPS[2][0], PAST_LEN)
    return {
        "x_prompt": jax.random.normal(ks[0], (BATCH, SEQ, D_MODEL), f32),
        "x_sample": jax.random.normal(ks[1], (DEC_BATCH, DEC_SEQ, D_MODEL), f32),
        "cache_a_kv": jax.random.normal(ks[2], (DEPTH, DEC_BATCH, la, 2, A_KV_HEADS, HEAD_DIM), f32),
        "cache_b1_kv": jax.random.normal(ks[3], (DEPTH, DEC_BATCH, l1, 2, B_HEADS, HEAD_DIM), f32),
        "cache_b2_kv": jax.random.normal(ks[4], (DEPTH, DEC_BATCH, l2, 2, B_HEADS, HEAD_DIM), f32),
        "cache_b3_kv": jax.random.normal(ks[5], (DEPTH, DEC_BATCH, l3, 2, B_HEADS, HEAD_DIM), f32),
        "rel_bias": 0.5 * jax.random.normal(ks[6], (N_BUCKETS, H_TOTAL), f32),
        "norm_gain": 1.0 + 0.1 * jax.random.normal(ks[7], (DEPTH, D_MODEL), f32),
        "w_in": jax.random.normal(ks[8], (DEPTH, D_MODEL, C_IN), f32) * D_MODEL ** -0.5,
        "q_gain_a": 1.0 + 0.1 * jax.random.normal(ks[9], (DEPTH, HEAD_DIM), f32),
        "k_gain_a": 1.0 + 0.1 * jax.random.normal(ks[10], (DEPTH, HEAD_DIM), f32),
        "sinks_a": 0.5 * jax.random.normal(ks[11], (DEPTH, A_Q_HEADS), f32),
        "q_gain_b": 1.0 + 0.1 * jax.random.normal(ks[12], (DEPTH, N_B_GROUPS, HEAD_DIM), f32),
        "k_gain_b": 1.0 + 0.1 * jax.random.normal(ks[13], (DEPTH, N_B_GROUPS, HEAD_DIM), f32),
        "w_up_a": jax.random.normal(ks[14], (DEPTH, A_WIDTH, D_MODEL), f32) * A_WIDTH ** -0.5,
        "w_up_b": jax.random.normal(ks[15], (DEPTH, B_WIDTH, D_MODEL), f32) * B_WIDTH ** -0.5,
        "w_out": jax.random.normal(ks[16], (DEPTH, D_MODEL, D_MODEL), f32) * D_MODEL ** -0.5,
    }


def reference(x_prompt, x_sample, cache_a_kv, cache_b1_kv, cache_b2_kv, cache_b3_kv, rel_bias, norm_gain, w_in,
              q_gain_a, k_gain_a, sinks_a, q_gain_b, k_gain_b, w_up_a, w_up_b, w_out):
    yp = x_prompt
    ys = x_sample
    p_states = []
    s_states = []
    for layer in range(DEPTH):
        params = (norm_gain[layer], w_in[layer], q_gain_a[layer], k_gain_a[layer], sinks_a[layer],
                  q_gain_b[layer], k_gain_b[layer], w_up_a[layer], w_up_b[layer], w_out[layer])
        yp, ps = prompt_layer(yp, rel_bias, *params)
        caches = (cache_a_kv[layer], cache_b1_kv[layer], cache_b2_kv[layer], cache_b3_kv[layer])
        ys, ss = sample_layer(ys, caches, rel_bias, *params)
        p_states.append(ps)
        s_states.append(ss)
    new_prompt_a_kv = jnp.stack([st[0] for st in p_states])
    new_prompt_b1_kv = jnp.stack([st[1] for st in p_states])
    new_prompt_b2_kv = jnp.stack([st[2] for st in p_states])
    new_prompt_b3_kv = jnp.stack([st[3] for st in p_states])
    new_sample_a_kv = jnp.stack([st[0] for st in s_states])
    new_sample_b1_kv = jnp.stack([st[1] for st in s_states])
    new_sample_b2_kv = jnp.stack([st[2] for st in s_states])
    new_sample_b3_kv = jnp.stack([st[3] for st in s_states])
    return (yp, ys, new_prompt_a_kv, new_prompt_b1_kv, new_prompt_b2_kv, new_prompt_b3_kv,
            new_sample_a_kv, new_sample_b1_kv, new_sample_b2_kv, new_sample_b3_kv)
```

```python
import math
import os
from contextlib import ExitStack

import numpy as np
import concourse.bass as bass
import concourse.mybir as mybir
from concourse.bass_utils import run_bass_kernel_spmd

F32 = mybir.dt.float32
BF16 = mybir.dt.bfloat16
AF = mybir.ActivationFunctionType
ALU = mybir.AluOpType
AX = mybir.AxisListType

SAME_ENGINE_SYNC = True
LEVEL = int(os.environ.get('KDBG_LEVEL', '99'))
NCLS = int(os.environ.get('KDBG_NCLS', '99'))
WITH_CACHE = os.environ.get('KDBG_NOSAMPLE', '0') != '1'
SUB = int(os.environ.get('KDBG_SUB', '99'))
XB = int(os.environ.get('KDBG_X', '0'))
EPS = 1e-6
NOWN = 2048
NB = 16
C_QA, C_KA, C_VA, C_GA = 0, 512, 640, 768
C_QB, C_KB, C_VB, C_GB, C_MA, C_MB = 1280, 2816, 4352, 5888, 6400, 7424


class Sched:
    ENGS = ("pe", "act", "dve", "pool", "sp")

    def __init__(self, nc, n_dma_sems=6):
        self.nc = nc
        self.ops = {e: [] for e in self.ENGS}
        self.lastw = {}
        self.readers = {}
        self.n_dma_sems = n_dma_sems
        self.dma_count = {"sp": 0, "pool": 0, "act": 0}
        self.dma_last = {}

    def last_all(self):
        return [(e, len(self.ops[e]) - 1) for e in self.ENGS if self.ops[e]]

    def op(self, eng, fn, reads=(), writes=(), dma=False, extra=()):
        ops = self.ops[eng]
        idx = len(ops)
        deps = set(extra)
        for b in reads:
            w = self.lastw.get(b)
            if w is not None:
                deps.add(w)
        for b in writes:
            w = self.lastw.get(b)
            if w is not None:
                deps.add(w)
            for r in self.readers.get(b, ()):
                deps.add(r)
        cdeps = {}
        ddeps = set()
        for (e, i) in deps:
            o = self.ops[e][i]
            if o["dma"]:
                ddeps.add(o["sig"])
            else:
                if e == eng and not dma and (e == "pe" or not SAME_ENGINE_SYNC):
                    continue
                cdeps[e] = max(cdeps.get(e, -1), i)
        rec = {"fn": fn, "dma": dma, "cdeps": cdeps, "ddeps": ddeps, "sig": None, "signaled": False}
        if dma:
            n = self.dma_count[eng]
            self.dma_count[eng] = n + 1
            slot = n % self.n_dma_sems
            val = 16 * (n // self.n_dma_sems + 1)
            rec["sig"] = (eng, slot, val)
            if val > 16:
                ddeps.add((eng, slot, val - 16))
            self.dma_last[(eng, slot)] = val
        ops.append(rec)
        me = (eng, idx)
        for b in writes:
            self.lastw[b] = me
            self.readers[b] = []
        for b in reads:
            if b in writes:
                continue
            self.readers.setdefault(b, []).append(me)
        return me

    def emit(self, sems, dsems, block):
        for e in self.ENGS:
            for o in self.ops[e]:
                for (de, di) in o["cdeps"].items():
                    self.ops[de][di]["signaled"] = True
        for e in self.ENGS:
            c = 0
            for o in self.ops[e]:
                if o["dma"]:
                    continue
                if o["signaled"]:
                    c += 1
                    o["sig"] = c
        allops = self.ops
        dma_last = self.dma_last

        def run(eng_name, eng):
            waited = {}
            for o in allops[eng_name]:
                for (de, di) in sorted(o["cdeps"].items()):
                    v = allops[de][di]["sig"]
                    key = ("c", de)
                    if waited.get(key, 0) >= v:
                        continue
                    eng.wait_ge(sems[de], v)
                    waited[key] = v
                for (qe, slot, v) in sorted(o["ddeps"]):
                    key = ("d", qe, slot)
                    if waited.get(key, 0) >= v:
                        continue
                    eng.wait_ge(dsems[(qe, slot)], v)
                    waited[key] = v
                ins = o["fn"](eng)
                if o["dma"]:
                    qe, slot, v = o["sig"]
                    ins.then_inc(dsems[(qe, slot)], 16)
                elif o["signaled"]:
                    ins.then_inc(sems[eng_name], 1)
            if eng_name == "sp":
                for (qe, slot), v in sorted(dma_last.items()):
                    if waited.get(("d", qe, slot), 0) >= v:
                        continue
                    eng.wait_ge(dsems[(qe, slot)], v)

        @block.tensor
        def _(eng):
            run("pe", eng)

        @block.scalar
        def _(eng):
            run("act", eng)

        @block.vector
        def _(eng):
            run("dve", eng)

        @block.gpsimd
        def _(eng):
            run("pool", eng)

        @block.sync
        def _(eng):
            run("sp", eng)


def t5_bucket_np(dist):
    d = np.maximum(dist, 0)
    df = np.maximum(d, 1).astype(np.float32)
    large = 16 + (np.log(df / np.float32(16)) / np.float32(math.log(2048 / 16)) * np.float32(16)).astype(np.int32)
    large = np.minimum(large, 31)
    return np.where(d < 16, d, large)


def onehot_tables():
    oh = np.zeros((3, 128, 384), np.float32)
    for di, dil in enumerate((1, 4, 16)):
        delta = np.arange(128)
        b = t5_bucket_np(delta * dil)
        oh[di, b, delta + 127] = 1.0
    return oh


GROUPS = {
    "b3": dict(d=16, di=2, hb=24, cq=C_QB + 1024, ck=C_KB + 1024, cv=C_VB + 1024, nkv=8),
    "b2": dict(d=4, di=1, hb=16, cq=C_QB + 512, ck=C_KB + 512, cv=C_VB + 512, nkv=8),
    "b1": dict(d=1, di=0, hb=8, cq=C_QB, ck=C_KB, cv=C_VB, nkv=8),
    "a": dict(d=1, di=0, hb=0, cq=C_QA, ck=C_KA, cv=C_VA, nkv=2),
}


def build_program(with_sample=True):
    nc = bass.Bass("TRN2", target_bir_lowering=False)

    def din(name, shape):
        return nc.dram_tensor(name, shape, F32, kind="ExternalInput").ap()

    def dout(name, shape):
        return nc.dram_tensor(name, shape, F32, kind="ExternalOutput").ap()

    x_ext = din("x_ext", [4096, 1024])
    x_s = din("x_s", [128, 1024])
    hv_in = din("hv", [128, 1])
    mask_in = din("mask2", [128, 2])
    w_in = din("w_in", [1024, 8448])
    ng_in = din("ng", [128, 8])
    relb_in = din("relb", [128, 128])
    oh_in = din("oh", [3, 128, 384])
    gq_a_in = din("gq_a", [128, 64])
    gk_a_in = din("gk_a", [128, 64])
    gq_b_in = din("gq_b", [128, 192])
    gk_b_in = din("gk_b", [128, 192])
    sinks_in = din("sinks", [128, 8])
    wup_a_in = din("wup_a", [512, 1024])
    wup_b_in = din("wup_b", [512, 1024])
    wout_in = din("wout", [1024, 1024])
    if WITH_CACHE:
        c_a_in = din("c_a", [16, 128, 256])
        c_b1_in = din("c_b1", [16, 128, 1024])
        c_b2_in = din("c_b2", [16, 512, 1024])
        c_b3_in = din("c_b3", [16, 2048, 1024])

    y_out = dout("y", [2048, 1024])
    ys_out = dout("ys", [64, 1024])
    nkv_out = {"a": dout("nkv_a", [128, 2, 128]), "b1": dout("nkv_b1", [128, 2, 512]),
               "b2": dout("nkv_b2", [512, 2, 512]), "b3": dout("nkv_b3", [2048, 2, 512])}
    nskv_out = {"a": dout("ns_a", [64, 2, 128]), "b1": dout("ns_b1", [64, 2, 512]),
                "b2": dout("ns_b2", [64, 2, 512]), "b3": dout("ns_b3", [64, 2, 512])}

    DBG = os.environ.get('KDBG_DUMP', '0') == '1'
    if DBG:
        dbg_o = dout("dbgo", [4, 128, 520])
    NTS = NOWN + 128
    Oscr = {g: nc.dram_tensor("oscr_" + g, [NTS, 520], F32) for g in ("b1", "b2", "b3")}
    Oa_scr = nc.dram_tensor("oscr_a", [NTS, 512], F32)
    EFscr = nc.dram_tensor("efscr", [3, 32, 384], F32)

    S = Sched(nc, n_dma_sems=6)
    rr = {"ev": 0}

    with ExitStack() as es:
        def sb(name, shape, dt):
            return es.enter_context(nc.sbuf_tensor(name, shape, dt))

        def ps(name, shape, dt):
            return es.enter_context(nc.psum_tensor(name, shape, dt))

        sems = {e: es.enter_context(nc.semaphore("s_" + e)) for e in ("pe", "act", "dve", "pool")}
        dsems = {(q, i): es.enter_context(nc.semaphore(f"d_{q}{i}")) for q in ("sp", "pool") for i in range(6)}

        xT_own = sb("xT_own", [128, 8, NOWN], BF16)
        xT_s = sb("xT_s", [128, 8, 128], BF16)
        Xs_f = sb("Xs_f", [128, 1024], F32)
        ident = sb("ident", [128, 128], BF16)
        Jm = sb("Jm", [128, 128], BF16)
        identf = sb("identf", [128, 128], F32)
        NG = sb("NG", [128, 8], F32)
        HV = sb("HV", [128, 1], F32)
        MASK = sb("MASK", [128, 2], F32)
        GQA = sb("GQA", [128, 64], F32)
        GKA = sb("GKA", [128, 64], F32)
        GQB = sb("GQB", [128, 192], F32)
        GKB = sb("GKB", [128, 192], F32)
        SNK = sb("SNK", [128, 8], F32)
        PJ = ps("PJ", [128, 1536], F32)
        TR = ps("TR", [128, 1024], BF16)
        SPp = ps("SPp", [128, 1024], F32)
        OPp = ps("OPp", [128, 1024], F32)

        block = es.enter_context(nc.Block())

        S.op("sp", lambda e: e.dma_start(out=NG[:], in_=ng_in), writes=["NG"], dma=True)
        S.op("sp", lambda e: e.dma_start(out=HV[:], in_=hv_in), writes=["HV"], dma=True)
        S.op("sp", lambda e: e.dma_start(out=MASK[:], in_=mask_in), writes=["MASK"], dma=True)
        S.op("sp", lambda e: e.dma_start(out=GQA[:], in_=gq_a_in), writes=["GQA"], dma=True)
        S.op("sp", lambda e: e.dma_start(out=GKA[:], in_=gk_a_in), writes=["GKA"], dma=True)
        S.op("sp", lambda e: e.dma_start(out=GQB[:], in_=gq_b_in), writes=["GQB"], dma=True)
        S.op("sp", lambda e: e.dma_start(out=GKB[:], in_=gk_b_in), writes=["GKB"], dma=True)
        S.op("sp", lambda e: e.dma_start(out=SNK[:], in_=sinks_in), writes=["SNK"], dma=True)
        S.op("dve", lambda e: e.tensor_scalar(out=GQA[:], in0=GQA[:], scalar1=0.125, scalar2=None, op0=ALU.mult), reads=["GQA"], writes=["GQA"])
        S.op("dve", lambda e: e.tensor_scalar(out=GQB[:], in0=GQB[:], scalar1=0.125, scalar2=None, op0=ALU.mult), reads=["GQB"], writes=["GQB"])
        S.op("act", lambda e: e.activation(out=SNK[:], in_=SNK[:], func=AF.Exp), reads=["SNK"], writes=["SNK"])
        S.op("pool", lambda e: e.memset(identf[:], 1.0), writes=["identf"])
        S.op("pool", lambda e: e.affine_select(out=identf[:], in_=identf[:], pattern=[[-1, 128]], compare_op=ALU.is_equal, fill=0.0, base=0, channel_multiplier=1), reads=["identf"], writes=["identf"])
        S.op("dve", lambda e: e.tensor_copy(out=ident[:], in_=identf[:]), reads=["identf"], writes=["ident"])
        S.op("pool", lambda e: e.memset(identf[:], 1.0), reads=["identf"], writes=["identf"])
        S.op("pool", lambda e: e.affine_select(out=identf[:], in_=identf[:], pattern=[[1, 128]], compare_op=ALU.is_equal, fill=0.0, base=-127, channel_multiplier=1), reads=["identf"], writes=["identf"])
        S.op("dve", lambda e: e.tensor_copy(out=Jm[:], in_=identf[:]), reads=["identf"], writes=["Jm"])

        RB = sb("RB", [128, 128], F32)
        OHs = sb("OHs", [128, 384], F32)
        EFs = sb("EFs", [128, 384], F32)
        if True:
            S.op("sp", lambda e: e.dma_start(out=RB[:], in_=relb_in), writes=["RB"], dma=True)
            for di in range(3):
                S.op("sp", lambda e, di=di: e.dma_start(out=OHs[:], in_=oh_in[di]), writes=["OHs"], dma=True)
                S.op("pe", lambda e: e.matmul(SPp[:, 0:384], lhsT=RB[:], rhs=OHs[:], start=True, stop=True), reads=["RB", "OHs"], writes=["SP0", "SP1"])
                S.op("act", lambda e: e.activation(out=EFs[:], in_=SPp[:, 0:384], func=AF.Exp), reads=["SP0", "SP1"], writes=["EFs"])
                S.op("dve", lambda e: e.memset(EFs[:, 0:127], 0.0), reads=["EFs"], writes=["EFs"])
                S.op("dve", lambda e: e.memset(EFs[:, 255:384], 0.0), reads=["EFs"], writes=["EFs"])
                S.op("sp", lambda e, di=di: e.dma_start(out=EFscr.ap()[di], in_=EFs[0:32, :]), reads=["EFs"], writes=[("EFscr", di)], dma=True)

        with ExitStack() as esA:
            def sbA(name, shape, dt):
                return esA.enter_context(nc.sbuf_tensor(name, shape, dt))

            xT_halo = sbA("xT_halo", [128, 8, NOWN], BF16)
            Wsb = sbA("Wsb", [128, 8, 1536], BF16)
            Wst = [sbA("Wst%d" % i, [128, 1536], F32) for i in range(2)]
            Xf = [sbA("Xf%d" % i, [128, 1024], F32) for i in range(2)]
            SQ = sbA("SQ", [128, 1024], F32)
            Xb = sbA("Xb", [128, 1024], BF16)
            SS = sbA("SS", [128, 16], F32)
            RS = sbA("RS", [128, 16], F32)
            QN = sbA("QN", [128, 1024], F32)
            QNb = sbA("QNb", [128, 512], BF16)
            KN = sbA("KN", [128, 512], F32)
            KNb = sbA("KNb", [128, 512], BF16)
            VF = sbA("VF", [128, 512], F32)
            QT = sbA("QT", [128, 4, 128], BF16)
            Kpad = [sbA("Kpad%d" % i, [128, 8, 128], BF16) for i in range(2)]
            V1 = [sbA("V1_%d" % i, [128, 8, 65], BF16) for i in range(2)]
            Et = sbA("Et", [128, 8, 256], BF16)
            Eh = sbA("Eh", [128, 256], F32)
            Ehb = sbA("Ehb", [128, 256], BF16)
            PEx = sbA("PEx", [128, 1024], BF16)
            Pt = [sbA("Pt%d" % i, [128, 1024], BF16) for i in range(2)]
            Ost = [sbA("Ost%d" % i, [128, 8, 65], F32) for i in range(2)]
            LL = sbA("LL", [128, 8], F32)
            OaT = [sbA("OaT%d" % i, [128, 512], F32) for i in range(2)]
            Ksb = [sbA("Ksb%d" % i, [128, 512], BF16) for i in range(2)]
            KTs = [sbA("KTs%d" % i, [128, 512], BF16) for i in range(2)]
            V1s = [sbA("V1s%d" % i, [128, 8, 65], BF16) for i in range(8)]
            Qbd = sbA("Qbd", [128, 4, 128, 2], BF16)
            PEs = sbA("PEs", [128, 32], F32)
            Zb = sbA("Zb", [128, 32, 192], BF16)
            S.op("pool", lambda e: e.memset(Zb[:], 0.0), writes=["Zb"])
            OsAcc = sbA("OsAcc", [128, 8, 65], F32)
            for i in range(8):
                S.op("pool", lambda e, i=i: e.memset(V1s[i][:], 1.0), writes=[("V1s", i)])

            for i in range(2):
                S.op("pool", lambda e, i=i: e.memset(Kpad[i][:], 0.0), writes=[("Kpad", i)])
                S.op("pool", lambda e, i=i: e.memset(V1[i][:], 1.0), writes=[("V1", i)])

            def prologue(src_ap, dst_tile, dst_key, col0, k, keep_f32=None):
                xf = Xf[k % 2] if keep_f32 is None else keep_f32
                xkey = ("Xf", k % 2) if keep_f32 is None else "Xs_f"
                S.op("sp", lambda e: e.dma_start(out=xf[:], in_=src_ap), writes=[xkey], dma=True)
                S.op("act", lambda e: e.activation(out=SQ[:], in_=xf[:], func=AF.Square), reads=[xkey], writes=["SQ"])
                S.op("dve", lambda e: e.reduce_sum(out=SS[:, 0:1], in_=SQ[:], axis=AX.X), reads=["SQ"], writes=["SS"])
                S.op("act", lambda e: e.activation(out=RS[:, 0:1], in_=SS[:, 0:1], func=AF.Sqrt, bias=EPS, scale=1.0 / 1024), reads=["SS"], writes=["RS"])
                S.op("dve", lambda e: e.reciprocal(out=RS[:, 0:1], in_=RS[:, 0:1]), reads=["RS"], writes=["RS"])
                S.op("dve", lambda e: e.tensor_scalar(out=Xb[:], in0=xf[:], scalar1=RS[:, 0:1], scalar2=None, op0=ALU.mult), reads=[xkey, "RS"], writes=["Xb"])
                for c in range(8):
                    S.op("pe", lambda e, c=c: e.transpose(out=TR[:, c * 128:(c + 1) * 128], in_=Xb[:, c * 128:(c + 1) * 128], identity=ident[:]), reads=["Xb", "ident"], writes=["TR"])
                S.op("act", lambda e: e.activation(out=dst_tile[:, :, col0:col0 + 128], in_=TR[:].rearrange("p (c t) -> p c t", c=8), func=AF.Copy), reads=["TR"], writes=[dst_key])

            if LEVEL >= 2:
                for t in range(NB):
                    prologue(x_ext[t * 128:(t + 1) * 128, :], xT_halo, ("xTh", t), t * 128, t)
                for t in range(NB):
                    prologue(x_ext[NOWN + t * 128:NOWN + (t + 1) * 128, :], xT_own, ("xTo", t), t * 128, t)
                prologue(x_s, xT_s, "xTs", 0, 0, keep_f32=Xs_f)
            XTH_ALL = [("xTh", t) for t in range(NB)]
            XTO_ALL = [("xTo", t) for t in range(NB)]

            wcnt = {"n": 0}

            def load_w(dst, col_ranges, key):
                off = 0
                for (c0, n) in col_ranges:
                    for c in range(8):
                        k = wcnt["n"] % 2
                        wcnt["n"] += 1
                        st = Wst[k]
                        S.op("sp", lambda e, c=c, c0=c0, n=n, st=st: e.dma_start(out=st[:, 0:n], in_=w_in[c * 128:(c + 1) * 128, c0:c0 + n]), writes=[("Wst", k)], dma=True)
                        eng = "pool" if (wcnt["n"] % 2) else "dve"
                        S.op(eng, lambda e, c=c, n=n, st=st, off=off: e.tensor_scalar(out=dst[:, c, off:off + n], in0=st[:, 0:n], scalar1=NG[:, c:c + 1], scalar2=None, op0=ALU.mult), reads=[("Wst", k), "NG"], writes=[key])
                    off += n

            def inproj(lhs_fn, xkeys, ncols, wtile=None, wkey="W", wcol0=0):
                wt = Wsb if wtile is None else wtile
                ng_ = (ncols + 511) // 512
                for g in range(ng_):
                    n = min(512, ncols - g * 512)
                    for c in range(8):
                        S.op("pe", lambda e, g=g, c=c, n=n: e.matmul(PJ[:, g * 512:g * 512 + n], lhsT=lhs_fn(c), rhs=wt[:, c, wcol0 + g * 512:wcol0 + g * 512 + n], start=(c == 0), stop=(c == 7)),
                             reads=list(xkeys) + [wkey], writes=[("PJ", g)])

            def build_E(gname):
                G = GROUPS[gname]
                for h in range(8):
                    src = bass.AP(tensor=EFscr, offset=(G["di"] * 32 + G["hb"] + h) * 384, ap=[[1, 128], [128, 2], [1, 128]])
                    S.op("sp", lambda e, src=src: e.dma_start(out=Eh[:].rearrange("p (a b) -> p a b", a=2), in_=src), reads=[("EFscr", G["di"])], writes=["Eh"], dma=True)
                    S.op("act", lambda e: e.activation(out=Ehb[:], in_=Eh[:], func=AF.Copy), reads=["Eh"], writes=["Ehb"])
                    S.op("pe", lambda e: e.matmul(SPp[:, 0:256], lhsT=Jm[:], rhs=Ehb[:], start=True, stop=True), reads=["Jm", "Ehb"], writes=["SP0"])
                    S.op("dve", lambda e, h=h: e.tensor_copy(out=Et[:, h, :], in_=SPp[:, 0:256]), reads=["SP0"], writes=["Et"])

            def qkv_block(gname, lhs_fn, xkeys, par, want_q, kv_out=None, nrows=128):
                G = GROUPS[gname]
                isA = gname == "a"
                nq = 512
                nk = 128 if isA else 512
                nkh = nk // 64
                if want_q:
                    ncols = nq + 2 * nk
                    qo, ko, vo = 0, nq, nq + nk
                    wc0 = 0
                else:
                    ncols = 2 * nk
                    qo, ko, vo = None, 0, nk
                    wc0 = nq
                inproj(lhs_fn, xkeys, ncols, wcol0=wc0)
                ngrp = (ncols + 511) // 512
                pjk = [("PJ", g) for g in range(ngrp)]
                nn = (nq + nk) if want_q else nk
                nh = nn // 64
                S.op("act", lambda e: e.activation(out=SQ[:, 0:nn], in_=PJ[:, 0:nn], func=AF.Square), reads=pjk, writes=["SQ"])
                S.op("dve", lambda e: e.reduce_sum(out=SS[:, 0:nh], in_=SQ[:, 0:nn].rearrange("p (h d) -> p h d", d=64), axis=AX.X), reads=["SQ"], writes=["SS"])
                S.op("act", lambda e: e.activation(out=RS[:, 0:nh], in_=SS[:, 0:nh], func=AF.Sqrt, bias=EPS, scale=1.0 / 64), reads=["SS"], writes=["RS"])
                S.op("dve", lambda e: e.reciprocal(out=RS[:, 0:nh], in_=RS[:, 0:nh]), reads=["RS"], writes=["RS"])
                S.op("dve", lambda e: e.tensor_tensor(out=QN[:, 0:nn].rearrange("p (h d) -> p h d", d=64), in0=PJ[:, 0:nn].rearrange("p (h d) -> p h d", d=64),
                                                       in1=RS[:, 0:nh].unsqueeze(2).broadcast_to([128, nh, 64]), op=ALU.mult), reads=pjk + ["RS"], writes=["QN"])
                if isA:
                    gq, gk = GQA[:, :], GKA[:, :]
                else:
                    gi = {"b1": 0, "b2": 1, "b3": 2}[gname]
                    gq, gk = GQB[:, gi * 64:(gi + 1) * 64], GKB[:, gi * 64:(gi + 1) * 64]
                if want_q and not (XB & 2):
                    S.op("pool", lambda e: e.tensor_tensor(out=QNb[:].rearrange("p (h d) -> p h d", d=64), in0=QN[:, 0:512].rearrange("p (h d) -> p h d", d=64),
                                                            in1=gq.unsqueeze(1).broadcast_to([128, 8, 64]), op=ALU.mult), reads=["QN", "GQA", "GQB"], writes=["QNb"])
                S.op("pool", lambda e: e.tensor_tensor(out=KN[:, 0:nk].rearrange("p (h d) -> p h d", d=64), in0=QN[:, ko:ko + nk].rearrange("p (h d) -> p h d", d=64),
                                                        in1=gk.unsqueeze(1).broadcast_to([128, nkh, 64]), op=ALU.mult), reads=["QN", "GKA", "GKB"], writes=["KN"])
                S.op("act", lambda e: e.activation(out=KNb[:, 0:nk], in_=KN[:, 0:nk], func=AF.Copy), reads=["KN"], writes=["KNb"])
                S.op("dve", lambda e: e.tensor_copy(out=V1[par][:, 0:nkh, 0:64], in_=PJ[:, vo:vo + nk].rearrange("p (h d) -> p h d", d=64)), reads=pjk, writes=[("V1", par)])
                if kv_out is not None and not (XB & 1):
                    S.op("act", lambda e: e.activation(out=VF[:, 0:nk], in_=PJ[:, vo:vo + nk], func=AF.Copy), reads=pjk, writes=["VF"])
                    ko_ap, vo_ap = kv_out
                    S.op("sp", lambda e: e.dma_start(out=ko_ap, in_=KN[0:nrows, 0:nk]), reads=["KN"], dma=True)
                    S.op("sp", lambda e: e.dma_start(out=vo_ap, in_=VF[0:nrows, 0:nk]), reads=["VF"], dma=True)
                if want_q and not (XB & 4):
                    for t in range(4):
                        src = QNb[:, t * 128:(t + 1) * 128]
                        S.op("pe", lambda e, t=t, src=src: e.transpose(out=TR[:, t * 128:(t + 1) * 128], in_=src, identity=ident[:]), reads=["QNb", "ident"], writes=["TR"])
                nkt = nk // 128
                for t in range(nkt):
                    S.op("pe", lambda e, t=t: e.transpose(out=TR[:, (4 + t) * 128:(5 + t) * 128], in_=KNb[:, t * 128:(t + 1) * 128], identity=ident[:]), reads=["KNb", "ident"], writes=["TR"])
                if want_q and not (XB & 8):
                    S.op("dve", lambda e: e.tensor_copy(out=QT[:].rearrange("p t k -> p (t k)"), in_=TR[:, 0:512]), reads=["TR"], writes=["QT"])
                trk = TR[:, 512:512 + nkt * 128].rearrange("p (t k) -> p t k", t=nkt)
                kp = Kpad[par][:, 0:2 * nkt, :].rearrange("p (t s) k -> p t s k", s=2)
                S.op("dve", lambda e: e.tensor_scalar(out=kp[:, :, 0, :], in0=trk, scalar1=MASK[:, 0:1], scalar2=None, op0=ALU.mult), reads=["TR", "MASK"], writes=[("Kpad", par)])
                S.op("act", lambda e: e.activation(out=kp[:, :, 1, :], in_=trk, func=AF.Copy, scale=MASK[:, 1:2]), reads=["TR", "MASK"], writes=[("Kpad", par)])

            def attend(gname, par, first, out_rows):
                isA = gname == "a"
                for hh in range(2):
                    for hl in range(4):
                        h = hh * 4 + hl
                        kidx = (h // 4) if isA else h
                        qt = (h % 4) if isA else (h // 2)
                        for blk in range(2):
                            pp = par if blk == 0 else 1 - par
                            S.op("pe", lambda e, hl=hl, blk=blk, pp=pp, kidx=kidx, qt=qt: e.matmul(SPp[:, hl * 256 + blk * 128: hl * 256 + (blk + 1) * 128], lhsT=Kpad[pp][:, kidx, :], rhs=QT[:, qt, :], start=True, stop=True),
                                 reads=[("Kpad", pp), "QT"], writes=["SP0", "SP1"])
                    S.op("act", lambda e: e.activation(out=PEx[:], in_=SPp[:], func=AF.Exp), reads=["SP0", "SP1"], writes=["PEx"])
                    S.op("dve", lambda e, hh=hh: e.tensor_tensor(out=Pt[hh][:], in0=PEx[:], in1=Et[:, hh * 4:(hh + 1) * 4, :].rearrange("p h k -> p (h k)"), op=ALU.mult), reads=["PEx", "Et"], writes=[("Pt", hh)])
                    if first:
                        pv = Pt[hh][:].rearrange("p (h b q) -> p h b q", h=4, b=2)[:, :, 1, :]
                        S.op("dve", lambda e, pv=pv: e.tensor_scalar(out=pv, in0=pv, scalar1=HV[:, 0:1], scalar2=None, op0=ALU.mult), reads=[("Pt", hh), "HV"], writes=[("Pt", hh)])
                    for hl in range(4):
                        h = hh * 4 + hl
                        kv = (h // 4) if isA else h
                        for blk in range(2):
                            pp = par if blk == 0 else 1 - par
                            S.op("pe", lambda e, h=h, hl=hl, blk=blk, pp=pp, kv=kv, hh=hh: e.matmul(OPp[:, h * 128:h * 128 + 65], lhsT=Pt[hh][:, hl * 256 + blk * 128: hl * 256 + (blk + 1) * 128], rhs=V1[pp][:, kv, :], start=(blk == 0), stop=(blk == 1)),
                                 reads=[("Pt", hh), ("V1", pp)], writes=[("OP", h // 4)])
                opk = [("OP", 0), ("OP", 1)]
                opv = OPp[:].rearrange("p (h c) -> p h c", h=8)
                k = rr["ev"] % 2
                rr["ev"] += 1
                if isA:
                    S.op("dve", lambda e: e.tensor_tensor(out=LL[:], in0=opv[:, :, 64], in1=SNK[:], op=ALU.add), reads=opk + ["SNK"], writes=["LL"])
                    S.op("dve", lambda e: e.reciprocal(out=LL[:], in_=LL[:]), reads=["LL"], writes=["LL"])
                    S.op("dve", lambda e: e.tensor_tensor(out=OaT[k][:].rearrange("p (h d) -> p h d", d=64), in0=opv[:, :, 0:64], in1=LL[:].unsqueeze(2).broadcast_to([128, 8, 64]), op=ALU.mult), reads=opk + ["LL"], writes=[("OaT", k)])
                    S.op("sp", lambda e: e.dma_start(out=out_rows, in_=OaT[k][:]), reads=[("OaT", k)], writes=["OSCR_a"], dma=True)
                else:
                    S.op("act", lambda e: e.activation(out=Ost[k][:], in_=opv[:, :, 0:65], func=AF.Copy), reads=opk, writes=[("Ost", k)])
                    S.op("sp", lambda e: e.dma_start(out=out_rows, in_=Ost[k][:].rearrange("p h c -> p (h c)")), reads=[("Ost", k)], writes=["OSCR_" + gname], dma=True)

            def sample_attn(gname):
                G = GROUPS[gname]
                d = G["d"]
                isA = gname == "a"
                nk = 128 if isA else 512
                kvw = 2 * nk
                nkt = nk // 128
                nkh = nk // 64
                cache = {"a": c_a_in, "b1": c_b1_in, "b2": c_b2_in, "b3": c_b3_in}[gname]
                nsk = nskv_out[gname]
                qkv_block(gname, lambda c: xT_s[:, c, 0:128], ["xTs"], 0, want_q=True, kv_out=(nsk[:, 0, :], nsk[:, 1, :]), nrows=64)
                S.op("dve", lambda e: e.tensor_scalar(out=Qbd[:, :, :, 0], in0=QT[:], scalar1=MASK[:, 0:1], scalar2=None, op0=ALU.mult), reads=["QT", "MASK"], writes=["Qbd"])
                S.op("pool", lambda e: e.tensor_scalar(out=Qbd[:, :, :, 1], in0=QT[:], scalar1=MASK[:, 1:2], scalar2=None, op0=ALU.mult), reads=["QT", "MASK"], writes=["Qbd"])
                if isA:
                    es_v = Et[:].rearrange("p (k g) c -> p g k c", k=2)[:, :, :, 127]
                else:
                    es_v = Et[:, :, 127]
                S.op("pool", lambda e: e.memset(OsAcc[:], 0.0), writes=["OsAcc"])
                for n in range(16):
                    for t in range(4):
                        tok = 4 * n + t
                        tokc = t
                        slot = tok % 2
                        KVt = Wst[slot]
                        kvk = ("Wst", slot)
                        if d == 1:
                            npc = 127 - t
                            S.op("sp", lambda e, n=n, t=t, npc=npc, KVt=KVt: e.dma_start(out=KVt[0:npc, 0:kvw], in_=cache[n, t + 1:128, :]), writes=[kvk], dma=True)
                            S.op("sp", lambda e, n=n, t=t, npc=npc, KVt=KVt: e.dma_start(out=KVt[npc:128, 0:nk], in_=KN[4 * n:4 * n + t + 1, 0:nk]), reads=["KN"], writes=[kvk], dma=True)
                            S.op("sp", lambda e, n=n, t=t, npc=npc, KVt=KVt: e.dma_start(out=KVt[npc:128, nk:kvw], in_=VF[4 * n:4 * n + t + 1, 0:nk]), reads=["VF"], writes=[kvk], dma=True)
                        else:
                            S.op("sp", lambda e, n=n, t=t, KVt=KVt: e.dma_start(out=KVt[0:127, 0:kvw], in_=cache[n, t + d:t + d + 126 * d + 1:d, :]), writes=[kvk], dma=True)
                            S.op("sp", lambda e, tok=tok, KVt=KVt: e.dma_start(out=KVt[127:128, 0:nk], in_=KN[tok:tok + 1, 0:nk]), reads=["KN"], writes=[kvk], dma=True)
                            S.op("sp", lambda e, tok=tok, KVt=KVt: e.dma_start(out=KVt[127:128, nk:kvw], in_=VF[tok:tok + 1, 0:nk]), reads=["VF"], writes=[kvk], dma=True)
                        vs = tok % 8
                        S.op("pool", lambda e, KVt=KVt, slot=slot: e.tensor_copy(out=Ksb[slot][:, 0:nk], in_=KVt[:, 0:nk]), reads=[kvk], writes=[("Ksb", slot)])
                        S.op("pool", lambda e, KVt=KVt, vs=vs: e.tensor_copy(out=V1s[vs][:, 0:nkh, 0:64], in_=KVt[:, nk:kvw].rearrange("p (h d) -> p h d", d=64)), reads=[kvk], writes=[("V1s", vs)])
                        for tt in range(nkt):
                            S.op("pe", lambda e, tt=tt, slot=slot: e.transpose(out=TR[:, tt * 128:(tt + 1) * 128], in_=Ksb[slot][:, tt * 128:(tt + 1) * 128], identity=ident[:]), reads=[("Ksb", slot), "ident"], writes=["TR"])
                        S.op("dve", lambda e, slot=slot: e.tensor_copy(out=KTs[slot][:, 0:nk], in_=TR[:, 0:nk]), reads=["TR"], writes=[("KTs", slot)])
                        for tp in range(4):
                            kt = 0 if isA else tp
                            S.op("pe", lambda e, tp=tp, kt=kt, slot=slot, tok=tok, tokc=tokc: e.matmul(SPp[:, tokc * 8 + tp * 2: tokc * 8 + tp * 2 + 2], lhsT=KTs[slot][:, kt * 128:(kt + 1) * 128], rhs=Qbd[:, tp, tok, :], start=True, stop=True),
                                 reads=[("KTs", slot), "Qbd"], writes=["SP0"])
                    S.op("act", lambda e: e.activation(out=PEs[:], in_=SPp[:, 0:32], func=AF.Exp), reads=["SP0"], writes=["PEs"])
                    if isA:
                        S.op("dve", lambda e: e.tensor_tensor(out=Zb[:, :, 63].rearrange("p (t g k) -> p t g k", t=4, g=4), in0=PEs[:].rearrange("p (t g k) -> p t g k", t=4, g=4),
                                                               in1=es_v.unsqueeze(1).broadcast_to([128, 4, 4, 2]), op=ALU.mult), reads=["PEs", "Et"], writes=["Zb"])
                    else:
                        S.op("dve", lambda e: e.tensor_tensor(out=Zb[:, :, 63].rearrange("p (t h) -> p t h", t=4), in0=PEs[:].rearrange("p (t h) -> p t h", t=4),
                                                               in1=es_v.unsqueeze(1).broadcast_to([128, 4, 8]), op=ALU.mult), reads=["PEs", "Et"], writes=["Zb"])
                    for h in range(8):
                        if isA:
                            col = (h % 4) * 2 + h // 4
                            kv = h // 4
                        else:
                            col = h
                            kv = h
                        for t in range(4):
                            tok = 4 * n + t
                            vs = tok % 8
                            S.op("pe", lambda e, h=h, col=col, kv=kv, t=t, vs=vs, tok=tok: e.matmul(OPp[:, h * 128:h * 128 + 65], lhsT=Zb[:, t * 8 + col, 63 - tok:191 - tok], rhs=V1s[vs][:, kv, :], start=(t == 0), stop=(t == 3)),
                                 reads=["Zb", ("V1s", vs)], writes=[("OP", h // 4)])
                    S.op("dve", lambda e: e.tensor_tensor(out=OsAcc[:], in0=OsAcc[:], in1=OPp[:].rearrange("p (h c) -> p h c", h=8)[:, :, 0:65], op=ALU.add), reads=["OsAcc", ("OP", 0), ("OP", 1)], writes=["OsAcc"])
                k = rr["ev"] % 2
                rr["ev"] += 1
                if isA:
                    S.op("dve", lambda e: e.tensor_tensor(out=LL[:], in0=OsAcc[:, :, 64], in1=SNK[:], op=ALU.add), reads=["OsAcc", "SNK"], writes=["LL"])
                    S.op("dve", lambda e: e.reciprocal(out=LL[:], in_=LL[:]), reads=["LL"], writes=["LL"])
                    S.op("dve", lambda e: e.tensor_tensor(out=OaT[k][:].rearrange("p (h d) -> p h d", d=64), in0=OsAcc[:, :, 0:64], in1=LL[:].unsqueeze(2).broadcast_to([128, 8, 64]), op=ALU.mult), reads=["OsAcc", "LL"], writes=[("OaT", k)])
                    S.op("sp", lambda e: e.dma_start(out=Oa_scr.ap()[NOWN:NOWN + 128, :], in_=OaT[k][:]), reads=[("OaT", k)], writes=["OSCR_a"], dma=True)
                else:
                    S.op("sp", lambda e: e.dma_start(out=Oscr[gname].ap()[NOWN:NOWN + 128, :], in_=OsAcc[:].rearrange("p h c -> p (h c)")), reads=["OsAcc"], writes=["OSCR_" + gname], dma=True)

            def attn_phase(gname):
                G = GROUPS[gname]
                d = G["d"]
                isA = gname == "a"
                nk = 128 if isA else 512
                if isA:
                    qr = [(C_QA + kvh * 256 + g * 64, 64) for g in range(4) for kvh in range(2)]
                else:
                    qr = [(G["cq"], 512)]
                load_w(Wsb, qr + [(G["ck"], nk), (G["cv"], nk)], "W")
                if SUB >= 2:
                    build_E(gname)
                oscr = Oa_scr.ap() if isA else Oscr[gname].ap()
                nkv = nkv_out[gname]
                ncb = NB // d
                win = {"a": 128, "b1": 128, "b2": 512, "b3": 2048}[gname]
                par = 0
                for r in range(min(d, NCLS) if SUB >= 3 else 0):
                    hs = NOWN - 128 * d + r
                    qkv_block(gname, lambda c, hs=hs: xT_halo[:, c, hs:hs + 127 * d + 1:d] if d > 1 else xT_halo[:, c, hs:hs + 128], XTH_ALL, par, want_q=False)
                    par ^= 1
                    for cb in range(ncb if SUB >= 4 else 0):
                        st = r + d * 128 * cb
                        kv_out = None
                        lo = NOWN - win
                        if st >= lo:
                            r0 = st - lo
                            if d > 1:
                                kv_out = (nkv[r0:r0 + 127 * d + 1:d, 0, :], nkv[r0:r0 + 127 * d + 1:d, 1, :])
                            else:
                                kv_out = (nkv[r0:r0 + 128, 0, :], nkv[r0:r0 + 128, 1, :])
                        qkv_block(gname, (lambda c, st=st: xT_own[:, c, st:st + 127 * d + 1:d]) if d > 1 else (lambda c, st=st: xT_own[:, c, st:st + 128]), XTO_ALL, par, want_q=True, kv_out=kv_out)
                        rows = oscr[st:st + 127 * d + 1:d, :] if d > 1 else oscr[st:st + 128, :]
                        if SUB >= 5:
                            attend(gname, par, first=(cb == 0), out_rows=rows)
                        par ^= 1
                if WITH_CACHE and SUB >= 6:
                    sample_attn(gname)

            for gi_, gname in enumerate(("b3", "b2", "b1", "a")):
                if LEVEL >= 3 + gi_:
                    attn_phase(gname)

        with ExitStack() as esF:
            def sbF(name, shape, dt):
                return esF.enter_context(nc.sbuf_tensor(name, shape, dt))

            Wg = sbF("Wg", [128, 8, 3072], BF16)
            WuA = sbF("WuA", [128, 4, 1024], BF16)
            WuB = sbF("WuB", [128, 4, 1024], BF16)
            Wo = sbF("Wo", [128, 8, 1024], BF16)
            Wst = [sbF("WstF%d" % i, [128, 1536], F32) for i in range(2)]
            O1 = sbF("O1", [128, 520], F32)
            O2 = sbF("O2", [128, 520], F32)
            O3 = sbF("O3", [128, 520], F32)
            OA = sbF("OA", [128, 512], F32)
            XR = sbF("XR", [128, 1024], F32)
            SG = sbF("SG", [128, 1024], F32)
            SM = sbF("SM", [128, 2048], F32)
            LB = sbF("LB", [128, 8], F32)
            U = sbF("U", [128, 1024], BF16)
            UT = sbF("UT", [128, 8, 128], BF16)
            M1 = sbF("M1", [128, 1024], F32)
            M2 = sbF("M2", [128, 1024], F32)
            MG = sbF("MG", [128, 1024], BF16)
            MT = sbF("MT", [128, 8, 128], BF16)
            Y = sbF("Y", [128, 1024], F32)

            wc = {"n": 0}
            BAR = S.last_all()

            def load_wF(dst_fn, src_ap_fn, ncols_list, key, scale_gain):
                for (c0, n, off) in ncols_list:
                    for c in range(dst_fn("nchunk")):
                        k = wc["n"] % 2
                        wc["n"] += 1
                        st = Wst[k]
                        S.op("sp", lambda e, c=c, c0=c0, n=n, st=st: e.dma_start(out=st[:, 0:n], in_=src_ap_fn(c, c0, n)), writes=[("WstF", k)], dma=True, extra=BAR)
                        eng = "pool" if (wc["n"] % 2) else "dve"
                        if scale_gain:
                            S.op(eng, lambda e, c=c, n=n, st=st, off=off: e.tensor_scalar(out=dst_fn(c)[:, off:off + n], in0=st[:, 0:n], scalar1=NG[:, c:c + 1], scalar2=None, op0=ALU.mult), reads=[("WstF", k), "NG"], writes=[key])
                        else:
                            S.op(eng, lambda e, c=c, n=n, st=st, off=off: e.tensor_copy(out=dst_fn(c)[:, off:off + n], in_=st[:, 0:n]), reads=[("WstF", k)], writes=[key])

            load_wF(lambda c: 8 if c == "nchunk" else Wg[:, c, :], lambda c, c0, n: w_in[c * 128:(c + 1) * 128, c0:c0 + n],
                    [(C_GA, 512, 0), (C_GB, 512, 512), (C_MA, 1024, 1024), (C_MB, 1024, 2048)], "Wg", True)
            load_wF(lambda c: 4 if c == "nchunk" else WuA[:, c, :], lambda c, c0, n: wup_a_in[c * 128:(c + 1) * 128, c0:c0 + n], [(0, 1024, 0)], "WuA", False)
            load_wF(lambda c: 4 if c == "nchunk" else WuB[:, c, :], lambda c, c0, n: wup_b_in[c * 128:(c + 1) * 128, c0:c0 + n], [(0, 1024, 0)], "WuB", False)
            load_wF(lambda c: 8 if c == "nchunk" else Wo[:, c, :], lambda c, c0, n: wout_in[c * 128:(c + 1) * 128, c0:c0 + n], [(0, 1024, 0)], "Wo", False)

            def final_block(lhs_fn, xkeys, row0, x_src, y_dst, nrows=128, xres=None):
                S.op("sp", lambda e: e.dma_start(out=O1[:], in_=Oscr["b1"].ap()[row0:row0 + 128, :]), reads=["OSCR_b1"], writes=["O1"], dma=True)
                S.op("sp", lambda e: e.dma_start(out=O2[:], in_=Oscr["b2"].ap()[row0:row0 + 128, :]), reads=["OSCR_b2"], writes=["O2"], dma=True)
                S.op("sp", lambda e: e.dma_start(out=O3[:], in_=Oscr["b3"].ap()[row0:row0 + 128, :]), reads=["OSCR_b3"], writes=["O3"], dma=True)
                S.op("sp", lambda e: e.dma_start(out=OA[:], in_=Oa_scr.ap()[row0:row0 + 128, :]), reads=["OSCR_a"], writes=["OA"], dma=True)
                if xres is not None and DBG:
                    S.op("sp", lambda e: e.dma_start(out=dbg_o[0], in_=O1[:]), reads=["O1"], dma=True)
                    S.op("sp", lambda e: e.dma_start(out=dbg_o[1], in_=O2[:]), reads=["O2"], dma=True)
                    S.op("sp", lambda e: e.dma_start(out=dbg_o[2], in_=O3[:]), reads=["O3"], dma=True)
                    S.op("sp", lambda e: e.dma_start(out=dbg_o[3, :, 0:512], in_=OA[:]), reads=["OA"], dma=True)
                if xres is None:
                    S.op("sp", lambda e: e.dma_start(out=XR[:], in_=x_src), writes=["XR"], dma=True)
                    xr, xrk = XR, "XR"
                else:
                    xr, xrk = xres, "Xs_f"
                for rnd in range(2):
                    for g in range(3):
                        for c in range(8):
                            S.op("pe", lambda e, g=g, c=c, rnd=rnd: e.matmul(PJ[:, g * 512:(g + 1) * 512], lhsT=lhs_fn(c), rhs=Wg[:, c, rnd * 1536 + g * 512: rnd * 1536 + (g + 1) * 512], start=(c == 0), stop=(c == 7)),
                                 reads=list(xkeys) + ["Wg"], writes=[("PJ", g)])
                    pjk = [("PJ", g) for g in range(3)]
                    if rnd == 0:
                        S.op("act", lambda e: e.activation(out=SG[:], in_=PJ[:, 0:1024], func=AF.Silu), reads=pjk, writes=["SG"])
                        S.op("act", lambda e: e.activation(out=SM[:, 0:512], in_=PJ[:, 1024:1536], func=AF.Sigmoid), reads=pjk, writes=["SM"])
                    else:
                        S.op("act", lambda e: e.activation(out=SM[:, 512:2048], in_=PJ[:, 0:1536], func=AF.Sigmoid), reads=pjk, writes=["SM"])
                S.op("dve", lambda e: e.tensor_tensor(out=O1[:], in0=O1[:], in1=O2[:], op=ALU.add), reads=["O1", "O2"], writes=["O1"])
                S.op("dve", lambda e: e.tensor_tensor(out=O1[:], in0=O1[:], in1=O3[:], op=ALU.add), reads=["O1", "O3"], writes=["O1"])
                o1v = O1[:].rearrange("p (h c) -> p h c", c=65)
                S.op("dve", lambda e: e.reciprocal(out=LB[:], in_=o1v[:, :, 64]), reads=["O1"], writes=["LB"])
                S.op("dve", lambda e: e.tensor_tensor(out=O2[:, 0:512].rearrange("p (h d) -> p h d", d=64), in0=o1v[:, :, 0:64], in1=LB[:].unsqueeze(2).broadcast_to([128, 8, 64]), op=ALU.mult), reads=["O1", "LB"], writes=["O2"])
                S.op("pool", lambda e: e.tensor_tensor(out=U[:, 0:512], in0=OA[:], in1=SG[:, 0:512], op=ALU.mult), reads=["OA", "SG"], writes=["U"])
                S.op("dve", lambda e: e.tensor_tensor(out=U[:, 512:1024], in0=O2[:, 0:512], in1=SG[:, 512:1024], op=ALU.mult), reads=["O2", "SG"], writes=["U"])
                for c in range(8):
                    S.op("pe", lambda e, c=c: e.transpose(out=TR[:, c * 128:(c + 1) * 128], in_=U[:, c * 128:(c + 1) * 128], identity=ident[:]), reads=["U", "ident"], writes=["TR"])
                S.op("act", lambda e: e.activation(out=UT[:], in_=TR[:].rearrange("p (c t) -> p c t", c=8), func=AF.Copy), reads=["TR"], writes=["UT"])
                for n in range(2):
                    for c in range(4):
                        S.op("pe", lambda e, n=n, c=c: e.matmul(SPp[:, n * 512:(n + 1) * 512], lhsT=UT[:, c, :], rhs=WuA[:, c, n * 512:(n + 1) * 512], start=(c == 0), stop=(c == 3)), reads=["UT", "WuA"], writes=[("SP", n)])
                for n in range(2):
                    for c in range(4):
                        S.op("pe", lambda e, n=n, c=c: e.matmul(OPp[:, n * 512:(n + 1) * 512], lhsT=UT[:, 4 + c, :], rhs=WuB[:, c, n * 512:(n + 1) * 512], start=(c == 0), stop=(c == 3)), reads=["UT", "WuB"], writes=[("OP", n)])
                S.op("dve", lambda e: e.tensor_tensor(out=M1[:], in0=SPp[:], in1=SM[:, 0:1024], op=ALU.mult), reads=[("SP", 0), ("SP", 1), "SM"], writes=["M1"])
                S.op("dve", lambda e: e.tensor_tensor(out=M2[:], in0=OPp[:], in1=SM[:, 1024:2048], op=ALU.mult), reads=[("OP", 0), ("OP", 1), "SM"], writes=["M2"])
                S.op("pool", lambda e: e.tensor_tensor(out=MG[:], in0=M1[:], in1=M2[:], op=ALU.add), reads=["M1", "M2"], writes=["MG"])
                for c in range(8):
                    S.op("pe", lambda e, c=c: e.transpose(out=TR[:, c * 128:(c + 1) * 128], in_=MG[:, c * 128:(c + 1) * 128], identity=ident[:]), reads=["MG", "ident"], writes=["TR"])
                S.op("act", lambda e: e.activation(out=MT[:], in_=TR[:].rearrange("p (c t) -> p c t", c=8), func=AF.Copy), reads=["TR"], writes=["MT"])
                for n in range(2):
                    for c in range(8):
                        S.op("pe", lambda e, n=n, c=c: e.matmul(SPp[:, n * 512:(n + 1) * 512], lhsT=MT[:, c, :], rhs=Wo[:, c, n * 512:(n + 1) * 512], start=(c == 0), stop=(c == 7)), reads=["MT", "Wo"], writes=[("SP", n)])
                S.op("dve", lambda e: e.tensor_tensor(out=Y[:], in0=SPp[:], in1=xr[:], op=ALU.add), reads=[("SP", 0), ("SP", 1), xrk], writes=["Y"])
                S.op("sp", lambda e: e.dma_start(out=y_dst, in_=Y[0:nrows, :]), reads=["Y"], dma=True)

            for t in range(NB if LEVEL >= 7 else 0):
                final_block(lambda c, t=t: xT_own[:, c, t * 128:(t + 1) * 128], [("xTo", t)], t * 128, x_ext[NOWN + t * 128:NOWN + (t + 1) * 128, :], y_out[t * 128:(t + 1) * 128, :])

            if WITH_CACHE and LEVEL >= 8:
                final_block(lambda c: xT_s[:, c, 0:128], ["xTs"], NOWN, None, ys_out, nrows=64, xres=Xs_f)

        S.emit(sems, dsems, block)
    return nc


def shared_inputs(rel_bias, norm_gain, w_in, q_gain_a, k_gain_a, sinks_a, q_gain_b, k_gain_b, w_up_a, w_up_b, w_out):
    relb = np.zeros((128, 128), np.float32)
    relb[:32, :32] = rel_bias
    oh = onehot_tables()
    mask2 = np.zeros((128, 2), np.float32)
    mask2[:64, 0] = 1.0
    mask2[64:, 1] = 1.0
    return {
        "mask2": mask2,
        "w_in": w_in[0], "ng": np.ascontiguousarray(norm_gain[0].reshape(8, 128).T), "relb": relb, "oh": oh,
        "gq_a": np.ascontiguousarray(np.broadcast_to(q_gain_a[0][None, :], (128, 64))),
        "gk_a": np.ascontiguousarray(np.broadcast_to(k_gain_a[0][None, :], (128, 64))),
        "gq_b": np.ascontiguousarray(np.broadcast_to(q_gain_b[0].reshape(1, 192), (128, 192))),
        "gk_b": np.ascontiguousarray(np.broadcast_to(k_gain_b[0].reshape(1, 192), (128, 192))),
        "sinks": np.ascontiguousarray(np.broadcast_to(sinks_a[0][None, :], (128, 8))),
        "wup_a": w_up_a[0], "wup_b": w_up_b[0], "wout": w_out[0],
    }


_CACHE = {}


def kernel(x_prompt, x_sample, cache_a_kv, cache_b1_kv, cache_b2_kv, cache_b3_kv, rel_bias, norm_gain, w_in,
           q_gain_a, k_gain_a, sinks_a, q_gain_b, k_gain_b, w_up_a, w_up_b, w_out):
    f = lambda a: np.ascontiguousarray(np.asarray(a, dtype=np.float32))
    x_prompt = f(x_prompt); x_sample = f(x_sample)
    cache_a_kv = f(cache_a_kv); cache_b1_kv = f(cache_b1_kv); cache_b2_kv = f(cache_b2_kv); cache_b3_kv = f(cache_b3_kv)
    rel_bias = f(rel_bias); norm_gain = f(norm_gain); w_in = f(w_in)
    q_gain_a = f(q_gain_a); k_gain_a = f(k_gain_a); sinks_a = f(sinks_a); q_gain_b = f(q_gain_b); k_gain_b = f(k_gain_b)
    w_up_a = f(w_up_a); w_up_b = f(w_up_b); w_out = f(w_out)

    nc = build_program()
    shared = shared_inputs(rel_bias, norm_gain, w_in, q_gain_a, k_gain_a, sinks_a, q_gain_b, k_gain_b, w_up_a, w_up_b, w_out)

    in_maps = []
    for c in range(8):
        b, h = c // 2, c % 2
        x_ext = np.zeros((4096, 1024), np.float32)
        if h == 1:
            x_ext[:] = x_prompt[b]
        else:
            x_ext[2048:] = x_prompt[b, :2048]
        xs = np.zeros((128, 1024), np.float32)
        xs[:64] = x_sample[16 * c:16 * c + 16].reshape(64, 1024)
        m = dict(shared)
        m["x_ext"] = x_ext
        m["x_s"] = xs
        m["hv"] = np.full((128, 1), float(h), np.float32)
        if WITH_CACHE:
          m["c_a"] = np.ascontiguousarray(cache_a_kv[0, 16 * c:16 * c + 16].reshape(16, 128, 256))
          m["c_b1"] = np.ascontiguousarray(cache_b1_kv[0, 16 * c:16 * c + 16].reshape(16, 128, 1024))
          m["c_b2"] = np.ascontiguousarray(cache_b2_kv[0, 16 * c:16 * c + 16].reshape(16, 512, 1024))
          m["c_b3"] = np.ascontiguousarray(cache_b3_kv[0, 16 * c:16 * c + 16].reshape(16, 2048, 1024))
        in_maps.append(m)
    res = run_bass_kernel_spmd(nc, in_maps, core_ids=list(range(8)))
    R = res.results
    y = np.zeros((4, 4096, 1024), np.float32)
    ys = np.zeros((128, 4, 1024), np.float32)
    for c in range(8):
        b, h = c // 2, c % 2
        y[b, h * 2048:(h + 1) * 2048] = R[c]["y"]
        ys[16 * c:16 * c + 16] = R[c]["ys"].reshape(16, 4, 1024)
    np_a = np.stack([R[2 * b + 1]["nkv_a"].reshape(128, 2, 2, 64) for b in range(4)])[None]
    np_b1 = np.stack([R[2 * b + 1]["nkv_b1"].reshape(128, 2, 8, 64) for b in range(4)])[None]
    np_b2 = np.stack([R[2 * b + 1]["nkv_b2"].reshape(512, 2, 8, 64) for b in range(4)])[None]
    np_b3 = np.stack([R[2 * b + 1]["nkv_b3"].reshape(2048, 2, 8, 64) for b in range(4)])[None]
    ns_a = np.concatenate([R[c]["ns_a"].reshape(16, 4, 2, 2, 64) for c in range(8)])[None]
    ns_b1 = np.concatenate([R[c]["ns_b1"].reshape(16, 4, 2, 8, 64) for c in range(8)])[None]
    ns_b2 = np.concatenate([R[c]["ns_b2"].reshape(16, 4, 2, 8, 64) for c in range(8)])[None]
    ns_b3 = np.concatenate([R[c]["ns_b3"].reshape(16, 4, 2, 8, 64) for c in range(8)])[None]
    return (y, ys, np_a, np_b1, np_b2, np_b3, ns_a, ns_b1, ns_b2, ns_b3)
```

```python
import math
import os
from contextlib import ExitStack

import numpy as np
import concourse.bass as bass
import concourse.mybir as mybir
from concourse.bass_utils import run_bass_kernel_spmd

F32 = mybir.dt.float32
BF16 = mybir.dt.bfloat16
AF = mybir.ActivationFunctionType
ALU = mybir.AluOpType
AX = mybir.AxisListType

SAME_ENGINE_SYNC = os.environ.get('KDBG_SES', '1') == '1'
LEVEL = int(os.environ.get('KDBG_LEVEL', '99'))
NCLS = int(os.environ.get('KDBG_NCLS', '99'))
WITH_CACHE = os.environ.get('KDBG_NOSAMPLE', '0') != '1'
SUB = int(os.environ.get('KDBG_SUB', '99'))
XB = int(os.environ.get('KDBG_X', '0'))
EPS = 1e-6
NOWN = 2048
NB = 16
C_QA, C_KA, C_VA, C_GA = 0, 512, 640, 768
C_QB, C_KB, C_VB, C_GB, C_MA, C_MB = 1280, 2816, 4352, 5888, 6400, 7424


class Sched:
    ENGS = ("pe", "act", "dve", "pool", "sp")

    def __init__(self, nc, n_dma_sems=6):
        self.nc = nc
        self.ops = {e: [] for e in self.ENGS}
        self.lastw = {}
        self.readers = {}
        self.n_dma_sems = n_dma_sems
        self.dma_count = {"sp": 0, "pool": 0, "act": 0}
        self.dma_last = {}

    def last_all(self):
        return [(e, len(self.ops[e]) - 1) for e in self.ENGS if self.ops[e]]

    def op(self, eng, fn, reads=(), writes=(), dma=False, extra=()):
        ops = self.ops[eng]
        idx = len(ops)
        deps = set(extra)
        for b in reads:
            w = self.lastw.get(b)
            if w is not None:
                deps.add(w)
        for b in writes:
            w = self.lastw.get(b)
            if w is not None:
                deps.add(w)
            for r in self.readers.get(b, ()):
                deps.add(r)
        cdeps = {}
        ddeps = set()
        for (e, i) in deps:
            o = self.ops[e][i]
            if o["dma"]:
                ddeps.add(o["sig"])
            else:
                if e == eng and not dma and (e == "pe" or not SAME_ENGINE_SYNC):
                    continue
                cdeps[e] = max(cdeps.get(e, -1), i)
        rec = {"fn": fn, "dma": dma, "cdeps": cdeps, "ddeps": ddeps, "sig": None, "signaled": False}
        if dma:
            n = self.dma_count[eng]
            self.dma_count[eng] = n + 1
            slot = n % self.n_dma_sems
            val = 16 * (n // self.n_dma_sems + 1)
            rec["sig"] = (eng, slot, val)
            if val > 16:
                ddeps.add((eng, slot, val - 16))
            self.dma_last[(eng, slot)] = val
        ops.append(rec)
        me = (eng, idx)
        for b in writes:
            self.lastw[b] = me
            self.readers[b] = []
        for b in reads:
            if b in writes:
                continue
            self.readers.setdefault(b, []).append(me)
        return me

    def emit(self, sems, dsems, block):
        for e in self.ENGS:
            for o in self.ops[e]:
                for (de, di) in o["cdeps"].items():
                    self.ops[de][di]["signaled"] = True
        for e in self.ENGS:
            c = 0
            for o in self.ops[e]:
                if o["dma"]:
                    continue
                if o["signaled"]:
                    c += 1
                    o["sig"] = c
        allops = self.ops
        dma_last = self.dma_last

        def run(eng_name, eng):
            waited = {}
            for o in allops[eng_name]:
                for (de, di) in sorted(o["cdeps"].items()):
                    v = allops[de][di]["sig"]
                    key = ("c", de)
                    if waited.get(key, 0) >= v:
                        continue
                    eng.wait_ge(sems[de], v)
                    waited[key] = v
                for (qe, slot, v) in sorted(o["ddeps"]):
                    key = ("d", qe, slot)
                    if waited.get(key, 0) >= v:
                        continue
                    eng.wait_ge(dsems[(qe, slot)], v)
                    waited[key] = v
                ins = o["fn"](eng)
                if o["dma"]:
                    qe, slot, v = o["sig"]
                    ins.then_inc(dsems[(qe, slot)], 16)
                elif o["signaled"]:
                    ins.then_inc(sems[eng_name], 1)
            if eng_name == "sp":
                for (qe, slot), v in sorted(dma_last.items()):
                    if waited.get(("d", qe, slot), 0) >= v:
                        continue
                    eng.wait_ge(dsems[(qe, slot)], v)

        @block.tensor
        def _(eng):
            run("pe", eng)

        @block.scalar
        def _(eng):
            run("act", eng)

        @block.vector
        def _(eng):
            run("dve", eng)

        @block.gpsimd
        def _(eng):
            run("pool", eng)

        @block.sync
        def _(eng):
            run("sp", eng)


def t5_bucket_np(dist):
    d = np.maximum(dist, 0)
    df = np.maximum(d, 1).astype(np.float32)
    large = 16 + (np.log(df / np.float32(16)) / np.float32(math.log(2048 / 16)) * np.float32(16)).astype(np.int32)
    large = np.minimum(large, 31)
    return np.where(d < 16, d, large)


def onehot_tables():
    oh = np.zeros((3, 128, 384), np.float32)
    for di, dil in enumerate((1, 4, 16)):
        delta = np.arange(128)
        b = t5_bucket_np(delta * dil)
        oh[di, b, delta + 127] = 1.0
    return oh


GROUPS = {
    "b3": dict(d=16, di=2, hb=24, cq=C_QB + 1024, ck=C_KB + 1024, cv=C_VB + 1024, nkv=8),
    "b2": dict(d=4, di=1, hb=16, cq=C_QB + 512, ck=C_KB + 512, cv=C_VB + 512, nkv=8),
    "b1": dict(d=1, di=0, hb=8, cq=C_QB, ck=C_KB, cv=C_VB, nkv=8),
    "a": dict(d=1, di=0, hb=0, cq=C_QA, ck=C_KA, cv=C_VA, nkv=2),
}


def build_program(with_sample=True):
    nc = bass.Bass("TRN2", target_bir_lowering=False)

    def din(name, shape):
        return nc.dram_tensor(name, shape, F32, kind="ExternalInput").ap()

    def dout(name, shape):
        return nc.dram_tensor(name, shape, F32, kind="ExternalOutput").ap()

    x_ext = din("x_ext", [4096, 1024])
    x_s = din("x_s", [128, 1024])
    hv_in = din("hv", [128, 1])
    mask_in = din("mask2", [128, 2])
    w_in = din("w_in", [1024, 8448])
    ng_in = din("ng", [128, 8])
    relb_in = din("relb", [128, 128])
    oh_in = din("oh", [3, 128, 384])
    gq_a_in = din("gq_a", [128, 64])
    gk_a_in = din("gk_a", [128, 64])
    gq_b_in = din("gq_b", [128, 192])
    gk_b_in = din("gk_b", [128, 192])
    sinks_in = din("sinks", [128, 8])
    wup_a_in = din("wup_a", [512, 1024])
    wup_b_in = din("wup_b", [512, 1024])
    wout_in = din("wout", [1024, 1024])
    if WITH_CACHE:
        c_a_in = din("c_a", [16, 128, 256])
        c_b1_in = din("c_b1", [16, 128, 1024])
        c_b2_in = din("c_b2", [16, 512, 1024])
        c_b3_in = din("c_b3", [16, 2048, 1024])

    y_out = dout("y", [2048, 1024])
    ys_out = dout("ys", [64, 1024])
    nkv_out = {"a": dout("nkv_a", [128, 2, 128]), "b1": dout("nkv_b1", [128, 2, 512]),
               "b2": dout("nkv_b2", [512, 2, 512]), "b3": dout("nkv_b3", [2048, 2, 512])}
    nskv_out = {"a": dout("ns_a", [64, 2, 128]), "b1": dout("ns_b1", [64, 2, 512]),
                "b2": dout("ns_b2", [64, 2, 512]), "b3": dout("ns_b3", [64, 2, 512])}

    DBG = os.environ.get('KDBG_DUMP', '0') == '1'
    if DBG:
        dbg_o = dout("dbgo", [4, 128, 520])
    NTS = NOWN + 128
    Oscr = {g: nc.dram_tensor("oscr_" + g, [NTS, 520], F32) for g in ("b1", "b2", "b3")}
    Oa_scr = nc.dram_tensor("oscr_a", [NTS, 512], F32)
    EFscr = nc.dram_tensor("efscr", [3, 32, 384], F32)

    S = Sched(nc, n_dma_sems=6)
    rr = {"ev": 0}

    with ExitStack() as es:
        def sb(name, shape, dt):
            return es.enter_context(nc.sbuf_tensor(name, shape, dt))

        def ps(name, shape, dt):
            return es.enter_context(nc.psum_tensor(name, shape, dt))

        sems = {e: es.enter_context(nc.semaphore("s_" + e)) for e in ("pe", "act", "dve", "pool")}
        dsems = {(q, i): es.enter_context(nc.semaphore(f"d_{q}{i}")) for q in ("sp", "pool") for i in range(6)}

        xT_own = sb("xT_own", [128, 8, NOWN], BF16)
        xT_s = sb("xT_s", [128, 8, 128], BF16)
        Xs_f = sb("Xs_f", [128, 1024], F32)
        ident = sb("ident", [128, 128], BF16)
        Jm = sb("Jm", [128, 128], BF16)
        identf = sb("identf", [128, 128], F32)
        NG = sb("NG", [128, 8], F32)
        HV = sb("HV", [128, 1], F32)
        MASK = sb("MASK", [128, 2], F32)
        GQA = sb("GQA", [128, 64], F32)
        GKA = sb("GKA", [128, 64], F32)
        GQB = sb("GQB", [128, 192], F32)
        GKB = sb("GKB", [128, 192], F32)
        SNK = sb("SNK", [128, 8], F32)
        PJ = ps("PJ", [128, 1536], F32)
        TR = ps("TR", [128, 1024], BF16)
        SPp = ps("SPp", [128, 1024], F32)
        OPp = ps("OPp", [128, 1024], F32)

        block = es.enter_context(nc.Block())

        S.op("sp", lambda e: e.dma_start(out=NG[:], in_=ng_in), writes=["NG"], dma=True)
        S.op("sp", lambda e: e.dma_start(out=HV[:], in_=hv_in), writes=["HV"], dma=True)
        S.op("sp", lambda e: e.dma_start(out=MASK[:], in_=mask_in), writes=["MASK"], dma=True)
        S.op("sp", lambda e: e.dma_start(out=GQA[:], in_=gq_a_in), writes=["GQA"], dma=True)
        S.op("sp", lambda e: e.dma_start(out=GKA[:], in_=gk_a_in), writes=["GKA"], dma=True)
        S.op("sp", lambda e: e.dma_start(out=GQB[:], in_=gq_b_in), writes=["GQB"], dma=True)
        S.op("sp", lambda e: e.dma_start(out=GKB[:], in_=gk_b_in), writes=["GKB"], dma=True)
        S.op("sp", lambda e: e.dma_start(out=SNK[:], in_=sinks_in), writes=["SNK"], dma=True)
        S.op("dve", lambda e: e.tensor_scalar(out=GQA[:], in0=GQA[:], scalar1=0.125, scalar2=None, op0=ALU.mult), reads=["GQA"], writes=["GQA"])
        S.op("dve", lambda e: e.tensor_scalar(out=GQB[:], in0=GQB[:], scalar1=0.125, scalar2=None, op0=ALU.mult), reads=["GQB"], writes=["GQB"])
        S.op("act", lambda e: e.activation(out=SNK[:], in_=SNK[:], func=AF.Exp), reads=["SNK"], writes=["SNK"])
        S.op("pool", lambda e: e.memset(identf[:], 1.0), writes=["identf"])
        S.op("pool", lambda e: e.affine_select(out=identf[:], in_=identf[:], pattern=[[-1, 128]], compare_op=ALU.is_equal, fill=0.0, base=0, channel_multiplier=1), reads=["identf"], writes=["identf"])
        S.op("dve", lambda e: e.tensor_copy(out=ident[:], in_=identf[:]), reads=["identf"], writes=["ident"])
        S.op("pool", lambda e: e.memset(identf[:], 1.0), reads=["identf"], writes=["identf"])
        S.op("pool", lambda e: e.affine_select(out=identf[:], in_=identf[:], pattern=[[1, 128]], compare_op=ALU.is_equal, fill=0.0, base=-127, channel_multiplier=1), reads=["identf"], writes=["identf"])
        S.op("dve", lambda e: e.tensor_copy(out=Jm[:], in_=identf[:]), reads=["identf"], writes=["Jm"])

        RB = sb("RB", [128, 128], F32)
        OHs = sb("OHs", [128, 384], F32)
        EFs = sb("EFs", [128, 384], F32)
        if True:
            S.op("sp", lambda e: e.dma_start(out=RB[:], in_=relb_in), writes=["RB"], dma=True)
            for di in range(3):
                S.op("sp", lambda e, di=di: e.dma_start(out=OHs[:], in_=oh_in[di]), writes=["OHs"], dma=True)
                S.op("pe", lambda e: e.matmul(SPp[:, 0:384], lhsT=RB[:], rhs=OHs[:], start=True, stop=True), reads=["RB", "OHs"], writes=["SP0", "SP1"])
                S.op("act", lambda e: e.activation(out=EFs[:], in_=SPp[:, 0:384], func=AF.Exp), reads=["SP0", "SP1"], writes=["EFs"])
                S.op("dve", lambda e: e.memset(EFs[:, 0:127], 0.0), reads=["EFs"], writes=["EFs"])
                S.op("dve", lambda e: e.memset(EFs[:, 255:384], 0.0), reads=["EFs"], writes=["EFs"])
                S.op("sp", lambda e, di=di: e.dma_start(out=EFscr.ap()[di], in_=EFs[0:32, :]), reads=["EFs"], writes=[("EFscr", di)], dma=True)

        with ExitStack() as esA:
            def sbA(name, shape, dt):
                return esA.enter_context(nc.sbuf_tensor(name, shape, dt))

            xT_halo = sbA("xT_halo", [128, 8, NOWN], BF16)
            Wsb = sbA("Wsb", [128, 8, 1536], BF16)
            Wst = [sbA("Wst%d" % i, [128, 1536], F32) for i in range(4)]
            Xf = [sbA("Xf%d" % i, [128, 1024], F32) for i in range(2)]
            SQ = sbA("SQ", [128, 1024], F32)
            Xb = sbA("Xb", [128, 1024], BF16)
            SS = sbA("SS", [128, 16], F32)
            RS = sbA("RS", [128, 16], F32)
            QN = sbA("QN", [128, 1024], F32)
            QNb = sbA("QNb", [128, 512], BF16)
            KN = sbA("KN", [128, 512], F32)
            KNb = sbA("KNb", [128, 512], BF16)
            VF = sbA("VF", [128, 512], F32)
            QT = sbA("QT", [128, 4, 128], BF16)
            Kpad = [sbA("Kpad%d" % i, [128, 8, 128], BF16) for i in range(2)]
            V1 = [sbA("V1_%d" % i, [128, 8, 65], BF16) for i in range(2)]
            Et = sbA("Et", [128, 8, 256], BF16)
            Eh = sbA("Eh", [128, 256], F32)
            Ehb = sbA("Ehb", [128, 256], BF16)
            PEx = sbA("PEx", [128, 1024], BF16)
            Pt = [sbA("Pt%d" % i, [128, 1024], BF16) for i in range(2)]
            Ost = [sbA("Ost%d" % i, [128, 8, 65], F32) for i in range(2)]
            LL = sbA("LL", [128, 8], F32)
            OaT = [sbA("OaT%d" % i, [128, 512], F32) for i in range(2)]
            Ksb = [sbA("Ksb%d" % i, [128, 512], BF16) for i in range(2)]
            KTs = [sbA("KTs%d" % i, [128, 512], BF16) for i in range(2)]
            V1s = [sbA("V1s%d" % i, [128, 8, 65], BF16) for i in range(8)]
            Qbd = sbA("Qbd", [128, 4, 128, 2], BF16)
            PEs = sbA("PEs", [128, 32], F32)
            Zb = sbA("Zb", [128, 32, 192], BF16)
            S.op("pool", lambda e: e.memset(Zb[:], 0.0), writes=["Zb"])
            OsAcc = sbA("OsAcc", [128, 8, 65], F32)
            for i in range(8):
                S.op("pool", lambda e, i=i: e.memset(V1s[i][:], 1.0), writes=[("V1s", i)])

            for i in range(2):
                S.op("pool", lambda e, i=i: e.memset(Kpad[i][:], 0.0), writes=[("Kpad", i)])
                S.op("pool", lambda e, i=i: e.memset(V1[i][:], 1.0), writes=[("V1", i)])

            def prologue(src_ap, dst_tile, dst_key, col0, k, keep_f32=None):
                xf = Xf[k % 2] if keep_f32 is None else keep_f32
                xkey = ("Xf", k % 2) if keep_f32 is None else "Xs_f"
                S.op("sp", lambda e: e.dma_start(out=xf[:], in_=src_ap), writes=[xkey], dma=True)
                S.op("act", lambda e: e.activation(out=SQ[:], in_=xf[:], func=AF.Square), reads=[xkey], writes=["SQ"])
                S.op("dve", lambda e: e.reduce_sum(out=SS[:, 0:1], in_=SQ[:], axis=AX.X), reads=["SQ"], writes=["SS"])
                S.op("act", lambda e: e.activation(out=RS[:, 0:1], in_=SS[:, 0:1], func=AF.Sqrt, bias=EPS, scale=1.0 / 1024), reads=["SS"], writes=["RS"])
                S.op("dve", lambda e: e.reciprocal(out=RS[:, 0:1], in_=RS[:, 0:1]), reads=["RS"], writes=["RS"])
                S.op("dve", lambda e: e.tensor_scalar(out=Xb[:], in0=xf[:], scalar1=RS[:, 0:1], scalar2=None, op0=ALU.mult), reads=[xkey, "RS"], writes=["Xb"])
                for c in range(8):
                    S.op("pe", lambda e, c=c: e.transpose(out=TR[:, c * 128:(c + 1) * 128], in_=Xb[:, c * 128:(c + 1) * 128], identity=ident[:]), reads=["Xb", "ident"], writes=["TR"])
                S.op("act", lambda e: e.activation(out=dst_tile[:, :, col0:col0 + 128], in_=TR[:].rearrange("p (c t) -> p c t", c=8), func=AF.Copy), reads=["TR"], writes=[dst_key])

            if LEVEL >= 2:
                for t in range(NB):
                    prologue(x_ext[t * 128:(t + 1) * 128, :], xT_halo, ("xTh", t), t * 128, t)
                for t in range(NB):
                    prologue(x_ext[NOWN + t * 128:NOWN + (t + 1) * 128, :], xT_own, ("xTo", t), t * 128, t)
                prologue(x_s, xT_s, "xTs", 0, 0, keep_f32=Xs_f)
            XTH_ALL = [("xTh", t) for t in range(NB)]
            XTO_ALL = [("xTo", t) for t in range(NB)]

            wcnt = {"n": 0}

            def load_w(dst, col_ranges, key):
                off = 0
                for (c0, n) in col_ranges:
                    for c in range(8):
                        k = wcnt["n"] % 4
                        wcnt["n"] += 1
                        st = Wst[k]
                        S.op("sp", lambda e, c=c, c0=c0, n=n, st=st: e.dma_start(out=st[:, 0:n], in_=w_in[c * 128:(c + 1) * 128, c0:c0 + n]), writes=[("Wst", k)], dma=True)
                        if wcnt["n"] % 2:
                            S.op("act", lambda e, c=c, n=n, st=st, off=off: e.activation(out=dst[:, c, off:off + n], in_=st[:, 0:n], func=AF.Copy, scale=NG[:, c:c + 1]), reads=[("Wst", k), "NG"], writes=[key])
                        else:
                            S.op("dve", lambda e, c=c, n=n, st=st, off=off: e.tensor_scalar(out=dst[:, c, off:off + n], in0=st[:, 0:n], scalar1=NG[:, c:c + 1], scalar2=None, op0=ALU.mult), reads=[("Wst", k), "NG"], writes=[key])
                    off += n

            def inproj(lhs_fn, xkeys, ncols, wtile=None, wkey="W", wcol0=0):
                wt = Wsb if wtile is None else wtile
                ng_ = (ncols + 511) // 512
                for g in range(ng_):
                    n = min(512, ncols - g * 512)
                    for c in range(8):
                        S.op("pe", lambda e, g=g, c=c, n=n: e.matmul(PJ[:, g * 512:g * 512 + n], lhsT=lhs_fn(c), rhs=wt[:, c, wcol0 + g * 512:wcol0 + g * 512 + n], start=(c == 0), stop=(c == 7)),
                             reads=list(xkeys) + [wkey], writes=[("PJ", g)])

            def build_E(gname):
                G = GROUPS[gname]
                for h in range(8):
                    src = bass.AP(tensor=EFscr, offset=(G["di"] * 32 + G["hb"] + h) * 384, ap=[[1, 128], [128, 2], [1, 128]])
                    S.op("sp", lambda e, src=src: e.dma_start(out=Eh[:].rearrange("p (a b) -> p a b", a=2), in_=src), reads=[("EFscr", G["di"])], writes=["Eh"], dma=True)
                    S.op("act", lambda e: e.activation(out=Ehb[:], in_=Eh[:], func=AF.Copy), reads=["Eh"], writes=["Ehb"])
                    S.op("pe", lambda e: e.matmul(SPp[:, 0:256], lhsT=Jm[:], rhs=Ehb[:], start=True, stop=True), reads=["Jm", "Ehb"], writes=["SP0"])
                    S.op("dve", lambda e, h=h: e.tensor_copy(out=Et[:, h, :], in_=SPp[:, 0:256]), reads=["SP0"], writes=["Et"])

            def qkv_block(gname, lhs_fn, xkeys, par, want_q, kv_out=None, nrows=128):
                G = GROUPS[gname]
                isA = gname == "a"
                nq = 512
                nk = 128 if isA else 512
                nkh = nk // 64
                if want_q:
                    ncols = nq + 2 * nk
                    qo, ko, vo = 0, nq, nq + nk
                    wc0 = 0
                else:
                    ncols = 2 * nk
                    qo, ko, vo = None, 0, nk
                    wc0 = nq
                inproj(lhs_fn, xkeys, ncols, wcol0=wc0)
                ngrp = (ncols + 511) // 512
                pjk = [("PJ", g) for g in range(ngrp)]
                nn = (nq + nk) if want_q else nk
                nh = nn // 64
                S.op("act", lambda e: e.activation(out=SQ[:, 0:nn], in_=PJ[:, 0:nn], func=AF.Square), reads=pjk, writes=["SQ"])
                S.op("dve", lambda e: e.reduce_sum(out=SS[:, 0:nh], in_=SQ[:, 0:nn].rearrange("p (h d) -> p h d", d=64), axis=AX.X), reads=["SQ"], writes=["SS"])
                S.op("act", lambda e: e.activation(out=RS[:, 0:nh], in_=SS[:, 0:nh], func=AF.Ln, bias=EPS, scale=1.0 / 64), reads=["SS"], writes=["RS"])
                S.op("act", lambda e: e.activation(out=RS[:, 0:nh], in_=RS[:, 0:nh], func=AF.Exp, scale=-0.5), reads=["RS"], writes=["RS"])
                S.op("dve", lambda e: e.tensor_tensor(out=QN[:, 0:nn].rearrange("p (h d) -> p h d", d=64), in0=PJ[:, 0:nn].rearrange("p (h d) -> p h d", d=64),
                                                       in1=RS[:, 0:nh].unsqueeze(2).broadcast_to([128, nh, 64]), op=ALU.mult), reads=pjk + ["RS"], writes=["QN"])
                if isA:
                    gq, gk = GQA[:, :], GKA[:, :]
                else:
                    gi = {"b1": 0, "b2": 1, "b3": 2}[gname]
                    gq, gk = GQB[:, gi * 64:(gi + 1) * 64], GKB[:, gi * 64:(gi + 1) * 64]
                if want_q and not (XB & 2):
                    S.op("dve", lambda e: e.tensor_tensor(out=QNb[:].rearrange("p (h d) -> p h d", d=64), in0=QN[:, 0:512].rearrange("p (h d) -> p h d", d=64),
                                                            in1=gq.unsqueeze(1).broadcast_to([128, 8, 64]), op=ALU.mult), reads=["QN", "GQA", "GQB"], writes=["QNb"])
                S.op("dve", lambda e: e.tensor_tensor(out=KN[:, 0:nk].rearrange("p (h d) -> p h d", d=64), in0=QN[:, ko:ko + nk].rearrange("p (h d) -> p h d", d=64),
                                                        in1=gk.unsqueeze(1).broadcast_to([128, nkh, 64]), op=ALU.mult), reads=["QN", "GKA", "GKB"], writes=["KN"])
                S.op("act", lambda e: e.activation(out=KNb[:, 0:nk], in_=KN[:, 0:nk], func=AF.Copy), reads=["KN"], writes=["KNb"])
                S.op("dve", lambda e: e.tensor_copy(out=V1[par][:, 0:nkh, 0:64], in_=PJ[:, vo:vo + nk].rearrange("p (h d) -> p h d", d=64)), reads=pjk, writes=[("V1", par)])
                if kv_out is not None and not (XB & 1):
                    S.op("act", lambda e: e.activation(out=VF[:, 0:nk], in_=PJ[:, vo:vo + nk], func=AF.Copy), reads=pjk, writes=["VF"])
                    ko_ap, vo_ap = kv_out
                    S.op("sp", lambda e: e.dma_start(out=ko_ap, in_=KN[0:nrows, 0:nk]), reads=["KN"], dma=True)
                    S.op("sp", lambda e: e.dma_start(out=vo_ap, in_=VF[0:nrows, 0:nk]), reads=["VF"], dma=True)
                if want_q and not (XB & 4):
                    for t in range(4):
                        src = QNb[:, t * 128:(t + 1) * 128]
                        S.op("pe", lambda e, t=t, src=src: e.transpose(out=TR[:, t * 128:(t + 1) * 128], in_=src, identity=ident[:]), reads=["QNb", "ident"], writes=["TR"])
                nkt = nk // 128
                for t in range(nkt):
                    S.op("pe", lambda e, t=t: e.transpose(out=TR[:, (4 + t) * 128:(5 + t) * 128], in_=KNb[:, t * 128:(t + 1) * 128], identity=ident[:]), reads=["KNb", "ident"], writes=["TR"])
                if want_q and not (XB & 8):
                    S.op("dve", lambda e: e.tensor_copy(out=QT[:].rearrange("p t k -> p (t k)"), in_=TR[:, 0:512]), reads=["TR"], writes=["QT"])
                trk = TR[:, 512:512 + nkt * 128].rearrange("p (t k) -> p t k", t=nkt)
                kp = Kpad[par][:, 0:2 * nkt, :].rearrange("p (t s) k -> p t s k", s=2)
                S.op("dve", lambda e: e.tensor_scalar(out=kp[:, :, 0, :], in0=trk, scalar1=MASK[:, 0:1], scalar2=None, op0=ALU.mult), reads=["TR", "MASK"], writes=[("Kpad", par)])
                S.op("act", lambda e: e.activation(out=kp[:, :, 1, :], in_=trk, func=AF.Copy, scale=MASK[:, 1:2]), reads=["TR", "MASK"], writes=[("Kpad", par)])

            def attend(gname, par, first, out_rows):
                isA = gname == "a"
                for hh in range(2):
                    for hl in range(4):
                        h = hh * 4 + hl
                        kidx = (h // 4) if isA else h
                        qt = (h % 4) if isA else (h // 2)
                        for blk in range(2):
                            pp = par if blk == 0 else 1 - par
                            S.op("pe", lambda e, hl=hl, blk=blk, pp=pp, kidx=kidx, qt=qt: e.matmul(SPp[:, hl * 256 + blk * 128: hl * 256 + (blk + 1) * 128], lhsT=Kpad[pp][:, kidx, :], rhs=QT[:, qt, :], start=True, stop=True),
                                 reads=[("Kpad", pp), "QT"], writes=["SP0", "SP1"])
                    S.op("act", lambda e: e.activation(out=PEx[:], in_=SPp[:], func=AF.Exp), reads=["SP0", "SP1"], writes=["PEx"])
                    S.op("dve", lambda e, hh=hh: e.tensor_tensor(out=Pt[hh][:], in0=PEx[:], in1=Et[:, hh * 4:(hh + 1) * 4, :].rearrange("p h k -> p (h k)"), op=ALU.mult), reads=["PEx", "Et"], writes=[("Pt", hh)])
                    if first:
                        pv = Pt[hh][:].rearrange("p (h b q) -> p h b q", h=4, b=2)[:, :, 1, :]
                        S.op("dve", lambda e, pv=pv: e.tensor_scalar(out=pv, in0=pv, scalar1=HV[:, 0:1], scalar2=None, op0=ALU.mult), reads=[("Pt", hh), "HV"], writes=[("Pt", hh)])
                    for hl in range(4):
                        h = hh * 4 + hl
                        kv = (h // 4) if isA else h
                        for blk in range(2):
                            pp = par if blk == 0 else 1 - par
                            S.op("pe", lambda e, h=h, hl=hl, blk=blk, pp=pp, kv=kv, hh=hh: e.matmul(OPp[:, h * 128:h * 128 + 65], lhsT=Pt[hh][:, hl * 256 + blk * 128: hl * 256 + (blk + 1) * 128], rhs=V1[pp][:, kv, :], start=(blk == 0), stop=(blk == 1)),
                                 reads=[("Pt", hh), ("V1", pp)], writes=[("OP", h // 4)])
                opk = [("OP", 0), ("OP", 1)]
                opv = OPp[:].rearrange("p (h c) -> p h c", h=8)
                k = rr["ev"] % 2
                rr["ev"] += 1
                if isA:
                    S.op("dve", lambda e: e.tensor_tensor(out=LL[:], in0=opv[:, :, 64], in1=SNK[:], op=ALU.add), reads=opk + ["SNK"], writes=["LL"])
                    S.op("dve", lambda e: e.reciprocal(out=LL[:], in_=LL[:]), reads=["LL"], writes=["LL"])
                    S.op("dve", lambda e: e.tensor_tensor(out=OaT[k][:].rearrange("p (h d) -> p h d", d=64), in0=opv[:, :, 0:64], in1=LL[:].unsqueeze(2).broadcast_to([128, 8, 64]), op=ALU.mult), reads=opk + ["LL"], writes=[("OaT", k)])
                    S.op("sp", lambda e: e.dma_start(out=out_rows, in_=OaT[k][:]), reads=[("OaT", k)], writes=["OSCR_a"], dma=True)
                else:
                    S.op("act", lambda e: e.activation(out=Ost[k][:], in_=opv[:, :, 0:65], func=AF.Copy), reads=opk, writes=[("Ost", k)])
                    S.op("sp", lambda e: e.dma_start(out=out_rows, in_=Ost[k][:].rearrange("p h c -> p (h c)")), reads=[("Ost", k)], writes=["OSCR_" + gname], dma=True)

            def sample_attn(gname):
                G = GROUPS[gname]
                d = G["d"]
                isA = gname == "a"
                nk = 128 if isA else 512
                kvw = 2 * nk
                nkt = nk // 128
                nkh = nk // 64
                cache = {"a": c_a_in, "b1": c_b1_in, "b2": c_b2_in, "b3": c_b3_in}[gname]
                nsk = nskv_out[gname]
                qkv_block(gname, lambda c: xT_s[:, c, 0:128], ["xTs"], 0, want_q=True, kv_out=(nsk[:, 0, :], nsk[:, 1, :]), nrows=64)
                S.op("dve", lambda e: e.tensor_scalar(out=Qbd[:, :, :, 0], in0=QT[:], scalar1=MASK[:, 0:1], scalar2=None, op0=ALU.mult), reads=["QT", "MASK"], writes=["Qbd"])
                S.op("dve", lambda e: e.tensor_scalar(out=Qbd[:, :, :, 1], in0=QT[:], scalar1=MASK[:, 1:2], scalar2=None, op0=ALU.mult), reads=["QT", "MASK"], writes=["Qbd"])
                if isA:
                    es_v = Et[:].rearrange("p (k g) c -> p g k c", k=2)[:, :, :, 127]
                else:
                    es_v = Et[:, :, 127]
                S.op("pool", lambda e: e.memset(OsAcc[:], 0.0), writes=["OsAcc"])
                for n in range(16):
                    for t in range(4):
                        tok = 4 * n + t
                        tokc = t
                        slot = tok % 2
                        kslot = tok % 4
                        KVt = Wst[kslot]
                        kvk = ("Wst", kslot)
                        if d == 1:
                            npc = 127 - t
                            S.op("sp", lambda e, n=n, t=t, npc=npc, KVt=KVt: e.dma_start(out=KVt[0:112, 0:kvw], in_=cache[n, t + 1:t + 113, :]), writes=[kvk], dma=True)
                            S.op("sp", lambda e, n=n, t=t, npc=npc, KVt=KVt: e.dma_start(out=KVt[112:npc, 0:kvw], in_=cache[n, t + 113:128, :]), writes=[kvk], dma=True)
                            S.op("sp", lambda e, n=n, t=t, npc=npc, KVt=KVt: e.dma_start(out=KVt[npc:128, 0:nk], in_=KN[4 * n:4 * n + t + 1, 0:nk]), reads=["KN"], writes=[kvk], dma=True)
                            S.op("sp", lambda e, n=n, t=t, npc=npc, KVt=KVt: e.dma_start(out=KVt[npc:128, nk:kvw], in_=VF[4 * n:4 * n + t + 1, 0:nk]), reads=["VF"], writes=[kvk], dma=True)
                        else:
                            S.op("sp", lambda e, n=n, t=t, KVt=KVt: e.dma_start(out=KVt[0:112, 0:kvw], in_=cache[n, t + d:t + d + 111 * d + 1:d, :]), writes=[kvk], dma=True)
                            S.op("sp", lambda e, n=n, t=t, KVt=KVt: e.dma_start(out=KVt[112:127, 0:kvw], in_=cache[n, t + 113 * d:t + 113 * d + 14 * d + 1:d, :]), writes=[kvk], dma=True)
                            S.op("sp", lambda e, tok=tok, KVt=KVt: e.dma_start(out=KVt[127:128, 0:nk], in_=KN[tok:tok + 1, 0:nk]), reads=["KN"], writes=[kvk], dma=True)
                            S.op("sp", lambda e, tok=tok, KVt=KVt: e.dma_start(out=KVt[127:128, nk:kvw], in_=VF[tok:tok + 1, 0:nk]), reads=["VF"], writes=[kvk], dma=True)
                        vs = tok % 8
                        S.op("act", lambda e, KVt=KVt, slot=slot: e.activation(out=Ksb[slot][:, 0:nk], in_=KVt[:, 0:nk], func=AF.Copy), reads=[kvk], writes=[("Ksb", slot)])
                        S.op("dve", lambda e, KVt=KVt, vs=vs: e.tensor_copy(out=V1s[vs][:, 0:nkh, 0:64], in_=KVt[:, nk:kvw].rearrange("p (h d) -> p h d", d=64)), reads=[kvk], writes=[("V1s", vs)])
                        for tt in range(nkt):
                            S.op("pe", lambda e, tt=tt, slot=slot: e.transpose(out=TR[:, tt * 128:(tt + 1) * 128], in_=Ksb[slot][:, tt * 128:(tt + 1) * 128], identity=ident[:]), reads=[("Ksb", slot), "ident"], writes=["TR"])
                        S.op("dve", lambda e, slot=slot: e.tensor_copy(out=KTs[slot][:, 0:nk], in_=TR[:, 0:nk]), reads=["TR"], writes=[("KTs", slot)])
                        for tp in range(4):
                            kt = 0 if isA else tp
                            S.op("pe", lambda e, tp=tp, kt=kt, slot=slot, tok=tok, tokc=tokc: e.matmul(SPp[:, tokc * 8 + tp * 2: tokc * 8 + tp * 2 + 2], lhsT=KTs[slot][:, kt * 128:(kt + 1) * 128], rhs=Qbd[:, tp, tok, :], start=True, stop=True),
                                 reads=[("KTs", slot), "Qbd"], writes=["SP0"])
                    S.op("act", lambda e: e.activation(out=PEs[:], in_=SPp[:, 0:32], func=AF.Exp), reads=["SP0"], writes=["PEs"])
                    if isA:
                        S.op("dve", lambda e: e.tensor_tensor(out=Zb[:, :, 63].rearrange("p (t g k) -> p t g k", t=4, g=4), in0=PEs[:].rearrange("p (t g k) -> p t g k", t=4, g=4),
                                                               in1=es_v.unsqueeze(1).broadcast_to([128, 4, 4, 2]), op=ALU.mult), reads=["PEs", "Et"], writes=["Zb"])
                    else:
                        S.op("dve", lambda e: e.tensor_tensor(out=Zb[:, :, 63].rearrange("p (t h) -> p t h", t=4), in0=PEs[:].rearrange("p (t h) -> p t h", t=4),
                                                               in1=es_v.unsqueeze(1).broadcast_to([128, 4, 8]), op=ALU.mult), reads=["PEs", "Et"], writes=["Zb"])
                    for h in range(8):
                        if isA:
                            col = (h % 4) * 2 + h // 4
                            kv = h // 4
                        else:
                            col = h
                            kv = h
                        for t in range(4):
                            tok = 4 * n + t
                            vs = tok % 8
                            S.op("pe", lambda e, h=h, col=col, kv=kv, t=t, vs=vs, tok=tok: e.matmul(OPp[:, h * 128:h * 128 + 65], lhsT=Zb[:, t * 8 + col, 63 - tok:191 - tok], rhs=V1s[vs][:, kv, :], start=(t == 0), stop=(t == 3)),
                                 reads=["Zb", ("V1s", vs)], writes=[("OP", h // 4)])
                    S.op("dve", lambda e: e.tensor_tensor(out=OsAcc[:], in0=OsAcc[:], in1=OPp[:].rearrange("p (h c) -> p h c", h=8)[:, :, 0:65], op=ALU.add), reads=["OsAcc", ("OP", 0), ("OP", 1)], writes=["OsAcc"])
                k = rr["ev"] % 2
                rr["ev"] += 1
                if isA:
                    S.op("dve", lambda e: e.tensor_tensor(out=LL[:], in0=OsAcc[:, :, 64], in1=SNK[:], op=ALU.add), reads=["OsAcc", "SNK"], writes=["LL"])
                    S.op("dve", lambda e: e.reciprocal(out=LL[:], in_=LL[:]), reads=["LL"], writes=["LL"])
                    S.op("dve", lambda e: e.tensor_tensor(out=OaT[k][:].rearrange("p (h d) -> p h d", d=64), in0=OsAcc[:, :, 0:64], in1=LL[:].unsqueeze(2).broadcast_to([128, 8, 64]), op=ALU.mult), reads=["OsAcc", "LL"], writes=[("OaT", k)])
                    S.op("sp", lambda e: e.dma_start(out=Oa_scr.ap()[NOWN:NOWN + 128, :], in_=OaT[k][:]), reads=[("OaT", k)], writes=["OSCR_a"], dma=True)
                else:
                    S.op("sp", lambda e: e.dma_start(out=Oscr[gname].ap()[NOWN:NOWN + 128, :], in_=OsAcc[:].rearrange("p h c -> p (h c)")), reads=["OsAcc"], writes=["OSCR_" + gname], dma=True)

            def attn_phase(gname):
                G = GROUPS[gname]
                d = G["d"]
                isA = gname == "a"
                nk = 128 if isA else 512
                if isA:
                    qr = [(C_QA + kvh * 256 + g * 64, 64) for g in range(4) for kvh in range(2)]
                else:
                    qr = [(G["cq"], 512)]
                load_w(Wsb, qr + [(G["ck"], nk), (G["cv"], nk)], "W")
                if SUB >= 2:
                    build_E(gname)
                oscr = Oa_scr.ap() if isA else Oscr[gname].ap()
                nkv = nkv_out[gname]
                ncb = NB // d
                win = {"a": 128, "b1": 128, "b2": 512, "b3": 2048}[gname]
                par = 0
                for r in range(min(d, NCLS) if SUB >= 3 else 0):
                    hs = NOWN - 128 * d + r
                    qkv_block(gname, lambda c, hs=hs: xT_halo[:, c, hs:hs + 127 * d + 1:d] if d > 1 else xT_halo[:, c, hs:hs + 128], XTH_ALL, par, want_q=False)
                    par ^= 1
                    for cb in range(ncb if SUB >= 4 else 0):
                        st = r + d * 128 * cb
                        kv_out = None
                        lo = NOWN - win
                        if st >= lo:
                            r0 = st - lo
                            if d > 1:
                                kv_out = (nkv[r0:r0 + 127 * d + 1:d, 0, :], nkv[r0:r0 + 127 * d + 1:d, 1, :])
                            else:
                                kv_out = (nkv[r0:r0 + 128, 0, :], nkv[r0:r0 + 128, 1, :])
                        qkv_block(gname, (lambda c, st=st: xT_own[:, c, st:st + 127 * d + 1:d]) if d > 1 else (lambda c, st=st: xT_own[:, c, st:st + 128]), XTO_ALL, par, want_q=True, kv_out=kv_out)
                        rows = oscr[st:st + 127 * d + 1:d, :] if d > 1 else oscr[st:st + 128, :]
                        if SUB >= 5:
                            attend(gname, par, first=(cb == 0), out_rows=rows)
                        par ^= 1
                if WITH_CACHE and SUB >= 6:
                    sample_attn(gname)

            for gi_, gname in enumerate(("b3", "b2", "b1", "a")):
                if LEVEL >= 3 + gi_:
                    attn_phase(gname)

        with ExitStack() as esF:
            def sbF(name, shape, dt):
                return esF.enter_context(nc.sbuf_tensor(name, shape, dt))

            Wg = sbF("Wg", [128, 8, 3072], BF16)
            WuA = sbF("WuA", [128, 4, 1024], BF16)
            WuB = sbF("WuB", [128, 4, 1024], BF16)
            Wo = sbF("Wo", [128, 8, 1024], BF16)
            Wst = [sbF("WstF%d" % i, [128, 1536], F32) for i in range(4)]
            O1 = sbF("O1", [128, 520], F32)
            O2 = sbF("O2", [128, 520], F32)
            O3 = sbF("O3", [128, 520], F32)
            OA = sbF("OA", [128, 512], F32)
            XR = sbF("XR", [128, 1024], F32)
            SG = sbF("SG", [128, 1024], F32)
            SM = sbF("SM", [128, 2048], F32)
            LB = sbF("LB", [128, 8], F32)
            U = sbF("U", [128, 1024], BF16)
            UT = sbF("UT", [128, 8, 128], BF16)
            M1 = sbF("M1", [128, 1024], F32)
            M2 = sbF("M2", [128, 1024], F32)
            MG = sbF("MG", [128, 1024], BF16)
            MT = sbF("MT", [128, 8, 128], BF16)
            Y = sbF("Y", [128, 1024], F32)

            wc = {"n": 0}
            BAR = S.last_all()

            def load_wF(dst_fn, src_ap_fn, ncols_list, key, scale_gain):
                for (c0, n, off) in ncols_list:
                    for c in range(dst_fn("nchunk")):
                        k = wc["n"] % 4
                        wc["n"] += 1
                        st = Wst[k]
                        S.op("sp", lambda e, c=c, c0=c0, n=n, st=st: e.dma_start(out=st[:, 0:n], in_=src_ap_fn(c, c0, n)), writes=[("WstF", k)], dma=True, extra=BAR)
                        useact = bool(wc["n"] % 2)
                        if scale_gain:
                            if useact:
                                S.op("act", lambda e, c=c, n=n, st=st, off=off: e.activation(out=dst_fn(c)[:, off:off + n], in_=st[:, 0:n], func=AF.Copy, scale=NG[:, c:c + 1]), reads=[("WstF", k), "NG"], writes=[key])
                            else:
                                S.op("dve", lambda e, c=c, n=n, st=st, off=off: e.tensor_scalar(out=dst_fn(c)[:, off:off + n], in0=st[:, 0:n], scalar1=NG[:, c:c + 1], scalar2=None, op0=ALU.mult), reads=[("WstF", k), "NG"], writes=[key])
                        else:
                            if useact:
                                S.op("act", lambda e, c=c, n=n, st=st, off=off: e.activation(out=dst_fn(c)[:, off:off + n], in_=st[:, 0:n], func=AF.Copy), reads=[("WstF", k)], writes=[key])
                            else:
                                S.op("dve", lambda e, c=c, n=n, st=st, off=off: e.tensor_copy(out=dst_fn(c)[:, off:off + n], in_=st[:, 0:n]), reads=[("WstF", k)], writes=[key])

            load_wF(lambda c: 8 if c == "nchunk" else Wg[:, c, :], lambda c, c0, n: w_in[c * 128:(c + 1) * 128, c0:c0 + n],
                    [(C_GA, 512, 0), (C_GB, 512, 512), (C_MA, 1024, 1024), (C_MB, 1024, 2048)], "Wg", True)
            load_wF(lambda c: 4 if c == "nchunk" else WuA[:, c, :], lambda c, c0, n: wup_a_in[c * 128:(c + 1) * 128, c0:c0 + n], [(0, 1024, 0)], "WuA", False)
            load_wF(lambda c: 4 if c == "nchunk" else WuB[:, c, :], lambda c, c0, n: wup_b_in[c * 128:(c + 1) * 128, c0:c0 + n], [(0, 1024, 0)], "WuB", False)
            load_wF(lambda c: 8 if c == "nchunk" else Wo[:, c, :], lambda c, c0, n: wout_in[c * 128:(c + 1) * 128, c0:c0 + n], [(0, 1024, 0)], "Wo", False)

            def final_block(lhs_fn, xkeys, row0, x_src, y_dst, nrows=128, xres=None):
                S.op("sp", lambda e: e.dma_start(out=O1[:], in_=Oscr["b1"].ap()[row0:row0 + 128, :]), reads=["OSCR_b1"], writes=["O1"], dma=True)
                S.op("sp", lambda e: e.dma_start(out=O2[:], in_=Oscr["b2"].ap()[row0:row0 + 128, :]), reads=["OSCR_b2"], writes=["O2"], dma=True)
                S.op("sp", lambda e: e.dma_start(out=O3[:], in_=Oscr["b3"].ap()[row0:row0 + 128, :]), reads=["OSCR_b3"], writes=["O3"], dma=True)
                S.op("sp", lambda e: e.dma_start(out=OA[:], in_=Oa_scr.ap()[row0:row0 + 128, :]), reads=["OSCR_a"], writes=["OA"], dma=True)
                if xres is not None and DBG:
                    S.op("sp", lambda e: e.dma_start(out=dbg_o[0], in_=O1[:]), reads=["O1"], dma=True)
                    S.op("sp", lambda e: e.dma_start(out=dbg_o[1], in_=O2[:]), reads=["O2"], dma=True)
                    S.op("sp", lambda e: e.dma_start(out=dbg_o[2], in_=O3[:]), reads=["O3"], dma=True)
                    S.op("sp", lambda e: e.dma_start(out=dbg_o[3, :, 0:512], in_=OA[:]), reads=["OA"], dma=True)
                if xres is None:
                    S.op("sp", lambda e: e.dma_start(out=XR[:], in_=x_src), writes=["XR"], dma=True)
                    xr, xrk = XR, "XR"
                else:
                    xr, xrk = xres, "Xs_f"
                for rnd in range(2):
                    for g in range(3):
                        for c in range(8):
                            S.op("pe", lambda e, g=g, c=c, rnd=rnd: e.matmul(PJ[:, g * 512:(g + 1) * 512], lhsT=lhs_fn(c), rhs=Wg[:, c, rnd * 1536 + g * 512: rnd * 1536 + (g + 1) * 512], start=(c == 0), stop=(c == 7)),
                                 reads=list(xkeys) + ["Wg"], writes=[("PJ", g)])
                    pjk = [("PJ", g) for g in range(3)]
                    if rnd == 0:
                        S.op("act", lambda e: e.activation(out=SG[:], in_=PJ[:, 0:1024], func=AF.Silu), reads=pjk, writes=["SG"])
                        S.op("act", lambda e: e.activation(out=SM[:, 0:512], in_=PJ[:, 1024:1536], func=AF.Sigmoid), reads=pjk, writes=["SM"])
                    else:
                        S.op("act", lambda e: e.activation(out=SM[:, 512:2048], in_=PJ[:, 0:1536], func=AF.Sigmoid), reads=pjk, writes=["SM"])
                S.op("dve", lambda e: e.tensor_tensor(out=O1[:], in0=O1[:], in1=O2[:], op=ALU.add), reads=["O1", "O2"], writes=["O1"])
                S.op("dve", lambda e: e.tensor_tensor(out=O1[:], in0=O1[:], in1=O3[:], op=ALU.add), reads=["O1", "O3"], writes=["O1"])
                o1v = O1[:].rearrange("p (h c) -> p h c", c=65)
                S.op("dve", lambda e: e.reciprocal(out=LB[:], in_=o1v[:, :, 64]), reads=["O1"], writes=["LB"])
                S.op("dve", lambda e: e.tensor_tensor(out=O2[:, 0:512].rearrange("p (h d) -> p h d", d=64), in0=o1v[:, :, 0:64], in1=LB[:].unsqueeze(2).broadcast_to([128, 8, 64]), op=ALU.mult), reads=["O1", "LB"], writes=["O2"])
                S.op("dve", lambda e: e.tensor_tensor(out=U[:, 0:512], in0=OA[:], in1=SG[:, 0:512], op=ALU.mult), reads=["OA", "SG"], writes=["U"])
                S.op("dve", lambda e: e.tensor_tensor(out=U[:, 512:1024], in0=O2[:, 0:512], in1=SG[:, 512:1024], op=ALU.mult), reads=["O2", "SG"], writes=["U"])
                for c in range(8):
                    S.op("pe", lambda e, c=c: e.transpose(out=TR[:, c * 128:(c + 1) * 128], in_=U[:, c * 128:(c + 1) * 128], identity=ident[:]), reads=["U", "ident"], writes=["TR"])
                S.op("act", lambda e: e.activation(out=UT[:], in_=TR[:].rearrange("p (c t) -> p c t", c=8), func=AF.Copy), reads=["TR"], writes=["UT"])
                for n in range(2):
                    for c in range(4):
                        S.op("pe", lambda e, n=n, c=c: e.matmul(SPp[:, n * 512:(n + 1) * 512], lhsT=UT[:, c, :], rhs=WuA[:, c, n * 512:(n + 1) * 512], start=(c == 0), stop=(c == 3)), reads=["UT", "WuA"], writes=[("SP", n)])
                for n in range(2):
                    for c in range(4):
                        S.op("pe", lambda e, n=n, c=c: e.matmul(OPp[:, n * 512:(n + 1) * 512], lhsT=UT[:, 4 + c, :], rhs=WuB[:, c, n * 512:(n + 1) * 512], start=(c == 0), stop=(c == 3)), reads=["UT", "WuB"], writes=[("OP", n)])
                S.op("dve", lambda e: e.tensor_tensor(out=M1[:], in0=SPp[:], in1=SM[:, 0:1024], op=ALU.mult), reads=[("SP", 0), ("SP", 1), "SM"], writes=["M1"])
                S.op("dve", lambda e: e.tensor_tensor(out=M2[:], in0=OPp[:], in1=SM[:, 1024:2048], op=ALU.mult), reads=[("OP", 0), ("OP", 1), "SM"], writes=["M2"])
                S.op("dve", lambda e: e.tensor_tensor(out=MG[:], in0=M1[:], in1=M2[:], op=ALU.add), reads=["M1", "M2"], writes=["MG"])
                for c in range(8):
                    S.op("pe", lambda e, c=c: e.transpose(out=TR[:, c * 128:(c + 1) * 128], in_=MG[:, c * 128:(c + 1) * 128], identity=ident[:]), reads=["MG", "ident"], writes=["TR"])
                S.op("act", lambda e: e.activation(out=MT[:], in_=TR[:].rearrange("p (c t) -> p c t", c=8), func=AF.Copy), reads=["TR"], writes=["MT"])
                for n in range(2):
                    for c in range(8):
                        S.op("pe", lambda e, n=n, c=c: e.matmul(SPp[:, n * 512:(n + 1) * 512], lhsT=MT[:, c, :], rhs=Wo[:, c, n * 512:(n + 1) * 512], start=(c == 0), stop=(c == 7)), reads=["MT", "Wo"], writes=[("SP", n)])
                S.op("dve", lambda e: e.tensor_tensor(out=Y[:], in0=SPp[:], in1=xr[:], op=ALU.add), reads=[("SP", 0), ("SP", 1), xrk], writes=["Y"])
                S.op("sp", lambda e: e.dma_start(out=y_dst, in_=Y[0:nrows, :]), reads=["Y"], dma=True)

            for t in range(NB if LEVEL >= 7 else 0):
                final_block(lambda c, t=t: xT_own[:, c, t * 128:(t + 1) * 128], [("xTo", t)], t * 128, x_ext[NOWN + t * 128:NOWN + (t + 1) * 128, :], y_out[t * 128:(t + 1) * 128, :])

            if WITH_CACHE and LEVEL >= 8:
                final_block(lambda c: xT_s[:, c, 0:128], ["xTs"], NOWN, None, ys_out, nrows=64, xres=Xs_f)

        S.emit(sems, dsems, block)
    return nc


def shared_inputs(rel_bias, norm_gain, w_in, q_gain_a, k_gain_a, sinks_a, q_gain_b, k_gain_b, w_up_a, w_up_b, w_out):
    relb = np.zeros((128, 128), np.float32)
    relb[:32, :32] = rel_bias
    oh = onehot_tables()
    mask2 = np.zeros((128, 2), np.float32)
    mask2[:64, 0] = 1.0
    mask2[64:, 1] = 1.0
    return {
        "mask2": mask2,
        "w_in": w_in[0], "ng": np.ascontiguousarray(norm_gain[0].reshape(8, 128).T), "relb": relb, "oh": oh,
        "gq_a": np.ascontiguousarray(np.broadcast_to(q_gain_a[0][None, :], (128, 64))),
        "gk_a": np.ascontiguousarray(np.broadcast_to(k_gain_a[0][None, :], (128, 64))),
        "gq_b": np.ascontiguousarray(np.broadcast_to(q_gain_b[0].reshape(1, 192), (128, 192))),
        "gk_b": np.ascontiguousarray(np.broadcast_to(k_gain_b[0].reshape(1, 192), (128, 192))),
        "sinks": np.ascontiguousarray(np.broadcast_to(sinks_a[0][None, :], (128, 8))),
        "wup_a": w_up_a[0], "wup_b": w_up_b[0], "wout": w_out[0],
    }


_CACHE = {}


def kernel(x_prompt, x_sample, cache_a_kv, cache_b1_kv, cache_b2_kv, cache_b3_kv, rel_bias, norm_gain, w_in,
           q_gain_a, k_gain_a, sinks_a, q_gain_b, k_gain_b, w_up_a, w_up_b, w_out):
    f = lambda a: np.ascontiguousarray(np.asarray(a, dtype=np.float32))
    x_prompt = f(x_prompt); x_sample = f(x_sample)
    cache_a_kv = f(cache_a_kv); cache_b1_kv = f(cache_b1_kv); cache_b2_kv = f(cache_b2_kv); cache_b3_kv = f(cache_b3_kv)
    rel_bias = f(rel_bias); norm_gain = f(norm_gain); w_in = f(w_in)
    q_gain_a = f(q_gain_a); k_gain_a = f(k_gain_a); sinks_a = f(sinks_a); q_gain_b = f(q_gain_b); k_gain_b = f(k_gain_b)
    w_up_a = f(w_up_a); w_up_b = f(w_up_b); w_out = f(w_out)

    nc = build_program()
    shared = shared_inputs(rel_bias, norm_gain, w_in, q_gain_a, k_gain_a, sinks_a, q_gain_b, k_gain_b, w_up_a, w_up_b, w_out)

    in_maps = []
    for c in range(8):
        b, h = c // 2, c % 2
        x_ext = np.zeros((4096, 1024), np.float32)
        if h == 1:
            x_ext[:] = x_prompt[b]
        else:
            x_ext[2048:] = x_prompt[b, :2048]
        xs = np.zeros((128, 1024), np.float32)
        xs[:64] = x_sample[16 * c:16 * c + 16].reshape(64, 1024)
        m = dict(shared)
        m["x_ext"] = x_ext
        m["x_s"] = xs
        m["hv"] = np.full((128, 1), float(h), np.float32)
        if WITH_CACHE:
          m["c_a"] = np.ascontiguousarray(cache_a_kv[0, 16 * c:16 * c + 16].reshape(16, 128, 256))
          m["c_b1"] = np.ascontiguousarray(cache_b1_kv[0, 16 * c:16 * c + 16].reshape(16, 128, 1024))
          m["c_b2"] = np.ascontiguousarray(cache_b2_kv[0, 16 * c:16 * c + 16].reshape(16, 512, 1024))
          m["c_b3"] = np.ascontiguousarray(cache_b3_kv[0, 16 * c:16 * c + 16].reshape(16, 2048, 1024))
        in_maps.append(m)
    res = run_bass_kernel_spmd(nc, in_maps, core_ids=list(range(8)))
    R = res.results
    y = np.zeros((4, 4096, 1024), np.float32)
    ys = np.zeros((128, 4, 1024), np.float32)
    for c in range(8):
        b, h = c // 2, c % 2
        y[b, h * 2048:(h + 1) * 2048] = R[c]["y"]
        ys[16 * c:16 * c + 16] = R[c]["ys"].reshape(16, 4, 1024)
    np_a = np.stack([R[2 * b + 1]["nkv_a"].reshape(128, 2, 2, 64) for b in range(4)])[None]
    np_b1 = np.stack([R[2 * b + 1]["nkv_b1"].reshape(128, 2, 8, 64) for b in range(4)])[None]
    np_b2 = np.stack([R[2 * b + 1]["nkv_b2"].reshape(512, 2, 8, 64) for b in range(4)])[None]
    np_b3 = np.stack([R[2 * b + 1]["nkv_b3"].reshape(2048, 2, 8, 64) for b in range(4)])[None]
    ns_a = np.concatenate([R[c]["ns_a"].reshape(16, 4, 2, 2, 64) for c in range(8)])[None]
    ns_b1 = np.concatenate([R[c]["ns_b1"].reshape(16, 4, 2, 8, 64) for c in range(8)])[None]
    ns_b2 = np.concatenate([R[c]["ns_b2"].reshape(16, 4, 2, 8, 64) for c in range(8)])[None]
    ns_b3 = np.concatenate([R[c]["ns_b3"].reshape(16, 4, 2, 8, 64) for c in range(8)])[None]
    return (y, ys, np_a, np_b1, np_b2, np_b3, ns_a, ns_b1, ns_b2, ns_b3)
```

```python
import math
import os
from contextlib import ExitStack

import numpy as np
import concourse.bass as bass
import concourse.mybir as mybir
from concourse.bass_utils import run_bass_kernel_spmd

F32 = mybir.dt.float32
BF16 = mybir.dt.bfloat16
AF = mybir.ActivationFunctionType
ALU = mybir.AluOpType
AX = mybir.AxisListType

SAME_ENGINE_SYNC = os.environ.get('KDBG_SES', '1') == '1'
LEVEL = int(os.environ.get('KDBG_LEVEL', '99'))
NCLS = int(os.environ.get('KDBG_NCLS', '99'))
NDS = 14
NKV = 2
WITH_CACHE = os.environ.get('KDBG_NOSAMPLE', '0') != '1'
SUB = int(os.environ.get('KDBG_SUB', '99'))
XB = int(os.environ.get('KDBG_X', '0'))
EPS = 1e-6
NOWN = 2048
NB = 16
C_QA, C_KA, C_VA, C_GA = 0, 512, 640, 768
C_QB, C_KB, C_VB, C_GB, C_MA, C_MB = 1280, 2816, 4352, 5888, 6400, 7424


class Sched:
    ENGS = ("pe", "act", "dve", "pool", "sp")

    def __init__(self, nc, n_dma_sems=6):
        self.nc = nc
        self.ops = {e: [] for e in self.ENGS}
        self.lastw = {}
        self.readers = {}
        self.n_dma_sems = n_dma_sems
        self.dma_count = {"sp": 0, "pool": 0, "act": 0}
        self.dma_last = {}

    def last_all(self):
        return [(e, len(self.ops[e]) - 1) for e in self.ENGS if self.ops[e]]

    def op(self, eng, fn, reads=(), writes=(), dma=False, extra=()):
        ops = self.ops[eng]
        idx = len(ops)
        deps = set(extra)
        for b in reads:
            w = self.lastw.get(b)
            if w is not None:
                deps.add(w)
        for b in writes:
            w = self.lastw.get(b)
            if w is not None:
                deps.add(w)
            for r in self.readers.get(b, ()):
                deps.add(r)
        cdeps = {}
        ddeps = set()
        for (e, i) in deps:
            o = self.ops[e][i]
            if o["dma"]:
                ddeps.add(o["sig"])
            else:
                if e == eng and not dma and (e == "pe" or not SAME_ENGINE_SYNC):
                    continue
                cdeps[e] = max(cdeps.get(e, -1), i)
        rec = {"fn": fn, "dma": dma, "cdeps": cdeps, "ddeps": ddeps, "sig": None, "signaled": False}
        if dma:
            n = self.dma_count[eng]
            self.dma_count[eng] = n + 1
            slot = n % self.n_dma_sems
            val = 16 * (n // self.n_dma_sems + 1)
            rec["sig"] = (eng, slot, val)
            if val > 16:
                ddeps.add((eng, slot, val - 16))
            self.dma_last[(eng, slot)] = val
        ops.append(rec)
        me = (eng, idx)
        for b in writes:
            self.lastw[b] = me
            self.readers[b] = []
        for b in reads:
            if b in writes:
                continue
            self.readers.setdefault(b, []).append(me)
        return me

    def emit(self, sems, dsems, block):
        for e in self.ENGS:
            for o in self.ops[e]:
                for (de, di) in o["cdeps"].items():
                    self.ops[de][di]["signaled"] = True
        for e in self.ENGS:
            c = 0
            for o in self.ops[e]:
                if o["dma"]:
                    continue
                if o["signaled"]:
                    c += 1
                    o["sig"] = c
        allops = self.ops
        dma_last = self.dma_last

        def run(eng_name, eng):
            waited = {}
            for o in allops[eng_name]:
                for (de, di) in sorted(o["cdeps"].items()):
                    v = allops[de][di]["sig"]
                    key = ("c", de)
                    if waited.get(key, 0) >= v:
                        continue
                    eng.wait_ge(sems[de], v)
                    waited[key] = v
                for (qe, slot, v) in sorted(o["ddeps"]):
                    key = ("d", qe, slot)
                    if waited.get(key, 0) >= v:
                        continue
                    eng.wait_ge(dsems[(qe, slot)], v)
                    waited[key] = v
                ins = o["fn"](eng)
                if o["dma"]:
                    qe, slot, v = o["sig"]
                    ins.then_inc(dsems[(qe, slot)], 16)
                elif o["signaled"]:
                    ins.then_inc(sems[eng_name], 1)
            if eng_name == "sp":
                for (qe, slot), v in sorted(dma_last.items()):
                    if waited.get(("d", qe, slot), 0) >= v:
                        continue
                    eng.wait_ge(dsems[(qe, slot)], v)

        @block.tensor
        def _(eng):
            run("pe", eng)

        @block.scalar
        def _(eng):
            run("act", eng)

        @block.vector
        def _(eng):
            run("dve", eng)

        @block.gpsimd
        def _(eng):
            run("pool", eng)

        @block.sync
        def _(eng):
            run("sp", eng)


def t5_bucket_np(dist):
    d = np.maximum(dist, 0)
    df = np.maximum(d, 1).astype(np.float32)
    large = 16 + (np.log(df / np.float32(16)) / np.float32(math.log(2048 / 16)) * np.float32(16)).astype(np.int32)
    large = np.minimum(large, 31)
    return np.where(d < 16, d, large)


def onehot_tables():
    oh = np.zeros((3, 128, 384), np.float32)
    for di, dil in enumerate((1, 4, 16)):
        delta = np.arange(128)
        b = t5_bucket_np(delta * dil)
        oh[di, b, delta + 127] = 1.0
    return oh


GROUPS = {
    "b3": dict(d=16, di=2, hb=24, cq=C_QB + 1024, ck=C_KB + 1024, cv=C_VB + 1024, nkv=8),
    "b2": dict(d=4, di=1, hb=16, cq=C_QB + 512, ck=C_KB + 512, cv=C_VB + 512, nkv=8),
    "b1": dict(d=1, di=0, hb=8, cq=C_QB, ck=C_KB, cv=C_VB, nkv=8),
    "a": dict(d=1, di=0, hb=0, cq=C_QA, ck=C_KA, cv=C_VA, nkv=2),
}


def build_program(with_sample=True):
    nc = bass.Bass("TRN2", target_bir_lowering=False)

    def din(name, shape):
        return nc.dram_tensor(name, shape, F32, kind="ExternalInput").ap()

    def dout(name, shape):
        return nc.dram_tensor(name, shape, F32, kind="ExternalOutput").ap()

    x_ext = din("x_ext", [4096, 1024])
    x_s = din("x_s", [128, 1024])
    hv_in = din("hv", [128, 1])
    mask_in = din("mask2", [128, 2])
    w_in = din("w_in", [1024, 8448])
    ng_in = din("ng", [128, 8])
    relb_in = din("relb", [128, 128])
    oh_in = din("oh", [3, 128, 384])
    gq_a_in = din("gq_a", [128, 64])
    gk_a_in = din("gk_a", [128, 64])
    gq_b_in = din("gq_b", [128, 192])
    gk_b_in = din("gk_b", [128, 192])
    sinks_in = din("sinks", [128, 8])
    wup_a_in = din("wup_a", [512, 1024])
    wup_b_in = din("wup_b", [512, 1024])
    wout_in = din("wout", [1024, 1024])
    if WITH_CACHE:
        c_a_in = din("c_a", [16, 128, 256])
        c_b1_in = din("c_b1", [16, 128, 1024])
        c_b2_in = din("c_b2", [16, 512, 1024])
        c_b3_in = din("c_b3", [16, 2048, 1024])

    y_out = dout("y", [2048, 1024])
    ys_out = dout("ys", [64, 1024])
    nkv_out = {"a": dout("nkv_a", [128, 2, 128]), "b1": dout("nkv_b1", [128, 2, 512]),
               "b2": dout("nkv_b2", [512, 2, 512]), "b3": dout("nkv_b3", [2048, 2, 512])}
    nskv_out = {"a": dout("ns_a", [64, 2, 128]), "b1": dout("ns_b1", [64, 2, 512]),
                "b2": dout("ns_b2", [64, 2, 512]), "b3": dout("ns_b3", [64, 2, 512])}

    DBG = os.environ.get('KDBG_DUMP', '0') == '1'
    if DBG:
        dbg_o = dout("dbgo", [4, 128, 520])
    NTS = NOWN + 128
    Oscr = {g: nc.dram_tensor("oscr_" + g, [NTS, 520], F32) for g in ("b1", "b2", "b3")}
    Oa_scr = nc.dram_tensor("oscr_a", [NTS, 512], F32)
    EFscr = nc.dram_tensor("efscr", [3, 32, 384], F32)

    S = Sched(nc, n_dma_sems=NDS)
    rr = {"ev": 0}

    with ExitStack() as es:
        def sb(name, shape, dt):
            return es.enter_context(nc.sbuf_tensor(name, shape, dt))

        def ps(name, shape, dt):
            return es.enter_context(nc.psum_tensor(name, shape, dt))

        sems = {e: es.enter_context(nc.semaphore("s_" + e)) for e in ("pe", "act", "dve", "pool")}
        dsems = {(q, i): es.enter_context(nc.semaphore(f"d_{q}{i}")) for q in ("sp", "pool") for i in range(NDS)}

        xT_own = sb("xT_own", [128, 8, NOWN], BF16)
        xT_s = sb("xT_s", [128, 8, 128], BF16)
        Xs_f = sb("Xs_f", [128, 1024], F32)
        ident = sb("ident", [128, 128], BF16)
        Jm = sb("Jm", [128, 128], BF16)
        identf = sb("identf", [128, 128], F32)
        NG = sb("NG", [128, 8], F32)
        HV = sb("HV", [128, 1], F32)
        MASK = sb("MASK", [128, 2], F32)
        GQA = sb("GQA", [128, 64], F32)
        GKA = sb("GKA", [128, 64], F32)
        GQB = sb("GQB", [128, 192], F32)
        GKB = sb("GKB", [128, 192], F32)
        SNK = sb("SNK", [128, 8], F32)
        PJ = ps("PJ", [128, 1536], F32)
        TR = ps("TR", [128, 1024], BF16)
        SPp = ps("SPp", [128, 1024], F32)
        OPp = ps("OPp", [128, 1024], F32)

        block = es.enter_context(nc.Block())

        S.op("sp", lambda e: e.dma_start(out=NG[:], in_=ng_in), writes=["NG"], dma=True)
        S.op("sp", lambda e: e.dma_start(out=HV[:], in_=hv_in), writes=["HV"], dma=True)
        S.op("sp", lambda e: e.dma_start(out=MASK[:], in_=mask_in), writes=["MASK"], dma=True)
        S.op("sp", lambda e: e.dma_start(out=GQA[:], in_=gq_a_in), writes=["GQA"], dma=True)
        S.op("sp", lambda e: e.dma_start(out=GKA[:], in_=gk_a_in), writes=["GKA"], dma=True)
        S.op("sp", lambda e: e.dma_start(out=GQB[:], in_=gq_b_in), writes=["GQB"], dma=True)
        S.op("sp", lambda e: e.dma_start(out=GKB[:], in_=gk_b_in), writes=["GKB"], dma=True)
        S.op("sp", lambda e: e.dma_start(out=SNK[:], in_=sinks_in), writes=["SNK"], dma=True)
        S.op("dve", lambda e: e.tensor_scalar(out=GQA[:], in0=GQA[:], scalar1=0.125, scalar2=None, op0=ALU.mult), reads=["GQA"], writes=["GQA"])
        S.op("dve", lambda e: e.tensor_scalar(out=GQB[:], in0=GQB[:], scalar1=0.125, scalar2=None, op0=ALU.mult), reads=["GQB"], writes=["GQB"])
        S.op("act", lambda e: e.activation(out=SNK[:], in_=SNK[:], func=AF.Exp), reads=["SNK"], writes=["SNK"])
        S.op("pool", lambda e: e.memset(identf[:], 1.0), writes=["identf"])
        S.op("pool", lambda e: e.affine_select(out=identf[:], in_=identf[:], pattern=[[-1, 128]], compare_op=ALU.is_equal, fill=0.0, base=0, channel_multiplier=1), reads=["identf"], writes=["identf"])
        S.op("dve", lambda e: e.tensor_copy(out=ident[:], in_=identf[:]), reads=["identf"], writes=["ident"])
        S.op("pool", lambda e: e.memset(identf[:], 1.0), reads=["identf"], writes=["identf"])
        S.op("pool", lambda e: e.affine_select(out=identf[:], in_=identf[:], pattern=[[1, 128]], compare_op=ALU.is_equal, fill=0.0, base=-127, channel_multiplier=1), reads=["identf"], writes=["identf"])
        S.op("dve", lambda e: e.tensor_copy(out=Jm[:], in_=identf[:]), reads=["identf"], writes=["Jm"])

        RB = sb("RB", [128, 128], F32)
        OHs = sb("OHs", [128, 384], F32)
        EFs = sb("EFs", [128, 384], F32)
        if True:
            S.op("sp", lambda e: e.dma_start(out=RB[:], in_=relb_in), writes=["RB"], dma=True)
            for di in range(3):
                S.op("sp", lambda e, di=di: e.dma_start(out=OHs[:], in_=oh_in[di]), writes=["OHs"], dma=True)
                S.op("pe", lambda e: e.matmul(SPp[:, 0:384], lhsT=RB[:], rhs=OHs[:], start=True, stop=True), reads=["RB", "OHs"], writes=[("SP", 0), ("SP", 1)])
                S.op("act", lambda e: e.activation(out=EFs[:], in_=SPp[:, 0:384], func=AF.Exp), reads=[("SP", 0), ("SP", 1)], writes=["EFs"])
                S.op("dve", lambda e: e.memset(EFs[:, 0:127], 0.0), reads=["EFs"], writes=["EFs"])
                S.op("dve", lambda e: e.memset(EFs[:, 255:384], 0.0), reads=["EFs"], writes=["EFs"])
                S.op("sp", lambda e, di=di: e.dma_start(out=EFscr.ap()[di], in_=EFs[0:32, :]), reads=["EFs"], writes=[("EFscr", di)], dma=True)

        with ExitStack() as esA:
            def sbA(name, shape, dt):
                return esA.enter_context(nc.sbuf_tensor(name, shape, dt))

            xT_halo = sbA("xT_halo", [128, 8, NOWN], BF16)
            Wsb = sbA("Wsb", [128, 8, 1536], BF16)
            Wst = [sbA("Wst%d" % i, [128, 1536], F32) for i in range(4)]
            Xf = [sbA("Xf%d" % i, [128, 1024], F32) for i in range(2)]
            SQ = sbA("SQ", [128, 1024], F32)
            Xb = sbA("Xb", [128, 1024], BF16)
            SS = sbA("SS", [128, 16], F32)
            RS = sbA("RS", [128, 16], F32)
            QN = sbA("QN", [128, 1024], F32)
            QNb = sbA("QNb", [128, 512], BF16)
            KN = sbA("KN", [128, 512], F32)
            KNb = sbA("KNb", [128, 512], BF16)
            VF = sbA("VF", [128, 512], F32)
            QT = sbA("QT", [128, 4, 128], BF16)
            Kpad = [sbA("Kpad%d" % i, [128, 8, 128], BF16) for i in range(3)]
            V1 = [sbA("V1_%d" % i, [128, 8, 65], BF16) for i in range(3)]
            Et = sbA("Et", [128, 8, 256], BF16)
            Eh = sbA("Eh", [128, 256], F32)
            Ehb = sbA("Ehb", [128, 256], BF16)
            PEx = sbA("PEx", [128, 1024], BF16)
            Pt = [sbA("Pt%d" % i, [128, 1024], BF16) for i in range(2)]
            Ost = [sbA("Ost%d" % i, [128, 8, 65], F32) for i in range(2)]
            LL = sbA("LL", [128, 8], F32)
            OaT = [sbA("OaT%d" % i, [128, 512], F32) for i in range(2)]
            Ksb = [sbA("Ksb%d" % i, [128, 512], BF16) for i in range(2)]
            KTs = [sbA("KTs%d" % i, [128, 512], BF16) for i in range(2)]
            V1s = [sbA("V1s%d" % i, [128, 8, 65], BF16) for i in range(8)]
            Qbd = sbA("Qbd", [128, 4, 128, 2], BF16)
            PEs = sbA("PEs", [128, 32], F32)
            Zb = sbA("Zb", [128, 32, 192], BF16)
            S.op("pool", lambda e: e.memset(Zb[:], 0.0), writes=["Zb"])
            OsAcc = sbA("OsAcc", [128, 8, 65], F32)
            for i in range(8):
                S.op("pool", lambda e, i=i: e.memset(V1s[i][:], 1.0), writes=[("V1s", i)])

            for i in range(3):
                S.op("pool", lambda e, i=i: e.memset(Kpad[i][:], 0.0), writes=[("Kpad", i)])
                S.op("pool", lambda e, i=i: e.memset(V1[i][:], 1.0), writes=[("V1", i)])

            def prologue(src_ap, dst_tile, dst_key, col0, k, keep_f32=None):
                xf = Xf[k % 2] if keep_f32 is None else keep_f32
                xkey = ("Xf", k % 2) if keep_f32 is None else "Xs_f"
                S.op("sp", lambda e: e.dma_start(out=xf[:], in_=src_ap), writes=[xkey], dma=True)
                S.op("act", lambda e: e.activation(out=SQ[:], in_=xf[:], func=AF.Square), reads=[xkey], writes=["SQ"])
                S.op("dve", lambda e: e.reduce_sum(out=SS[:, 0:1], in_=SQ[:], axis=AX.X), reads=["SQ"], writes=["SS"])
                S.op("act", lambda e: e.activation(out=RS[:, 0:1], in_=SS[:, 0:1], func=AF.Sqrt, bias=EPS, scale=1.0 / 1024), reads=["SS"], writes=["RS"])
                S.op("dve", lambda e: e.reciprocal(out=RS[:, 0:1], in_=RS[:, 0:1]), reads=["RS"], writes=["RS"])
                S.op("dve", lambda e: e.tensor_scalar(out=Xb[:], in0=xf[:], scalar1=RS[:, 0:1], scalar2=None, op0=ALU.mult), reads=[xkey, "RS"], writes=["Xb"])
                for c in range(8):
                    S.op("pe", lambda e, c=c: e.transpose(out=TR[:, c * 128:(c + 1) * 128], in_=Xb[:, c * 128:(c + 1) * 128], identity=ident[:]), reads=["Xb", "ident"], writes=["TR"])
                S.op("act", lambda e: e.activation(out=dst_tile[:, :, col0:col0 + 128], in_=TR[:].rearrange("p (c t) -> p c t", c=8), func=AF.Copy), reads=["TR"], writes=[dst_key])

            if LEVEL >= 2:
                for t in range(NB):
                    prologue(x_ext[t * 128:(t + 1) * 128, :], xT_halo, ("xTh", t), t * 128, t)
                for t in range(NB):
                    prologue(x_ext[NOWN + t * 128:NOWN + (t + 1) * 128, :], xT_own, ("xTo", t), t * 128, t)
                prologue(x_s, xT_s, "xTs", 0, 0, keep_f32=Xs_f)
            XTH_ALL = [("xTh", t) for t in range(NB)]
            XTO_ALL = [("xTo", t) for t in range(NB)]

            wcnt = {"n": 0}
            WK = {}
            WKR = {}

            def load_w(dst, col_ranges, key):
                WK[key] = []
                off = 0
                for (c0, n) in col_ranges:
                    for c in range(8):
                        k = wcnt["n"] % 4
                        wcnt["n"] += 1
                        st = Wst[k]
                        S.op("sp", lambda e, c=c, c0=c0, n=n, st=st: e.dma_start(out=st[:, 0:n], in_=w_in[c * 128:(c + 1) * 128, c0:c0 + n]), writes=[("Wst", k)], dma=True)
                        if wcnt["n"] % 2:
                            S.op("act", lambda e, c=c, n=n, st=st, off=off: e.activation(out=dst[:, c, off:off + n], in_=st[:, 0:n], func=AF.Copy, scale=NG[:, c:c + 1]), reads=[("Wst", k), "NG"] + [("Wrd", key)], writes=[(key, c, off)])
                        else:
                            S.op("dve", lambda e, c=c, n=n, st=st, off=off: e.tensor_scalar(out=dst[:, c, off:off + n], in0=st[:, 0:n], scalar1=NG[:, c:c + 1], scalar2=None, op0=ALU.mult), reads=[("Wst", k), "NG"] + [("Wrd", key)], writes=[(key, c, off)])
                        WK[key].append((key, c, off))
                    off += n

            def inproj(lhs_fn, xkeys, ncols, wtile=None, wkey="W", wcol0=0):
                wt = Wsb if wtile is None else wtile
                ng_ = (ncols + 511) // 512
                for g in range(ng_):
                    n = min(512, ncols - g * 512)
                    for c in range(8):
                        S.op("pe", lambda e, g=g, c=c, n=n: e.matmul(PJ[:, g * 512:g * 512 + n], lhsT=lhs_fn(c), rhs=wt[:, c, wcol0 + g * 512:wcol0 + g * 512 + n], start=(c == 0), stop=(c == 7)),
                             reads=list(xkeys) + WK.get(wkey, [wkey]), writes=[("PJ", g), ("Wrd", wkey)])

            def build_E(gname):
                G = GROUPS[gname]
                for h in range(8):
                    src = bass.AP(tensor=EFscr, offset=(G["di"] * 32 + G["hb"] + h) * 384, ap=[[1, 128], [128, 2], [1, 128]])
                    S.op("sp", lambda e, src=src: e.dma_start(out=Eh[:].rearrange("p (a b) -> p a b", a=2), in_=src), reads=[("EFscr", G["di"])], writes=["Eh"], dma=True)
                    S.op("act", lambda e: e.activation(out=Ehb[:], in_=Eh[:], func=AF.Copy), reads=["Eh"], writes=["Ehb"])
                    S.op("pe", lambda e: e.matmul(SPp[:, 0:256], lhsT=Jm[:], rhs=Ehb[:], start=True, stop=True), reads=["Jm", "Ehb"], writes=[("SP", 0)])
                    S.op("dve", lambda e, h=h: e.tensor_copy(out=Et[:, h, :], in_=SPp[:, 0:256]), reads=[("SP", 0)], writes=["Et"])

            def qkv_block(gname, lhs_fn, xkeys, par, want_q, kv_out=None, nrows=128):
                isA = gname == "a"
                nq = 512
                nk = 128 if isA else 512
                nkh = nk // 64
                if want_q:
                    ncols = nq + 2 * nk
                    ko, vo = nq, nq + nk
                    wc0 = 0
                else:
                    ncols = 2 * nk
                    ko, vo = 0, nk
                    wc0 = nq
                ngrp = (ncols + 511) // 512
                pjk = [("PJ", g) for g in range(ngrp)]
                nn = (nq + nk) if want_q else nk
                nh = nn // 64
                if isA:
                    gq, gk = GQA[:, :], GKA[:, :]
                else:
                    gi = {"b1": 0, "b2": 1, "b3": 2}[gname]
                    gq, gk = GQB[:, gi * 64:(gi + 1) * 64], GKB[:, gi * 64:(gi + 1) * 64]
                nkt = nk // 128

                def proj():
                    inproj(lhs_fn, xkeys, ncols, wcol0=wc0)

                def norm():
                    S.op("act", lambda e: e.activation(out=SQ[:, 0:nn], in_=PJ[:, 0:nn], func=AF.Square), reads=pjk, writes=["SQ"])
                    S.op("dve", lambda e: e.reduce_sum(out=SS[:, 0:nh], in_=SQ[:, 0:nn].rearrange("p (h d) -> p h d", d=64), axis=AX.X), reads=["SQ"], writes=["SS"])
                    S.op("act", lambda e: e.activation(out=RS[:, 0:nh], in_=SS[:, 0:nh], func=AF.Ln, bias=EPS, scale=1.0 / 64), reads=["SS"], writes=["RS"])
                    S.op("act", lambda e: e.activation(out=RS[:, 0:nh], in_=RS[:, 0:nh], func=AF.Exp, scale=-0.5), reads=["RS"], writes=["RS"])
                    S.op("dve", lambda e: e.tensor_tensor(out=QN[:, 0:nn].rearrange("p (h d) -> p h d", d=64), in0=PJ[:, 0:nn].rearrange("p (h d) -> p h d", d=64),
                                                           in1=RS[:, 0:nh].unsqueeze(2).broadcast_to([128, nh, 64]), op=ALU.mult), reads=pjk + ["RS"], writes=["QN"])
                    S.op("dve", lambda e: e.tensor_copy(out=V1[par][:, 0:nkh, 0:64], in_=PJ[:, vo:vo + nk].rearrange("p (h d) -> p h d", d=64)), reads=pjk, writes=[("V1", par)])
                    if kv_out is not None:
                        S.op("act", lambda e: e.activation(out=VF[:, 0:nk], in_=PJ[:, vo:vo + nk], func=AF.Copy), reads=pjk, writes=["VF"])
                        S.op("sp", lambda e: e.dma_start(out=kv_out[1], in_=VF[0:nrows, 0:nk]), reads=["VF"], dma=True)

                def rest():
                    if want_q:
                        S.op("dve", lambda e: e.tensor_tensor(out=QNb[:].rearrange("p (h d) -> p h d", d=64), in0=QN[:, 0:512].rearrange("p (h d) -> p h d", d=64),
                                                                in1=gq.unsqueeze(1).broadcast_to([128, 8, 64]), op=ALU.mult), reads=["QN", "GQA", "GQB"], writes=["QNb"])
                    S.op("dve", lambda e: e.tensor_tensor(out=KN[:, 0:nk].rearrange("p (h d) -> p h d", d=64), in0=QN[:, ko:ko + nk].rearrange("p (h d) -> p h d", d=64),
                                                            in1=gk.unsqueeze(1).broadcast_to([128, nkh, 64]), op=ALU.mult), reads=["QN", "GKA", "GKB"], writes=["KN"])
                    S.op("act", lambda e: e.activation(out=KNb[:, 0:nk], in_=KN[:, 0:nk], func=AF.Copy), reads=["KN"], writes=["KNb"])
                    if kv_out is not None:
                        S.op("sp", lambda e: e.dma_start(out=kv_out[0], in_=KN[0:nrows, 0:nk]), reads=["KN"], dma=True)
                    if want_q:
                        for t in range(4):
                            src = QNb[:, t * 128:(t + 1) * 128]
                            S.op("pe", lambda e, t=t, src=src: e.transpose(out=TR[:, t * 128:(t + 1) * 128], in_=src, identity=ident[:]), reads=["QNb", "ident"], writes=["TR"])
                    for t in range(nkt):
                        S.op("pe", lambda e, t=t: e.transpose(out=TR[:, (4 + t) * 128:(5 + t) * 128], in_=KNb[:, t * 128:(t + 1) * 128], identity=ident[:]), reads=["KNb", "ident"], writes=["TR"])
                    if want_q:
                        S.op("dve", lambda e: e.tensor_copy(out=QT[:].rearrange("p t k -> p (t k)"), in_=TR[:, 0:512]), reads=["TR"], writes=["QT"])
                    trk = TR[:, 512:512 + nkt * 128].rearrange("p (t k) -> p t k", t=nkt)
                    kp = Kpad[par][:, 0:2 * nkt, :].rearrange("p (t s) k -> p t s k", s=2)
                    S.op("dve", lambda e: e.tensor_scalar(out=kp[:, :, 0, :], in0=trk, scalar1=MASK[:, 0:1], scalar2=None, op0=ALU.mult), reads=["TR", "MASK"], writes=[("Kpad", par)])
                    S.op("act", lambda e: e.activation(out=kp[:, :, 1, :], in_=trk, func=AF.Copy, scale=MASK[:, 1:2]), reads=["TR", "MASK"], writes=[("Kpad", par)])

                return proj, norm, rest

            def attend(gname, cur, prv, first, out_rows):
                isA = gname == "a"

                def st(g):
                    bank = g % 2
                    for hl in range(2):
                        h = 2 * g + hl
                        kidx = (h // 4) if isA else h
                        qt = (h % 4) if isA else (h // 2)
                        for blk in range(2):
                            pp = cur if blk == 0 else prv
                            c0 = bank * 512 + hl * 256 + blk * 128
                            S.op("pe", lambda e, c0=c0, pp=pp, kidx=kidx, qt=qt: e.matmul(SPp[:, c0:c0 + 128], lhsT=Kpad[pp][:, kidx, :], rhs=QT[:, qt, :], start=True, stop=True),
                                 reads=[("Kpad", pp), "QT"], writes=[("SP", bank)])

                def ex(g):
                    bank = g % 2
                    S.op("act", lambda e: e.activation(out=PEx[:, bank * 512:(bank + 1) * 512], in_=SPp[:, bank * 512:(bank + 1) * 512], func=AF.Exp), reads=[("SP", bank)], writes=[("PEx", bank)])
                    pt = Pt[g // 2][:, (g % 2) * 512:(g % 2 + 1) * 512]
                    S.op("dve", lambda e: e.tensor_tensor(out=pt, in0=PEx[:, bank * 512:(bank + 1) * 512], in1=Et[:, 2 * g:2 * g + 2, :].rearrange("p h k -> p (h k)"), op=ALU.mult), reads=[("PEx", bank), "Et"], writes=[("Pt", g)])
                    if first:
                        pv_ = pt.rearrange("p (h b q) -> p h b q", h=2, b=2)[:, :, 1, :]
                        S.op("dve", lambda e: e.tensor_scalar(out=pv_, in0=pv_, scalar1=HV[:, 0:1], scalar2=None, op0=ALU.mult), reads=[("Pt", g), "HV"], writes=[("Pt", g)])

                def pv(g):
                    pt = Pt[g // 2][:, (g % 2) * 512:(g % 2 + 1) * 512]
                    for hl in range(2):
                        h = 2 * g + hl
                        kv = (h // 4) if isA else h
                        for blk in range(2):
                            pp = cur if blk == 0 else prv
                            S.op("pe", lambda e, h=h, hl=hl, blk=blk, pp=pp, kv=kv: e.matmul(OPp[:, h * 128:h * 128 + 65], lhsT=pt[:, hl * 256 + blk * 128: hl * 256 + (blk + 1) * 128], rhs=V1[pp][:, kv, :], start=(blk == 0), stop=(blk == 1)),
                                 reads=[("Pt", g), ("V1", pp)], writes=[("OP", h // 4)])

                st(0)
                st(1)
                for g in range(4):
                    ex(g)
                    pv(g)
                    if g + 2 < 4:
                        st(g + 2)
                opk = [("OP", 0), ("OP", 1)]
                opv = OPp[:].rearrange("p (h c) -> p h c", h=8)
                k = rr["ev"] % 2
                rr["ev"] += 1
                if isA:
                    S.op("dve", lambda e: e.tensor_tensor(out=LL[:], in0=opv[:, :, 64], in1=SNK[:], op=ALU.add), reads=opk + ["SNK"], writes=["LL"])
                    S.op("dve", lambda e: e.reciprocal(out=LL[:], in_=LL[:]), reads=["LL"], writes=["LL"])
                    S.op("dve", lambda e: e.tensor_tensor(out=OaT[k][:].rearrange("p (h d) -> p h d", d=64), in0=opv[:, :, 0:64], in1=LL[:].unsqueeze(2).broadcast_to([128, 8, 64]), op=ALU.mult), reads=opk + ["LL"], writes=[("OaT", k)])
                    S.op("sp", lambda e: e.dma_start(out=out_rows, in_=OaT[k][:]), reads=[("OaT", k)], writes=["OSCR_a"], dma=True)
                else:
                    S.op("act", lambda e: e.activation(out=Ost[k][:], in_=opv[:, :, 0:65], func=AF.Copy), reads=opk, writes=[("Ost", k)])
                    S.op("sp", lambda e: e.dma_start(out=out_rows, in_=Ost[k][:].rearrange("p h c -> p (h c)")), reads=[("Ost", k)], writes=["OSCR_" + gname], dma=True)

            def sample_attn(gname):
                G = GROUPS[gname]
                d = G["d"]
                isA = gname == "a"
                nk = 128 if isA else 512
                kvw = 2 * nk
                nkt = nk // 128
                nkh = nk // 64
                cache = {"a": c_a_in, "b1": c_b1_in, "b2": c_b2_in, "b3": c_b3_in}[gname]
                nsk = nskv_out[gname]
                for stg in qkv_block(gname, lambda c: xT_s[:, c, 0:128], ["xTs"], 0, want_q=True, kv_out=(nsk[:, 0, :], nsk[:, 1, :]), nrows=64):
                    stg()
                S.op("dve", lambda e: e.tensor_scalar(out=Qbd[:, :, :, 0], in0=QT[:], scalar1=MASK[:, 0:1], scalar2=None, op0=ALU.mult), reads=["QT", "MASK"], writes=["Qbd"])
                S.op("dve", lambda e: e.tensor_scalar(out=Qbd[:, :, :, 1], in0=QT[:], scalar1=MASK[:, 1:2], scalar2=None, op0=ALU.mult), reads=["QT", "MASK"], writes=["Qbd"])
                if isA:
                    es_v = Et[:].rearrange("p (k g) c -> p g k c", k=2)[:, :, :, 127]
                else:
                    es_v = Et[:, :, 127]
                S.op("pool", lambda e: e.memset(OsAcc[:], 0.0), writes=["OsAcc"])
                for n in range(16):
                    for t in range(4):
                        tok = 4 * n + t
                        tokc = t
                        slot = tok % 2
                        kslot = tok % (NKV + 4)
                        if kslot < NKV:
                            KVt = Xf[kslot]
                            kvk = ("Xf", kslot)
                        else:
                            KVt = Wst[kslot - NKV]
                            kvk = ("Wst", kslot - NKV)
                        if d == 1:
                            npc = 127 - t
                            S.op("sp", lambda e, n=n, t=t, npc=npc, KVt=KVt: e.dma_start(out=KVt[0:112, 0:kvw], in_=cache[n, t + 1:t + 113, :]), writes=[(kvk, 0)], dma=True)
                            S.op("sp", lambda e, n=n, t=t, npc=npc, KVt=KVt: e.dma_start(out=KVt[112:npc, 0:kvw], in_=cache[n, t + 113:128, :]), writes=[(kvk, 1)], dma=True)
                            S.op("sp", lambda e, n=n, t=t, npc=npc, KVt=KVt: e.dma_start(out=KVt[npc:128, 0:nk], in_=KN[4 * n:4 * n + t + 1, 0:nk]), reads=["KN"], writes=[(kvk, 2)], dma=True)
                            S.op("sp", lambda e, n=n, t=t, npc=npc, KVt=KVt: e.dma_start(out=KVt[npc:128, nk:kvw], in_=VF[4 * n:4 * n + t + 1, 0:nk]), reads=["VF"], writes=[(kvk, 3)], dma=True)
                        else:
                            S.op("sp", lambda e, n=n, t=t, KVt=KVt: e.dma_start(out=KVt[0:112, 0:kvw], in_=cache[n, t + d:t + d + 111 * d + 1:d, :]), writes=[(kvk, 0)], dma=True)
                            S.op("sp", lambda e, n=n, t=t, KVt=KVt: e.dma_start(out=KVt[112:127, 0:kvw], in_=cache[n, t + 113 * d:t + 113 * d + 14 * d + 1:d, :]), writes=[(kvk, 1)], dma=True)
                            S.op("sp", lambda e, tok=tok, KVt=KVt: e.dma_start(out=KVt[127:128, 0:nk], in_=KN[tok:tok + 1, 0:nk]), reads=["KN"], writes=[(kvk, 2)], dma=True)
                            S.op("sp", lambda e, tok=tok, KVt=KVt: e.dma_start(out=KVt[127:128, nk:kvw], in_=VF[tok:tok + 1, 0:nk]), reads=["VF"], writes=[(kvk, 3)], dma=True)
                        vs = tok % 8
                        S.op("act", lambda e, KVt=KVt, slot=slot: e.activation(out=Ksb[slot][:, 0:nk], in_=KVt[:, 0:nk], func=AF.Copy), reads=[(kvk, 0), (kvk, 1), (kvk, 2), (kvk, 3)], writes=[("Ksb", slot)])
                        S.op("dve", lambda e, KVt=KVt, vs=vs: e.tensor_copy(out=V1s[vs][:, 0:nkh, 0:64], in_=KVt[:, nk:kvw].rearrange("p (h d) -> p h d", d=64)), reads=[(kvk, 0), (kvk, 1), (kvk, 2), (kvk, 3)], writes=[("V1s", vs)])
                        for tt in range(nkt):
                            S.op("pe", lambda e, tt=tt, slot=slot: e.transpose(out=TR[:, tt * 128:(tt + 1) * 128], in_=Ksb[slot][:, tt * 128:(tt + 1) * 128], identity=ident[:]), reads=[("Ksb", slot), "ident"], writes=["TR"])
                        S.op("dve", lambda e, slot=slot: e.tensor_copy(out=KTs[slot][:, 0:nk], in_=TR[:, 0:nk]), reads=["TR"], writes=[("KTs", slot)])
                        for tp in range(4):
                            kt = 0 if isA else tp
                            S.op("pe", lambda e, tp=tp, kt=kt, slot=slot, tok=tok, tokc=tokc: e.matmul(SPp[:, tokc * 8 + tp * 2: tokc * 8 + tp * 2 + 2], lhsT=KTs[slot][:, kt * 128:(kt + 1) * 128], rhs=Qbd[:, tp, tok, :], start=True, stop=True),
                                 reads=[("KTs", slot), "Qbd"], writes=[("SP", 0)])
                    S.op("act", lambda e: e.activation(out=PEs[:], in_=SPp[:, 0:32], func=AF.Exp), reads=[("SP", 0)], writes=["PEs"])
                    if isA:
                        S.op("dve", lambda e: e.tensor_tensor(out=Zb[:, :, 63].rearrange("p (t g k) -> p t g k", t=4, g=4), in0=PEs[:].rearrange("p (t g k) -> p t g k", t=4, g=4),
                                                               in1=es_v.unsqueeze(1).broadcast_to([128, 4, 4, 2]), op=ALU.mult), reads=["PEs", "Et"], writes=["Zb"])
                    else:
                        S.op("dve", lambda e: e.tensor_tensor(out=Zb[:, :, 63].rearrange("p (t h) -> p t h", t=4), in0=PEs[:].rearrange("p (t h) -> p t h", t=4),
                                                               in1=es_v.unsqueeze(1).broadcast_to([128, 4, 8]), op=ALU.mult), reads=["PEs", "Et"], writes=["Zb"])
                    for h in range(8):
                        if isA:
                            col = (h % 4) * 2 + h // 4
                            kv = h // 4
                        else:
                            col = h
                            kv = h
                        for t in range(4):
                            tok = 4 * n + t
                            vs = tok % 8
                            S.op("pe", lambda e, h=h, col=col, kv=kv, t=t, vs=vs, tok=tok: e.matmul(OPp[:, h * 128:h * 128 + 65], lhsT=Zb[:, t * 8 + col, 63 - tok:191 - tok], rhs=V1s[vs][:, kv, :], start=(t == 0), stop=(t == 3)),
                                 reads=["Zb", ("V1s", vs)], writes=[("OP", h // 4)])
                    S.op("dve", lambda e: e.tensor_tensor(out=OsAcc[:], in0=OsAcc[:], in1=OPp[:].rearrange("p (h c) -> p h c", h=8)[:, :, 0:65], op=ALU.add), reads=["OsAcc", ("OP", 0), ("OP", 1)], writes=["OsAcc"])
                k = rr["ev"] % 2
                rr["ev"] += 1
                if isA:
                    S.op("dve", lambda e: e.tensor_tensor(out=LL[:], in0=OsAcc[:, :, 64], in1=SNK[:], op=ALU.add), reads=["OsAcc", "SNK"], writes=["LL"])
                    S.op("dve", lambda e: e.reciprocal(out=LL[:], in_=LL[:]), reads=["LL"], writes=["LL"])
                    S.op("dve", lambda e: e.tensor_tensor(out=OaT[k][:].rearrange("p (h d) -> p h d", d=64), in0=OsAcc[:, :, 0:64], in1=LL[:].unsqueeze(2).broadcast_to([128, 8, 64]), op=ALU.mult), reads=["OsAcc", "LL"], writes=[("OaT", k)])
                    S.op("sp", lambda e: e.dma_start(out=Oa_scr.ap()[NOWN:NOWN + 128, :], in_=OaT[k][:]), reads=[("OaT", k)], writes=["OSCR_a"], dma=True)
                else:
                    S.op("sp", lambda e: e.dma_start(out=Oscr[gname].ap()[NOWN:NOWN + 128, :], in_=OsAcc[:].rearrange("p h c -> p (h c)")), reads=["OsAcc"], writes=["OSCR_" + gname], dma=True)

            def attn_phase(gname):
                G = GROUPS[gname]
                d = G["d"]
                isA = gname == "a"
                nk = 128 if isA else 512
                if isA:
                    qr = [(C_QA + kvh * 256 + g * 64, 64) for g in range(4) for kvh in range(2)]
                else:
                    qr = [(G["cq"], 512)]
                load_w(Wsb, qr + [(G["ck"], nk), (G["cv"], nk)], "W")
                if SUB >= 2:
                    build_E(gname)
                oscr = Oa_scr.ap() if isA else Oscr[gname].ap()
                nkv = nkv_out[gname]
                ncb = NB // d
                win = {"a": 128, "b1": 128, "b2": 512, "b3": 2048}[gname]
                blocks = []
                cnt = 0
                for r in range(min(d, NCLS)):
                    hs = NOWN - 128 * d + r
                    lhs_h = (lambda c, hs=hs: xT_halo[:, c, hs:hs + 127 * d + 1:d]) if d > 1 else (lambda c, hs=hs: xT_halo[:, c, hs:hs + 128])
                    blocks.append(dict(stages=qkv_block(gname, lhs_h, XTH_ALL, cnt % 3, want_q=False), att=None))
                    cnt += 1
                    for cb in range(ncb):
                        st = r + d * 128 * cb
                        kv_out = None
                        lo = NOWN - win
                        if st >= lo:
                            r0 = st - lo
                            if d > 1:
                                kv_out = (nkv[r0:r0 + 127 * d + 1:d, 0, :], nkv[r0:r0 + 127 * d + 1:d, 1, :])
                            else:
                                kv_out = (nkv[r0:r0 + 128, 0, :], nkv[r0:r0 + 128, 1, :])
                        lhs_o = (lambda c, st=st: xT_own[:, c, st:st + 127 * d + 1:d]) if d > 1 else (lambda c, st=st: xT_own[:, c, st:st + 128])
                        rows = oscr[st:st + 127 * d + 1:d, :] if d > 1 else oscr[st:st + 128, :]
                        blocks.append(dict(stages=qkv_block(gname, lhs_o, XTO_ALL, cnt % 3, want_q=True, kv_out=kv_out),
                                           att=(cnt % 3, (cnt - 1) % 3, cb == 0, rows)))
                        cnt += 1
                if blocks:
                    blocks[0]["stages"][0]()
                for i, b in enumerate(blocks):
                    b["stages"][1]()
                    if i + 1 < len(blocks):
                        blocks[i + 1]["stages"][0]()
                    b["stages"][2]()
                    if b["att"] is not None:
                        cur, prv, first, rows = b["att"]
                        attend(gname, cur, prv, first, rows)
                if WITH_CACHE and SUB >= 6:
                    sample_attn(gname)

            for gi_, gname in enumerate(("b3", "b2", "b1", "a")):
                if LEVEL >= 3 + gi_:
                    attn_phase(gname)

        with ExitStack() as esF:
            def sbF(name, shape, dt):
                return esF.enter_context(nc.sbuf_tensor(name, shape, dt))

            Wg = sbF("Wg", [128, 8, 3072], BF16)
            WuA = sbF("WuA", [128, 4, 1024], BF16)
            WuB = sbF("WuB", [128, 4, 1024], BF16)
            Wo = sbF("Wo", [128, 8, 1024], BF16)
            Wst = [sbF("WstF%d" % i, [128, 1536], F32) for i in range(4)]
            O1 = sbF("O1", [128, 520], F32)
            O2 = sbF("O2", [128, 520], F32)
            O3 = sbF("O3", [128, 520], F32)
            OA = sbF("OA", [128, 512], F32)
            XR = sbF("XR", [128, 1024], F32)
            SG = sbF("SG", [128, 1024], F32)
            SM = sbF("SM", [128, 2048], F32)
            LB = sbF("LB", [128, 8], F32)
            U = sbF("U", [128, 1024], BF16)
            UT = sbF("UT", [128, 8, 128], BF16)
            M1 = sbF("M1", [128, 1024], F32)
            M2 = sbF("M2", [128, 1024], F32)
            MG = sbF("MG", [128, 1024], BF16)
            MT = sbF("MT", [128, 8, 128], BF16)
            Y = sbF("Y", [128, 1024], F32)

            wc = {"n": 0}
            BAR = S.last_all()

            def load_wF(dst_fn, src_ap_fn, ncols_list, key, scale_gain):
                for (c0, n, off) in ncols_list:
                    for c in range(dst_fn("nchunk")):
                        k = wc["n"] % 4
                        wc["n"] += 1
                        st = Wst[k]
                        S.op("sp", lambda e, c=c, c0=c0, n=n, st=st: e.dma_start(out=st[:, 0:n], in_=src_ap_fn(c, c0, n)), writes=[("WstF", k)], dma=True, extra=BAR)
                        useact = bool(wc["n"] % 2)
                        if scale_gain:
                            if useact:
                                S.op("act", lambda e, c=c, n=n, st=st, off=off: e.activation(out=dst_fn(c)[:, off:off + n], in_=st[:, 0:n], func=AF.Copy, scale=NG[:, c:c + 1]), reads=[("WstF", k), "NG"], writes=[key])
                            else:
                                S.op("dve", lambda e, c=c, n=n, st=st, off=off: e.tensor_scalar(out=dst_fn(c)[:, off:off + n], in0=st[:, 0:n], scalar1=NG[:, c:c + 1], scalar2=None, op0=ALU.mult), reads=[("WstF", k), "NG"], writes=[key])
                        else:
                            if useact:
                                S.op("act", lambda e, c=c, n=n, st=st, off=off: e.activation(out=dst_fn(c)[:, off:off + n], in_=st[:, 0:n], func=AF.Copy), reads=[("WstF", k)], writes=[key])
                            else:
                                S.op("dve", lambda e, c=c, n=n, st=st, off=off: e.tensor_copy(out=dst_fn(c)[:, off:off + n], in_=st[:, 0:n]), reads=[("WstF", k)], writes=[key])

            load_wF(lambda c: 8 if c == "nchunk" else Wg[:, c, :], lambda c, c0, n: w_in[c * 128:(c + 1) * 128, c0:c0 + n],
                    [(C_GA, 512, 0), (C_GB, 512, 512), (C_MA, 1024, 1024), (C_MB, 1024, 2048)], "Wg", True)
            load_wF(lambda c: 4 if c == "nchunk" else WuA[:, c, :], lambda c, c0, n: wup_a_in[c * 128:(c + 1) * 128, c0:c0 + n], [(0, 1024, 0)], "WuA", False)
            load_wF(lambda c: 4 if c == "nchunk" else WuB[:, c, :], lambda c, c0, n: wup_b_in[c * 128:(c + 1) * 128, c0:c0 + n], [(0, 1024, 0)], "WuB", False)
            load_wF(lambda c: 8 if c == "nchunk" else Wo[:, c, :], lambda c, c0, n: wout_in[c * 128:(c + 1) * 128, c0:c0 + n], [(0, 1024, 0)], "Wo", False)

            def final_block(lhs_fn, xkeys, row0, x_src, y_dst, nrows=128, xres=None):
                S.op("sp", lambda e: e.dma_start(out=O1[:], in_=Oscr["b1"].ap()[row0:row0 + 128, :]), reads=["OSCR_b1"], writes=["O1"], dma=True)
                S.op("sp", lambda e: e.dma_start(out=O2[:], in_=Oscr["b2"].ap()[row0:row0 + 128, :]), reads=["OSCR_b2"], writes=["O2"], dma=True)
                S.op("sp", lambda e: e.dma_start(out=O3[:], in_=Oscr["b3"].ap()[row0:row0 + 128, :]), reads=["OSCR_b3"], writes=["O3"], dma=True)
                S.op("sp", lambda e: e.dma_start(out=OA[:], in_=Oa_scr.ap()[row0:row0 + 128, :]), reads=["OSCR_a"], writes=["OA"], dma=True)
                if xres is not None and DBG:
                    S.op("sp", lambda e: e.dma_start(out=dbg_o[0], in_=O1[:]), reads=["O1"], dma=True)
                    S.op("sp", lambda e: e.dma_start(out=dbg_o[1], in_=O2[:]), reads=["O2"], dma=True)
                    S.op("sp", lambda e: e.dma_start(out=dbg_o[2], in_=O3[:]), reads=["O3"], dma=True)
                    S.op("sp", lambda e: e.dma_start(out=dbg_o[3, :, 0:512], in_=OA[:]), reads=["OA"], dma=True)
                if xres is None:
                    S.op("sp", lambda e: e.dma_start(out=XR[:], in_=x_src), writes=["XR"], dma=True)
                    xr, xrk = XR, "XR"
                else:
                    xr, xrk = xres, "Xs_f"
                for rnd in range(2):
                    for g in range(3):
                        for c in range(8):
                            S.op("pe", lambda e, g=g, c=c, rnd=rnd: e.matmul(PJ[:, g * 512:(g + 1) * 512], lhsT=lhs_fn(c), rhs=Wg[:, c, rnd * 1536 + g * 512: rnd * 1536 + (g + 1) * 512], start=(c == 0), stop=(c == 7)),
                                 reads=list(xkeys) + ["Wg"], writes=[("PJ", g)])
                    pjk = [("PJ", g) for g in range(3)]
                    if rnd == 0:
                        S.op("act", lambda e: e.activation(out=SG[:], in_=PJ[:, 0:1024], func=AF.Silu), reads=pjk, writes=["SG"])
                        S.op("act", lambda e: e.activation(out=SM[:, 0:512], in_=PJ[:, 1024:1536], func=AF.Sigmoid), reads=pjk, writes=["SM"])
                    else:
                        S.op("act", lambda e: e.activation(out=SM[:, 512:2048], in_=PJ[:, 0:1536], func=AF.Sigmoid), reads=pjk, writes=["SM"])
                S.op("dve", lambda e: e.tensor_tensor(out=O1[:], in0=O1[:], in1=O2[:], op=ALU.add), reads=["O1", "O2"], writes=["O1"])
                S.op("dve", lambda e: e.tensor_tensor(out=O1[:], in0=O1[:], in1=O3[:], op=ALU.add), reads=["O1", "O3"], writes=["O1"])
                o1v = O1[:].rearrange("p (h c) -> p h c", c=65)
                S.op("dve", lambda e: e.reciprocal(out=LB[:], in_=o1v[:, :, 64]), reads=["O1"], writes=["LB"])
                S.op("dve", lambda e: e.tensor_tensor(out=O2[:, 0:512].rearrange("p (h d) -> p h d", d=64), in0=o1v[:, :, 0:64], in1=LB[:].unsqueeze(2).broadcast_to([128, 8, 64]), op=ALU.mult), reads=["O1", "LB"], writes=["O2"])
                S.op("dve", lambda e: e.tensor_tensor(out=U[:, 0:512], in0=OA[:], in1=SG[:, 0:512], op=ALU.mult), reads=["OA", "SG"], writes=["U"])
                S.op("dve", lambda e: e.tensor_tensor(out=U[:, 512:1024], in0=O2[:, 0:512], in1=SG[:, 512:1024], op=ALU.mult), reads=["O2", "SG"], writes=["U"])
                for c in range(8):
                    S.op("pe", lambda e, c=c: e.transpose(out=TR[:, c * 128:(c + 1) * 128], in_=U[:, c * 128:(c + 1) * 128], identity=ident[:]), reads=["U", "ident"], writes=["TR"])
                S.op("act", lambda e: e.activation(out=UT[:], in_=TR[:].rearrange("p (c t) -> p c t", c=8), func=AF.Copy), reads=["TR"], writes=["UT"])
                for n in range(2):
                    for c in range(4):
                        S.op("pe", lambda e, n=n, c=c: e.matmul(SPp[:, n * 512:(n + 1) * 512], lhsT=UT[:, c, :], rhs=WuA[:, c, n * 512:(n + 1) * 512], start=(c == 0), stop=(c == 3)), reads=["UT", "WuA"], writes=[("SP", n)])
                for n in range(2):
                    for c in range(4):
                        S.op("pe", lambda e, n=n, c=c: e.matmul(OPp[:, n * 512:(n + 1) * 512], lhsT=UT[:, 4 + c, :], rhs=WuB[:, c, n * 512:(n + 1) * 512], start=(c == 0), stop=(c == 3)), reads=["UT", "WuB"], writes=[("OP", n)])
                S.op("dve", lambda e: e.tensor_tensor(out=M1[:], in0=SPp[:], in1=SM[:, 0:1024], op=ALU.mult), reads=[("SP", 0), ("SP", 1), "SM"], writes=["M1"])
                S.op("dve", lambda e: e.tensor_tensor(out=M2[:], in0=OPp[:], in1=SM[:, 1024:2048], op=ALU.mult), reads=[("OP", 0), ("OP", 1), "SM"], writes=["M2"])
                S.op("dve", lambda e: e.tensor_tensor(out=MG[:], in0=M1[:], in1=M2[:], op=ALU.add), reads=["M1", "M2"], writes=["MG"])
                for c in range(8):
                    S.op("pe", lambda e, c=c: e.transpose(out=TR[:, c * 128:(c + 1) * 128], in_=MG[:, c * 128:(c + 1) * 128], identity=ident[:]), reads=["MG", "ident"], writes=["TR"])
                S.op("act", lambda e: e.activation(out=MT[:], in_=TR[:].rearrange("p (c t) -> p c t", c=8), func=AF.Copy), reads=["TR"], writes=["MT"])
                for n in range(2):
                    for c in range(8):
                        S.op("pe", lambda e, n=n, c=c: e.matmul(SPp[:, n * 512:(n + 1) * 512], lhsT=MT[:, c, :], rhs=Wo[:, c, n * 512:(n + 1) * 512], start=(c == 0), stop=(c == 7)), reads=["MT", "Wo"], writes=[("SP", n)])
                S.op("dve", lambda e: e.tensor_tensor(out=Y[:], in0=SPp[:], in1=xr[:], op=ALU.add), reads=[("SP", 0), ("SP", 1), xrk], writes=["Y"])
                S.op("sp", lambda e: e.dma_start(out=y_dst, in_=Y[0:nrows, :]), reads=["Y"], dma=True)

            for t in range(NB if LEVEL >= 7 else 0):
                final_block(lambda c, t=t: xT_own[:, c, t * 128:(t + 1) * 128], [("xTo", t)], t * 128, x_ext[NOWN + t * 128:NOWN + (t + 1) * 128, :], y_out[t * 128:(t + 1) * 128, :])

            if WITH_CACHE and LEVEL >= 8:
                final_block(lambda c: xT_s[:, c, 0:128], ["xTs"], NOWN, None, ys_out, nrows=64, xres=Xs_f)

        S.emit(sems, dsems, block)
    return nc


def shared_inputs(rel_bias, norm_gain, w_in, q_gain_a, k_gain_a, sinks_a, q_gain_b, k_gain_b, w_up_a, w_up_b, w_out):
    relb = np.zeros((128, 128), np.float32)
    relb[:32, :32] = rel_bias
    oh = onehot_tables()
    mask2 = np.zeros((128, 2), np.float32)
    mask2[:64, 0] = 1.0
    mask2[64:, 1] = 1.0
    return {
        "mask2": mask2,
        "w_in": w_in[0], "ng": np.ascontiguousarray(norm_gain[0].reshape(8, 128).T), "relb": relb, "oh": oh,
        "gq_a": np.ascontiguousarray(np.broadcast_to(q_gain_a[0][None, :], (128, 64))),
        "gk_a": np.ascontiguousarray(np.broadcast_to(k_gain_a[0][None, :], (128, 64))),
        "gq_b": np.ascontiguousarray(np.broadcast_to(q_gain_b[0].reshape(1, 192), (128, 192))),
        "gk_b": np.ascontiguousarray(np.broadcast_to(k_gain_b[0].reshape(1, 192), (128, 192))),
        "sinks": np.ascontiguousarray(np.broadcast_to(sinks_a[0][None, :], (128, 8))),
        "wup_a": w_up_a[0], "wup_b": w_up_b[0], "wout": w_out[0],
    }


_CACHE = {}


def kernel(x_prompt, x_sample, cache_a_kv, cache_b1_kv, cache_b2_kv, cache_b3_kv, rel_bias, norm_gain, w_in,
           q_gain_a, k_gain_a, sinks_a, q_gain_b, k_gain_b, w_up_a, w_up_b, w_out):
    f = lambda a: np.ascontiguousarray(np.asarray(a, dtype=np.float32))
    x_prompt = f(x_prompt); x_sample = f(x_sample)
    cache_a_kv = f(cache_a_kv); cache_b1_kv = f(cache_b1_kv); cache_b2_kv = f(cache_b2_kv); cache_b3_kv = f(cache_b3_kv)
    rel_bias = f(rel_bias); norm_gain = f(norm_gain); w_in = f(w_in)
    q_gain_a = f(q_gain_a); k_gain_a = f(k_gain_a); sinks_a = f(sinks_a); q_gain_b = f(q_gain_b); k_gain_b = f(k_gain_b)
    w_up_a = f(w_up_a); w_up_b = f(w_up_b); w_out = f(w_out)

    nc = build_program()
    shared = shared_inputs(rel_bias, norm_gain, w_in, q_gain_a, k_gain_a, sinks_a, q_gain_b, k_gain_b, w_up_a, w_up_b, w_out)

    in_maps = []
    for c in range(8):
        b, h = c // 2, c % 2
        x_ext = np.zeros((4096, 1024), np.float32)
        if h == 1:
            x_ext[:] = x_prompt[b]
        else:
            x_ext[2048:] = x_prompt[b, :2048]
        xs = np.zeros((128, 1024), np.float32)
        xs[:64] = x_sample[16 * c:16 * c + 16].reshape(64, 1024)
        m = dict(shared)
        m["x_ext"] = x_ext
        m["x_s"] = xs
        m["hv"] = np.full((128, 1), float(h), np.float32)
        if WITH_CACHE:
          m["c_a"] = np.ascontiguousarray(cache_a_kv[0, 16 * c:16 * c + 16].reshape(16, 128, 256))
          m["c_b1"] = np.ascontiguousarray(cache_b1_kv[0, 16 * c:16 * c + 16].reshape(16, 128, 1024))
          m["c_b2"] = np.ascontiguousarray(cache_b2_kv[0, 16 * c:16 * c + 16].reshape(16, 512, 1024))
          m["c_b3"] = np.ascontiguousarray(cache_b3_kv[0, 16 * c:16 * c + 16].reshape(16, 2048, 1024))
        in_maps.append(m)
    res = run_bass_kernel_spmd(nc, in_maps, core_ids=list(range(8)))
    R = res.results
    y = np.zeros((4, 4096, 1024), np.float32)
    ys = np.zeros((128, 4, 1024), np.float32)
    for c in range(8):
        b, h = c // 2, c % 2
        y[b, h * 2048:(h + 1) * 2048] = R[c]["y"]
        ys[16 * c:16 * c + 16] = R[c]["ys"].reshape(16, 4, 1024)
    np_a = np.stack([R[2 * b + 1]["nkv_a"].reshape(128, 2, 2, 64) for b in range(4)])[None]
    np_b1 = np.stack([R[2 * b + 1]["nkv_b1"].reshape(128, 2, 8, 64) for b in range(4)])[None]
    np_b2 = np.stack([R[2 * b + 1]["nkv_b2"].reshape(512, 2, 8, 64) for b in range(4)])[None]
    np_b3 = np.stack([R[2 * b + 1]["nkv_b3"].reshape(2048, 2, 8, 64) for b in range(4)])[None]
    ns_a = np.concatenate([R[c]["ns_a"].reshape(16, 4, 2, 2, 64) for c in range(8)])[None]
    ns_b1 = np.concatenate([R[c]["ns_b1"].reshape(16, 4, 2, 8, 64) for c in range(8)])[None]
    ns_b2 = np.concatenate([R[c]["ns_b2"].reshape(16, 4, 2, 8, 64) for c in range(8)])[None]
    ns_b3 = np.concatenate([R[c]["ns_b3"].reshape(16, 4, 2, 8, 64) for c in range(8)])[None]
    return (y, ys, np_a, np_b1, np_b2, np_b3, ns_a, ns_b1, ns_b2, ns_b3)
```

```python
import math
import os
from contextlib import ExitStack

import numpy as np
import concourse.bass as bass
import concourse.mybir as mybir
from concourse.bass_utils import run_bass_kernel_spmd

F32 = mybir.dt.float32
BF16 = mybir.dt.bfloat16
AF = mybir.ActivationFunctionType
ALU = mybir.AluOpType
AX = mybir.AxisListType

SAME_ENGINE_SYNC = os.environ.get('KDBG_SES', '1') == '1'
LEVEL = int(os.environ.get('KDBG_LEVEL', '99'))
NCLS = int(os.environ.get('KDBG_NCLS', '99'))
NDS = 14
NKV = 2
WITH_CACHE = os.environ.get('KDBG_NOSAMPLE', '0') != '1'
SUB = int(os.environ.get('KDBG_SUB', '99'))
XB = int(os.environ.get('KDBG_X', '0'))
EPS = 1e-6
NOWN = 2048
NB = 16
C_QA, C_KA, C_VA, C_GA = 0, 512, 640, 768
C_QB, C_KB, C_VB, C_GB, C_MA, C_MB = 1280, 2816, 4352, 5888, 6400, 7424


class Sched:
    ENGS = ("pe", "act", "dve", "pool", "sp")

    def __init__(self, nc, n_dma_sems=6):
        self.nc = nc
        self.ops = {e: [] for e in self.ENGS}
        self.lastw = {}
        self.readers = {}
        self.n_dma_sems = n_dma_sems
        self.dma_count = {"sp": 0, "pool": 0, "act": 0}
        self.dma_last = {}

    def last_all(self):
        return [(e, len(self.ops[e]) - 1) for e in self.ENGS if self.ops[e]]

    def op(self, eng, fn, reads=(), writes=(), dma=False, extra=()):
        ops = self.ops[eng]
        idx = len(ops)
        deps = set(extra)
        for b in reads:
            w = self.lastw.get(b)
            if w is not None:
                deps.add(w)
        for b in writes:
            w = self.lastw.get(b)
            if w is not None:
                deps.add(w)
            for r in self.readers.get(b, ()):
                deps.add(r)
        cdeps = {}
        ddeps = set()
        for (e, i) in deps:
            o = self.ops[e][i]
            if o["dma"]:
                ddeps.add(o["sig"])
            else:
                if e == eng and not dma and (e == "pe" or not SAME_ENGINE_SYNC):
                    continue
                cdeps[e] = max(cdeps.get(e, -1), i)
        rec = {"fn": fn, "dma": dma, "cdeps": cdeps, "ddeps": ddeps, "sig": None, "signaled": False}
        if dma:
            n = self.dma_count[eng]
            self.dma_count[eng] = n + 1
            slot = n % self.n_dma_sems
            val = 16 * (n // self.n_dma_sems + 1)
            rec["sig"] = (eng, slot, val)
            if val > 16:
                ddeps.add((eng, slot, val - 16))
            self.dma_last[(eng, slot)] = val
        ops.append(rec)
        me = (eng, idx)
        for b in writes:
            self.lastw[b] = me
            self.readers[b] = []
        for b in reads:
            if b in writes:
                continue
            self.readers.setdefault(b, []).append(me)
        return me

    def emit(self, sems, dsems, block):
        for e in self.ENGS:
            for o in self.ops[e]:
                for (de, di) in o["cdeps"].items():
                    self.ops[de][di]["signaled"] = True
        for e in self.ENGS:
            c = 0
            for o in self.ops[e]:
                if o["dma"]:
                    continue
                if o["signaled"]:
                    c += 1
                    o["sig"] = c
        allops = self.ops
        dma_last = self.dma_last

        def run(eng_name, eng):
            waited = {}
            for o in allops[eng_name]:
                for (de, di) in sorted(o["cdeps"].items()):
                    v = allops[de][di]["sig"]
                    key = ("c", de)
                    if waited.get(key, 0) >= v:
                        continue
                    eng.wait_ge(sems[de], v)
                    waited[key] = v
                for (qe, slot, v) in sorted(o["ddeps"]):
                    key = ("d", qe, slot)
                    if waited.get(key, 0) >= v:
                        continue
                    eng.wait_ge(dsems[(qe, slot)], v)
                    waited[key] = v
                ins = o["fn"](eng)
                if o["dma"]:
                    qe, slot, v = o["sig"]
                    ins.then_inc(dsems[(qe, slot)], 16)
                elif o["signaled"]:
                    ins.then_inc(sems[eng_name], 1)
            if eng_name == "sp":
                for (qe, slot), v in sorted(dma_last.items()):
                    if waited.get(("d", qe, slot), 0) >= v:
                        continue
                    eng.wait_ge(dsems[(qe, slot)], v)

        @block.tensor
        def _(eng):
            run("pe", eng)

        @block.scalar
        def _(eng):
            run("act", eng)

        @block.vector
        def _(eng):
            run("dve", eng)

        @block.gpsimd
        def _(eng):
            run("pool", eng)

        @block.sync
        def _(eng):
            run("sp", eng)


def t5_bucket_np(dist):
    d = np.maximum(dist, 0)
    df = np.maximum(d, 1).astype(np.float32)
    large = 16 + (np.log(df / np.float32(16)) / np.float32(math.log(2048 / 16)) * np.float32(16)).astype(np.int32)
    large = np.minimum(large, 31)
    return np.where(d < 16, d, large)


def onehot_tables():
    oh = np.zeros((3, 128, 384), np.float32)
    for di, dil in enumerate((1, 4, 16)):
        delta = np.arange(128)
        b = t5_bucket_np(delta * dil)
        oh[di, b, delta + 127] = 1.0
    return oh


GROUPS = {
    "b3": dict(d=16, di=2, hb=24, cq=C_QB + 1024, ck=C_KB + 1024, cv=C_VB + 1024, nkv=8),
    "b2": dict(d=4, di=1, hb=16, cq=C_QB + 512, ck=C_KB + 512, cv=C_VB + 512, nkv=8),
    "b1": dict(d=1, di=0, hb=8, cq=C_QB, ck=C_KB, cv=C_VB, nkv=8),
    "a": dict(d=1, di=0, hb=0, cq=C_QA, ck=C_KA, cv=C_VA, nkv=2),
}


def build_program(with_sample=True):
    nc = bass.Bass("TRN2", target_bir_lowering=False)

    def din(name, shape):
        return nc.dram_tensor(name, shape, F32, kind="ExternalInput").ap()

    def dout(name, shape):
        return nc.dram_tensor(name, shape, F32, kind="ExternalOutput").ap()

    x_ext = din("x_ext", [4096, 1024])
    x_s = din("x_s", [128, 1024])
    hv_in = din("hv", [128, 1])
    mask_in = din("mask2", [128, 2])
    w_in = din("w_in", [1024, 8448])
    ng_in = din("ng", [128, 8])
    relb_in = din("relb", [128, 128])
    oh_in = din("oh", [3, 128, 384])
    gq_a_in = din("gq_a", [128, 64])
    gk_a_in = din("gk_a", [128, 64])
    gq_b_in = din("gq_b", [128, 192])
    gk_b_in = din("gk_b", [128, 192])
    sinks_in = din("sinks", [128, 8])
    wup_a_in = din("wup_a", [512, 1024])
    wup_b_in = din("wup_b", [512, 1024])
    wout_in = din("wout", [1024, 1024])
    if WITH_CACHE:
        c_a_in = din("c_a", [16, 128, 256])
        c_b1_in = din("c_b1", [16, 128, 1024])
        c_b2_in = din("c_b2", [16, 512, 1024])
        c_b3_in = din("c_b3", [16, 2048, 1024])

    y_out = dout("y", [2048, 1024])
    ys_out = dout("ys", [64, 1024])
    nkv_out = {"a": dout("nkv_a", [128, 2, 128]), "b1": dout("nkv_b1", [128, 2, 512]),
               "b2": dout("nkv_b2", [512, 2, 512]), "b3": dout("nkv_b3", [2048, 2, 512])}
    nskv_out = {"a": dout("ns_a", [64, 2, 128]), "b1": dout("ns_b1", [64, 2, 512]),
                "b2": dout("ns_b2", [64, 2, 512]), "b3": dout("ns_b3", [64, 2, 512])}

    DBG = os.environ.get('KDBG_DUMP', '0') == '1'
    if DBG:
        dbg_o = dout("dbgo", [4, 128, 520])
    NTS = NOWN + 128
    Oscr = {g: nc.dram_tensor("oscr_" + g, [NTS, 520], F32) for g in ("b1", "b2", "b3")}
    Oa_scr = nc.dram_tensor("oscr_a", [NTS, 512], F32)
    EFscr = nc.dram_tensor("efscr", [3, 32, 384], F32)

    S = Sched(nc, n_dma_sems=NDS)
    rr = {"ev": 0}

    with ExitStack() as es:
        def sb(name, shape, dt):
            return es.enter_context(nc.sbuf_tensor(name, shape, dt))

        def ps(name, shape, dt):
            return es.enter_context(nc.psum_tensor(name, shape, dt))

        sems = {e: es.enter_context(nc.semaphore("s_" + e)) for e in ("pe", "act", "dve", "pool")}
        dsems = {(q, i): es.enter_context(nc.semaphore(f"d_{q}{i}")) for q in ("sp", "pool") for i in range(NDS)}

        xT_own = sb("xT_own", [128, 8, NOWN], BF16)
        xT_s = sb("xT_s", [128, 8, 128], BF16)
        Xs_f = sb("Xs_f", [128, 1024], F32)
        ident = sb("ident", [128, 128], BF16)
        Jm = sb("Jm", [128, 128], BF16)
        identf = sb("identf", [128, 128], F32)
        NG = sb("NG", [128, 8], F32)
        HV = sb("HV", [128, 1], F32)
        MASK = sb("MASK", [128, 2], F32)
        GQA = sb("GQA", [128, 64], F32)
        GKA = sb("GKA", [128, 64], F32)
        GQB = sb("GQB", [128, 192], F32)
        GKB = sb("GKB", [128, 192], F32)
        SNK = sb("SNK", [128, 8], F32)
        PJ = ps("PJ", [128, 1536], F32)
        TR = ps("TR", [128, 1024], BF16)
        SPp = ps("SPp", [128, 1024], F32)
        OPp = ps("OPp", [128, 1024], F32)

        block = es.enter_context(nc.Block())

        S.op("sp", lambda e: e.dma_start(out=NG[:], in_=ng_in), writes=["NG"], dma=True)
        S.op("sp", lambda e: e.dma_start(out=HV[:], in_=hv_in), writes=["HV"], dma=True)
        S.op("sp", lambda e: e.dma_start(out=MASK[:], in_=mask_in), writes=["MASK"], dma=True)
        S.op("sp", lambda e: e.dma_start(out=GQA[:], in_=gq_a_in), writes=["GQA"], dma=True)
        S.op("sp", lambda e: e.dma_start(out=GKA[:], in_=gk_a_in), writes=["GKA"], dma=True)
        S.op("sp", lambda e: e.dma_start(out=GQB[:], in_=gq_b_in), writes=["GQB"], dma=True)
        S.op("sp", lambda e: e.dma_start(out=GKB[:], in_=gk_b_in), writes=["GKB"], dma=True)
        S.op("sp", lambda e: e.dma_start(out=SNK[:], in_=sinks_in), writes=["SNK"], dma=True)
        S.op("dve", lambda e: e.tensor_scalar(out=GQA[:], in0=GQA[:], scalar1=0.125, scalar2=None, op0=ALU.mult), reads=["GQA"], writes=["GQA"])
        S.op("dve", lambda e: e.tensor_scalar(out=GQB[:], in0=GQB[:], scalar1=0.125, scalar2=None, op0=ALU.mult), reads=["GQB"], writes=["GQB"])
        S.op("act", lambda e: e.activation(out=SNK[:], in_=SNK[:], func=AF.Exp), reads=["SNK"], writes=["SNK"])
        S.op("pool", lambda e: e.memset(identf[:], 1.0), writes=["identf"])
        S.op("pool", lambda e: e.affine_select(out=identf[:], in_=identf[:], pattern=[[-1, 128]], compare_op=ALU.is_equal, fill=0.0, base=0, channel_multiplier=1), reads=["identf"], writes=["identf"])
        S.op("dve", lambda e: e.tensor_copy(out=ident[:], in_=identf[:]), reads=["identf"], writes=["ident"])
        S.op("pool", lambda e: e.memset(identf[:], 1.0), reads=["identf"], writes=["identf"])
        S.op("pool", lambda e: e.affine_select(out=identf[:], in_=identf[:], pattern=[[1, 128]], compare_op=ALU.is_equal, fill=0.0, base=-127, channel_multiplier=1), reads=["identf"], writes=["identf"])
        S.op("dve", lambda e: e.tensor_copy(out=Jm[:], in_=identf[:]), reads=["identf"], writes=["Jm"])

        RB = sb("RB", [128, 128], F32)
        OHs = sb("OHs", [128, 384], F32)
        EFs = sb("EFs", [128, 384], F32)
        if True:
            S.op("sp", lambda e: e.dma_start(out=RB[:], in_=relb_in), writes=["RB"], dma=True)
            for di in range(3):
                S.op("sp", lambda e, di=di: e.dma_start(out=OHs[:], in_=oh_in[di]), writes=["OHs"], dma=True)
                S.op("pe", lambda e: e.matmul(SPp[:, 0:384], lhsT=RB[:], rhs=OHs[:], start=True, stop=True), reads=["RB", "OHs"], writes=[("SP", 0), ("SP", 1)])
                S.op("act", lambda e: e.activation(out=EFs[:], in_=SPp[:, 0:384], func=AF.Exp), reads=[("SP", 0), ("SP", 1)], writes=["EFs"])
                S.op("dve", lambda e: e.memset(EFs[:, 0:127], 0.0), reads=["EFs"], writes=["EFs"])
                S.op("dve", lambda e: e.memset(EFs[:, 255:384], 0.0), reads=["EFs"], writes=["EFs"])
                S.op("sp", lambda e, di=di: e.dma_start(out=EFscr.ap()[di], in_=EFs[0:32, :]), reads=["EFs"], writes=[("EFscr", di)], dma=True)

        with ExitStack() as esA:
            def sbA(name, shape, dt):
                return esA.enter_context(nc.sbuf_tensor(name, shape, dt))

            xT_halo = sbA("xT_halo", [128, 8, NOWN], BF16)
            Wsb = sbA("Wsb", [128, 8, 1536], BF16)
            Wst = [sbA("Wst%d" % i, [128, 1536], F32) for i in range(4)]
            Xf = [sbA("Xf%d" % i, [128, 1024], F32) for i in range(2)]
            SQ = sbA("SQ", [128, 1024], F32)
            Xb = sbA("Xb", [128, 1024], BF16)
            SS = sbA("SS", [128, 16], F32)
            RS = sbA("RS", [128, 16], F32)
            QN = sbA("QN", [128, 1024], F32)
            QNb = sbA("QNb", [128, 512], BF16)
            KN = sbA("KN", [128, 512], F32)
            KNb = sbA("KNb", [128, 512], BF16)
            VF = sbA("VF", [128, 512], F32)
            QT = sbA("QT", [128, 4, 128], BF16)
            Kpad = [sbA("Kpad%d" % i, [128, 8, 128], BF16) for i in range(3)]
            V1 = [sbA("V1_%d" % i, [128, 8, 65], BF16) for i in range(3)]
            Et = sbA("Et", [128, 8, 256], BF16)
            Eh = sbA("Eh", [128, 256], F32)
            Ehb = sbA("Ehb", [128, 256], BF16)
            PEx = sbA("PEx", [128, 1024], BF16)
            Pt = [sbA("Pt%d" % i, [128, 1024], BF16) for i in range(2)]
            Ost = [sbA("Ost%d" % i, [128, 8, 65], F32) for i in range(2)]
            LL = sbA("LL", [128, 8], F32)
            OaT = [sbA("OaT%d" % i, [128, 512], F32) for i in range(2)]
            Ksb = [sbA("Ksb%d" % i, [128, 512], BF16) for i in range(2)]
            KTs = [sbA("KTs%d" % i, [128, 512], BF16) for i in range(2)]
            V1s = [sbA("V1s%d" % i, [128, 8, 65], BF16) for i in range(8)]
            Qbd = sbA("Qbd", [128, 4, 128, 2], BF16)
            PEs = sbA("PEs", [128, 32], F32)
            Zb = sbA("Zb", [128, 32, 192], BF16)
            S.op("pool", lambda e: e.memset(Zb[:], 0.0), writes=["Zb"])
            OsAcc = sbA("OsAcc", [128, 8, 65], F32)
            for i in range(8):
                S.op("pool", lambda e, i=i: e.memset(V1s[i][:], 1.0), writes=[("V1s", i)])

            for i in range(3):
                S.op("pool", lambda e, i=i: e.memset(Kpad[i][:], 0.0), writes=[("Kpad", i)])
                S.op("pool", lambda e, i=i: e.memset(V1[i][:], 1.0), writes=[("V1", i)])

            def prologue(src_ap, dst_tile, dst_key, col0, k, keep_f32=None):
                xf = Xf[k % 2] if keep_f32 is None else keep_f32
                xkey = ("Xf", k % 2) if keep_f32 is None else "Xs_f"
                S.op("sp", lambda e: e.dma_start(out=xf[:], in_=src_ap), writes=[xkey], dma=True)
                S.op("act", lambda e: e.activation(out=SQ[:], in_=xf[:], func=AF.Square), reads=[xkey], writes=["SQ"])
                S.op("dve", lambda e: e.reduce_sum(out=SS[:, 0:1], in_=SQ[:], axis=AX.X), reads=["SQ"], writes=["SS"])
                S.op("act", lambda e: e.activation(out=RS[:, 0:1], in_=SS[:, 0:1], func=AF.Sqrt, bias=EPS, scale=1.0 / 1024), reads=["SS"], writes=["RS"])
                S.op("dve", lambda e: e.reciprocal(out=RS[:, 0:1], in_=RS[:, 0:1]), reads=["RS"], writes=["RS"])
                S.op("dve", lambda e: e.tensor_scalar(out=Xb[:], in0=xf[:], scalar1=RS[:, 0:1], scalar2=None, op0=ALU.mult), reads=[xkey, "RS"], writes=["Xb"])
                for c in range(8):
                    S.op("pe", lambda e, c=c: e.transpose(out=TR[:, c * 128:(c + 1) * 128], in_=Xb[:, c * 128:(c + 1) * 128], identity=ident[:]), reads=["Xb", "ident"], writes=["TR"])
                S.op("act", lambda e: e.activation(out=dst_tile[:, :, col0:col0 + 128], in_=TR[:].rearrange("p (c t) -> p c t", c=8), func=AF.Copy), reads=["TR"], writes=[dst_key])

            if LEVEL >= 2:
                for t in range(NB):
                    prologue(x_ext[t * 128:(t + 1) * 128, :], xT_halo, ("xTh", t), t * 128, t)
                for t in range(NB):
                    prologue(x_ext[NOWN + t * 128:NOWN + (t + 1) * 128, :], xT_own, ("xTo", t), t * 128, t)
                prologue(x_s, xT_s, "xTs", 0, 0, keep_f32=Xs_f)
            XTH_ALL = [("xTh", t) for t in range(NB)]
            XTO_ALL = [("xTo", t) for t in range(NB)]

            wcnt = {"n": 0}
            WK = {}
            WKR = {}

            def load_w(dst, col_ranges, key):
                WK[key] = []
                off = 0
                for (c0, n) in col_ranges:
                    for c in range(8):
                        k = wcnt["n"] % 4
                        wcnt["n"] += 1
                        st = Wst[k]
                        S.op("sp", lambda e, c=c, c0=c0, n=n, st=st: e.dma_start(out=st[:, 0:n], in_=w_in[c * 128:(c + 1) * 128, c0:c0 + n]), writes=[("Wst", k)], dma=True)
                        if wcnt["n"] % 2:
                            S.op("act", lambda e, c=c, n=n, st=st, off=off: e.activation(out=dst[:, c, off:off + n], in_=st[:, 0:n], func=AF.Copy, scale=NG[:, c:c + 1]), reads=[("Wst", k), "NG"] + [("Wrd", key)], writes=[(key, c, off)])
                        else:
                            S.op("dve", lambda e, c=c, n=n, st=st, off=off: e.tensor_scalar(out=dst[:, c, off:off + n], in0=st[:, 0:n], scalar1=NG[:, c:c + 1], scalar2=None, op0=ALU.mult), reads=[("Wst", k), "NG"] + [("Wrd", key)], writes=[(key, c, off)])
                        WK[key].append((key, c, off))
                    off += n

            def inproj(lhs_fn, xkeys, ncols, wtile=None, wkey="W", wcol0=0):
                wt = Wsb if wtile is None else wtile
                ng_ = (ncols + 511) // 512
                for g in range(ng_):
                    n = min(512, ncols - g * 512)
                    for c in range(8):
                        S.op("pe", lambda e, g=g, c=c, n=n: e.matmul(PJ[:, g * 512:g * 512 + n], lhsT=lhs_fn(c), rhs=wt[:, c, wcol0 + g * 512:wcol0 + g * 512 + n], start=(c == 0), stop=(c == 7)),
                             reads=list(xkeys) + WK.get(wkey, [wkey]), writes=[("PJ", g), ("Wrd", wkey)])

            def build_E(gname):
                G = GROUPS[gname]
                for h in range(8):
                    src = bass.AP(tensor=EFscr, offset=(G["di"] * 32 + G["hb"] + h) * 384, ap=[[1, 128], [128, 2], [1, 128]])
                    S.op("sp", lambda e, src=src: e.dma_start(out=Eh[:].rearrange("p (a b) -> p a b", a=2), in_=src), reads=[("EFscr", G["di"])], writes=["Eh"], dma=True)
                    S.op("act", lambda e: e.activation(out=Ehb[:], in_=Eh[:], func=AF.Copy), reads=["Eh"], writes=["Ehb"])
                    S.op("pe", lambda e: e.matmul(SPp[:, 0:256], lhsT=Jm[:], rhs=Ehb[:], start=True, stop=True), reads=["Jm", "Ehb"], writes=[("SP", 0)])
                    S.op("dve", lambda e, h=h: e.tensor_copy(out=Et[:, h, :], in_=SPp[:, 0:256]), reads=[("SP", 0)], writes=["Et"])

            def qkv_block(gname, lhs_fn, xkeys, par, want_q, kv_out=None, nrows=128):
                isA = gname == "a"
                nq = 512
                nk = 128 if isA else 512
                nkh = nk // 64
                if want_q:
                    ncols = nq + 2 * nk
                    ko, vo = nq, nq + nk
                    wc0 = 0
                else:
                    ncols = 2 * nk
                    ko, vo = 0, nk
                    wc0 = nq
                ngrp = (ncols + 511) // 512
                pjk = [("PJ", g) for g in range(ngrp)]
                nn = (nq + nk) if want_q else nk
                nh = nn // 64
                if isA:
                    gq, gk = GQA[:, :], GKA[:, :]
                else:
                    gi = {"b1": 0, "b2": 1, "b3": 2}[gname]
                    gq, gk = GQB[:, gi * 64:(gi + 1) * 64], GKB[:, gi * 64:(gi + 1) * 64]
                nkt = nk // 128

                def proj():
                    inproj(lhs_fn, xkeys, ncols, wcol0=wc0)

                def norm():
                    S.op("act", lambda e: e.activation(out=SQ[:, 0:nn], in_=PJ[:, 0:nn], func=AF.Square), reads=pjk, writes=["SQ"])
                    S.op("dve", lambda e: e.reduce_sum(out=SS[:, 0:nh], in_=SQ[:, 0:nn].rearrange("p (h d) -> p h d", d=64), axis=AX.X), reads=["SQ"], writes=["SS"])
                    S.op("act", lambda e: e.activation(out=RS[:, 0:nh], in_=SS[:, 0:nh], func=AF.Ln, bias=EPS, scale=1.0 / 64), reads=["SS"], writes=["RS"])
                    S.op("act", lambda e: e.activation(out=RS[:, 0:nh], in_=RS[:, 0:nh], func=AF.Exp, scale=-0.5), reads=["RS"], writes=["RS"])
                    S.op("dve", lambda e: e.tensor_tensor(out=QN[:, 0:nn].rearrange("p (h d) -> p h d", d=64), in0=PJ[:, 0:nn].rearrange("p (h d) -> p h d", d=64),
                                                           in1=RS[:, 0:nh].unsqueeze(2).broadcast_to([128, nh, 64]), op=ALU.mult), reads=pjk + ["RS"], writes=["QN"])
                    S.op("dve", lambda e: e.tensor_copy(out=V1[par][:, 0:nkh, 0:64], in_=PJ[:, vo:vo + nk].rearrange("p (h d) -> p h d", d=64)), reads=pjk, writes=[("V1", par)])
                    if kv_out is not None:
                        S.op("act", lambda e: e.activation(out=VF[:, 0:nk], in_=PJ[:, vo:vo + nk], func=AF.Copy), reads=pjk, writes=["VF"])
                        S.op("sp", lambda e: e.dma_start(out=kv_out[1], in_=VF[0:nrows, 0:nk]), reads=["VF"], dma=True)

                def rest():
                    if want_q:
                        S.op("dve", lambda e: e.tensor_tensor(out=QNb[:].rearrange("p (h d) -> p h d", d=64), in0=QN[:, 0:512].rearrange("p (h d) -> p h d", d=64),
                                                                in1=gq.unsqueeze(1).broadcast_to([128, 8, 64]), op=ALU.mult), reads=["QN", "GQA", "GQB"], writes=["QNb"])
                    S.op("dve", lambda e: e.tensor_tensor(out=KN[:, 0:nk].rearrange("p (h d) -> p h d", d=64), in0=QN[:, ko:ko + nk].rearrange("p (h d) -> p h d", d=64),
                                                            in1=gk.unsqueeze(1).broadcast_to([128, nkh, 64]), op=ALU.mult), reads=["QN", "GKA", "GKB"], writes=["KN"])
                    S.op("act", lambda e: e.activation(out=KNb[:, 0:nk], in_=KN[:, 0:nk], func=AF.Copy), reads=["KN"], writes=["KNb"])
                    if kv_out is not None:
                        S.op("sp", lambda e: e.dma_start(out=kv_out[0], in_=KN[0:nrows, 0:nk]), reads=["KN"], dma=True)
                    if want_q:
                        for t in range(4):
                            src = QNb[:, t * 128:(t + 1) * 128]
                            S.op("pe", lambda e, t=t, src=src: e.transpose(out=TR[:, t * 128:(t + 1) * 128], in_=src, identity=ident[:]), reads=["QNb", "ident"], writes=["TR"])
                    for t in range(nkt):
                        S.op("pe", lambda e, t=t: e.transpose(out=TR[:, (4 + t) * 128:(5 + t) * 128], in_=KNb[:, t * 128:(t + 1) * 128], identity=ident[:]), reads=["KNb", "ident"], writes=["TR"])
                    if want_q:
                        S.op("dve", lambda e: e.tensor_copy(out=QT[:].rearrange("p t k -> p (t k)"), in_=TR[:, 0:512]), reads=["TR"], writes=["QT"])
                    trk = TR[:, 512:512 + nkt * 128].rearrange("p (t k) -> p t k", t=nkt)
                    kp = Kpad[par][:, 0:2 * nkt, :].rearrange("p (t s) k -> p t s k", s=2)
                    S.op("dve", lambda e: e.tensor_scalar(out=kp[:, :, 0, :], in0=trk, scalar1=MASK[:, 0:1], scalar2=None, op0=ALU.mult), reads=["TR", "MASK"], writes=[("Kpad", par)])
                    S.op("act", lambda e: e.activation(out=kp[:, :, 1, :], in_=trk, func=AF.Copy, scale=MASK[:, 1:2]), reads=["TR", "MASK"], writes=[("Kpad", par)])

                return proj, norm, rest

            def attend(gname, cur, prv, first, out_rows):
                isA = gname == "a"

                def st(g):
                    bank = g % 2
                    for hl in range(2):
                        h = 2 * g + hl
                        kidx = (h // 4) if isA else h
                        qt = (h % 4) if isA else (h // 2)
                        for blk in range(2):
                            pp = cur if blk == 0 else prv
                            c0 = bank * 512 + hl * 256 + blk * 128
                            S.op("pe", lambda e, c0=c0, pp=pp, kidx=kidx, qt=qt: e.matmul(SPp[:, c0:c0 + 128], lhsT=Kpad[pp][:, kidx, :], rhs=QT[:, qt, :], start=True, stop=True),
                                 reads=[("Kpad", pp), "QT"], writes=[("SP", bank)])

                def ex(g):
                    bank = g % 2
                    S.op("act", lambda e: e.activation(out=PEx[:, bank * 512:(bank + 1) * 512], in_=SPp[:, bank * 512:(bank + 1) * 512], func=AF.Exp), reads=[("SP", bank)], writes=[("PEx", bank)])
                    pt = Pt[g // 2][:, (g % 2) * 512:(g % 2 + 1) * 512]
                    S.op("dve", lambda e: e.tensor_tensor(out=pt, in0=PEx[:, bank * 512:(bank + 1) * 512], in1=Et[:, 2 * g:2 * g + 2, :].rearrange("p h k -> p (h k)"), op=ALU.mult), reads=[("PEx", bank), "Et"], writes=[("Pt", g)])
                    if first:
                        pv_ = pt.rearrange("p (h b q) -> p h b q", h=2, b=2)[:, :, 1, :]
                        S.op("dve", lambda e: e.tensor_scalar(out=pv_, in0=pv_, scalar1=HV[:, 0:1], scalar2=None, op0=ALU.mult), reads=[("Pt", g), "HV"], writes=[("Pt", g)])

                def pv(g):
                    pt = Pt[g // 2][:, (g % 2) * 512:(g % 2 + 1) * 512]
                    for hl in range(2):
                        h = 2 * g + hl
                        kv = (h // 4) if isA else h
                        for blk in range(2):
                            pp = cur if blk == 0 else prv
                            S.op("pe", lambda e, h=h, hl=hl, blk=blk, pp=pp, kv=kv: e.matmul(OPp[:, h * 128:h * 128 + 65], lhsT=pt[:, hl * 256 + blk * 128: hl * 256 + (blk + 1) * 128], rhs=V1[pp][:, kv, :], start=(blk == 0), stop=(blk == 1)),
                                 reads=[("Pt", g), ("V1", pp)], writes=[("OP", h // 4)])

                st(0)
                st(1)
                for g in range(4):
                    ex(g)
                    pv(g)
                    if g + 2 < 4:
                        st(g + 2)
                opk = [("OP", 0), ("OP", 1)]
                opv = OPp[:].rearrange("p (h c) -> p h c", h=8)
                k = rr["ev"] % 2
                rr["ev"] += 1
                if isA:
                    S.op("dve", lambda e: e.tensor_tensor(out=LL[:], in0=opv[:, :, 64], in1=SNK[:], op=ALU.add), reads=opk + ["SNK"], writes=["LL"])
                    S.op("dve", lambda e: e.reciprocal(out=LL[:], in_=LL[:]), reads=["LL"], writes=["LL"])
                    S.op("dve", lambda e: e.tensor_tensor(out=OaT[k][:].rearrange("p (h d) -> p h d", d=64), in0=opv[:, :, 0:64], in1=LL[:].unsqueeze(2).broadcast_to([128, 8, 64]), op=ALU.mult), reads=opk + ["LL"], writes=[("OaT", k)])
                    S.op("sp", lambda e: e.dma_start(out=out_rows, in_=OaT[k][:]), reads=[("OaT", k)], writes=["OSCR_a"], dma=True)
                else:
                    S.op("act", lambda e: e.activation(out=Ost[k][:], in_=opv[:, :, 0:65], func=AF.Copy), reads=opk, writes=[("Ost", k)])
                    S.op("sp", lambda e: e.dma_start(out=out_rows, in_=Ost[k][:].rearrange("p h c -> p (h c)")), reads=[("Ost", k)], writes=["OSCR_" + gname], dma=True)

            def sample_attn(gname):
                G = GROUPS[gname]
                d = G["d"]
                isA = gname == "a"
                nk = 128 if isA else 512
                kvw = 2 * nk
                nkt = nk // 128
                nkh = nk // 64
                cache = {"a": c_a_in, "b1": c_b1_in, "b2": c_b2_in, "b3": c_b3_in}[gname]
                nsk = nskv_out[gname]
                for stg in qkv_block(gname, lambda c: xT_s[:, c, 0:128], ["xTs"], 0, want_q=True, kv_out=(nsk[:, 0, :], nsk[:, 1, :]), nrows=64):
                    stg()
                S.op("dve", lambda e: e.tensor_scalar(out=Qbd[:, :, :, 0], in0=QT[:], scalar1=MASK[:, 0:1], scalar2=None, op0=ALU.mult), reads=["QT", "MASK"], writes=["Qbd"])
                S.op("dve", lambda e: e.tensor_scalar(out=Qbd[:, :, :, 1], in0=QT[:], scalar1=MASK[:, 1:2], scalar2=None, op0=ALU.mult), reads=["QT", "MASK"], writes=["Qbd"])
                if isA:
                    es_v = Et[:].rearrange("p (k g) c -> p g k c", k=2)[:, :, :, 127]
                else:
                    es_v = Et[:, :, 127]
                S.op("pool", lambda e: e.memset(OsAcc[:], 0.0), writes=["OsAcc"])
                for n in range(16):
                    for t in range(4):
                        tok = 4 * n + t
                        tokc = t
                        slot = tok % 2
                        kslot = tok % (NKV + 4)
                        if kslot < NKV:
                            KVt = Xf[kslot]
                            kvk = ("Xf", kslot)
                        else:
                            KVt = Wst[kslot - NKV]
                            kvk = ("Wst", kslot - NKV)
                        dq = "sp"
                        if d == 1:
                            npc = 127 - t
                            S.op(dq, lambda e, n=n, t=t, npc=npc, KVt=KVt: e.dma_start(out=KVt[0:112, 0:kvw], in_=cache[n, t + 1:t + 113, :]), writes=[(kvk, 0)], dma=True)
                            S.op(dq, lambda e, n=n, t=t, npc=npc, KVt=KVt: e.dma_start(out=KVt[112:npc, 0:kvw], in_=cache[n, t + 113:128, :]), writes=[(kvk, 1)], dma=True)
                            S.op(dq, lambda e, n=n, t=t, npc=npc, KVt=KVt: e.dma_start(out=KVt[npc:128, 0:nk], in_=KN[4 * n:4 * n + t + 1, 0:nk]), reads=["KN"], writes=[(kvk, 2)], dma=True)
                            S.op(dq, lambda e, n=n, t=t, npc=npc, KVt=KVt: e.dma_start(out=KVt[npc:128, nk:kvw], in_=VF[4 * n:4 * n + t + 1, 0:nk]), reads=["VF"], writes=[(kvk, 3)], dma=True)
                        else:
                            S.op(dq, lambda e, n=n, t=t, KVt=KVt: e.dma_start(out=KVt[0:112, 0:kvw], in_=cache[n, t + d:t + d + 111 * d + 1:d, :]), writes=[(kvk, 0)], dma=True)
                            S.op(dq, lambda e, n=n, t=t, KVt=KVt: e.dma_start(out=KVt[112:127, 0:kvw], in_=cache[n, t + 113 * d:t + 113 * d + 14 * d + 1:d, :]), writes=[(kvk, 1)], dma=True)
                            S.op(dq, lambda e, tok=tok, KVt=KVt: e.dma_start(out=KVt[127:128, 0:nk], in_=KN[tok:tok + 1, 0:nk]), reads=["KN"], writes=[(kvk, 2)], dma=True)
                            S.op(dq, lambda e, tok=tok, KVt=KVt: e.dma_start(out=KVt[127:128, nk:kvw], in_=VF[tok:tok + 1, 0:nk]), reads=["VF"], writes=[(kvk, 3)], dma=True)
                        vs = tok % 8
                        S.op("act", lambda e, KVt=KVt, slot=slot: e.activation(out=Ksb[slot][:, 0:nk], in_=KVt[:, 0:nk], func=AF.Copy), reads=[(kvk, 0), (kvk, 1), (kvk, 2), (kvk, 3)], writes=[("Ksb", slot)])
                        S.op("dve", lambda e, KVt=KVt, vs=vs: e.tensor_copy(out=V1s[vs][:, 0:nkh, 0:64], in_=KVt[:, nk:kvw].rearrange("p (h d) -> p h d", d=64)), reads=[(kvk, 0), (kvk, 1), (kvk, 2), (kvk, 3)], writes=[("V1s", vs)])
                        for tt in range(nkt):
                            S.op("pe", lambda e, tt=tt, slot=slot: e.transpose(out=TR[:, tt * 128:(tt + 1) * 128], in_=Ksb[slot][:, tt * 128:(tt + 1) * 128], identity=ident[:]), reads=[("Ksb", slot), "ident"], writes=["TR"])
                        S.op("dve", lambda e, slot=slot: e.tensor_copy(out=KTs[slot][:, 0:nk], in_=TR[:, 0:nk]), reads=["TR"], writes=[("KTs", slot)])
                        for tp in range(4):
                            kt = 0 if isA else tp
                            S.op("pe", lambda e, tp=tp, kt=kt, slot=slot, tok=tok, tokc=tokc: e.matmul(SPp[:, tokc * 8 + tp * 2: tokc * 8 + tp * 2 + 2], lhsT=KTs[slot][:, kt * 128:(kt + 1) * 128], rhs=Qbd[:, tp, tok, :], start=True, stop=True),
                                 reads=[("KTs", slot), "Qbd"], writes=[("SP", 0)])
                    S.op("act", lambda e: e.activation(out=PEs[:], in_=SPp[:, 0:32], func=AF.Exp), reads=[("SP", 0)], writes=["PEs"])
                    if isA:
                        S.op("dve", lambda e: e.tensor_tensor(out=Zb[:, :, 63].rearrange("p (t g k) -> p t g k", t=4, g=4), in0=PEs[:].rearrange("p (t g k) -> p t g k", t=4, g=4),
                                                               in1=es_v.unsqueeze(1).broadcast_to([128, 4, 4, 2]), op=ALU.mult), reads=["PEs", "Et"], writes=["Zb"])
                    else:
                        S.op("dve", lambda e: e.tensor_tensor(out=Zb[:, :, 63].rearrange("p (t h) -> p t h", t=4), in0=PEs[:].rearrange("p (t h) -> p t h", t=4),
                                                               in1=es_v.unsqueeze(1).broadcast_to([128, 4, 8]), op=ALU.mult), reads=["PEs", "Et"], writes=["Zb"])
                    for h in range(8):
                        if isA:
                            col = (h % 4) * 2 + h // 4
                            kv = h // 4
                        else:
                            col = h
                            kv = h
                        for t in range(4):
                            tok = 4 * n + t
                            vs = tok % 8
                            S.op("pe", lambda e, h=h, col=col, kv=kv, t=t, vs=vs, tok=tok: e.matmul(OPp[:, h * 128:h * 128 + 65], lhsT=Zb[:, t * 8 + col, 63 - tok:191 - tok], rhs=V1s[vs][:, kv, :], start=(t == 0), stop=(t == 3)),
                                 reads=["Zb", ("V1s", vs)], writes=[("OP", h // 4)])
                    S.op("dve", lambda e: e.tensor_tensor(out=OsAcc[:], in0=OsAcc[:], in1=OPp[:].rearrange("p (h c) -> p h c", h=8)[:, :, 0:65], op=ALU.add), reads=["OsAcc", ("OP", 0), ("OP", 1)], writes=["OsAcc"])
                k = rr["ev"] % 2
                rr["ev"] += 1
                if isA:
                    S.op("dve", lambda e: e.tensor_tensor(out=LL[:], in0=OsAcc[:, :, 64], in1=SNK[:], op=ALU.add), reads=["OsAcc", "SNK"], writes=["LL"])
                    S.op("dve", lambda e: e.reciprocal(out=LL[:], in_=LL[:]), reads=["LL"], writes=["LL"])
                    S.op("dve", lambda e: e.tensor_tensor(out=OaT[k][:].rearrange("p (h d) -> p h d", d=64), in0=OsAcc[:, :, 0:64], in1=LL[:].unsqueeze(2).broadcast_to([128, 8, 64]), op=ALU.mult), reads=["OsAcc", "LL"], writes=[("OaT", k)])
                    S.op("sp", lambda e: e.dma_start(out=Oa_scr.ap()[NOWN:NOWN + 128, :], in_=OaT[k][:]), reads=[("OaT", k)], writes=["OSCR_a"], dma=True)
                else:
                    S.op("sp", lambda e: e.dma_start(out=Oscr[gname].ap()[NOWN:NOWN + 128, :], in_=OsAcc[:].rearrange("p h c -> p (h c)")), reads=["OsAcc"], writes=["OSCR_" + gname], dma=True)

            def attn_phase(gname):
                G = GROUPS[gname]
                d = G["d"]
                isA = gname == "a"
                nk = 128 if isA else 512
                if isA:
                    qr = [(C_QA + kvh * 256 + g * 64, 64) for g in range(4) for kvh in range(2)]
                else:
                    qr = [(G["cq"], 512)]
                load_w(Wsb, qr + [(G["ck"], nk), (G["cv"], nk)], "W")
                if SUB >= 2:
                    build_E(gname)
                oscr = Oa_scr.ap() if isA else Oscr[gname].ap()
                nkv = nkv_out[gname]
                ncb = NB // d
                win = {"a": 128, "b1": 128, "b2": 512, "b3": 2048}[gname]
                blocks = []
                cnt = 0
                for r in range(min(d, NCLS)):
                    hs = NOWN - 128 * d + r
                    lhs_h = (lambda c, hs=hs: xT_halo[:, c, hs:hs + 127 * d + 1:d]) if d > 1 else (lambda c, hs=hs: xT_halo[:, c, hs:hs + 128])
                    blocks.append(dict(stages=qkv_block(gname, lhs_h, XTH_ALL, cnt % 3, want_q=False), att=None))
                    cnt += 1
                    for cb in range(ncb):
                        st = r + d * 128 * cb
                        kv_out = None
                        lo = NOWN - win
                        if st >= lo:
                            r0 = st - lo
                            if d > 1:
                                kv_out = (nkv[r0:r0 + 127 * d + 1:d, 0, :], nkv[r0:r0 + 127 * d + 1:d, 1, :])
                            else:
                                kv_out = (nkv[r0:r0 + 128, 0, :], nkv[r0:r0 + 128, 1, :])
                        lhs_o = (lambda c, st=st: xT_own[:, c, st:st + 127 * d + 1:d]) if d > 1 else (lambda c, st=st: xT_own[:, c, st:st + 128])
                        rows = oscr[st:st + 127 * d + 1:d, :] if d > 1 else oscr[st:st + 128, :]
                        blocks.append(dict(stages=qkv_block(gname, lhs_o, XTO_ALL, cnt % 3, want_q=True, kv_out=kv_out),
                                           att=(cnt % 3, (cnt - 1) % 3, cb == 0, rows)))
                        cnt += 1
                if blocks:
                    blocks[0]["stages"][0]()
                for i, b in enumerate(blocks):
                    b["stages"][1]()
                    if i + 1 < len(blocks):
                        blocks[i + 1]["stages"][0]()
                    b["stages"][2]()
                    if b["att"] is not None:
                        cur, prv, first, rows = b["att"]
                        attend(gname, cur, prv, first, rows)
                if WITH_CACHE and SUB >= 6:
                    sample_attn(gname)

            for gi_, gname in enumerate(("b3", "b2", "b1", "a")):
                if LEVEL >= 3 + gi_:
                    attn_phase(gname)

        with ExitStack() as esF:
            def sbF(name, shape, dt):
                return esF.enter_context(nc.sbuf_tensor(name, shape, dt))

            Wg = sbF("Wg", [128, 8, 3072], BF16)
            WuA = sbF("WuA", [128, 4, 1024], BF16)
            WuB = sbF("WuB", [128, 4, 1024], BF16)
            Wo = sbF("Wo", [128, 8, 1024], BF16)
            Wst = [sbF("WstF%d" % i, [128, 1536], F32) for i in range(4)]
            O1 = sbF("O1", [128, 520], F32)
            O2 = sbF("O2", [128, 520], F32)
            O3 = sbF("O3", [128, 520], F32)
            OA = sbF("OA", [128, 512], F32)
            XR = sbF("XR", [128, 1024], F32)
            SG = sbF("SG", [128, 1024], F32)
            SM2 = [sbF("SM%d" % i, [128, 2048], F32) for i in range(2)]
            OB = sbF("OB", [128, 512], F32)
            SGs = sbF("SGs", [128, 1024], F32)
            LB = sbF("LB", [128, 8], F32)
            U = sbF("U", [128, 1024], BF16)
            UT = sbF("UT", [128, 8, 128], BF16)
            M1 = sbF("M1", [128, 1024], F32)
            M2 = sbF("M2", [128, 1024], F32)
            MG = sbF("MG", [128, 1024], BF16)
            MT = sbF("MT", [128, 8, 128], BF16)
            Y = sbF("Y", [128, 1024], F32)

            wc = {"n": 0}
            BAR = S.last_all()

            WKF = {}

            def load_wF(dst_fn, src_ap_fn, ncols_list, key, scale_gain):
                WKF[key] = []
                for (c0, n, off) in ncols_list:
                    for c in range(dst_fn("nchunk")):
                        k = wc["n"] % 4
                        wc["n"] += 1
                        st = Wst[k]
                        S.op("sp", lambda e, c=c, c0=c0, n=n, st=st: e.dma_start(out=st[:, 0:n], in_=src_ap_fn(c, c0, n)), writes=[("WstF", k)], dma=True, extra=BAR)
                        useact = bool(wc["n"] % 2)
                        WKF[key].append((key, c, off))
                        if scale_gain:
                            if useact:
                                S.op("act", lambda e, c=c, n=n, st=st, off=off: e.activation(out=dst_fn(c)[:, off:off + n], in_=st[:, 0:n], func=AF.Copy, scale=NG[:, c:c + 1]), reads=[("WstF", k), "NG"], writes=[(key, c, off)])
                            else:
                                S.op("dve", lambda e, c=c, n=n, st=st, off=off: e.tensor_scalar(out=dst_fn(c)[:, off:off + n], in0=st[:, 0:n], scalar1=NG[:, c:c + 1], scalar2=None, op0=ALU.mult), reads=[("WstF", k), "NG"], writes=[(key, c, off)])
                        else:
                            if useact:
                                S.op("act", lambda e, c=c, n=n, st=st, off=off: e.activation(out=dst_fn(c)[:, off:off + n], in_=st[:, 0:n], func=AF.Copy), reads=[("WstF", k)], writes=[(key, c, off)])
                            else:
                                S.op("dve", lambda e, c=c, n=n, st=st, off=off: e.tensor_copy(out=dst_fn(c)[:, off:off + n], in_=st[:, 0:n]), reads=[("WstF", k)], writes=[(key, c, off)])

            load_wF(lambda c: 8 if c == "nchunk" else Wg[:, c, :], lambda c, c0, n: w_in[c * 128:(c + 1) * 128, c0:c0 + n],
                    [(C_GA, 512, 0), (C_GB, 512, 512), (C_MA, 1024, 1024), (C_MB, 1024, 2048)], "Wg", True)
            load_wF(lambda c: 4 if c == "nchunk" else WuA[:, c, :], lambda c, c0, n: wup_a_in[c * 128:(c + 1) * 128, c0:c0 + n], [(0, 1024, 0)], "WuA", False)
            load_wF(lambda c: 4 if c == "nchunk" else WuB[:, c, :], lambda c, c0, n: wup_b_in[c * 128:(c + 1) * 128, c0:c0 + n], [(0, 1024, 0)], "WuB", False)
            load_wF(lambda c: 8 if c == "nchunk" else Wo[:, c, :], lambda c, c0, n: wout_in[c * 128:(c + 1) * 128, c0:c0 + n], [(0, 1024, 0)], "Wo", False)

            def final_block(lhs_fn, xkeys, row0, x_src, y_dst, k, nrows=128, xres=None):
                SMk = SM2[k]
                smk = ("SM", k)
                pjk = [("PJ", g) for g in range(3)]

                def loads():
                    S.op("sp", lambda e: e.dma_start(out=O1[:], in_=Oscr["b1"].ap()[row0:row0 + 128, :]), reads=["OSCR_b1"], writes=["O1"], dma=True)
                    S.op("sp", lambda e: e.dma_start(out=O2[:], in_=Oscr["b2"].ap()[row0:row0 + 128, :]), reads=["OSCR_b2"], writes=["O2"], dma=True)
                    S.op("sp", lambda e: e.dma_start(out=O3[:], in_=Oscr["b3"].ap()[row0:row0 + 128, :]), reads=["OSCR_b3"], writes=["O3"], dma=True)
                    S.op("sp", lambda e: e.dma_start(out=OA[:], in_=Oa_scr.ap()[row0:row0 + 128, :]), reads=["OSCR_a"], writes=["OA"], dma=True)

                def gates(rnd):
                    for g in range(3):
                        for c in range(8):
                            S.op("pe", lambda e, g=g, c=c: e.matmul(PJ[:, g * 512:(g + 1) * 512], lhsT=lhs_fn(c), rhs=Wg[:, c, rnd * 1536 + g * 512: rnd * 1536 + (g + 1) * 512], start=(c == 0), stop=(c == 7)),
                                 reads=list(xkeys) + WKF["Wg"], writes=[("PJ", g)])
                    if rnd == 0:
                        S.op("act", lambda e: e.activation(out=SGs[:], in_=PJ[:, 0:1024], func=AF.Sigmoid), reads=pjk, writes=["SGs"])
                        S.op("act", lambda e: e.activation(out=SMk[:, 0:512], in_=PJ[:, 1024:1536], func=AF.Sigmoid), reads=pjk, writes=[smk])
                        S.op("dve", lambda e: e.tensor_tensor(out=SG[:], in0=PJ[:, 0:1024], in1=SGs[:], op=ALU.mult), reads=pjk + ["SGs"], writes=["SG"])
                    else:
                        S.op("act", lambda e: e.activation(out=SMk[:, 512:2048], in_=PJ[:, 0:1536], func=AF.Sigmoid), reads=pjk, writes=[smk])

                def gating():
                    S.op("dve", lambda e: e.tensor_tensor(out=O1[:], in0=O1[:], in1=O2[:], op=ALU.add), reads=["O1", "O2"], writes=["O1"])
                    S.op("dve", lambda e: e.tensor_tensor(out=O1[:], in0=O1[:], in1=O3[:], op=ALU.add), reads=["O1", "O3"], writes=["O1"])
                    o1v = O1[:].rearrange("p (h c) -> p h c", c=65)
                    S.op("dve", lambda e: e.reciprocal(out=LB[:], in_=o1v[:, :, 64]), reads=["O1"], writes=["LB"])
                    S.op("dve", lambda e: e.tensor_tensor(out=OB[:].rearrange("p (h d) -> p h d", d=64), in0=o1v[:, :, 0:64], in1=LB[:].unsqueeze(2).broadcast_to([128, 8, 64]), op=ALU.mult), reads=["O1", "LB"], writes=["OB"])
                    S.op("dve", lambda e: e.tensor_tensor(out=U[:, 0:512], in0=OA[:], in1=SG[:, 0:512], op=ALU.mult), reads=["OA", "SG"], writes=["U"])
                    S.op("dve", lambda e: e.tensor_tensor(out=U[:, 512:1024], in0=OB[:], in1=SG[:, 512:1024], op=ALU.mult), reads=["OB", "SG"], writes=["U"])

                def up():
                    for c in range(8):
                        S.op("pe", lambda e, c=c: e.transpose(out=TR[:, c * 128:(c + 1) * 128], in_=U[:, c * 128:(c + 1) * 128], identity=ident[:]), reads=["U", "ident"], writes=["TR"])
                    S.op("act", lambda e: e.activation(out=UT[:], in_=TR[:].rearrange("p (c t) -> p c t", c=8), func=AF.Copy), reads=["TR"], writes=["UT"])
                    for n in range(2):
                        for c in range(4):
                            S.op("pe", lambda e, n=n, c=c: e.matmul(SPp[:, n * 512:(n + 1) * 512], lhsT=UT[:, c, :], rhs=WuA[:, c, n * 512:(n + 1) * 512], start=(c == 0), stop=(c == 3)), reads=["UT"] + WKF["WuA"], writes=[("SP", n)])
                    for n in range(2):
                        for c in range(4):
                            S.op("pe", lambda e, n=n, c=c: e.matmul(OPp[:, n * 512:(n + 1) * 512], lhsT=UT[:, 4 + c, :], rhs=WuB[:, c, n * 512:(n + 1) * 512], start=(c == 0), stop=(c == 3)), reads=["UT"] + WKF["WuB"], writes=[("OP", n)])

                def merge():
                    S.op("dve", lambda e: e.tensor_tensor(out=M1[:], in0=SPp[:], in1=SMk[:, 0:1024], op=ALU.mult), reads=[("SP", 0), ("SP", 1), smk], writes=["M1"])
                    S.op("dve", lambda e: e.tensor_tensor(out=M2[:], in0=OPp[:], in1=SMk[:, 1024:2048], op=ALU.mult), reads=[("OP", 0), ("OP", 1), smk], writes=["M2"])
                    S.op("dve", lambda e: e.tensor_tensor(out=MG[:], in0=M1[:], in1=M2[:], op=ALU.add), reads=["M1", "M2"], writes=["MG"])

                def outp():
                    for c in range(8):
                        S.op("pe", lambda e, c=c: e.transpose(out=TR[:, c * 128:(c + 1) * 128], in_=MG[:, c * 128:(c + 1) * 128], identity=ident[:]), reads=["MG", "ident"], writes=["TR"])
                    S.op("act", lambda e: e.activation(out=MT[:], in_=TR[:].rearrange("p (c t) -> p c t", c=8), func=AF.Copy), reads=["TR"], writes=["MT"])
                    for n in range(2):
                        for c in range(8):
                            S.op("pe", lambda e, n=n, c=c: e.matmul(SPp[:, n * 512:(n + 1) * 512], lhsT=MT[:, c, :], rhs=Wo[:, c, n * 512:(n + 1) * 512], start=(c == 0), stop=(c == 7)), reads=["MT"] + WKF["Wo"], writes=[("SP", n)])

                def resid():
                    if xres is None:
                        S.op("sp", lambda e: e.dma_start(out=XR[:], in_=x_src), writes=["XR"], dma=True)
                        xr, xrk = XR, "XR"
                    else:
                        xr, xrk = xres, "Xs_f"
                    S.op("dve", lambda e: e.tensor_tensor(out=Y[:], in0=SPp[:], in1=xr[:], op=ALU.add), reads=[("SP", 0), ("SP", 1), xrk], writes=["Y"])
                    S.op("sp", lambda e: e.dma_start(out=y_dst, in_=Y[0:nrows, :]), reads=["Y"], dma=True)

                return dict(loads=loads, gates=gates, gating=gating, up=up, merge=merge, outp=outp, resid=resid)

            fblocks = []
            for t in range(NB if LEVEL >= 7 else 0):
                fblocks.append(final_block(lambda c, t=t: xT_own[:, c, t * 128:(t + 1) * 128], [("xTo", t)], t * 128, x_ext[NOWN + t * 128:NOWN + (t + 1) * 128, :], y_out[t * 128:(t + 1) * 128, :], len(fblocks) % 2))
            if WITH_CACHE and LEVEL >= 8:
                fblocks.append(final_block(lambda c: xT_s[:, c, 0:128], ["xTs"], NOWN, None, ys_out, len(fblocks) % 2, nrows=64, xres=Xs_f))
            if fblocks:
                fblocks[0]["loads"]()
                fblocks[0]["gates"](0)
                fblocks[0]["gates"](1)
            for i, fb in enumerate(fblocks):
                nxt = fblocks[i + 1] if i + 1 < len(fblocks) else None
                fb["gating"]()
                if i > 0:
                    fblocks[i - 1]["resid"]()
                if nxt is not None:
                    nxt["loads"]()
                    nxt["gates"](0)
                fb["up"]()
                if nxt is not None:
                    nxt["gates"](1)
                fb["merge"]()
                fb["outp"]()
            if fblocks:
                fblocks[-1]["resid"]()

        S.emit(sems, dsems, block)
    return nc


def shared_inputs(rel_bias, norm_gain, w_in, q_gain_a, k_gain_a, sinks_a, q_gain_b, k_gain_b, w_up_a, w_up_b, w_out):
    relb = np.zeros((128, 128), np.float32)
    relb[:32, :32] = rel_bias
    oh = onehot_tables()
    mask2 = np.zeros((128, 2), np.float32)
    mask2[:64, 0] = 1.0
    mask2[64:, 1] = 1.0
    return {
        "mask2": mask2,
        "w_in": w_in[0], "ng": np.ascontiguousarray(norm_gain[0].reshape(8, 128).T), "relb": relb, "oh": oh,
        "gq_a": np.ascontiguousarray(np.broadcast_to(q_gain_a[0][None, :], (128, 64))),
        "gk_a": np.ascontiguousarray(np.broadcast_to(k_gain_a[0][None, :], (128, 64))),
        "gq_b": np.ascontiguousarray(np.broadcast_to(q_gain_b[0].reshape(1, 192), (128, 192))),
        "gk_b": np.ascontiguousarray(np.broadcast_to(k_gain_b[0].reshape(1, 192), (128, 192))),
        "sinks": np.ascontiguousarray(np.broadcast_to(sinks_a[0][None, :], (128, 8))),
        "wup_a": w_up_a[0], "wup_b": w_up_b[0], "wout": w_out[0],
    }


_CACHE = {}


def kernel(x_prompt, x_sample, cache_a_kv, cache_b1_kv, cache_b2_kv, cache_b3_kv, rel_bias, norm_gain, w_in,
           q_gain_a, k_gain_a, sinks_a, q_gain_b, k_gain_b, w_up_a, w_up_b, w_out):
    f = lambda a: np.ascontiguousarray(np.asarray(a, dtype=np.float32))
    x_prompt = f(x_prompt); x_sample = f(x_sample)
    cache_a_kv = f(cache_a_kv); cache_b1_kv = f(cache_b1_kv); cache_b2_kv = f(cache_b2_kv); cache_b3_kv = f(cache_b3_kv)
    rel_bias = f(rel_bias); norm_gain = f(norm_gain); w_in = f(w_in)
    q_gain_a = f(q_gain_a); k_gain_a = f(k_gain_a); sinks_a = f(sinks_a); q_gain_b = f(q_gain_b); k_gain_b = f(k_gain_b)
    w_up_a = f(w_up_a); w_up_b = f(w_up_b); w_out = f(w_out)

    nc = build_program()
    shared = shared_inputs(rel_bias, norm_gain, w_in, q_gain_a, k_gain_a, sinks_a, q_gain_b, k_gain_b, w_up_a, w_up_b, w_out)

    in_maps = []
    for c in range(8):
        b, h = c // 2, c % 2
        x_ext = np.zeros((4096, 1024), np.float32)
        if h == 1:
            x_ext[:] = x_prompt[b]
        else:
            x_ext[2048:] = x_prompt[b, :2048]
        xs = np.zeros((128, 1024), np.float32)
        xs[:64] = x_sample[16 * c:16 * c + 16].reshape(64, 1024)
        m = dict(shared)
        m["x_ext"] = x_ext
        m["x_s"] = xs
        m["hv"] = np.full((128, 1), float(h), np.float32)
        if WITH_CACHE:
          m["c_a"] = np.ascontiguousarray(cache_a_kv[0, 16 * c:16 * c + 16].reshape(16, 128, 256))
          m["c_b1"] = np.ascontiguousarray(cache_b1_kv[0, 16 * c:16 * c + 16].reshape(16, 128, 1024))
          m["c_b2"] = np.ascontiguousarray(cache_b2_kv[0, 16 * c:16 * c + 16].reshape(16, 512, 1024))
          m["c_b3"] = np.ascontiguousarray(cache_b3_kv[0, 16 * c:16 * c + 16].reshape(16, 2048, 1024))
        in_maps.append(m)
    res = run_bass_kernel_spmd(nc, in_maps, core_ids=list(range(8)))
    R = res.results
    y = np.zeros((4, 4096, 1024), np.float32)
    ys = np.zeros((128, 4, 1024), np.float32)
    for c in range(8):
        b, h = c // 2, c % 2
        y[b, h * 2048:(h + 1) * 2048] = R[c]["y"]
        ys[16 * c:16 * c + 16] = R[c]["ys"].reshape(16, 4, 1024)
    np_a = np.stack([R[2 * b + 1]["nkv_a"].reshape(128, 2, 2, 64) for b in range(4)])[None]
    np_b1 = np.stack([R[2 * b + 1]["nkv_b1"].reshape(128, 2, 8, 64) for b in range(4)])[None]
    np_b2 = np.stack([R[2 * b + 1]["nkv_b2"].reshape(512, 2, 8, 64) for b in range(4)])[None]
    np_b3 = np.stack([R[2 * b + 1]["nkv_b3"].reshape(2048, 2, 8, 64) for b in range(4)])[None]
    ns_a = np.concatenate([R[c]["ns_a"].reshape(16, 4, 2, 2, 64) for c in range(8)])[None]
    ns_b1 = np.concatenate([R[c]["ns_b1"].reshape(16, 4, 2, 8, 64) for c in range(8)])[None]
    ns_b2 = np.concatenate([R[c]["ns_b2"].reshape(16, 4, 2, 8, 64) for c in range(8)])[None]
    ns_b3 = np.concatenate([R[c]["ns_b3"].reshape(16, 4, 2, 8, 64) for c in range(8)])[None]
    return (y, ys, np_a, np_b1, np_b2, np_b3, ns_a, ns_b1, ns_b2, ns_b3)
```

```python
import math
import os
from contextlib import ExitStack

import numpy as np
import concourse.bass as bass
import concourse.mybir as mybir
from concourse.bass_utils import run_bass_kernel_spmd

F32 = mybir.dt.float32
BF16 = mybir.dt.bfloat16
AF = mybir.ActivationFunctionType
ALU = mybir.AluOpType
AX = mybir.AxisListType

SAME_ENGINE_SYNC = os.environ.get('KDBG_SES', '1') == '1'
LEVEL = int(os.environ.get('KDBG_LEVEL', '99'))
NCLS = int(os.environ.get('KDBG_NCLS', '99'))
NDS = 14
NKV = 2
WITH_CACHE = os.environ.get('KDBG_NOSAMPLE', '0') != '1'
SUB = int(os.environ.get('KDBG_SUB', '99'))
XB = int(os.environ.get('KDBG_X', '0'))
EPS = 1e-6
NOWN = 2048
NB = 16
C_QA, C_KA, C_VA, C_GA = 0, 512, 640, 768
C_QB, C_KB, C_VB, C_GB, C_MA, C_MB = 1280, 2816, 4352, 5888, 6400, 7424


class Sched:
    ENGS = ("pe", "act", "dve", "pool", "sp")

    def __init__(self, nc, n_dma_sems=6):
        self.nc = nc
        self.ops = {e: [] for e in self.ENGS}
        self.lastw = {}
        self.readers = {}
        self.n_dma_sems = n_dma_sems
        self.dma_count = {"sp": 0, "pool": 0, "act": 0}
        self.dma_last = {}

    def last_all(self):
        return [(e, len(self.ops[e]) - 1) for e in self.ENGS if self.ops[e]]

    def op(self, eng, fn, reads=(), writes=(), dma=False, extra=()):
        ops = self.ops[eng]
        idx = len(ops)
        deps = set(extra)
        for b in reads:
            w = self.lastw.get(b)
            if w is not None:
                deps.add(w)
        for b in writes:
            w = self.lastw.get(b)
            if w is not None:
                deps.add(w)
            for r in self.readers.get(b, ()):
                deps.add(r)
        cdeps = {}
        ddeps = set()
        for (e, i) in deps:
            o = self.ops[e][i]
            if o["dma"]:
                ddeps.add(o["sig"])
            else:
                if e == eng and not dma and (e == "pe" or not SAME_ENGINE_SYNC):
                    continue
                cdeps[e] = max(cdeps.get(e, -1), i)
        rec = {"fn": fn, "dma": dma, "cdeps": cdeps, "ddeps": ddeps, "sig": None, "signaled": False}
        if dma:
            n = self.dma_count[eng]
            self.dma_count[eng] = n + 1
            slot = n % self.n_dma_sems
            val = 16 * (n // self.n_dma_sems + 1)
            rec["sig"] = (eng, slot, val)
            if val > 16:
                ddeps.add((eng, slot, val - 16))
            self.dma_last[(eng, slot)] = val
        ops.append(rec)
        me = (eng, idx)
        for b in writes:
            self.lastw[b] = me
            self.readers[b] = []
        for b in reads:
            if b in writes:
                continue
            self.readers.setdefault(b, []).append(me)
        return me

    def emit(self, sems, dsems, block):
        for e in self.ENGS:
            for o in self.ops[e]:
                for (de, di) in o["cdeps"].items():
                    self.ops[de][di]["signaled"] = True
        for e in self.ENGS:
            c = 0
            for o in self.ops[e]:
                if o["dma"]:
                    continue
                if o["signaled"]:
                    c += 1
                    o["sig"] = c
        allops = self.ops
        dma_last = self.dma_last

        def run(eng_name, eng):
            waited = {}
            for o in allops[eng_name]:
                for (de, di) in sorted(o["cdeps"].items()):
                    v = allops[de][di]["sig"]
                    key = ("c", de)
                    if waited.get(key, 0) >= v:
                        continue
                    eng.wait_ge(sems[de], v)
                    waited[key] = v
                for (qe, slot, v) in sorted(o["ddeps"]):
                    key = ("d", qe, slot)
                    if waited.get(key, 0) >= v:
                        continue
                    eng.wait_ge(dsems[(qe, slot)], v)
                    waited[key] = v
                ins = o["fn"](eng)
                if o["dma"]:
                    qe, slot, v = o["sig"]
                    ins.then_inc(dsems[(qe, slot)], 16)
                elif o["signaled"]:
                    ins.then_inc(sems[eng_name], 1)
            if eng_name == "sp":
                for (qe, slot), v in sorted(dma_last.items()):
                    if waited.get(("d", qe, slot), 0) >= v:
                        continue
                    eng.wait_ge(dsems[(qe, slot)], v)

        @block.tensor
        def _(eng):
            run("pe", eng)

        @block.scalar
        def _(eng):
            run("act", eng)

        @block.vector
        def _(eng):
            run("dve", eng)

        @block.gpsimd
        def _(eng):
            run("pool", eng)

        @block.sync
        def _(eng):
            run("sp", eng)


def t5_bucket_np(dist):
    d = np.maximum(dist, 0)
    df = np.maximum(d, 1).astype(np.float32)
    large = 16 + (np.log(df / np.float32(16)) / np.float32(math.log(2048 / 16)) * np.float32(16)).astype(np.int32)
    large = np.minimum(large, 31)
    return np.where(d < 16, d, large)


def onehot_tables():
    oh = np.zeros((3, 128, 384), np.float32)
    for di, dil in enumerate((1, 4, 16)):
        delta = np.arange(128)
        b = t5_bucket_np(delta * dil)
        oh[di, b, delta + 127] = 1.0
    return oh


GROUPS = {
    "b3": dict(d=16, di=2, hb=24, cq=C_QB + 1024, ck=C_KB + 1024, cv=C_VB + 1024, nkv=8),
    "b2": dict(d=4, di=1, hb=16, cq=C_QB + 512, ck=C_KB + 512, cv=C_VB + 512, nkv=8),
    "b1": dict(d=1, di=0, hb=8, cq=C_QB, ck=C_KB, cv=C_VB, nkv=8),
    "a": dict(d=1, di=0, hb=0, cq=C_QA, ck=C_KA, cv=C_VA, nkv=2),
}


def build_program(with_sample=True):
    nc = bass.Bass("TRN2", target_bir_lowering=False)

    def din(name, shape):
        return nc.dram_tensor(name, shape, F32, kind="ExternalInput").ap()

    def dout(name, shape):
        return nc.dram_tensor(name, shape, F32, kind="ExternalOutput").ap()

    x_ext = din("x_ext", [4096, 1024])
    x_s = din("x_s", [128, 1024])
    hv_in = din("hv", [128, 1])
    mask_in = din("mask2", [128, 2])
    w_in = din("w_in", [1024, 8448])
    ng_in = din("ng", [128, 8])
    relb_in = din("relb", [128, 128])
    oh_in = din("oh", [3, 128, 384])
    gq_a_in = din("gq_a", [128, 64])
    gk_a_in = din("gk_a", [128, 64])
    gq_b_in = din("gq_b", [128, 192])
    gk_b_in = din("gk_b", [128, 192])
    sinks_in = din("sinks", [128, 8])
    wup_a_in = din("wup_a", [512, 1024])
    wup_b_in = din("wup_b", [512, 1024])
    wout_in = din("wout", [1024, 1024])
    if WITH_CACHE:
        c_a_in = din("c_a", [16, 128, 256])
        c_b1_in = din("c_b1", [16, 128, 1024])
        c_b2_in = din("c_b2", [16, 512, 1024])
        c_b3_in = din("c_b3", [16, 2048, 1024])

    y_out = dout("y", [2048, 1024])
    ys_out = dout("ys", [64, 1024])
    nkv_out = {"a": dout("nkv_a", [128, 2, 128]), "b1": dout("nkv_b1", [128, 2, 512]),
               "b2": dout("nkv_b2", [512, 2, 512]), "b3": dout("nkv_b3", [2048, 2, 512])}
    nskv_out = {"a": dout("ns_a", [64, 2, 128]), "b1": dout("ns_b1", [64, 2, 512]),
                "b2": dout("ns_b2", [64, 2, 512]), "b3": dout("ns_b3", [64, 2, 512])}

    DBG = os.environ.get('KDBG_DUMP', '0') == '1'
    if DBG:
        dbg_o = dout("dbgo", [4, 128, 520])
    NTS = NOWN + 128
    Oscr = {g: nc.dram_tensor("oscr_" + g, [NTS, 520], F32) for g in ("b1", "b2", "b3")}
    Oa_scr = nc.dram_tensor("oscr_a", [NTS, 512], F32)
    EFscr = nc.dram_tensor("efscr", [3, 32, 384], F32)

    S = Sched(nc, n_dma_sems=NDS)
    rr = {"ev": 0}

    with ExitStack() as es:
        def sb(name, shape, dt):
            return es.enter_context(nc.sbuf_tensor(name, shape, dt))

        def ps(name, shape, dt):
            return es.enter_context(nc.psum_tensor(name, shape, dt))

        sems = {e: es.enter_context(nc.semaphore("s_" + e)) for e in ("pe", "act", "dve", "pool")}
        dsems = {(q, i): es.enter_context(nc.semaphore(f"d_{q}{i}")) for q in ("sp", "pool") for i in range(NDS)}

        xT_own = sb("xT_own", [128, 8, NOWN], BF16)
        xT_s = sb("xT_s", [128, 8, 128], BF16)
        Xs_f = sb("Xs_f", [128, 1024], F32)
        ident = sb("ident", [128, 128], BF16)
        Jm = sb("Jm", [128, 128], BF16)
        identf = sb("identf", [128, 128], F32)
        NG = sb("NG", [128, 8], F32)
        HV = sb("HV", [128, 1], F32)
        MASK = sb("MASK", [128, 2], F32)
        GQA = sb("GQA", [128, 64], F32)
        GKA = sb("GKA", [128, 64], F32)
        GQB = sb("GQB", [128, 192], F32)
        GKB = sb("GKB", [128, 192], F32)
        SNK = sb("SNK", [128, 8], F32)
        PJ = ps("PJ", [128, 1536], F32)
        TR = ps("TR", [128, 1024], BF16)
        SPp = ps("SPp", [128, 1024], F32)
        OPp = ps("OPp", [128, 1024], F32)

        block = es.enter_context(nc.Block())

        S.op("sp", lambda e: e.dma_start(out=NG[:], in_=ng_in), writes=["NG"], dma=True)
        S.op("sp", lambda e: e.dma_start(out=HV[:], in_=hv_in), writes=["HV"], dma=True)
        S.op("sp", lambda e: e.dma_start(out=MASK[:], in_=mask_in), writes=["MASK"], dma=True)
        S.op("sp", lambda e: e.dma_start(out=GQA[:], in_=gq_a_in), writes=["GQA"], dma=True)
        S.op("sp", lambda e: e.dma_start(out=GKA[:], in_=gk_a_in), writes=["GKA"], dma=True)
        S.op("sp", lambda e: e.dma_start(out=GQB[:], in_=gq_b_in), writes=["GQB"], dma=True)
        S.op("sp", lambda e: e.dma_start(out=GKB[:], in_=gk_b_in), writes=["GKB"], dma=True)
        S.op("sp", lambda e: e.dma_start(out=SNK[:], in_=sinks_in), writes=["SNK"], dma=True)
        S.op("dve", lambda e: e.tensor_scalar(out=GQA[:], in0=GQA[:], scalar1=0.125, scalar2=None, op0=ALU.mult), reads=["GQA"], writes=["GQA"])
        S.op("dve", lambda e: e.tensor_scalar(out=GQB[:], in0=GQB[:], scalar1=0.125, scalar2=None, op0=ALU.mult), reads=["GQB"], writes=["GQB"])
        S.op("act", lambda e: e.activation(out=SNK[:], in_=SNK[:], func=AF.Exp), reads=["SNK"], writes=["SNK"])
        S.op("pool", lambda e: e.memset(identf[:], 1.0), writes=["identf"])
        S.op("pool", lambda e: e.affine_select(out=identf[:], in_=identf[:], pattern=[[-1, 128]], compare_op=ALU.is_equal, fill=0.0, base=0, channel_multiplier=1), reads=["identf"], writes=["identf"])
        S.op("dve", lambda e: e.tensor_copy(out=ident[:], in_=identf[:]), reads=["identf"], writes=["ident"])
        S.op("pool", lambda e: e.memset(identf[:], 1.0), reads=["identf"], writes=["identf"])
        S.op("pool", lambda e: e.affine_select(out=identf[:], in_=identf[:], pattern=[[1, 128]], compare_op=ALU.is_equal, fill=0.0, base=-127, channel_multiplier=1), reads=["identf"], writes=["identf"])
        S.op("dve", lambda e: e.tensor_copy(out=Jm[:], in_=identf[:]), reads=["identf"], writes=["Jm"])

        RB = sb("RB", [128, 128], F32)
        OHs = sb("OHs", [128, 384], F32)
        EFs = sb("EFs", [128, 384], F32)
        if True:
            S.op("sp", lambda e: e.dma_start(out=RB[:], in_=relb_in), writes=["RB"], dma=True)
            for di in range(3):
                S.op("sp", lambda e, di=di: e.dma_start(out=OHs[:], in_=oh_in[di]), writes=["OHs"], dma=True)
                S.op("pe", lambda e: e.matmul(SPp[:, 0:384], lhsT=RB[:], rhs=OHs[:], start=True, stop=True), reads=["RB", "OHs"], writes=[("SP", 0), ("SP", 1)])
                S.op("act", lambda e: e.activation(out=EFs[:], in_=SPp[:, 0:384], func=AF.Exp), reads=[("SP", 0), ("SP", 1)], writes=["EFs"])
                S.op("dve", lambda e: e.memset(EFs[:, 0:127], 0.0), reads=["EFs"], writes=["EFs"])
                S.op("dve", lambda e: e.memset(EFs[:, 255:384], 0.0), reads=["EFs"], writes=["EFs"])
                S.op("sp", lambda e, di=di: e.dma_start(out=EFscr.ap()[di], in_=EFs[0:32, :]), reads=["EFs"], writes=[("EFscr", di)], dma=True)

        with ExitStack() as esA:
            def sbA(name, shape, dt):
                return esA.enter_context(nc.sbuf_tensor(name, shape, dt))

            xT_halo = sbA("xT_halo", [128, 8, NOWN], BF16)
            Wsb = sbA("Wsb", [128, 8, 1536], BF16)
            Wst = [sbA("Wst%d" % i, [128, 1536], F32) for i in range(4)]
            Xf = [sbA("Xf%d" % i, [128, 1024], F32) for i in range(2)]
            SQ = sbA("SQ", [128, 1024], F32)
            Xb = sbA("Xb", [128, 1024], BF16)
            SS = sbA("SS", [128, 16], F32)
            SSp = sbA("SSp", [128, 2], F32)
            RSp = sbA("RSp", [128, 2], F32)
            RS = sbA("RS", [128, 16], F32)
            QN = sbA("QN", [128, 1024], F32)
            QNb = sbA("QNb", [128, 512], BF16)
            KNV = sbA("KNV", [128, 1024], F32)
            KN = KNV[:, 0:512]
            VF = KNV[:, 512:1024]
            KNb = sbA("KNb", [128, 512], BF16)
            QT = sbA("QT", [128, 4, 128], BF16)
            Kpad = [sbA("Kpad%d" % i, [128, 8, 128], BF16) for i in range(3)]
            V1 = [sbA("V1_%d" % i, [128, 8, 65], BF16) for i in range(3)]
            Et = sbA("Et", [128, 8, 256], BF16)
            Eh = sbA("Eh", [128, 256], F32)
            Ehb = sbA("Ehb", [128, 256], BF16)
            PEx = sbA("PEx", [128, 1024], BF16)
            Pt = [sbA("Pt%d" % i, [128, 1024], BF16) for i in range(2)]
            Ost = [sbA("Ost%d" % i, [128, 8, 65], F32) for i in range(2)]
            LL = sbA("LL", [128, 8], F32)
            OaT = [sbA("OaT%d" % i, [128, 512], F32) for i in range(2)]
            Ksb = [sbA("Ksb%d" % i, [128, 512], BF16) for i in range(2)]
            KTs = [sbA("KTs%d" % i, [128, 512], BF16) for i in range(2)]
            V1s = [sbA("V1s%d" % i, [128, 8, 65], BF16) for i in range(8)]
            Qbd = sbA("Qbd", [128, 4, 128, 2], BF16)
            PEs = sbA("PEs", [128, 32], F32)
            Zb = sbA("Zb", [128, 32, 192], BF16)
            S.op("pool", lambda e: e.memset(Zb[:], 0.0), writes=["Zb"])
            OsAcc = sbA("OsAcc", [128, 8, 65], F32)
            for i in range(8):
                S.op("pool", lambda e, i=i: e.memset(V1s[i][:], 1.0), writes=[("V1s", i)])

            for i in range(3):
                S.op("pool", lambda e, i=i: e.memset(Kpad[i][:], 0.0), writes=[("Kpad", i)])
                S.op("pool", lambda e, i=i: e.memset(V1[i][:], 1.0), writes=[("V1", i)])

            def prologue(src_ap, dst_tile, dst_key, col0, k, keep_f32=None):
                kk = k % 2
                xf = Xf[kk] if keep_f32 is None else keep_f32
                xkey = ("Xf", kk) if keep_f32 is None else "Xs_f"
                sq, sqk = (SQ, ["SQ"]) if kk == 0 else (QN, ["QN"])
                xb, xbk = (Xb, ["Xb"]) if kk == 0 else (PEx, [("PEx", 0), ("PEx", 1)])
                ssk, rsk = ("SSp", kk), ("RSp", kk)

                def head():
                    S.op("sp", lambda e: e.dma_start(out=xf[:], in_=src_ap), writes=[xkey], dma=True)
                    S.op("act", lambda e: e.activation(out=sq[:], in_=xf[:], func=AF.Square), reads=[xkey], writes=sqk)
                    S.op("dve", lambda e: e.reduce_sum(out=SSp[:, kk:kk + 1], in_=sq[:], axis=AX.X), reads=sqk, writes=[ssk])
                    S.op("act", lambda e: e.activation(out=RSp[:, kk:kk + 1], in_=SSp[:, kk:kk + 1], func=AF.Sqrt, bias=EPS, scale=1.0 / 1024), reads=[ssk], writes=[rsk])
                    S.op("dve", lambda e: e.reciprocal(out=RSp[:, kk:kk + 1], in_=RSp[:, kk:kk + 1]), reads=[rsk], writes=[rsk])
                    S.op("dve", lambda e: e.tensor_scalar(out=xb[:], in0=xf[:], scalar1=RSp[:, kk:kk + 1], scalar2=None, op0=ALU.mult), reads=[xkey, rsk], writes=xbk)

                def tail():
                    for c in range(8):
                        S.op("pe", lambda e, c=c: e.transpose(out=TR[:, c * 128:(c + 1) * 128], in_=xb[:, c * 128:(c + 1) * 128], identity=ident[:]), reads=xbk + ["ident"], writes=["TR"])
                    S.op("act", lambda e: e.activation(out=dst_tile[:, :, col0:col0 + 128], in_=TR[:].rearrange("p (c t) -> p c t", c=8), func=AF.Copy), reads=["TR"], writes=[dst_key])

                return head, tail

            if LEVEL >= 2:
                pro = []
                for t in range(NB):
                    pro.append(prologue(x_ext[t * 128:(t + 1) * 128, :], xT_halo, ("xTh", t), t * 128, len(pro)))
                for t in range(NB):
                    pro.append(prologue(x_ext[NOWN + t * 128:NOWN + (t + 1) * 128, :], xT_own, ("xTo", t), t * 128, len(pro)))
                pro.append(prologue(x_s, xT_s, "xTs", 0, len(pro), keep_f32=Xs_f))
                pro[0][0]()
                for i, (hd, tl) in enumerate(pro):
                    if i + 1 < len(pro):
                        pro[i + 1][0]()
                    tl()
            XTH_ALL = [("xTh", t) for t in range(NB)]
            XTO_ALL = [("xTo", t) for t in range(NB)]

            wcnt = {"n": 0}
            WK = {}
            WKR = {}

            def load_w(dst, col_ranges, key):
                WK[key] = []
                off = 0
                for (c0, n) in col_ranges:
                    for c in range(8):
                        k = wcnt["n"] % 4
                        wcnt["n"] += 1
                        st = Wst[k]
                        S.op("sp", lambda e, c=c, c0=c0, n=n, st=st: e.dma_start(out=st[:, 0:n], in_=w_in[c * 128:(c + 1) * 128, c0:c0 + n]), writes=[("Wst", k)], dma=True)
                        if wcnt["n"] % 2:
                            S.op("act", lambda e, c=c, n=n, st=st, off=off: e.activation(out=dst[:, c, off:off + n], in_=st[:, 0:n], func=AF.Copy, scale=NG[:, c:c + 1]), reads=[("Wst", k), "NG"] + [("Wrd", key)], writes=[(key, c, off)])
                        else:
                            S.op("dve", lambda e, c=c, n=n, st=st, off=off: e.tensor_scalar(out=dst[:, c, off:off + n], in0=st[:, 0:n], scalar1=NG[:, c:c + 1], scalar2=None, op0=ALU.mult), reads=[("Wst", k), "NG"] + [("Wrd", key)], writes=[(key, c, off)])
                        WK[key].append((key, c, off))
                    off += n

            def inproj(lhs_fn, xkeys, ncols, wtile=None, wkey="W", wcol0=0):
                wt = Wsb if wtile is None else wtile
                ng_ = (ncols + 511) // 512
                for g in range(ng_):
                    n = min(512, ncols - g * 512)
                    for c in range(8):
                        S.op("pe", lambda e, g=g, c=c, n=n: e.matmul(PJ[:, g * 512:g * 512 + n], lhsT=lhs_fn(c), rhs=wt[:, c, wcol0 + g * 512:wcol0 + g * 512 + n], start=(c == 0), stop=(c == 7)),
                             reads=list(xkeys) + WK.get(wkey, [wkey]), writes=[("PJ", g), ("Wrd", wkey)])

            def build_E(gname):
                G = GROUPS[gname]
                for h in range(8):
                    src = bass.AP(tensor=EFscr, offset=(G["di"] * 32 + G["hb"] + h) * 384, ap=[[1, 128], [128, 2], [1, 128]])
                    S.op("sp", lambda e, src=src: e.dma_start(out=Eh[:].rearrange("p (a b) -> p a b", a=2), in_=src), reads=[("EFscr", G["di"])], writes=["Eh"], dma=True)
                    S.op("act", lambda e: e.activation(out=Ehb[:], in_=Eh[:], func=AF.Copy), reads=["Eh"], writes=["Ehb"])
                    S.op("pe", lambda e: e.matmul(SPp[:, 0:256], lhsT=Jm[:], rhs=Ehb[:], start=True, stop=True), reads=["Jm", "Ehb"], writes=[("SP", 0)])
                    S.op("dve", lambda e, h=h: e.tensor_copy(out=Et[:, h, :], in_=SPp[:, 0:256]), reads=[("SP", 0)], writes=["Et"])

            def qkv_block(gname, lhs_fn, xkeys, par, want_q, kv_out=None, nrows=128):
                isA = gname == "a"
                nq = 512
                nk = 128 if isA else 512
                nkh = nk // 64
                if want_q:
                    ncols = nq + 2 * nk
                    ko, vo = nq, nq + nk
                    wc0 = 0
                else:
                    ncols = 2 * nk
                    ko, vo = 0, nk
                    wc0 = nq
                ngrp = (ncols + 511) // 512
                pjk = [("PJ", g) for g in range(ngrp)]
                nn = (nq + nk) if want_q else nk
                nh = nn // 64
                if isA:
                    gq, gk = GQA[:, :], GKA[:, :]
                else:
                    gi = {"b1": 0, "b2": 1, "b3": 2}[gname]
                    gq, gk = GQB[:, gi * 64:(gi + 1) * 64], GKB[:, gi * 64:(gi + 1) * 64]
                nkt = nk // 128

                def proj():
                    inproj(lhs_fn, xkeys, ncols, wcol0=wc0)

                def norm():
                    S.op("act", lambda e: e.activation(out=SQ[:, 0:nn], in_=PJ[:, 0:nn], func=AF.Square), reads=pjk, writes=["SQ"])
                    S.op("dve", lambda e: e.reduce_sum(out=SS[:, 0:nh], in_=SQ[:, 0:nn].rearrange("p (h d) -> p h d", d=64), axis=AX.X), reads=["SQ"], writes=["SS"])
                    S.op("act", lambda e: e.activation(out=RS[:, 0:nh], in_=SS[:, 0:nh], func=AF.Ln, bias=EPS, scale=1.0 / 64), reads=["SS"], writes=["RS"])
                    S.op("act", lambda e: e.activation(out=RS[:, 0:nh], in_=RS[:, 0:nh], func=AF.Exp, scale=-0.5), reads=["RS"], writes=["RS"])
                    S.op("dve", lambda e: e.tensor_tensor(out=QN[:, 0:nn].rearrange("p (h d) -> p h d", d=64), in0=PJ[:, 0:nn].rearrange("p (h d) -> p h d", d=64),
                                                           in1=RS[:, 0:nh].unsqueeze(2).broadcast_to([128, nh, 64]), op=ALU.mult), reads=pjk + ["RS"], writes=["QN"])
                    S.op("dve", lambda e: e.tensor_copy(out=V1[par][:, 0:nkh, 0:64], in_=PJ[:, vo:vo + nk].rearrange("p (h d) -> p h d", d=64)), reads=pjk, writes=[("V1", par)])
                    if kv_out is not None:
                        S.op("act", lambda e: e.activation(out=VF[:, 0:nk], in_=PJ[:, vo:vo + nk], func=AF.Copy), reads=pjk, writes=["VF"])
                        S.op("sp", lambda e: e.dma_start(out=kv_out[1], in_=VF[0:nrows, 0:nk]), reads=["VF"], dma=True)

                def rest():
                    if want_q:
                        S.op("dve", lambda e: e.tensor_tensor(out=QNb[:].rearrange("p (h d) -> p h d", d=64), in0=QN[:, 0:512].rearrange("p (h d) -> p h d", d=64),
                                                                in1=gq.unsqueeze(1).broadcast_to([128, 8, 64]), op=ALU.mult), reads=["QN", "GQA", "GQB"], writes=["QNb"])
                    S.op("dve", lambda e: e.tensor_tensor(out=KN[:, 0:nk].rearrange("p (h d) -> p h d", d=64), in0=QN[:, ko:ko + nk].rearrange("p (h d) -> p h d", d=64),
                                                            in1=gk.unsqueeze(1).broadcast_to([128, nkh, 64]), op=ALU.mult), reads=["QN", "GKA", "GKB"], writes=["KN"])
                    S.op("act", lambda e: e.activation(out=KNb[:, 0:nk], in_=KN[:, 0:nk], func=AF.Copy), reads=["KN"], writes=["KNb"])
                    if kv_out is not None:
                        S.op("sp", lambda e: e.dma_start(out=kv_out[0], in_=KN[0:nrows, 0:nk]), reads=["KN"], dma=True)
                    if want_q:
                        for t in range(4):
                            src = QNb[:, t * 128:(t + 1) * 128]
                            S.op("pe", lambda e, t=t, src=src: e.transpose(out=TR[:, t * 128:(t + 1) * 128], in_=src, identity=ident[:]), reads=["QNb", "ident"], writes=["TR"])
                    for t in range(nkt):
                        S.op("pe", lambda e, t=t: e.transpose(out=TR[:, (4 + t) * 128:(5 + t) * 128], in_=KNb[:, t * 128:(t + 1) * 128], identity=ident[:]), reads=["KNb", "ident"], writes=["TR"])
                    if want_q:
                        S.op("dve", lambda e: e.tensor_copy(out=QT[:].rearrange("p t k -> p (t k)"), in_=TR[:, 0:512]), reads=["TR"], writes=["QT"])
                    trk = TR[:, 512:512 + nkt * 128].rearrange("p (t k) -> p t k", t=nkt)
                    kp = Kpad[par][:, 0:2 * nkt, :].rearrange("p (t s) k -> p t s k", s=2)
                    S.op("dve", lambda e: e.tensor_scalar(out=kp[:, :, 0, :], in0=trk, scalar1=MASK[:, 0:1], scalar2=None, op0=ALU.mult), reads=["TR", "MASK"], writes=[("Kpad", par)])
                    S.op("act", lambda e: e.activation(out=kp[:, :, 1, :], in_=trk, func=AF.Copy, scale=MASK[:, 1:2]), reads=["TR", "MASK"], writes=[("Kpad", par)])

                return proj, norm, rest

            def attend(gname, cur, prv, first, out_rows):
                isA = gname == "a"

                def st(g):
                    bank = g % 2
                    for hl in range(2):
                        h = 2 * g + hl
                        kidx = (h // 4) if isA else h
                        qt = (h % 4) if isA else (h // 2)
                        for blk in range(2):
                            pp = cur if blk == 0 else prv
                            c0 = bank * 512 + hl * 256 + blk * 128
                            S.op("pe", lambda e, c0=c0, pp=pp, kidx=kidx, qt=qt: e.matmul(SPp[:, c0:c0 + 128], lhsT=Kpad[pp][:, kidx, :], rhs=QT[:, qt, :], start=True, stop=True),
                                 reads=[("Kpad", pp), "QT"], writes=[("SP", bank)])

                def ex(g):
                    bank = g % 2
                    S.op("act", lambda e: e.activation(out=PEx[:, bank * 512:(bank + 1) * 512], in_=SPp[:, bank * 512:(bank + 1) * 512], func=AF.Exp), reads=[("SP", bank)], writes=[("PEx", bank)])
                    pt = Pt[g // 2][:, (g % 2) * 512:(g % 2 + 1) * 512]
                    S.op("dve", lambda e: e.tensor_tensor(out=pt, in0=PEx[:, bank * 512:(bank + 1) * 512], in1=Et[:, 2 * g:2 * g + 2, :].rearrange("p h k -> p (h k)"), op=ALU.mult), reads=[("PEx", bank), "Et"], writes=[("Pt", g)])
                    if first:
                        pv_ = pt.rearrange("p (h b q) -> p h b q", h=2, b=2)[:, :, 1, :]
                        S.op("dve", lambda e: e.tensor_scalar(out=pv_, in0=pv_, scalar1=HV[:, 0:1], scalar2=None, op0=ALU.mult), reads=[("Pt", g), "HV"], writes=[("Pt", g)])

                def pv(g):
                    pt = Pt[g // 2][:, (g % 2) * 512:(g % 2 + 1) * 512]
                    for hl in range(2):
                        h = 2 * g + hl
                        kv = (h // 4) if isA else h
                        for blk in range(2):
                            pp = cur if blk == 0 else prv
                            S.op("pe", lambda e, h=h, hl=hl, blk=blk, pp=pp, kv=kv: e.matmul(OPp[:, h * 128:h * 128 + 65], lhsT=pt[:, hl * 256 + blk * 128: hl * 256 + (blk + 1) * 128], rhs=V1[pp][:, kv, :], start=(blk == 0), stop=(blk == 1)),
                                 reads=[("Pt", g), ("V1", pp)], writes=[("OP", h // 4)])

                st(0)
                st(1)
                for g in range(4):
                    ex(g)
                    pv(g)
                    if g + 2 < 4:
                        st(g + 2)
                opk = [("OP", 0), ("OP", 1)]
                opv = OPp[:].rearrange("p (h c) -> p h c", h=8)
                k = rr["ev"] % 2
                rr["ev"] += 1
                if isA:
                    S.op("dve", lambda e: e.tensor_tensor(out=LL[:], in0=opv[:, :, 64], in1=SNK[:], op=ALU.add), reads=opk + ["SNK"], writes=["LL"])
                    S.op("dve", lambda e: e.reciprocal(out=LL[:], in_=LL[:]), reads=["LL"], writes=["LL"])
                    S.op("dve", lambda e: e.tensor_tensor(out=OaT[k][:].rearrange("p (h d) -> p h d", d=64), in0=opv[:, :, 0:64], in1=LL[:].unsqueeze(2).broadcast_to([128, 8, 64]), op=ALU.mult), reads=opk + ["LL"], writes=[("OaT", k)])
                    S.op("sp", lambda e: e.dma_start(out=out_rows, in_=OaT[k][:]), reads=[("OaT", k)], writes=["OSCR_a"], dma=True)
                else:
                    S.op("act", lambda e: e.activation(out=Ost[k][:], in_=opv[:, :, 0:65], func=AF.Copy), reads=opk, writes=[("Ost", k)])
                    S.op("sp", lambda e: e.dma_start(out=out_rows, in_=Ost[k][:].rearrange("p h c -> p (h c)")), reads=[("Ost", k)], writes=["OSCR_" + gname], dma=True)

            def sample_attn(gname):
                G = GROUPS[gname]
                d = G["d"]
                isA = gname == "a"
                nk = 128 if isA else 512
                kvw = 2 * nk
                nkt = nk // 128
                nkh = nk // 64
                cache = {"a": c_a_in, "b1": c_b1_in, "b2": c_b2_in, "b3": c_b3_in}[gname]
                nsk = nskv_out[gname]
                for stg in qkv_block(gname, lambda c: xT_s[:, c, 0:128], ["xTs"], 0, want_q=True, kv_out=(nsk[:, 0, :], nsk[:, 1, :]), nrows=64):
                    stg()
                S.op("dve", lambda e: e.tensor_scalar(out=Qbd[:, :, :, 0], in0=QT[:], scalar1=MASK[:, 0:1], scalar2=None, op0=ALU.mult), reads=["QT", "MASK"], writes=["Qbd"])
                S.op("dve", lambda e: e.tensor_scalar(out=Qbd[:, :, :, 1], in0=QT[:], scalar1=MASK[:, 1:2], scalar2=None, op0=ALU.mult), reads=["QT", "MASK"], writes=["Qbd"])
                if isA:
                    es_v = Et[:].rearrange("p (k g) c -> p g k c", k=2)[:, :, :, 127]
                else:
                    es_v = Et[:, :, 127]
                S.op("pool", lambda e: e.memset(OsAcc[:], 0.0), writes=["OsAcc"])
                for n in range(16):
                    for t in range(4):
                        tok = 4 * n + t
                        tokc = t
                        slot = tok % 2
                        kslot = tok % (NKV + 4)
                        if kslot < NKV:
                            KVt = Xf[kslot]
                            kvk = ("Xf", kslot)
                        else:
                            KVt = Wst[kslot - NKV]
                            kvk = ("Wst", kslot - NKV)
                        dq = "sp"
                        if d == 1:
                            npc = 127 - t
                            r0_, r1_ = 4 * n, 4 * n + t + 1
                            S.op(dq, lambda e, n=n, t=t, npc=npc, KVt=KVt: e.dma_start(out=KVt[0:112, 0:kvw], in_=cache[n, t + 1:t + 113, :]), writes=[(kvk, 0)], dma=True)
                            S.op(dq, lambda e, n=n, t=t, npc=npc, KVt=KVt: e.dma_start(out=KVt[112:npc, 0:kvw], in_=cache[n, t + 113:128, :]), writes=[(kvk, 1)], dma=True)
                        else:
                            npc = 127
                            r0_, r1_ = tok, tok + 1
                            S.op(dq, lambda e, n=n, t=t, KVt=KVt: e.dma_start(out=KVt[0:112, 0:kvw], in_=cache[n, t + d:t + d + 111 * d + 1:d, :]), writes=[(kvk, 0)], dma=True)
                            S.op(dq, lambda e, n=n, t=t, KVt=KVt: e.dma_start(out=KVt[112:127, 0:kvw], in_=cache[n, t + 113 * d:t + 113 * d + 14 * d + 1:d, :]), writes=[(kvk, 1)], dma=True)
                        if isA:
                            src_new = KNV[r0_:r1_, :].rearrange("p (a b) -> p a b", a=2)[:, :, 0:128]
                            dst_new = KVt[npc:128, 0:256].rearrange("p (a b) -> p a b", a=2)
                        else:
                            src_new = KNV[r0_:r1_, :]
                            dst_new = KVt[npc:128, 0:1024]
                        S.op(dq, lambda e, src_new=src_new, dst_new=dst_new: e.dma_start(out=dst_new, in_=src_new), reads=["KN", "VF"], writes=[(kvk, 2)], dma=True)
                        vs = tok % 8
                        S.op("act", lambda e, KVt=KVt, slot=slot: e.activation(out=Ksb[slot][:, 0:nk], in_=KVt[:, 0:nk], func=AF.Copy), reads=[(kvk, 0), (kvk, 1), (kvk, 2)], writes=[("Ksb", slot)])
                        S.op("dve", lambda e, KVt=KVt, vs=vs: e.tensor_copy(out=V1s[vs][:, 0:nkh, 0:64], in_=KVt[:, nk:kvw].rearrange("p (h d) -> p h d", d=64)), reads=[(kvk, 0), (kvk, 1), (kvk, 2)], writes=[("V1s", vs)])
                        for tt in range(nkt):
                            S.op("pe", lambda e, tt=tt, slot=slot: e.transpose(out=TR[:, tt * 128:(tt + 1) * 128], in_=Ksb[slot][:, tt * 128:(tt + 1) * 128], identity=ident[:]), reads=[("Ksb", slot), "ident"], writes=["TR"])
                        S.op("dve", lambda e, slot=slot: e.tensor_copy(out=KTs[slot][:, 0:nk], in_=TR[:, 0:nk]), reads=["TR"], writes=[("KTs", slot)])
                        for tp in range(4):
                            kt = 0 if isA else tp
                            S.op("pe", lambda e, tp=tp, kt=kt, slot=slot, tok=tok, tokc=tokc: e.matmul(SPp[:, tokc * 8 + tp * 2: tokc * 8 + tp * 2 + 2], lhsT=KTs[slot][:, kt * 128:(kt + 1) * 128], rhs=Qbd[:, tp, tok, :], start=True, stop=True),
                                 reads=[("KTs", slot), "Qbd"], writes=[("SP", 0)])
                    S.op("act", lambda e: e.activation(out=PEs[:], in_=SPp[:, 0:32], func=AF.Exp), reads=[("SP", 0)], writes=["PEs"])
                    if isA:
                        S.op("dve", lambda e: e.tensor_tensor(out=Zb[:, :, 63].rearrange("p (t g k) -> p t g k", t=4, g=4), in0=PEs[:].rearrange("p (t g k) -> p t g k", t=4, g=4),
                                                               in1=es_v.unsqueeze(1).broadcast_to([128, 4, 4, 2]), op=ALU.mult), reads=["PEs", "Et"], writes=["Zb"])
                    else:
                        S.op("dve", lambda e: e.tensor_tensor(out=Zb[:, :, 63].rearrange("p (t h) -> p t h", t=4), in0=PEs[:].rearrange("p (t h) -> p t h", t=4),
                                                               in1=es_v.unsqueeze(1).broadcast_to([128, 4, 8]), op=ALU.mult), reads=["PEs", "Et"], writes=["Zb"])
                    for h in range(8):
                        if isA:
                            col = (h % 4) * 2 + h // 4
                            kv = h // 4
                        else:
                            col = h
                            kv = h
                        for t in range(4):
                            tok = 4 * n + t
                            vs = tok % 8
                            S.op("pe", lambda e, h=h, col=col, kv=kv, t=t, vs=vs, tok=tok: e.matmul(OPp[:, h * 128:h * 128 + 65], lhsT=Zb[:, t * 8 + col, 63 - tok:191 - tok], rhs=V1s[vs][:, kv, :], start=(t == 0), stop=(t == 3)),
                                 reads=["Zb", ("V1s", vs)], writes=[("OP", h // 4)])
                    S.op("dve", lambda e: e.tensor_tensor(out=OsAcc[:], in0=OsAcc[:], in1=OPp[:].rearrange("p (h c) -> p h c", h=8)[:, :, 0:65], op=ALU.add), reads=["OsAcc", ("OP", 0), ("OP", 1)], writes=["OsAcc"])
                k = rr["ev"] % 2
                rr["ev"] += 1
                if isA:
                    S.op("dve", lambda e: e.tensor_tensor(out=LL[:], in0=OsAcc[:, :, 64], in1=SNK[:], op=ALU.add), reads=["OsAcc", "SNK"], writes=["LL"])
                    S.op("dve", lambda e: e.reciprocal(out=LL[:], in_=LL[:]), reads=["LL"], writes=["LL"])
                    S.op("dve", lambda e: e.tensor_tensor(out=OaT[k][:].rearrange("p (h d) -> p h d", d=64), in0=OsAcc[:, :, 0:64], in1=LL[:].unsqueeze(2).broadcast_to([128, 8, 64]), op=ALU.mult), reads=["OsAcc", "LL"], writes=[("OaT", k)])
                    S.op("sp", lambda e: e.dma_start(out=Oa_scr.ap()[NOWN:NOWN + 128, :], in_=OaT[k][:]), reads=[("OaT", k)], writes=["OSCR_a"], dma=True)
                else:
                    S.op("sp", lambda e: e.dma_start(out=Oscr[gname].ap()[NOWN:NOWN + 128, :], in_=OsAcc[:].rearrange("p h c -> p (h c)")), reads=["OsAcc"], writes=["OSCR_" + gname], dma=True)

            def attn_phase(gname):
                G = GROUPS[gname]
                d = G["d"]
                isA = gname == "a"
                nk = 128 if isA else 512
                if isA:
                    qr = [(C_QA + kvh * 256 + g * 64, 64) for g in range(4) for kvh in range(2)]
                else:
                    qr = [(G["cq"], 512)]
                load_w(Wsb, qr + [(G["ck"], nk), (G["cv"], nk)], "W")
                if SUB >= 2:
                    build_E(gname)
                oscr = Oa_scr.ap() if isA else Oscr[gname].ap()
                nkv = nkv_out[gname]
                ncb = NB // d
                win = {"a": 128, "b1": 128, "b2": 512, "b3": 2048}[gname]
                blocks = []
                cnt = 0
                for r in range(min(d, NCLS)):
                    hs = NOWN - 128 * d + r
                    lhs_h = (lambda c, hs=hs: xT_halo[:, c, hs:hs + 127 * d + 1:d]) if d > 1 else (lambda c, hs=hs: xT_halo[:, c, hs:hs + 128])
                    blocks.append(dict(stages=qkv_block(gname, lhs_h, XTH_ALL, cnt % 3, want_q=False), att=None))
                    cnt += 1
                    for cb in range(ncb):
                        st = r + d * 128 * cb
                        kv_out = None
                        lo = NOWN - win
                        if st >= lo:
                            r0 = st - lo
                            if d > 1:
                                kv_out = (nkv[r0:r0 + 127 * d + 1:d, 0, :], nkv[r0:r0 + 127 * d + 1:d, 1, :])
                            else:
                                kv_out = (nkv[r0:r0 + 128, 0, :], nkv[r0:r0 + 128, 1, :])
                        lhs_o = (lambda c, st=st: xT_own[:, c, st:st + 127 * d + 1:d]) if d > 1 else (lambda c, st=st: xT_own[:, c, st:st + 128])
                        rows = oscr[st:st + 127 * d + 1:d, :] if d > 1 else oscr[st:st + 128, :]
                        blocks.append(dict(stages=qkv_block(gname, lhs_o, XTO_ALL, cnt % 3, want_q=True, kv_out=kv_out),
                                           att=(cnt % 3, (cnt - 1) % 3, cb == 0, rows)))
                        cnt += 1
                if blocks:
                    blocks[0]["stages"][0]()
                for i, b in enumerate(blocks):
                    b["stages"][1]()
                    if i + 1 < len(blocks):
                        blocks[i + 1]["stages"][0]()
                    b["stages"][2]()
                    if b["att"] is not None:
                        cur, prv, first, rows = b["att"]
                        attend(gname, cur, prv, first, rows)
                if WITH_CACHE and SUB >= 6:
                    sample_attn(gname)

            for gi_, gname in enumerate(("b3", "b2", "b1", "a")):
                if LEVEL >= 3 + gi_:
                    attn_phase(gname)

        with ExitStack() as esF:
            def sbF(name, shape, dt):
                return esF.enter_context(nc.sbuf_tensor(name, shape, dt))

            Wg = sbF("Wg", [128, 8, 3072], BF16)
            WuA = sbF("WuA", [128, 4, 1024], BF16)
            WuB = sbF("WuB", [128, 4, 1024], BF16)
            Wo = sbF("Wo", [128, 8, 1024], BF16)
            Wst = [sbF("WstF%d" % i, [128, 1536], F32) for i in range(4)]
            O1 = sbF("O1", [128, 520], F32)
            O2 = sbF("O2", [128, 520], F32)
            O3 = sbF("O3", [128, 520], F32)
            OA = sbF("OA", [128, 512], F32)
            XR = sbF("XR", [128, 1024], F32)
            SG = sbF("SG", [128, 1024], F32)
            SM2 = [sbF("SM%d" % i, [128, 2048], F32) for i in range(2)]
            OB = sbF("OB", [128, 512], F32)
            SGs = sbF("SGs", [128, 1024], F32)
            LB = sbF("LB", [128, 8], F32)
            U = sbF("U", [128, 1024], BF16)
            UT = sbF("UT", [128, 8, 128], BF16)
            M1 = sbF("M1", [128, 1024], F32)
            M2 = sbF("M2", [128, 1024], F32)
            MG = sbF("MG", [128, 1024], BF16)
            MT = sbF("MT", [128, 8, 128], BF16)
            Y = sbF("Y", [128, 1024], F32)

            wc = {"n": 0}
            BAR = S.last_all()

            WKF = {}

            def load_wF(dst_fn, src_ap_fn, ncols_list, key, scale_gain):
                WKF[key] = []
                for (c0, n, off) in ncols_list:
                    for c in range(dst_fn("nchunk")):
                        k = wc["n"] % 4
                        wc["n"] += 1
                        st = Wst[k]
                        S.op("sp", lambda e, c=c, c0=c0, n=n, st=st: e.dma_start(out=st[:, 0:n], in_=src_ap_fn(c, c0, n)), writes=[("WstF", k)], dma=True, extra=BAR)
                        useact = bool(wc["n"] % 2)
                        WKF[key].append((key, c, off))
                        if scale_gain:
                            if useact:
                                S.op("act", lambda e, c=c, n=n, st=st, off=off: e.activation(out=dst_fn(c)[:, off:off + n], in_=st[:, 0:n], func=AF.Copy, scale=NG[:, c:c + 1]), reads=[("WstF", k), "NG"], writes=[(key, c, off)])
                            else:
                                S.op("dve", lambda e, c=c, n=n, st=st, off=off: e.tensor_scalar(out=dst_fn(c)[:, off:off + n], in0=st[:, 0:n], scalar1=NG[:, c:c + 1], scalar2=None, op0=ALU.mult), reads=[("WstF", k), "NG"], writes=[(key, c, off)])
                        else:
                            if useact:
                                S.op("act", lambda e, c=c, n=n, st=st, off=off: e.activation(out=dst_fn(c)[:, off:off + n], in_=st[:, 0:n], func=AF.Copy), reads=[("WstF", k)], writes=[(key, c, off)])
                            else:
                                S.op("dve", lambda e, c=c, n=n, st=st, off=off: e.tensor_copy(out=dst_fn(c)[:, off:off + n], in_=st[:, 0:n]), reads=[("WstF", k)], writes=[(key, c, off)])

            load_wF(lambda c: 8 if c == "nchunk" else Wg[:, c, :], lambda c, c0, n: w_in[c * 128:(c + 1) * 128, c0:c0 + n],
                    [(C_GA, 512, 0), (C_GB, 512, 512), (C_MA, 1024, 1024), (C_MB, 1024, 2048)], "Wg", True)
            load_wF(lambda c: 4 if c == "nchunk" else WuA[:, c, :], lambda c, c0, n: wup_a_in[c * 128:(c + 1) * 128, c0:c0 + n], [(0, 1024, 0)], "WuA", False)
            load_wF(lambda c: 4 if c == "nchunk" else WuB[:, c, :], lambda c, c0, n: wup_b_in[c * 128:(c + 1) * 128, c0:c0 + n], [(0, 1024, 0)], "WuB", False)
            load_wF(lambda c: 8 if c == "nchunk" else Wo[:, c, :], lambda c, c0, n: wout_in[c * 128:(c + 1) * 128, c0:c0 + n], [(0, 1024, 0)], "Wo", False)

            def final_block(lhs_fn, xkeys, row0, x_src, y_dst, k, nrows=128, xres=None):
                SMk = SM2[k]
                smk = ("SM", k)
                pjk = [("PJ", g) for g in range(3)]

                def loads():
                    S.op("sp", lambda e: e.dma_start(out=O1[:], in_=Oscr["b1"].ap()[row0:row0 + 128, :]), reads=["OSCR_b1"], writes=["O1"], dma=True)
                    S.op("sp", lambda e: e.dma_start(out=O2[:], in_=Oscr["b2"].ap()[row0:row0 + 128, :]), reads=["OSCR_b2"], writes=["O2"], dma=True)
                    S.op("sp", lambda e: e.dma_start(out=O3[:], in_=Oscr["b3"].ap()[row0:row0 + 128, :]), reads=["OSCR_b3"], writes=["O3"], dma=True)
                    S.op("sp", lambda e: e.dma_start(out=OA[:], in_=Oa_scr.ap()[row0:row0 + 128, :]), reads=["OSCR_a"], writes=["OA"], dma=True)

                def gates(rnd):
                    for g in range(3):
                        for c in range(8):
                            S.op("pe", lambda e, g=g, c=c: e.matmul(PJ[:, g * 512:(g + 1) * 512], lhsT=lhs_fn(c), rhs=Wg[:, c, rnd * 1536 + g * 512: rnd * 1536 + (g + 1) * 512], start=(c == 0), stop=(c == 7)),
                                 reads=list(xkeys) + WKF["Wg"], writes=[("PJ", g)])
                    if rnd == 0:
                        S.op("act", lambda e: e.activation(out=SGs[:], in_=PJ[:, 0:1024], func=AF.Sigmoid), reads=pjk, writes=["SGs"])
                        S.op("act", lambda e: e.activation(out=SMk[:, 0:512], in_=PJ[:, 1024:1536], func=AF.Sigmoid), reads=pjk, writes=[smk])
                        S.op("dve", lambda e: e.tensor_tensor(out=SG[:], in0=PJ[:, 0:1024], in1=SGs[:], op=ALU.mult), reads=pjk + ["SGs"], writes=["SG"])
                    else:
                        S.op("act", lambda e: e.activation(out=SMk[:, 512:2048], in_=PJ[:, 0:1536], func=AF.Sigmoid), reads=pjk, writes=[smk])

                def gating():
                    S.op("dve", lambda e: e.tensor_tensor(out=O1[:], in0=O1[:], in1=O2[:], op=ALU.add), reads=["O1", "O2"], writes=["O1"])
                    S.op("dve", lambda e: e.tensor_tensor(out=O1[:], in0=O1[:], in1=O3[:], op=ALU.add), reads=["O1", "O3"], writes=["O1"])
                    o1v = O1[:].rearrange("p (h c) -> p h c", c=65)
                    S.op("dve", lambda e: e.reciprocal(out=LB[:], in_=o1v[:, :, 64]), reads=["O1"], writes=["LB"])
                    S.op("dve", lambda e: e.tensor_tensor(out=OB[:].rearrange("p (h d) -> p h d", d=64), in0=o1v[:, :, 0:64], in1=LB[:].unsqueeze(2).broadcast_to([128, 8, 64]), op=ALU.mult), reads=["O1", "LB"], writes=["OB"])
                    S.op("dve", lambda e: e.tensor_tensor(out=U[:, 0:512], in0=OA[:], in1=SG[:, 0:512], op=ALU.mult), reads=["OA", "SG"], writes=["U"])
                    S.op("dve", lambda e: e.tensor_tensor(out=U[:, 512:1024], in0=OB[:], in1=SG[:, 512:1024], op=ALU.mult), reads=["OB", "SG"], writes=["U"])

                def up():
                    for c in range(8):
                        S.op("pe", lambda e, c=c: e.transpose(out=TR[:, c * 128:(c + 1) * 128], in_=U[:, c * 128:(c + 1) * 128], identity=ident[:]), reads=["U", "ident"], writes=["TR"])
                    S.op("act", lambda e: e.activation(out=UT[:], in_=TR[:].rearrange("p (c t) -> p c t", c=8), func=AF.Copy), reads=["TR"], writes=["UT"])
                    for n in range(2):
                        for c in range(4):
                            S.op("pe", lambda e, n=n, c=c: e.matmul(SPp[:, n * 512:(n + 1) * 512], lhsT=UT[:, c, :], rhs=WuA[:, c, n * 512:(n + 1) * 512], start=(c == 0), stop=(c == 3)), reads=["UT"] + WKF["WuA"], writes=[("SP", n)])
                    for n in range(2):
                        for c in range(4):
                            S.op("pe", lambda e, n=n, c=c: e.matmul(OPp[:, n * 512:(n + 1) * 512], lhsT=UT[:, 4 + c, :], rhs=WuB[:, c, n * 512:(n + 1) * 512], start=(c == 0), stop=(c == 3)), reads=["UT"] + WKF["WuB"], writes=[("OP", n)])

                def merge():
                    S.op("dve", lambda e: e.tensor_tensor(out=M1[:], in0=SPp[:], in1=SMk[:, 0:1024], op=ALU.mult), reads=[("SP", 0), ("SP", 1), smk], writes=["M1"])
                    S.op("dve", lambda e: e.tensor_tensor(out=M2[:], in0=OPp[:], in1=SMk[:, 1024:2048], op=ALU.mult), reads=[("OP", 0), ("OP", 1), smk], writes=["M2"])
                    S.op("dve", lambda e: e.tensor_tensor(out=MG[:], in0=M1[:], in1=M2[:], op=ALU.add), reads=["M1", "M2"], writes=["MG"])

                def outp():
                    for c in range(8):
                        S.op("pe", lambda e, c=c: e.transpose(out=TR[:, c * 128:(c + 1) * 128], in_=MG[:, c * 128:(c + 1) * 128], identity=ident[:]), reads=["MG", "ident"], writes=["TR"])
                    S.op("act", lambda e: e.activation(out=MT[:], in_=TR[:].rearrange("p (c t) -> p c t", c=8), func=AF.Copy), reads=["TR"], writes=["MT"])
                    for n in range(2):
                        for c in range(8):
                            S.op("pe", lambda e, n=n, c=c: e.matmul(SPp[:, n * 512:(n + 1) * 512], lhsT=MT[:, c, :], rhs=Wo[:, c, n * 512:(n + 1) * 512], start=(c == 0), stop=(c == 7)), reads=["MT"] + WKF["Wo"], writes=[("SP", n)])

                def resid():
                    if xres is None:
                        S.op("sp", lambda e: e.dma_start(out=XR[:], in_=x_src), writes=["XR"], dma=True)
                        xr, xrk = XR, "XR"
                    else:
                        xr, xrk = xres, "Xs_f"
                    S.op("dve", lambda e: e.tensor_tensor(out=Y[:], in0=SPp[:], in1=xr[:], op=ALU.add), reads=[("SP", 0), ("SP", 1), xrk], writes=["Y"])
                    S.op("sp", lambda e: e.dma_start(out=y_dst, in_=Y[0:nrows, :]), reads=["Y"], dma=True)

                return dict(loads=loads, gates=gates, gating=gating, up=up, merge=merge, outp=outp, resid=resid)

            fblocks = []
            for t in range(NB if LEVEL >= 7 else 0):
                fblocks.append(final_block(lambda c, t=t: xT_own[:, c, t * 128:(t + 1) * 128], [("xTo", t)], t * 128, x_ext[NOWN + t * 128:NOWN + (t + 1) * 128, :], y_out[t * 128:(t + 1) * 128, :], len(fblocks) % 2))
            if WITH_CACHE and LEVEL >= 8:
                fblocks.append(final_block(lambda c: xT_s[:, c, 0:128], ["xTs"], NOWN, None, ys_out, len(fblocks) % 2, nrows=64, xres=Xs_f))
            if fblocks:
                fblocks[0]["loads"]()
                fblocks[0]["gates"](0)
                fblocks[0]["gates"](1)
            for i, fb in enumerate(fblocks):
                nxt = fblocks[i + 1] if i + 1 < len(fblocks) else None
                fb["gating"]()
                if i > 0:
                    fblocks[i - 1]["resid"]()
                if nxt is not None:
                    nxt["loads"]()
                    nxt["gates"](0)
                fb["up"]()
                if nxt is not None:
                    nxt["gates"](1)
                fb["merge"]()
                fb["outp"]()
            if fblocks:
                fblocks[-1]["resid"]()

        S.emit(sems, dsems, block)
    return nc


def shared_inputs(rel_bias, norm_gain, w_in, q_gain_a, k_gain_a, sinks_a, q_gain_b, k_gain_b, w_up_a, w_up_b, w_out):
    relb = np.zeros((128, 128), np.float32)
    relb[:32, :32] = rel_bias
    oh = onehot_tables()
    mask2 = np.zeros((128, 2), np.float32)
    mask2[:64, 0] = 1.0
    mask2[64:, 1] = 1.0
    return {
        "mask2": mask2,
        "w_in": w_in[0], "ng": np.ascontiguousarray(norm_gain[0].reshape(8, 128).T), "relb": relb, "oh": oh,
        "gq_a": np.ascontiguousarray(np.broadcast_to(q_gain_a[0][None, :], (128, 64))),
        "gk_a": np.ascontiguousarray(np.broadcast_to(k_gain_a[0][None, :], (128, 64))),
        "gq_b": np.ascontiguousarray(np.broadcast_to(q_gain_b[0].reshape(1, 192), (128, 192))),
        "gk_b": np.ascontiguousarray(np.broadcast_to(k_gain_b[0].reshape(1, 192), (128, 192))),
        "sinks": np.ascontiguousarray(np.broadcast_to(sinks_a[0][None, :], (128, 8))),
        "wup_a": w_up_a[0], "wup_b": w_up_b[0], "wout": w_out[0],
    }


_CACHE = {}


def kernel(x_prompt, x_sample, cache_a_kv, cache_b1_kv, cache_b2_kv, cache_b3_kv, rel_bias, norm_gain, w_in,
           q_gain_a, k_gain_a, sinks_a, q_gain_b, k_gain_b, w_up_a, w_up_b, w_out):
    f = lambda a: np.ascontiguousarray(np.asarray(a, dtype=np.float32))
    x_prompt = f(x_prompt); x_sample = f(x_sample)
    cache_a_kv = f(cache_a_kv); cache_b1_kv = f(cache_b1_kv); cache_b2_kv = f(cache_b2_kv); cache_b3_kv = f(cache_b3_kv)
    rel_bias = f(rel_bias); norm_gain = f(norm_gain); w_in = f(w_in)
    q_gain_a = f(q_gain_a); k_gain_a = f(k_gain_a); sinks_a = f(sinks_a); q_gain_b = f(q_gain_b); k_gain_b = f(k_gain_b)
    w_up_a = f(w_up_a); w_up_b = f(w_up_b); w_out = f(w_out)

    nc = build_program()
    shared = shared_inputs(rel_bias, norm_gain, w_in, q_gain_a, k_gain_a, sinks_a, q_gain_b, k_gain_b, w_up_a, w_up_b, w_out)

    in_maps = []
    for c in range(8):
        b, h = c // 2, c % 2
        x_ext = np.zeros((4096, 1024), np.float32)
        if h == 1:
            x_ext[:] = x_prompt[b]
        else:
            x_ext[2048:] = x_prompt[b, :2048]
        xs = np.zeros((128, 1024), np.float32)
        xs[:64] = x_sample[16 * c:16 * c + 16].reshape(64, 1024)
        m = dict(shared)
        m["x_ext"] = x_ext
        m["x_s"] = xs
        m["hv"] = np.full((128, 1), float(h), np.float32)
        if WITH_CACHE:
          m["c_a"] = np.ascontiguousarray(cache_a_kv[0, 16 * c:16 * c + 16].reshape(16, 128, 256))
          m["c_b1"] = np.ascontiguousarray(cache_b1_kv[0, 16 * c:16 * c + 16].reshape(16, 128, 1024))
          m["c_b2"] = np.ascontiguousarray(cache_b2_kv[0, 16 * c:16 * c + 16].reshape(16, 512, 1024))
          m["c_b3"] = np.ascontiguousarray(cache_b3_kv[0, 16 * c:16 * c + 16].reshape(16, 2048, 1024))
        in_maps.append(m)
    res = run_bass_kernel_spmd(nc, in_maps, core_ids=list(range(8)))
    R = res.results
    y = np.zeros((4, 4096, 1024), np.float32)
    ys = np.zeros((128, 4, 1024), np.float32)
    for c in range(8):
        b, h = c // 2, c % 2
        y[b, h * 2048:(h + 1) * 2048] = R[c]["y"]
        ys[16 * c:16 * c + 16] = R[c]["ys"].reshape(16, 4, 1024)
    np_a = np.stack([R[2 * b + 1]["nkv_a"].reshape(128, 2, 2, 64) for b in range(4)])[None]
    np_b1 = np.stack([R[2 * b + 1]["nkv_b1"].reshape(128, 2, 8, 64) for b in range(4)])[None]
    np_b2 = np.stack([R[2 * b + 1]["nkv_b2"].reshape(512, 2, 8, 64) for b in range(4)])[None]
    np_b3 = np.stack([R[2 * b + 1]["nkv_b3"].reshape(2048, 2, 8, 64) for b in range(4)])[None]
    ns_a = np.concatenate([R[c]["ns_a"].reshape(16, 4, 2, 2, 64) for c in range(8)])[None]
    ns_b1 = np.concatenate([R[c]["ns_b1"].reshape(16, 4, 2, 8, 64) for c in range(8)])[None]
    ns_b2 = np.concatenate([R[c]["ns_b2"].reshape(16, 4, 2, 8, 64) for c in range(8)])[None]
    ns_b3 = np.concatenate([R[c]["ns_b3"].reshape(16, 4, 2, 8, 64) for c in range(8)])[None]
    return (y, ys, np_a, np_b1, np_b2, np_b3, ns_a, ns_b1, ns_b2, ns_b3)
```

```python
import math
import os
from contextlib import ExitStack

import numpy as np
import concourse.bass as bass
import concourse.mybir as mybir
from concourse.bass_utils import run_bass_kernel_spmd

F32 = mybir.dt.float32
BF16 = mybir.dt.bfloat16
AF = mybir.ActivationFunctionType
ALU = mybir.AluOpType
AX = mybir.AxisListType

SAME_ENGINE_SYNC = os.environ.get('KDBG_SES', '1') == '1'
LEVEL = int(os.environ.get('KDBG_LEVEL', '99'))
NCLS = int(os.environ.get('KDBG_NCLS', '99'))
NDS = 14
NKV = 2
WITH_CACHE = os.environ.get('KDBG_NOSAMPLE', '0') != '1'
SUB = int(os.environ.get('KDBG_SUB', '99'))
XB = int(os.environ.get('KDBG_X', '0'))
EPS = 1e-6
NOWN = 2048
NB = 16
C_QA, C_KA, C_VA, C_GA = 0, 512, 640, 768
C_QB, C_KB, C_VB, C_GB, C_MA, C_MB = 1280, 2816, 4352, 5888, 6400, 7424


class Sched:
    ENGS = ("pe", "act", "dve", "pool", "sp")

    def __init__(self, nc, n_dma_sems=6):
        self.nc = nc
        self.ops = {e: [] for e in self.ENGS}
        self.lastw = {}
        self.readers = {}
        self.n_dma_sems = n_dma_sems
        self.dma_count = {"sp": 0, "pool": 0, "act": 0}
        self.dma_last = {}

    def last_all(self):
        return [(e, len(self.ops[e]) - 1) for e in self.ENGS if self.ops[e]]

    def op(self, eng, fn, reads=(), writes=(), dma=False, extra=()):
        ops = self.ops[eng]
        idx = len(ops)
        deps = set(extra)
        for b in reads:
            w = self.lastw.get(b)
            if w is not None:
                deps.add(w)
        for b in writes:
            w = self.lastw.get(b)
            if w is not None:
                deps.add(w)
            for r in self.readers.get(b, ()):
                deps.add(r)
        cdeps = {}
        ddeps = set()
        for (e, i) in deps:
            o = self.ops[e][i]
            if o["dma"]:
                ddeps.add(o["sig"])
            else:
                if e == eng and not dma and (e == "pe" or not SAME_ENGINE_SYNC):
                    continue
                cdeps[e] = max(cdeps.get(e, -1), i)
        rec = {"fn": fn, "dma": dma, "cdeps": cdeps, "ddeps": ddeps, "sig": None, "signaled": False}
        if dma:
            n = self.dma_count[eng]
            self.dma_count[eng] = n + 1
            slot = n % self.n_dma_sems
            val = 16 * (n // self.n_dma_sems + 1)
            rec["sig"] = (eng, slot, val)
            if val > 16:
                ddeps.add((eng, slot, val - 16))
            self.dma_last[(eng, slot)] = val
        ops.append(rec)
        me = (eng, idx)
        for b in writes:
            self.lastw[b] = me
            self.readers[b] = []
        for b in reads:
            if b in writes:
                continue
            self.readers.setdefault(b, []).append(me)
        return me

    def emit(self, sems, dsems, block):
        for e in self.ENGS:
            for o in self.ops[e]:
                for (de, di) in o["cdeps"].items():
                    self.ops[de][di]["signaled"] = True
        for e in self.ENGS:
            c = 0
            for o in self.ops[e]:
                if o["dma"]:
                    continue
                if o["signaled"]:
                    c += 1
                    o["sig"] = c
        allops = self.ops
        dma_last = self.dma_last

        def run(eng_name, eng):
            waited = {}
            for o in allops[eng_name]:
                for (de, di) in sorted(o["cdeps"].items()):
                    v = allops[de][di]["sig"]
                    key = ("c", de)
                    if waited.get(key, 0) >= v:
                        continue
                    eng.wait_ge(sems[de], v)
                    waited[key] = v
                for (qe, slot, v) in sorted(o["ddeps"]):
                    key = ("d", qe, slot)
                    if waited.get(key, 0) >= v:
                        continue
                    eng.wait_ge(dsems[(qe, slot)], v)
                    waited[key] = v
                ins = o["fn"](eng)
                if o["dma"]:
                    qe, slot, v = o["sig"]
                    ins.then_inc(dsems[(qe, slot)], 16)
                elif o["signaled"]:
                    ins.then_inc(sems[eng_name], 1)
            if eng_name == "sp":
                for (qe, slot), v in sorted(dma_last.items()):
                    if waited.get(("d", qe, slot), 0) >= v:
                        continue
                    eng.wait_ge(dsems[(qe, slot)], v)

        @block.tensor
        def _(eng):
            run("pe", eng)

        @block.scalar
        def _(eng):
            run("act", eng)

        @block.vector
        def _(eng):
            run("dve", eng)

        @block.gpsimd
        def _(eng):
            run("pool", eng)

        @block.sync
        def _(eng):
            run("sp", eng)


def t5_bucket_np(dist):
    d = np.maximum(dist, 0)
    df = np.maximum(d, 1).astype(np.float32)
    large = 16 + (np.log(df / np.float32(16)) / np.float32(math.log(2048 / 16)) * np.float32(16)).astype(np.int32)
    large = np.minimum(large, 31)
    return np.where(d < 16, d, large)


def onehot_tables():
    oh = np.zeros((3, 128, 384), np.float32)
    for di, dil in enumerate((1, 4, 16)):
        delta = np.arange(128)
        b = t5_bucket_np(delta * dil)
        oh[di, b, delta + 127] = 1.0
    return oh


GROUPS = {
    "b3": dict(d=16, di=2, hb=24, cq=C_QB + 1024, ck=C_KB + 1024, cv=C_VB + 1024, nkv=8),
    "b2": dict(d=4, di=1, hb=16, cq=C_QB + 512, ck=C_KB + 512, cv=C_VB + 512, nkv=8),
    "b1": dict(d=1, di=0, hb=8, cq=C_QB, ck=C_KB, cv=C_VB, nkv=8),
    "a": dict(d=1, di=0, hb=0, cq=C_QA, ck=C_KA, cv=C_VA, nkv=2),
}


def build_program(with_sample=True):
    nc = bass.Bass("TRN2", target_bir_lowering=False)

    def din(name, shape):
        return nc.dram_tensor(name, shape, F32, kind="ExternalInput").ap()

    def dout(name, shape):
        return nc.dram_tensor(name, shape, F32, kind="ExternalOutput").ap()

    x_ext = din("x_ext", [4096, 1024])
    x_s = din("x_s", [128, 1024])
    hv_in = din("hv", [128, 1])
    mask_in = din("mask2", [128, 2])
    w_in = din("w_in", [1024, 8448])
    ng_in = din("ng", [128, 8])
    relb_in = din("relb", [128, 128])
    oh_in = din("oh", [3, 128, 384])
    gq_a_in = din("gq_a", [128, 64])
    gk_a_in = din("gk_a", [128, 64])
    gq_b_in = din("gq_b", [128, 192])
    gk_b_in = din("gk_b", [128, 192])
    sinks_in = din("sinks", [128, 8])
    wup_a_in = din("wup_a", [512, 1024])
    wup_b_in = din("wup_b", [512, 1024])
    wout_in = din("wout", [1024, 1024])
    if WITH_CACHE:
        c_a_in = din("c_a", [16, 128, 256])
        c_b1_in = din("c_b1", [16, 128, 1024])
        c_b2_in = din("c_b2", [16, 512, 1024])
        c_b3_in = din("c_b3", [16, 2048, 1024])

    y_out = dout("y", [2048, 1024])
    ys_out = dout("ys", [64, 1024])
    nkv_out = {"a": dout("nkv_a", [128, 2, 128]), "b1": dout("nkv_b1", [128, 2, 512]),
               "b2": dout("nkv_b2", [512, 2, 512]), "b3": dout("nkv_b3", [2048, 2, 512])}
    nskv_out = {"a": dout("ns_a", [64, 2, 128]), "b1": dout("ns_b1", [64, 2, 512]),
                "b2": dout("ns_b2", [64, 2, 512]), "b3": dout("ns_b3", [64, 2, 512])}

    DBG = os.environ.get('KDBG_DUMP', '0') == '1'
    if DBG:
        dbg_o = dout("dbgo", [4, 128, 520])
    NTS = NOWN + 128
    Oscr = {g: nc.dram_tensor("oscr_" + g, [NTS, 520], F32) for g in ("b1", "b2", "b3")}
    Oa_scr = nc.dram_tensor("oscr_a", [NTS, 512], F32)
    EFscr = nc.dram_tensor("efscr", [3, 32, 384], F32)

    S = Sched(nc, n_dma_sems=NDS)
    rr = {"ev": 0}

    with ExitStack() as es:
        def sb(name, shape, dt):
            return es.enter_context(nc.sbuf_tensor(name, shape, dt))

        def ps(name, shape, dt):
            return es.enter_context(nc.psum_tensor(name, shape, dt))

        sems = {e: es.enter_context(nc.semaphore("s_" + e)) for e in ("pe", "act", "dve", "pool")}
        dsems = {(q, i): es.enter_context(nc.semaphore(f"d_{q}{i}")) for q in ("sp", "pool") for i in range(NDS)}

        xT_own = sb("xT_own", [128, 8, NOWN], BF16)
        xT_s = sb("xT_s", [128, 8, 128], BF16)
        Xs_f = sb("Xs_f", [128, 1024], F32)
        ident = sb("ident", [128, 128], BF16)
        Jm = sb("Jm", [128, 128], BF16)
        identf = sb("identf", [128, 128], F32)
        NG = sb("NG", [128, 8], F32)
        HV = sb("HV", [128, 1], F32)
        MASK = sb("MASK", [128, 2], F32)
        GQA = sb("GQA", [128, 64], F32)
        GKA = sb("GKA", [128, 64], F32)
        GQB = sb("GQB", [128, 192], F32)
        GKB = sb("GKB", [128, 192], F32)
        SNK = sb("SNK", [128, 8], F32)
        PJ = ps("PJ", [128, 1536], F32)
        TR = ps("TR", [128, 1024], BF16)
        SPp = ps("SPp", [128, 1024], F32)
        OPp = ps("OPp", [128, 1024], F32)

        block = es.enter_context(nc.Block())

        S.op("sp", lambda e: e.dma_start(out=NG[:], in_=ng_in), writes=["NG"], dma=True)
        S.op("sp", lambda e: e.dma_start(out=HV[:], in_=hv_in), writes=["HV"], dma=True)
        S.op("sp", lambda e: e.dma_start(out=MASK[:], in_=mask_in), writes=["MASK"], dma=True)
        S.op("sp", lambda e: e.dma_start(out=GQA[:], in_=gq_a_in), writes=["GQA"], dma=True)
        S.op("sp", lambda e: e.dma_start(out=GKA[:], in_=gk_a_in), writes=["GKA"], dma=True)
        S.op("sp", lambda e: e.dma_start(out=GQB[:], in_=gq_b_in), writes=["GQB"], dma=True)
        S.op("sp", lambda e: e.dma_start(out=GKB[:], in_=gk_b_in), writes=["GKB"], dma=True)
        S.op("sp", lambda e: e.dma_start(out=SNK[:], in_=sinks_in), writes=["SNK"], dma=True)
        S.op("dve", lambda e: e.tensor_scalar(out=GQA[:], in0=GQA[:], scalar1=0.125, scalar2=None, op0=ALU.mult), reads=["GQA"], writes=["GQA"])
        S.op("dve", lambda e: e.tensor_scalar(out=GQB[:], in0=GQB[:], scalar1=0.125, scalar2=None, op0=ALU.mult), reads=["GQB"], writes=["GQB"])
        S.op("act", lambda e: e.activation(out=SNK[:], in_=SNK[:], func=AF.Exp), reads=["SNK"], writes=["SNK"])
        S.op("pool", lambda e: e.memset(identf[:], 1.0), writes=["identf"])
        S.op("pool", lambda e: e.affine_select(out=identf[:], in_=identf[:], pattern=[[-1, 128]], compare_op=ALU.is_equal, fill=0.0, base=0, channel_multiplier=1), reads=["identf"], writes=["identf"])
        S.op("dve", lambda e: e.tensor_copy(out=ident[:], in_=identf[:]), reads=["identf"], writes=["ident"])
        S.op("pool", lambda e: e.memset(identf[:], 1.0), reads=["identf"], writes=["identf"])
        S.op("pool", lambda e: e.affine_select(out=identf[:], in_=identf[:], pattern=[[1, 128]], compare_op=ALU.is_equal, fill=0.0, base=-127, channel_multiplier=1), reads=["identf"], writes=["identf"])
        S.op("dve", lambda e: e.tensor_copy(out=Jm[:], in_=identf[:]), reads=["identf"], writes=["Jm"])

        RB = sb("RB", [128, 128], F32)
        OHs = sb("OHs", [128, 384], F32)
        EFs = sb("EFs", [128, 384], F32)
        if True:
            S.op("sp", lambda e: e.dma_start(out=RB[:], in_=relb_in), writes=["RB"], dma=True)
            for di in range(3):
                S.op("sp", lambda e, di=di: e.dma_start(out=OHs[:], in_=oh_in[di]), writes=["OHs"], dma=True)
                S.op("pe", lambda e: e.matmul(SPp[:, 0:384], lhsT=RB[:], rhs=OHs[:], start=True, stop=True), reads=["RB", "OHs"], writes=[("SP", 0), ("SP", 1)])
                S.op("act", lambda e: e.activation(out=EFs[:], in_=SPp[:, 0:384], func=AF.Exp), reads=[("SP", 0), ("SP", 1)], writes=["EFs"])
                S.op("dve", lambda e: e.memset(EFs[:, 0:127], 0.0), reads=["EFs"], writes=["EFs"])
                S.op("dve", lambda e: e.memset(EFs[:, 255:384], 0.0), reads=["EFs"], writes=["EFs"])
                S.op("sp", lambda e, di=di: e.dma_start(out=EFscr.ap()[di], in_=EFs[0:32, :]), reads=["EFs"], writes=[("EFscr", di)], dma=True)

        with ExitStack() as esA:
            def sbA(name, shape, dt):
                return esA.enter_context(nc.sbuf_tensor(name, shape, dt))

            xT_halo = sbA("xT_halo", [128, 8, NOWN], BF16)
            Wsb = sbA("Wsb", [128, 8, 1536], BF16)
            Wst = [sbA("Wst%d" % i, [128, 1536], F32) for i in range(4)]
            Xf = [sbA("Xf%d" % i, [128, 1024], F32) for i in range(2)]
            SQ = sbA("SQ", [128, 1024], F32)
            Xb = sbA("Xb", [128, 1024], BF16)
            SS = sbA("SS", [128, 16], F32)
            SSp = sbA("SSp", [128, 2], F32)
            RSp = sbA("RSp", [128, 2], F32)
            RS = sbA("RS", [128, 16], F32)
            QN = sbA("QN", [128, 1024], F32)
            QNb = sbA("QNb", [128, 512], BF16)
            KNV = sbA("KNV", [128, 1024], F32)
            KN = KNV[:, 0:512]
            VF = KNV[:, 512:1024]
            KNb = sbA("KNb", [128, 512], BF16)
            QT = sbA("QT", [128, 4, 128], BF16)
            Kpad = [sbA("Kpad%d" % i, [128, 8, 128], BF16) for i in range(3)]
            V1 = [sbA("V1_%d" % i, [128, 8, 65], BF16) for i in range(3)]
            Et = sbA("Et", [128, 8, 256], BF16)
            Eh2 = [sbA("Eh0", [128, 256], F32), EFs[:, 0:256]]
            Ehb2 = [sbA("Ehb%d" % i, [128, 256], BF16) for i in range(2)]
            PEx = sbA("PEx", [128, 1024], BF16)
            Pt = [sbA("Pt%d" % i, [128, 1024], BF16) for i in range(2)]
            Ost = [sbA("Ost%d" % i, [128, 8, 65], F32) for i in range(2)]
            LL = sbA("LL", [128, 8], F32)
            OaT = [sbA("OaT%d" % i, [128, 512], F32) for i in range(2)]
            Ksb = [sbA("Ksb%d" % i, [128, 512], BF16) for i in range(2)]
            KTs = [sbA("KTs%d" % i, [128, 512], BF16) for i in range(2)]
            V1s = [sbA("V1s%d" % i, [128, 8, 65], BF16) for i in range(8)]
            Qbd = sbA("Qbd", [128, 4, 128, 2], BF16)
            PEs = sbA("PEs", [128, 32], F32)
            Zb = sbA("Zb", [128, 32, 192], BF16)
            S.op("pool", lambda e: e.memset(Zb[:], 0.0), writes=["Zb"])
            OsAcc = sbA("OsAcc", [128, 8, 65], F32)
            for i in range(8):
                S.op("pool", lambda e, i=i: e.memset(V1s[i][:], 1.0), writes=[("V1s", i)])

            for i in range(3):
                S.op("pool", lambda e, i=i: e.memset(Kpad[i][:], 0.0), writes=[("Kpad", i)])
                S.op("pool", lambda e, i=i: e.memset(V1[i][:], 1.0), writes=[("V1", i)])

            def prologue(src_ap, dst_tile, dst_key, col0, k, keep_f32=None):
                kk = k % 2
                xf = Xf[kk] if keep_f32 is None else keep_f32
                xkey = ("Xf", kk) if keep_f32 is None else "Xs_f"
                sq, sqk = (SQ, ["SQ"]) if kk == 0 else (QN, ["QN"])
                xb, xbk = (Xb, ["Xb"]) if kk == 0 else (PEx, [("PEx", 0), ("PEx", 1)])
                ssk, rsk = ("SSp", kk), ("RSp", kk)

                def head():
                    S.op("sp", lambda e: e.dma_start(out=xf[:], in_=src_ap), writes=[xkey], dma=True)
                    S.op("act", lambda e: e.activation(out=sq[:], in_=xf[:], func=AF.Square), reads=[xkey], writes=sqk)
                    S.op("dve", lambda e: e.reduce_sum(out=SSp[:, kk:kk + 1], in_=sq[:], axis=AX.X), reads=sqk, writes=[ssk])
                    S.op("act", lambda e: e.activation(out=RSp[:, kk:kk + 1], in_=SSp[:, kk:kk + 1], func=AF.Sqrt, bias=EPS, scale=1.0 / 1024), reads=[ssk], writes=[rsk])
                    S.op("dve", lambda e: e.reciprocal(out=RSp[:, kk:kk + 1], in_=RSp[:, kk:kk + 1]), reads=[rsk], writes=[rsk])
                    S.op("dve", lambda e: e.tensor_scalar(out=xb[:], in0=xf[:], scalar1=RSp[:, kk:kk + 1], scalar2=None, op0=ALU.mult), reads=[xkey, rsk], writes=xbk)

                def tail():
                    for c in range(8):
                        S.op("pe", lambda e, c=c: e.transpose(out=TR[:, c * 128:(c + 1) * 128], in_=xb[:, c * 128:(c + 1) * 128], identity=ident[:]), reads=xbk + ["ident"], writes=["TR"])
                    S.op("act", lambda e: e.activation(out=dst_tile[:, :, col0:col0 + 128], in_=TR[:].rearrange("p (c t) -> p c t", c=8), func=AF.Copy), reads=["TR"], writes=[dst_key])

                return head, tail

            if LEVEL >= 2:
                pro = []
                for t in range(NB):
                    pro.append(prologue(x_ext[t * 128:(t + 1) * 128, :], xT_halo, ("xTh", t), t * 128, len(pro)))
                for t in range(NB):
                    pro.append(prologue(x_ext[NOWN + t * 128:NOWN + (t + 1) * 128, :], xT_own, ("xTo", t), t * 128, len(pro)))
                pro.append(prologue(x_s, xT_s, "xTs", 0, len(pro), keep_f32=Xs_f))
                pro[0][0]()
                for i, (hd, tl) in enumerate(pro):
                    if i + 1 < len(pro):
                        pro[i + 1][0]()
                    tl()
            XTH_ALL = [("xTh", t) for t in range(NB)]
            XTO_ALL = [("xTo", t) for t in range(NB)]

            wcnt = {"n": 0}
            ETK = [("Et_h", h) for h in range(8)]
            WK = {}
            WKR = {}

            def load_w(dst, col_ranges, key):
                WK[key] = []
                off = 0
                for (c0, n) in col_ranges:
                    for c in range(8):
                        k = wcnt["n"] % 4
                        wcnt["n"] += 1
                        st = Wst[k]
                        S.op("sp", lambda e, c=c, c0=c0, n=n, st=st: e.dma_start(out=st[:, 0:n], in_=w_in[c * 128:(c + 1) * 128, c0:c0 + n]), writes=[("Wst", k)], dma=True)
                        if wcnt["n"] % 2:
                            S.op("act", lambda e, c=c, n=n, st=st, off=off: e.activation(out=dst[:, c, off:off + n], in_=st[:, 0:n], func=AF.Copy, scale=NG[:, c:c + 1]), reads=[("Wst", k), "NG"] + [("Wrd", key)], writes=[(key, c, off)])
                        else:
                            S.op("dve", lambda e, c=c, n=n, st=st, off=off: e.tensor_scalar(out=dst[:, c, off:off + n], in0=st[:, 0:n], scalar1=NG[:, c:c + 1], scalar2=None, op0=ALU.mult), reads=[("Wst", k), "NG"] + [("Wrd", key)], writes=[(key, c, off)])
                        WK[key].append((key, c, off))
                    off += n

            def inproj(lhs_fn, xkeys, ncols, wtile=None, wkey="W", wcol0=0):
                wt = Wsb if wtile is None else wtile
                ng_ = (ncols + 511) // 512
                for g in range(ng_):
                    n = min(512, ncols - g * 512)
                    for c in range(8):
                        S.op("pe", lambda e, g=g, c=c, n=n: e.matmul(PJ[:, g * 512:g * 512 + n], lhsT=lhs_fn(c), rhs=wt[:, c, wcol0 + g * 512:wcol0 + g * 512 + n], start=(c == 0), stop=(c == 7)),
                             reads=list(xkeys) + WK.get(wkey, [wkey]), writes=[("PJ", g), ("Wrd", wkey)])

            def build_E(gname):
                G = GROUPS[gname]
                for h in range(8):
                    kk = h % 2
                    eh, ehb = Eh2[kk], Ehb2[kk]
                    src = bass.AP(tensor=EFscr, offset=(G["di"] * 32 + G["hb"] + h) * 384, ap=[[1, 128], [128, 2], [1, 128]])
                    S.op("sp", lambda e, src=src, eh=eh: e.dma_start(out=eh[:].rearrange("p (a b) -> p a b", a=2), in_=src), reads=[("EFscr", G["di"])], writes=[("Eh", kk), "EFs"] if kk == 1 else [("Eh", kk)], dma=True)
                    S.op("act", lambda e, eh=eh, ehb=ehb: e.activation(out=ehb[:], in_=eh[:], func=AF.Copy), reads=[("Eh", kk)], writes=[("Ehb", kk)])
                    S.op("pe", lambda e, ehb=ehb, kk=kk: e.matmul(SPp[:, kk * 512:kk * 512 + 256], lhsT=Jm[:], rhs=ehb[:], start=True, stop=True), reads=["Jm", ("Ehb", kk)], writes=[("SP", kk)])
                    S.op("dve", lambda e, h=h, kk=kk: e.tensor_copy(out=Et[:, h, :], in_=SPp[:, kk * 512:kk * 512 + 256]), reads=[("SP", kk)], writes=[("Et_h", h)])

            def qkv_block(gname, lhs_fn, xkeys, par, want_q, kv_out=None, nrows=128):
                isA = gname == "a"
                nq = 512
                nk = 128 if isA else 512
                nkh = nk // 64
                if want_q:
                    ncols = nq + 2 * nk
                    ko, vo = nq, nq + nk
                    wc0 = 0
                else:
                    ncols = 2 * nk
                    ko, vo = 0, nk
                    wc0 = nq
                ngrp = (ncols + 511) // 512
                pjk = [("PJ", g) for g in range(ngrp)]
                nn = (nq + nk) if want_q else nk
                nh = nn // 64
                if isA:
                    gq, gk = GQA[:, :], GKA[:, :]
                else:
                    gi = {"b1": 0, "b2": 1, "b3": 2}[gname]
                    gq, gk = GQB[:, gi * 64:(gi + 1) * 64], GKB[:, gi * 64:(gi + 1) * 64]
                nkt = nk // 128

                def proj():
                    inproj(lhs_fn, xkeys, ncols, wcol0=wc0)

                def norm():
                    S.op("act", lambda e: e.activation(out=SQ[:, 0:nn], in_=PJ[:, 0:nn], func=AF.Square), reads=pjk, writes=["SQ"])
                    S.op("dve", lambda e: e.reduce_sum(out=SS[:, 0:nh], in_=SQ[:, 0:nn].rearrange("p (h d) -> p h d", d=64), axis=AX.X), reads=["SQ"], writes=["SS"])
                    S.op("act", lambda e: e.activation(out=RS[:, 0:nh], in_=SS[:, 0:nh], func=AF.Ln, bias=EPS, scale=1.0 / 64), reads=["SS"], writes=["RS"])
                    S.op("act", lambda e: e.activation(out=RS[:, 0:nh], in_=RS[:, 0:nh], func=AF.Exp, scale=-0.5), reads=["RS"], writes=["RS"])
                    S.op("dve", lambda e: e.tensor_tensor(out=QN[:, 0:nn].rearrange("p (h d) -> p h d", d=64), in0=PJ[:, 0:nn].rearrange("p (h d) -> p h d", d=64),
                                                           in1=RS[:, 0:nh].unsqueeze(2).broadcast_to([128, nh, 64]), op=ALU.mult), reads=pjk + ["RS"], writes=["QN"])
                    S.op("dve", lambda e: e.tensor_copy(out=V1[par][:, 0:nkh, 0:64], in_=PJ[:, vo:vo + nk].rearrange("p (h d) -> p h d", d=64)), reads=pjk, writes=[("V1", par)])
                    if kv_out is not None:
                        S.op("act", lambda e: e.activation(out=VF[:, 0:nk], in_=PJ[:, vo:vo + nk], func=AF.Copy), reads=pjk, writes=["VF"])
                        S.op("sp", lambda e: e.dma_start(out=kv_out[1], in_=VF[0:nrows, 0:nk]), reads=["VF"], dma=True)

                def rest():
                    if want_q:
                        S.op("dve", lambda e: e.tensor_tensor(out=QNb[:].rearrange("p (h d) -> p h d", d=64), in0=QN[:, 0:512].rearrange("p (h d) -> p h d", d=64),
                                                                in1=gq.unsqueeze(1).broadcast_to([128, 8, 64]), op=ALU.mult), reads=["QN", "GQA", "GQB"], writes=["QNb"])
                    S.op("dve", lambda e: e.tensor_tensor(out=KN[:, 0:nk].rearrange("p (h d) -> p h d", d=64), in0=QN[:, ko:ko + nk].rearrange("p (h d) -> p h d", d=64),
                                                            in1=gk.unsqueeze(1).broadcast_to([128, nkh, 64]), op=ALU.mult), reads=["QN", "GKA", "GKB"], writes=["KN"])
                    S.op("act", lambda e: e.activation(out=KNb[:, 0:nk], in_=KN[:, 0:nk], func=AF.Copy), reads=["KN"], writes=["KNb"])
                    if kv_out is not None:
                        S.op("sp", lambda e: e.dma_start(out=kv_out[0], in_=KN[0:nrows, 0:nk]), reads=["KN"], dma=True)
                    if want_q:
                        for t in range(4):
                            src = QNb[:, t * 128:(t + 1) * 128]
                            S.op("pe", lambda e, t=t, src=src: e.transpose(out=TR[:, t * 128:(t + 1) * 128], in_=src, identity=ident[:]), reads=["QNb", "ident"], writes=["TR"])
                    for t in range(nkt):
                        S.op("pe", lambda e, t=t: e.transpose(out=TR[:, (4 + t) * 128:(5 + t) * 128], in_=KNb[:, t * 128:(t + 1) * 128], identity=ident[:]), reads=["KNb", "ident"], writes=["TR"])
                    if want_q:
                        S.op("dve", lambda e: e.tensor_copy(out=QT[:].rearrange("p t k -> p (t k)"), in_=TR[:, 0:512]), reads=["TR"], writes=["QT"])
                    trk = TR[:, 512:512 + nkt * 128].rearrange("p (t k) -> p t k", t=nkt)
                    kp = Kpad[par][:, 0:2 * nkt, :].rearrange("p (t s) k -> p t s k", s=2)
                    S.op("dve", lambda e: e.tensor_scalar(out=kp[:, :, 0, :], in0=trk, scalar1=MASK[:, 0:1], scalar2=None, op0=ALU.mult), reads=["TR", "MASK"], writes=[("Kpad", par)])
                    S.op("act", lambda e: e.activation(out=kp[:, :, 1, :], in_=trk, func=AF.Copy, scale=MASK[:, 1:2]), reads=["TR", "MASK"], writes=[("Kpad", par)])

                return proj, norm, rest

            def attend(gname, cur, prv, first, out_rows):
                isA = gname == "a"

                def st(g):
                    bank = g % 2
                    for hl in range(2):
                        h = 2 * g + hl
                        kidx = (h // 4) if isA else h
                        qt = (h % 4) if isA else (h // 2)
                        for blk in range(2):
                            pp = cur if blk == 0 else prv
                            c0 = bank * 512 + hl * 256 + blk * 128
                            S.op("pe", lambda e, c0=c0, pp=pp, kidx=kidx, qt=qt: e.matmul(SPp[:, c0:c0 + 128], lhsT=Kpad[pp][:, kidx, :], rhs=QT[:, qt, :], start=True, stop=True),
                                 reads=[("Kpad", pp), "QT"], writes=[("SP", bank)])

                def ex(g):
                    bank = g % 2
                    S.op("act", lambda e: e.activation(out=PEx[:, bank * 512:(bank + 1) * 512], in_=SPp[:, bank * 512:(bank + 1) * 512], func=AF.Exp), reads=[("SP", bank)], writes=[("PEx", bank)])
                    pt = Pt[g // 2][:, (g % 2) * 512:(g % 2 + 1) * 512]
                    S.op("dve", lambda e: e.tensor_tensor(out=pt, in0=PEx[:, bank * 512:(bank + 1) * 512], in1=Et[:, 2 * g:2 * g + 2, :].rearrange("p h k -> p (h k)"), op=ALU.mult), reads=[("PEx", bank)] + ETK, writes=[("Pt", g)])
                    if first:
                        pv_ = pt.rearrange("p (h b q) -> p h b q", h=2, b=2)[:, :, 1, :]
                        S.op("dve", lambda e: e.tensor_scalar(out=pv_, in0=pv_, scalar1=HV[:, 0:1], scalar2=None, op0=ALU.mult), reads=[("Pt", g), "HV"], writes=[("Pt", g)])

                def pv(g):
                    pt = Pt[g // 2][:, (g % 2) * 512:(g % 2 + 1) * 512]
                    for hl in range(2):
                        h = 2 * g + hl
                        kv = (h // 4) if isA else h
                        for blk in range(2):
                            pp = cur if blk == 0 else prv
                            S.op("pe", lambda e, h=h, hl=hl, blk=blk, pp=pp, kv=kv: e.matmul(OPp[:, h * 128:h * 128 + 65], lhsT=pt[:, hl * 256 + blk * 128: hl * 256 + (blk + 1) * 128], rhs=V1[pp][:, kv, :], start=(blk == 0), stop=(blk == 1)),
                                 reads=[("Pt", g), ("V1", pp)], writes=[("OP", h // 4)])

                st(0)
                st(1)
                for g in range(4):
                    ex(g)
                    pv(g)
                    if g + 2 < 4:
                        st(g + 2)
                opk = [("OP", 0), ("OP", 1)]
                opv = OPp[:].rearrange("p (h c) -> p h c", h=8)
                k = rr["ev"] % 2
                rr["ev"] += 1
                if isA:
                    S.op("dve", lambda e: e.tensor_tensor(out=LL[:], in0=opv[:, :, 64], in1=SNK[:], op=ALU.add), reads=opk + ["SNK"], writes=["LL"])
                    S.op("dve", lambda e: e.reciprocal(out=LL[:], in_=LL[:]), reads=["LL"], writes=["LL"])
                    S.op("dve", lambda e: e.tensor_tensor(out=OaT[k][:].rearrange("p (h d) -> p h d", d=64), in0=opv[:, :, 0:64], in1=LL[:].unsqueeze(2).broadcast_to([128, 8, 64]), op=ALU.mult), reads=opk + ["LL"], writes=[("OaT", k)])
                    S.op("sp", lambda e: e.dma_start(out=out_rows, in_=OaT[k][:]), reads=[("OaT", k)], writes=["OSCR_a"], dma=True)
                else:
                    S.op("act", lambda e: e.activation(out=Ost[k][:], in_=opv[:, :, 0:65], func=AF.Copy), reads=opk, writes=[("Ost", k)])
                    S.op("sp", lambda e: e.dma_start(out=out_rows, in_=Ost[k][:].rearrange("p h c -> p (h c)")), reads=[("Ost", k)], writes=["OSCR_" + gname], dma=True)

            def sample_attn(gname):
                G = GROUPS[gname]
                d = G["d"]
                isA = gname == "a"
                nk = 128 if isA else 512
                kvw = 2 * nk
                nkt = nk // 128
                nkh = nk // 64
                cache = {"a": c_a_in, "b1": c_b1_in, "b2": c_b2_in, "b3": c_b3_in}[gname]
                nsk = nskv_out[gname]
                for stg in qkv_block(gname, lambda c: xT_s[:, c, 0:128], ["xTs"], 0, want_q=True, kv_out=(nsk[:, 0, :], nsk[:, 1, :]), nrows=64):
                    stg()
                if NEXTG[gname] is not None and LEVEL >= 99:
                    load_phase_w(NEXTG[gname])
                S.op("dve", lambda e: e.tensor_scalar(out=Qbd[:, :, :, 0], in0=QT[:], scalar1=MASK[:, 0:1], scalar2=None, op0=ALU.mult), reads=["QT", "MASK"], writes=["Qbd"])
                S.op("dve", lambda e: e.tensor_scalar(out=Qbd[:, :, :, 1], in0=QT[:], scalar1=MASK[:, 1:2], scalar2=None, op0=ALU.mult), reads=["QT", "MASK"], writes=["Qbd"])
                if isA:
                    es_v = Et[:].rearrange("p (k g) c -> p g k c", k=2)[:, :, :, 127]
                else:
                    es_v = Et[:, :, 127]
                S.op("pool", lambda e: e.memset(OsAcc[:], 0.0), writes=["OsAcc"])
                for n in range(16):
                    for t in range(4):
                        tok = 4 * n + t
                        tokc = t
                        slot = tok % 2
                        kslot = tok % (NKV + 4)
                        if kslot < NKV:
                            KVt = Xf[kslot]
                            kvk = ("Xf", kslot)
                        else:
                            KVt = Wst[kslot - NKV]
                            kvk = ("Wst", kslot - NKV)
                        dq = "sp"
                        if d == 1:
                            npc = 127 - t
                            r0_, r1_ = 4 * n, 4 * n + t + 1
                            S.op(dq, lambda e, n=n, t=t, npc=npc, KVt=KVt: e.dma_start(out=KVt[0:112, 0:kvw], in_=cache[n, t + 1:t + 113, :]), writes=[(kvk, 0)], dma=True)
                            S.op(dq, lambda e, n=n, t=t, npc=npc, KVt=KVt: e.dma_start(out=KVt[112:npc, 0:kvw], in_=cache[n, t + 113:128, :]), writes=[(kvk, 1)], dma=True)
                        else:
                            npc = 127
                            r0_, r1_ = tok, tok + 1
                            S.op(dq, lambda e, n=n, t=t, KVt=KVt: e.dma_start(out=KVt[0:112, 0:kvw], in_=cache[n, t + d:t + d + 111 * d + 1:d, :]), writes=[(kvk, 0)], dma=True)
                            S.op(dq, lambda e, n=n, t=t, KVt=KVt: e.dma_start(out=KVt[112:127, 0:kvw], in_=cache[n, t + 113 * d:t + 113 * d + 14 * d + 1:d, :]), writes=[(kvk, 1)], dma=True)
                        if isA:
                            src_new = KNV[r0_:r1_, :].rearrange("p (a b) -> p a b", a=2)[:, :, 0:128]
                            dst_new = KVt[npc:128, 0:256].rearrange("p (a b) -> p a b", a=2)
                        else:
                            src_new = KNV[r0_:r1_, :]
                            dst_new = KVt[npc:128, 0:1024]
                        S.op(dq, lambda e, src_new=src_new, dst_new=dst_new: e.dma_start(out=dst_new, in_=src_new), reads=["KN", "VF"], writes=[(kvk, 2)], dma=True)
                        vs = tok % 8
                        S.op("act", lambda e, KVt=KVt, slot=slot: e.activation(out=Ksb[slot][:, 0:nk], in_=KVt[:, 0:nk], func=AF.Copy), reads=[(kvk, 0), (kvk, 1), (kvk, 2)], writes=[("Ksb", slot)])
                        S.op("dve", lambda e, KVt=KVt, vs=vs: e.tensor_copy(out=V1s[vs][:, 0:nkh, 0:64], in_=KVt[:, nk:kvw].rearrange("p (h d) -> p h d", d=64)), reads=[(kvk, 0), (kvk, 1), (kvk, 2)], writes=[("V1s", vs)])
                        for tt in range(nkt):
                            S.op("pe", lambda e, tt=tt, slot=slot: e.transpose(out=TR[:, tt * 128:(tt + 1) * 128], in_=Ksb[slot][:, tt * 128:(tt + 1) * 128], identity=ident[:]), reads=[("Ksb", slot), "ident"], writes=["TR"])
                        S.op("dve", lambda e, slot=slot: e.tensor_copy(out=KTs[slot][:, 0:nk], in_=TR[:, 0:nk]), reads=["TR"], writes=[("KTs", slot)])
                        for tp in range(4):
                            kt = 0 if isA else tp
                            S.op("pe", lambda e, tp=tp, kt=kt, slot=slot, tok=tok, tokc=tokc: e.matmul(SPp[:, tokc * 8 + tp * 2: tokc * 8 + tp * 2 + 2], lhsT=KTs[slot][:, kt * 128:(kt + 1) * 128], rhs=Qbd[:, tp, tok, :], start=True, stop=True),
                                 reads=[("KTs", slot), "Qbd"], writes=[("SP", 0)])
                    S.op("act", lambda e: e.activation(out=PEs[:], in_=SPp[:, 0:32], func=AF.Exp), reads=[("SP", 0)], writes=["PEs"])
                    if isA:
                        S.op("dve", lambda e: e.tensor_tensor(out=Zb[:, :, 63].rearrange("p (t g k) -> p t g k", t=4, g=4), in0=PEs[:].rearrange("p (t g k) -> p t g k", t=4, g=4),
                                                               in1=es_v.unsqueeze(1).broadcast_to([128, 4, 4, 2]), op=ALU.mult), reads=["PEs"] + ETK, writes=["Zb"])
                    else:
                        S.op("dve", lambda e: e.tensor_tensor(out=Zb[:, :, 63].rearrange("p (t h) -> p t h", t=4), in0=PEs[:].rearrange("p (t h) -> p t h", t=4),
                                                               in1=es_v.unsqueeze(1).broadcast_to([128, 4, 8]), op=ALU.mult), reads=["PEs"] + ETK, writes=["Zb"])
                    for h in range(8):
                        if isA:
                            col = (h % 4) * 2 + h // 4
                            kv = h // 4
                        else:
                            col = h
                            kv = h
                        for t in range(4):
                            tok = 4 * n + t
                            vs = tok % 8
                            S.op("pe", lambda e, h=h, col=col, kv=kv, t=t, vs=vs, tok=tok: e.matmul(OPp[:, h * 128:h * 128 + 65], lhsT=Zb[:, t * 8 + col, 63 - tok:191 - tok], rhs=V1s[vs][:, kv, :], start=(t == 0), stop=(t == 3)),
                                 reads=["Zb", ("V1s", vs)], writes=[("OP", h // 4)])
                    S.op("dve", lambda e: e.tensor_tensor(out=OsAcc[:], in0=OsAcc[:], in1=OPp[:].rearrange("p (h c) -> p h c", h=8)[:, :, 0:65], op=ALU.add), reads=["OsAcc", ("OP", 0), ("OP", 1)], writes=["OsAcc"])
                k = rr["ev"] % 2
                rr["ev"] += 1
                if isA:
                    S.op("dve", lambda e: e.tensor_tensor(out=LL[:], in0=OsAcc[:, :, 64], in1=SNK[:], op=ALU.add), reads=["OsAcc", "SNK"], writes=["LL"])
                    S.op("dve", lambda e: e.reciprocal(out=LL[:], in_=LL[:]), reads=["LL"], writes=["LL"])
                    S.op("dve", lambda e: e.tensor_tensor(out=OaT[k][:].rearrange("p (h d) -> p h d", d=64), in0=OsAcc[:, :, 0:64], in1=LL[:].unsqueeze(2).broadcast_to([128, 8, 64]), op=ALU.mult), reads=["OsAcc", "LL"], writes=[("OaT", k)])
                    S.op("sp", lambda e: e.dma_start(out=Oa_scr.ap()[NOWN:NOWN + 128, :], in_=OaT[k][:]), reads=[("OaT", k)], writes=["OSCR_a"], dma=True)
                else:
                    S.op("sp", lambda e: e.dma_start(out=Oscr[gname].ap()[NOWN:NOWN + 128, :], in_=OsAcc[:].rearrange("p h c -> p (h c)")), reads=["OsAcc"], writes=["OSCR_" + gname], dma=True)

            WLOADED = {}
            NEXTG = {"b3": "b2", "b2": "b1", "b1": "a", "a": None}

            def load_phase_w(gname):
                G = GROUPS[gname]
                isA = gname == "a"
                nk = 128 if isA else 512
                if isA:
                    qr = [(C_QA + kvh * 256 + g * 64, 64) for g in range(4) for kvh in range(2)]
                else:
                    qr = [(G["cq"], 512)]
                load_w(Wsb, qr + [(G["ck"], nk), (G["cv"], nk)], "W")
                WLOADED[gname] = True

            def attn_phase(gname):
                G = GROUPS[gname]
                d = G["d"]
                isA = gname == "a"
                nk = 128 if isA else 512
                if not WLOADED.get(gname):
                    load_phase_w(gname)
                if SUB >= 2:
                    build_E(gname)
                oscr = Oa_scr.ap() if isA else Oscr[gname].ap()
                nkv = nkv_out[gname]
                ncb = NB // d
                win = {"a": 128, "b1": 128, "b2": 512, "b3": 2048}[gname]
                blocks = []
                cnt = 0
                for r in range(min(d, NCLS)):
                    hs = NOWN - 128 * d + r
                    lhs_h = (lambda c, hs=hs: xT_halo[:, c, hs:hs + 127 * d + 1:d]) if d > 1 else (lambda c, hs=hs: xT_halo[:, c, hs:hs + 128])
                    blocks.append(dict(stages=qkv_block(gname, lhs_h, XTH_ALL, cnt % 3, want_q=False), att=None))
                    cnt += 1
                    for cb in range(ncb):
                        st = r + d * 128 * cb
                        kv_out = None
                        lo = NOWN - win
                        if st >= lo:
                            r0 = st - lo
                            if d > 1:
                                kv_out = (nkv[r0:r0 + 127 * d + 1:d, 0, :], nkv[r0:r0 + 127 * d + 1:d, 1, :])
                            else:
                                kv_out = (nkv[r0:r0 + 128, 0, :], nkv[r0:r0 + 128, 1, :])
                        lhs_o = (lambda c, st=st: xT_own[:, c, st:st + 127 * d + 1:d]) if d > 1 else (lambda c, st=st: xT_own[:, c, st:st + 128])
                        rows = oscr[st:st + 127 * d + 1:d, :] if d > 1 else oscr[st:st + 128, :]
                        blocks.append(dict(stages=qkv_block(gname, lhs_o, XTO_ALL, cnt % 3, want_q=True, kv_out=kv_out),
                                           att=(cnt % 3, (cnt - 1) % 3, cb == 0, rows)))
                        cnt += 1
                if blocks:
                    blocks[0]["stages"][0]()
                for i, b in enumerate(blocks):
                    b["stages"][1]()
                    if i + 1 < len(blocks):
                        blocks[i + 1]["stages"][0]()
                    b["stages"][2]()
                    if b["att"] is not None:
                        cur, prv, first, rows = b["att"]
                        attend(gname, cur, prv, first, rows)
                if WITH_CACHE and SUB >= 6:
                    sample_attn(gname)

            for gi_, gname in enumerate(("b3", "b2", "b1", "a")):
                if LEVEL >= 3 + gi_:
                    attn_phase(gname)

        with ExitStack() as esF:
            def sbF(name, shape, dt):
                return esF.enter_context(nc.sbuf_tensor(name, shape, dt))

            Wg = sbF("Wg", [128, 8, 3072], BF16)
            WuA = sbF("WuA", [128, 4, 1024], BF16)
            WuB = sbF("WuB", [128, 4, 1024], BF16)
            Wo = sbF("Wo", [128, 8, 1024], BF16)
            Wst = [sbF("WstF%d" % i, [128, 1536], F32) for i in range(4)]
            O1 = sbF("O1", [128, 520], F32)
            O2 = sbF("O2", [128, 520], F32)
            O3 = sbF("O3", [128, 520], F32)
            OA = sbF("OA", [128, 512], F32)
            XR = sbF("XR", [128, 1024], F32)
            SG = sbF("SG", [128, 1024], F32)
            SM2 = [sbF("SM%d" % i, [128, 2048], F32) for i in range(2)]
            OB = sbF("OB", [128, 512], F32)
            SGs = sbF("SGs", [128, 1024], F32)
            LB = sbF("LB", [128, 8], F32)
            U = sbF("U", [128, 1024], BF16)
            UT = sbF("UT", [128, 8, 128], BF16)
            M1 = sbF("M1", [128, 1024], F32)
            M2 = sbF("M2", [128, 1024], F32)
            MG = sbF("MG", [128, 1024], BF16)
            MT = sbF("MT", [128, 8, 128], BF16)
            Y = sbF("Y", [128, 1024], F32)

            wc = {"n": 0}
            BAR = S.last_all()

            WKF = {}

            def load_wF(dst_fn, src_ap_fn, ncols_list, key, scale_gain):
                WKF[key] = []
                for (c0, n, off) in ncols_list:
                    for c in range(dst_fn("nchunk")):
                        k = wc["n"] % 4
                        wc["n"] += 1
                        st = Wst[k]
                        S.op("sp", lambda e, c=c, c0=c0, n=n, st=st: e.dma_start(out=st[:, 0:n], in_=src_ap_fn(c, c0, n)), writes=[("WstF", k)], dma=True, extra=BAR)
                        useact = bool(wc["n"] % 2)
                        WKF[key].append((key, c, off))
                        if scale_gain:
                            if useact:
                                S.op("act", lambda e, c=c, n=n, st=st, off=off: e.activation(out=dst_fn(c)[:, off:off + n], in_=st[:, 0:n], func=AF.Copy, scale=NG[:, c:c + 1]), reads=[("WstF", k), "NG"], writes=[(key, c, off)])
                            else:
                                S.op("dve", lambda e, c=c, n=n, st=st, off=off: e.tensor_scalar(out=dst_fn(c)[:, off:off + n], in0=st[:, 0:n], scalar1=NG[:, c:c + 1], scalar2=None, op0=ALU.mult), reads=[("WstF", k), "NG"], writes=[(key, c, off)])
                        else:
                            if useact:
                                S.op("act", lambda e, c=c, n=n, st=st, off=off: e.activation(out=dst_fn(c)[:, off:off + n], in_=st[:, 0:n], func=AF.Copy), reads=[("WstF", k)], writes=[(key, c, off)])
                            else:
                                S.op("dve", lambda e, c=c, n=n, st=st, off=off: e.tensor_copy(out=dst_fn(c)[:, off:off + n], in_=st[:, 0:n]), reads=[("WstF", k)], writes=[(key, c, off)])

            load_wF(lambda c: 8 if c == "nchunk" else Wg[:, c, :], lambda c, c0, n: w_in[c * 128:(c + 1) * 128, c0:c0 + n],
                    [(C_GA, 512, 0), (C_GB, 512, 512), (C_MA, 1024, 1024), (C_MB, 1024, 2048)], "Wg", True)
            load_wF(lambda c: 4 if c == "nchunk" else WuA[:, c, :], lambda c, c0, n: wup_a_in[c * 128:(c + 1) * 128, c0:c0 + n], [(0, 1024, 0)], "WuA", False)
            load_wF(lambda c: 4 if c == "nchunk" else WuB[:, c, :], lambda c, c0, n: wup_b_in[c * 128:(c + 1) * 128, c0:c0 + n], [(0, 1024, 0)], "WuB", False)
            load_wF(lambda c: 8 if c == "nchunk" else Wo[:, c, :], lambda c, c0, n: wout_in[c * 128:(c + 1) * 128, c0:c0 + n], [(0, 1024, 0)], "Wo", False)

            def final_block(lhs_fn, xkeys, row0, x_src, y_dst, k, nrows=128, xres=None):
                SMk = SM2[k]
                smk = ("SM", k)
                pjk = [("PJ", g) for g in range(3)]

                def loads():
                    S.op("sp", lambda e: e.dma_start(out=O1[:], in_=Oscr["b1"].ap()[row0:row0 + 128, :]), reads=["OSCR_b1"], writes=["O1"], dma=True)
                    S.op("sp", lambda e: e.dma_start(out=O2[:], in_=Oscr["b2"].ap()[row0:row0 + 128, :]), reads=["OSCR_b2"], writes=["O2"], dma=True)
                    S.op("sp", lambda e: e.dma_start(out=O3[:], in_=Oscr["b3"].ap()[row0:row0 + 128, :]), reads=["OSCR_b3"], writes=["O3"], dma=True)
                    S.op("sp", lambda e: e.dma_start(out=OA[:], in_=Oa_scr.ap()[row0:row0 + 128, :]), reads=["OSCR_a"], writes=["OA"], dma=True)

                def gates(rnd):
                    for g in range(3):
                        for c in range(8):
                            S.op("pe", lambda e, g=g, c=c: e.matmul(PJ[:, g * 512:(g + 1) * 512], lhsT=lhs_fn(c), rhs=Wg[:, c, rnd * 1536 + g * 512: rnd * 1536 + (g + 1) * 512], start=(c == 0), stop=(c == 7)),
                                 reads=list(xkeys) + WKF["Wg"], writes=[("PJ", g)])
                    if rnd == 0:
                        S.op("act", lambda e: e.activation(out=SGs[:], in_=PJ[:, 0:1024], func=AF.Sigmoid), reads=pjk, writes=["SGs"])
                        S.op("act", lambda e: e.activation(out=SMk[:, 0:512], in_=PJ[:, 1024:1536], func=AF.Sigmoid), reads=pjk, writes=[smk])
                        S.op("dve", lambda e: e.tensor_tensor(out=SG[:], in0=PJ[:, 0:1024], in1=SGs[:], op=ALU.mult), reads=pjk + ["SGs"], writes=["SG"])
                    else:
                        S.op("act", lambda e: e.activation(out=SMk[:, 512:2048], in_=PJ[:, 0:1536], func=AF.Sigmoid), reads=pjk, writes=[smk])

                def gating():
                    S.op("dve", lambda e: e.tensor_tensor(out=O1[:], in0=O1[:], in1=O2[:], op=ALU.add), reads=["O1", "O2"], writes=["O1"])
                    S.op("dve", lambda e: e.tensor_tensor(out=O1[:], in0=O1[:], in1=O3[:], op=ALU.add), reads=["O1", "O3"], writes=["O1"])
                    o1v = O1[:].rearrange("p (h c) -> p h c", c=65)
                    S.op("dve", lambda e: e.reciprocal(out=LB[:], in_=o1v[:, :, 64]), reads=["O1"], writes=["LB"])
                    S.op("dve", lambda e: e.tensor_tensor(out=OB[:].rearrange("p (h d) -> p h d", d=64), in0=o1v[:, :, 0:64], in1=LB[:].unsqueeze(2).broadcast_to([128, 8, 64]), op=ALU.mult), reads=["O1", "LB"], writes=["OB"])
                    S.op("dve", lambda e: e.tensor_tensor(out=U[:, 0:512], in0=OA[:], in1=SG[:, 0:512], op=ALU.mult), reads=["OA", "SG"], writes=["U"])
                    S.op("dve", lambda e: e.tensor_tensor(out=U[:, 512:1024], in0=OB[:], in1=SG[:, 512:1024], op=ALU.mult), reads=["OB", "SG"], writes=["U"])

                def up():
                    for c in range(8):
                        S.op("pe", lambda e, c=c: e.transpose(out=TR[:, c * 128:(c + 1) * 128], in_=U[:, c * 128:(c + 1) * 128], identity=ident[:]), reads=["U", "ident"], writes=["TR"])
                    S.op("act", lambda e: e.activation(out=UT[:], in_=TR[:].rearrange("p (c t) -> p c t", c=8), func=AF.Copy), reads=["TR"], writes=["UT"])
                    for n in range(2):
                        for c in range(4):
                            S.op("pe", lambda e, n=n, c=c: e.matmul(SPp[:, n * 512:(n + 1) * 512], lhsT=UT[:, c, :], rhs=WuA[:, c, n * 512:(n + 1) * 512], start=(c == 0), stop=(c == 3)), reads=["UT"] + WKF["WuA"], writes=[("SP", n)])
                    for n in range(2):
                        for c in range(4):
                            S.op("pe", lambda e, n=n, c=c: e.matmul(OPp[:, n * 512:(n + 1) * 512], lhsT=UT[:, 4 + c, :], rhs=WuB[:, c, n * 512:(n + 1) * 512], start=(c == 0), stop=(c == 3)), reads=["UT"] + WKF["WuB"], writes=[("OP", n)])

                def merge():
                    S.op("dve", lambda e: e.tensor_tensor(out=M1[:], in0=SPp[:], in1=SMk[:, 0:1024], op=ALU.mult), reads=[("SP", 0), ("SP", 1), smk], writes=["M1"])
                    S.op("dve", lambda e: e.tensor_tensor(out=M2[:], in0=OPp[:], in1=SMk[:, 1024:2048], op=ALU.mult), reads=[("OP", 0), ("OP", 1), smk], writes=["M2"])
                    S.op("dve", lambda e: e.tensor_tensor(out=MG[:], in0=M1[:], in1=M2[:], op=ALU.add), reads=["M1", "M2"], writes=["MG"])

                def outp():
                    for c in range(8):
                        S.op("pe", lambda e, c=c: e.transpose(out=TR[:, c * 128:(c + 1) * 128], in_=MG[:, c * 128:(c + 1) * 128], identity=ident[:]), reads=["MG", "ident"], writes=["TR"])
                    S.op("act", lambda e: e.activation(out=MT[:], in_=TR[:].rearrange("p (c t) -> p c t", c=8), func=AF.Copy), reads=["TR"], writes=["MT"])
                    for n in range(2):
                        for c in range(8):
                            S.op("pe", lambda e, n=n, c=c: e.matmul(SPp[:, n * 512:(n + 1) * 512], lhsT=MT[:, c, :], rhs=Wo[:, c, n * 512:(n + 1) * 512], start=(c == 0), stop=(c == 7)), reads=["MT"] + WKF["Wo"], writes=[("SP", n)])

                def resid():
                    if xres is None:
                        S.op("sp", lambda e: e.dma_start(out=XR[:], in_=x_src), writes=["XR"], dma=True)
                        xr, xrk = XR, "XR"
                    else:
                        xr, xrk = xres, "Xs_f"
                    S.op("dve", lambda e: e.tensor_tensor(out=Y[:], in0=SPp[:], in1=xr[:], op=ALU.add), reads=[("SP", 0), ("SP", 1), xrk], writes=["Y"])
                    S.op("sp", lambda e: e.dma_start(out=y_dst, in_=Y[0:nrows, :]), reads=["Y"], dma=True)

                return dict(loads=loads, gates=gates, gating=gating, up=up, merge=merge, outp=outp, resid=resid)

            fblocks = []
            for t in range(NB if LEVEL >= 7 else 0):
                fblocks.append(final_block(lambda c, t=t: xT_own[:, c, t * 128:(t + 1) * 128], [("xTo", t)], t * 128, x_ext[NOWN + t * 128:NOWN + (t + 1) * 128, :], y_out[t * 128:(t + 1) * 128, :], len(fblocks) % 2))
            if WITH_CACHE and LEVEL >= 8:
                fblocks.append(final_block(lambda c: xT_s[:, c, 0:128], ["xTs"], NOWN, None, ys_out, len(fblocks) % 2, nrows=64, xres=Xs_f))
            if fblocks:
                fblocks[0]["loads"]()
                fblocks[0]["gates"](0)
                fblocks[0]["gates"](1)
            for i, fb in enumerate(fblocks):
                nxt = fblocks[i + 1] if i + 1 < len(fblocks) else None
                fb["gating"]()
                if i > 0:
                    fblocks[i - 1]["resid"]()
                if nxt is not None:
                    nxt["loads"]()
                    nxt["gates"](0)
                fb["up"]()
                if nxt is not None:
                    nxt["gates"](1)
                fb["merge"]()
                fb["outp"]()
            if fblocks:
                fblocks[-1]["resid"]()

        S.emit(sems, dsems, block)
    return nc


def shared_inputs(rel_bias, norm_gain, w_in, q_gain_a, k_gain_a, sinks_a, q_gain_b, k_gain_b, w_up_a, w_up_b, w_out):
    relb = np.zeros((128, 128), np.float32)
    relb[:32, :32] = rel_bias
    oh = onehot_tables()
    mask2 = np.zeros((128, 2), np.float32)
    mask2[:64, 0] = 1.0
    mask2[64:, 1] = 1.0
    return {
        "mask2": mask2,
        "w_in": w_in[0], "ng": np.ascontiguousarray(norm_gain[0].reshape(8, 128).T), "relb": relb, "oh": oh,
        "gq_a": np.ascontiguousarray(np.broadcast_to(q_gain_a[0][None, :], (128, 64))),
        "gk_a": np.ascontiguousarray(np.broadcast_to(k_gain_a[0][None, :], (128, 64))),
        "gq_b": np.ascontiguousarray(np.broadcast_to(q_gain_b[0].reshape(1, 192), (128, 192))),
        "gk_b": np.ascontiguousarray(np.broadcast_to(k_gain_b[0].reshape(1, 192), (128, 192))),
        "sinks": np.ascontiguousarray(np.broadcast_to(sinks_a[0][None, :], (128, 8))),
        "wup_a": w_up_a[0], "wup_b": w_up_b[0], "wout": w_out[0],
    }


_CACHE = {}


def kernel(x_prompt, x_sample, cache_a_kv, cache_b1_kv, cache_b2_kv, cache_b3_kv, rel_bias, norm_gain, w_in,
           q_gain_a, k_gain_a, sinks_a, q_gain_b, k_gain_b, w_up_a, w_up_b, w_out):
    f = lambda a: np.ascontiguousarray(np.asarray(a, dtype=np.float32))
    x_prompt = f(x_prompt); x_sample = f(x_sample)
    cache_a_kv = f(cache_a_kv); cache_b1_kv = f(cache_b1_kv); cache_b2_kv = f(cache_b2_kv); cache_b3_kv = f(cache_b3_kv)
    rel_bias = f(rel_bias); norm_gain = f(norm_gain); w_in = f(w_in)
    q_gain_a = f(q_gain_a); k_gain_a = f(k_gain_a); sinks_a = f(sinks_a); q_gain_b = f(q_gain_b); k_gain_b = f(k_gain_b)
    w_up_a = f(w_up_a); w_up_b = f(w_up_b); w_out = f(w_out)

    nc = build_program()
    shared = shared_inputs(rel_bias, norm_gain, w_in, q_gain_a, k_gain_a, sinks_a, q_gain_b, k_gain_b, w_up_a, w_up_b, w_out)

    in_maps = []
    for c in range(8):
        b, h = c // 2, c % 2
        x_ext = np.zeros((4096, 1024), np.float32)
        if h == 1:
            x_ext[:] = x_prompt[b]
        else:
            x_ext[2048:] = x_prompt[b, :2048]
        xs = np.zeros((128, 1024), np.float32)
        xs[:64] = x_sample[16 * c:16 * c + 16].reshape(64, 1024)
        m = dict(shared)
        m["x_ext"] = x_ext
        m["x_s"] = xs
        m["hv"] = np.full((128, 1), float(h), np.float32)
        if WITH_CACHE:
          m["c_a"] = np.ascontiguousarray(cache_a_kv[0, 16 * c:16 * c + 16].reshape(16, 128, 256))
          m["c_b1"] = np.ascontiguousarray(cache_b1_kv[0, 16 * c:16 * c + 16].reshape(16, 128, 1024))
          m["c_b2"] = np.ascontiguousarray(cache_b2_kv[0, 16 * c:16 * c + 16].reshape(16, 512, 1024))
          m["c_b3"] = np.ascontiguousarray(cache_b3_kv[0, 16 * c:16 * c + 16].reshape(16, 2048, 1024))
        in_maps.append(m)
    res = run_bass_kernel_spmd(nc, in_maps, core_ids=list(range(8)))
    R = res.results
    y = np.zeros((4, 4096, 1024), np.float32)
    ys = np.zeros((128, 4, 1024), np.float32)
    for c in range(8):
        b, h = c // 2, c % 2
        y[b, h * 2048:(h + 1) * 2048] = R[c]["y"]
        ys[16 * c:16 * c + 16] = R[c]["ys"].reshape(16, 4, 1024)
    np_a = np.stack([R[2 * b + 1]["nkv_a"].reshape(128, 2, 2, 64) for b in range(4)])[None]
    np_b1 = np.stack([R[2 * b + 1]["nkv_b1"].reshape(128, 2, 8, 64) for b in range(4)])[None]
    np_b2 = np.stack([R[2 * b + 1]["nkv_b2"].reshape(512, 2, 8, 64) for b in range(4)])[None]
    np_b3 = np.stack([R[2 * b + 1]["nkv_b3"].reshape(2048, 2, 8, 64) for b in range(4)])[None]
    ns_a = np.concatenate([R[c]["ns_a"].reshape(16, 4, 2, 2, 64) for c in range(8)])[None]
    ns_b1 = np.concatenate([R[c]["ns_b1"].reshape(16, 4, 2, 8, 64) for c in range(8)])[None]
    ns_b2 = np.concatenate([R[c]["ns_b2"].reshape(16, 4, 2, 8, 64) for c in range(8)])[None]
    ns_b3 = np.concatenate([R[c]["ns_b3"].reshape(16, 4, 2, 8, 64) for c in range(8)])[None]
    return (y, ys, np_a, np_b1, np_b2, np_b3, ns_a, ns_b1, ns_b2, ns_b3)
```

```python
import math
import os
from contextlib import ExitStack

import numpy as np
import concourse.bass as bass
import concourse.mybir as mybir
from concourse.bass_utils import run_bass_kernel_spmd

F32 = mybir.dt.float32
BF16 = mybir.dt.bfloat16
AF = mybir.ActivationFunctionType
ALU = mybir.AluOpType
AX = mybir.AxisListType

SAME_ENGINE_SYNC = os.environ.get('KDBG_SES', '1') == '1'
LEVEL = int(os.environ.get('KDBG_LEVEL', '99'))
NCLS = int(os.environ.get('KDBG_NCLS', '99'))
NDS = 14
NKV = 2
WITH_CACHE = os.environ.get('KDBG_NOSAMPLE', '0') != '1'
SUB = int(os.environ.get('KDBG_SUB', '99'))
XB = int(os.environ.get('KDBG_X', '0'))
EPS = 1e-6
NOWN = 2048
NB = 16
C_QA, C_KA, C_VA, C_GA = 0, 512, 640, 768
C_QB, C_KB, C_VB, C_GB, C_MA, C_MB = 1280, 2816, 4352, 5888, 6400, 7424


class Sched:
    ENGS = ("pe", "act", "dve", "pool", "sp")

    def __init__(self, nc, n_dma_sems=6):
        self.nc = nc
        self.ops = {e: [] for e in self.ENGS}
        self.lastw = {}
        self.readers = {}
        self.n_dma_sems = n_dma_sems
        self.dma_count = {"sp": 0, "pool": 0, "act": 0}
        self.dma_last = {}

    def last_all(self):
        return [(e, len(self.ops[e]) - 1) for e in self.ENGS if self.ops[e]]

    def op(self, eng, fn, reads=(), writes=(), dma=False, extra=()):
        ops = self.ops[eng]
        idx = len(ops)
        deps = set(extra)
        for b in reads:
            w = self.lastw.get(b)
            if w is not None:
                deps.add(w)
        for b in writes:
            w = self.lastw.get(b)
            if w is not None:
                deps.add(w)
            for r in self.readers.get(b, ()):
                deps.add(r)
        cdeps = {}
        ddeps = set()
        for (e, i) in deps:
            o = self.ops[e][i]
            if o["dma"]:
                ddeps.add(o["sig"])
            else:
                if e == eng and not dma and (e == "pe" or not SAME_ENGINE_SYNC):
                    continue
                cdeps[e] = max(cdeps.get(e, -1), i)
        rec = {"fn": fn, "dma": dma, "cdeps": cdeps, "ddeps": ddeps, "sig": None, "signaled": False}
        if dma:
            n = self.dma_count[eng]
            self.dma_count[eng] = n + 1
            slot = n % self.n_dma_sems
            val = 16 * (n // self.n_dma_sems + 1)
            rec["sig"] = (eng, slot, val)
            if val > 16:
                ddeps.add((eng, slot, val - 16))
            self.dma_last[(eng, slot)] = val
        ops.append(rec)
        me = (eng, idx)
        for b in writes:
            self.lastw[b] = me
            self.readers[b] = []
        for b in reads:
            if b in writes:
                continue
            self.readers.setdefault(b, []).append(me)
        return me

    def emit(self, sems, dsems, block):
        for e in self.ENGS:
            for o in self.ops[e]:
                for (de, di) in o["cdeps"].items():
                    self.ops[de][di]["signaled"] = True
        for e in self.ENGS:
            c = 0
            for o in self.ops[e]:
                if o["dma"]:
                    continue
                if o["signaled"]:
                    c += 1
                    o["sig"] = c
        allops = self.ops
        dma_last = self.dma_last

        def run(eng_name, eng):
            waited = {}
            for o in allops[eng_name]:
                for (de, di) in sorted(o["cdeps"].items()):
                    v = allops[de][di]["sig"]
                    key = ("c", de)
                    if waited.get(key, 0) >= v:
                        continue
                    eng.wait_ge(sems[de], v)
                    waited[key] = v
                for (qe, slot, v) in sorted(o["ddeps"]):
                    key = ("d", qe, slot)
                    if waited.get(key, 0) >= v:
                        continue
                    eng.wait_ge(dsems[(qe, slot)], v)
                    waited[key] = v
                ins = o["fn"](eng)
                if o["dma"]:
                    qe, slot, v = o["sig"]
                    ins.then_inc(dsems[(qe, slot)], 16)
                elif o["signaled"]:
                    ins.then_inc(sems[eng_name], 1)
            if eng_name == "sp":
                for (qe, slot), v in sorted(dma_last.items()):
                    if waited.get(("d", qe, slot), 0) >= v:
                        continue
                    eng.wait_ge(dsems[(qe, slot)], v)

        @block.tensor
        def _(eng):
            run("pe", eng)

        @block.scalar
        def _(eng):
            run("act", eng)

        @block.vector
        def _(eng):
            run("dve", eng)

        @block.gpsimd
        def _(eng):
            run("pool", eng)

        @block.sync
        def _(eng):
            run("sp", eng)


def t5_bucket_np(dist):
    d = np.maximum(dist, 0)
    df = np.maximum(d, 1).astype(np.float32)
    large = 16 + (np.log(df / np.float32(16)) / np.float32(math.log(2048 / 16)) * np.float32(16)).astype(np.int32)
    large = np.minimum(large, 31)
    return np.where(d < 16, d, large)


def onehot_tables():
    oh = np.zeros((3, 128, 384), np.float32)
    for di, dil in enumerate((1, 4, 16)):
        delta = np.arange(128)
        b = t5_bucket_np(delta * dil)
        oh[di, b, delta + 127] = 1.0
    return oh


GROUPS = {
    "b3": dict(d=16, di=2, hb=24, cq=C_QB + 1024, ck=C_KB + 1024, cv=C_VB + 1024, nkv=8),
    "b2": dict(d=4, di=1, hb=16, cq=C_QB + 512, ck=C_KB + 512, cv=C_VB + 512, nkv=8),
    "b1": dict(d=1, di=0, hb=8, cq=C_QB, ck=C_KB, cv=C_VB, nkv=8),
    "a": dict(d=1, di=0, hb=0, cq=C_QA, ck=C_KA, cv=C_VA, nkv=2),
}


def build_program(with_sample=True):
    nc = bass.Bass("TRN2", target_bir_lowering=False)

    def din(name, shape):
        return nc.dram_tensor(name, shape, F32, kind="ExternalInput").ap()

    def dout(name, shape):
        return nc.dram_tensor(name, shape, F32, kind="ExternalOutput").ap()

    x_ext = din("x_ext", [4096, 1024])
    x_s = din("x_s", [128, 1024])
    hv_in = din("hv", [128, 1])
    mask_in = din("mask2", [128, 2])
    w_in = din("w_in", [1024, 8448])
    ng_in = din("ng", [128, 8])
    relb_in = din("relb", [128, 128])
    oh_in = din("oh", [3, 128, 384])
    gq_a_in = din("gq_a", [128, 64])
    gk_a_in = din("gk_a", [128, 64])
    gq_b_in = din("gq_b", [128, 192])
    gk_b_in = din("gk_b", [128, 192])
    sinks_in = din("sinks", [128, 8])
    wup_a_in = din("wup_a", [512, 1024])
    wup_b_in = din("wup_b", [512, 1024])
    wout_in = din("wout", [1024, 1024])
    if WITH_CACHE:
        c_a_in = din("c_a", [16, 128, 256])
        c_b1_in = din("c_b1", [16, 128, 1024])
        c_b2_in = din("c_b2", [16, 512, 1024])
        c_b3_in = din("c_b3", [16, 2048, 1024])

    y_out = dout("y", [2048, 1024])
    ys_out = dout("ys", [64, 1024])
    nkv_out = {"a": dout("nkv_a", [128, 2, 128]), "b1": dout("nkv_b1", [128, 2, 512]),
               "b2": dout("nkv_b2", [512, 2, 512]), "b3": dout("nkv_b3", [2048, 2, 512])}
    nskv_out = {"a": dout("ns_a", [64, 2, 128]), "b1": dout("ns_b1", [64, 2, 512]),
                "b2": dout("ns_b2", [64, 2, 512]), "b3": dout("ns_b3", [64, 2, 512])}

    DBG = os.environ.get('KDBG_DUMP', '0') == '1'
    if DBG:
        dbg_o = dout("dbgo", [4, 128, 520])
    NTS = NOWN + 128
    Oscr = {g: nc.dram_tensor("oscr_" + g, [NTS, 520], F32) for g in ("b1", "b2", "b3")}
    Oa_scr = nc.dram_tensor("oscr_a", [NTS, 512], F32)
    EFscr = nc.dram_tensor("efscr", [3, 32, 384], F32)

    S = Sched(nc, n_dma_sems=NDS)
    rr = {"ev": 0}

    with ExitStack() as es:
        def sb(name, shape, dt):
            return es.enter_context(nc.sbuf_tensor(name, shape, dt))

        def ps(name, shape, dt):
            return es.enter_context(nc.psum_tensor(name, shape, dt))

        sems = {e: es.enter_context(nc.semaphore("s_" + e)) for e in ("pe", "act", "dve", "pool")}
        dsems = {(q, i): es.enter_context(nc.semaphore(f"d_{q}{i}")) for q in ("sp", "pool") for i in range(NDS)}

        xT_own = sb("xT_own", [128, 8, NOWN], BF16)
        xT_s = sb("xT_s", [128, 8, 128], BF16)
        Xs_f = sb("Xs_f", [128, 1024], F32)
        ident = sb("ident", [128, 128], BF16)
        Jm = sb("Jm", [128, 128], BF16)
        identf = sb("identf", [128, 128], F32)
        NG = sb("NG", [128, 8], F32)
        HV = sb("HV", [128, 1], F32)
        MASK = sb("MASK", [128, 2], F32)
        GQA = sb("GQA", [128, 64], F32)
        GKA = sb("GKA", [128, 64], F32)
        GQB = sb("GQB", [128, 192], F32)
        GKB = sb("GKB", [128, 192], F32)
        SNK = sb("SNK", [128, 8], F32)
        PJ = ps("PJ", [128, 1536], F32)
        TR = ps("TR", [128, 1024], BF16)
        SPp = ps("SPp", [128, 1024], F32)
        OPp = ps("OPp", [128, 1024], F32)

        block = es.enter_context(nc.Block())

        S.op("sp", lambda e: e.dma_start(out=NG[:], in_=ng_in), writes=["NG"], dma=True)
        S.op("sp", lambda e: e.dma_start(out=HV[:], in_=hv_in), writes=["HV"], dma=True)
        S.op("sp", lambda e: e.dma_start(out=MASK[:], in_=mask_in), writes=["MASK"], dma=True)
        S.op("sp", lambda e: e.dma_start(out=GQA[:], in_=gq_a_in), writes=["GQA"], dma=True)
        S.op("sp", lambda e: e.dma_start(out=GKA[:], in_=gk_a_in), writes=["GKA"], dma=True)
        S.op("sp", lambda e: e.dma_start(out=GQB[:], in_=gq_b_in), writes=["GQB"], dma=True)
        S.op("sp", lambda e: e.dma_start(out=GKB[:], in_=gk_b_in), writes=["GKB"], dma=True)
        S.op("sp", lambda e: e.dma_start(out=SNK[:], in_=sinks_in), writes=["SNK"], dma=True)
        S.op("dve", lambda e: e.tensor_scalar(out=GQA[:], in0=GQA[:], scalar1=0.125, scalar2=None, op0=ALU.mult), reads=["GQA"], writes=["GQA"])
        S.op("dve", lambda e: e.tensor_scalar(out=GQB[:], in0=GQB[:], scalar1=0.125, scalar2=None, op0=ALU.mult), reads=["GQB"], writes=["GQB"])
        S.op("act", lambda e: e.activation(out=SNK[:], in_=SNK[:], func=AF.Exp), reads=["SNK"], writes=["SNK"])
        S.op("pool", lambda e: e.memset(identf[:], 1.0), writes=["identf"])
        S.op("pool", lambda e: e.affine_select(out=identf[:], in_=identf[:], pattern=[[-1, 128]], compare_op=ALU.is_equal, fill=0.0, base=0, channel_multiplier=1), reads=["identf"], writes=["identf"])
        S.op("dve", lambda e: e.tensor_copy(out=ident[:], in_=identf[:]), reads=["identf"], writes=["ident"])
        S.op("pool", lambda e: e.memset(identf[:], 1.0), reads=["identf"], writes=["identf"])
        S.op("pool", lambda e: e.affine_select(out=identf[:], in_=identf[:], pattern=[[1, 128]], compare_op=ALU.is_equal, fill=0.0, base=-127, channel_multiplier=1), reads=["identf"], writes=["identf"])
        S.op("dve", lambda e: e.tensor_copy(out=Jm[:], in_=identf[:]), reads=["identf"], writes=["Jm"])

        RB = sb("RB", [128, 128], F32)
        OHs = sb("OHs", [128, 384], F32)
        EFs = sb("EFs", [128, 384], F32)
        if True:
            S.op("sp", lambda e: e.dma_start(out=RB[:], in_=relb_in), writes=["RB"], dma=True)
            for di in range(3):
                S.op("sp", lambda e, di=di: e.dma_start(out=OHs[:], in_=oh_in[di]), writes=["OHs"], dma=True)
                S.op("pe", lambda e: e.matmul(SPp[:, 0:384], lhsT=RB[:], rhs=OHs[:], start=True, stop=True), reads=["RB", "OHs"], writes=[("SP", 0), ("SP", 1)])
                S.op("act", lambda e: e.activation(out=EFs[:], in_=SPp[:, 0:384], func=AF.Exp), reads=[("SP", 0), ("SP", 1)], writes=["EFs"])
                S.op("dve", lambda e: e.memset(EFs[:, 0:127], 0.0), reads=["EFs"], writes=["EFs"])
                S.op("dve", lambda e: e.memset(EFs[:, 255:384], 0.0), reads=["EFs"], writes=["EFs"])
                S.op("sp", lambda e, di=di: e.dma_start(out=EFscr.ap()[di], in_=EFs[0:32, :]), reads=["EFs"], writes=[("EFscr", di)], dma=True)

        with ExitStack() as esA:
            def sbA(name, shape, dt):
                return esA.enter_context(nc.sbuf_tensor(name, shape, dt))

            xT_halo = sbA("xT_halo", [128, 8, NOWN], BF16)
            Wsb = sbA("Wsb", [128, 8, 1536], BF16)
            Wst = [sbA("Wst%d" % i, [128, 1536], F32) for i in range(4)]
            Xf = [sbA("Xf%d" % i, [128, 1024], F32) for i in range(2)]
            SQ = sbA("SQ", [128, 1024], F32)
            Xb = sbA("Xb", [128, 1024], BF16)
            SS = sbA("SS", [128, 16], F32)
            SSp = sbA("SSp", [128, 2], F32)
            RSp = sbA("RSp", [128, 2], F32)
            RS = sbA("RS", [128, 16], F32)
            QN = sbA("QN", [128, 1024], F32)
            QNb = sbA("QNb", [128, 512], BF16)
            KNV = sbA("KNV", [128, 1024], F32)
            KN = KNV[:, 0:512]
            VF = KNV[:, 512:1024]
            KNb = sbA("KNb", [128, 512], BF16)
            QT = sbA("QT", [128, 4, 128], BF16)
            Kpad = [sbA("Kpad%d" % i, [128, 8, 128], BF16) for i in range(3)]
            V1 = [sbA("V1_%d" % i, [128, 8, 65], BF16) for i in range(3)]
            Et = sbA("Et", [128, 8, 256], BF16)
            Eh2 = [sbA("Eh0", [128, 256], F32), EFs[:, 0:256]]
            Ehb2 = [sbA("Ehb%d" % i, [128, 256], BF16) for i in range(2)]
            PEx = sbA("PEx", [128, 1024], BF16)
            Pt = [sbA("Pt%d" % i, [128, 1024], BF16) for i in range(2)]
            Ost = [sbA("Ost%d" % i, [128, 8, 65], F32) for i in range(2)]
            LL = sbA("LL", [128, 8], F32)
            OaT = [sbA("OaT%d" % i, [128, 512], F32) for i in range(2)]
            Ksb = [sbA("Ksb%d" % i, [128, 512], BF16) for i in range(2)]
            KTs = [sbA("KTs%d" % i, [128, 512], BF16) for i in range(2)]
            V1s = [sbA("V1s%d" % i, [128, 8, 65], BF16) for i in range(8)]
            Qbd = sbA("Qbd", [128, 4, 128, 2], BF16)
            PEs = sbA("PEs", [128, 32], F32)
            Zb = sbA("Zb", [128, 32, 192], BF16)
            S.op("pool", lambda e: e.memset(Zb[:], 0.0), writes=["Zb"])
            OsAcc = sbA("OsAcc", [128, 8, 65], F32)
            for i in range(8):
                S.op("pool", lambda e, i=i: e.memset(V1s[i][:], 1.0), writes=[("V1s", i)])

            for i in range(3):
                S.op("pool", lambda e, i=i: e.memset(Kpad[i][:], 0.0), writes=[("Kpad", i)])
                S.op("pool", lambda e, i=i: e.memset(V1[i][:], 1.0), writes=[("V1", i)])

            def prologue(src_ap, dst_tile, dst_key, col0, k, keep_f32=None):
                kk = k % 2
                xf = Xf[kk] if keep_f32 is None else keep_f32
                xkey = ("Xf", kk) if keep_f32 is None else "Xs_f"
                sq, sqk = (SQ, ["SQ"]) if kk == 0 else (QN, ["QN"])
                xb, xbk = (Xb, ["Xb"]) if kk == 0 else (PEx, [("PEx", 0), ("PEx", 1)])
                ssk, rsk = ("SSp", kk), ("RSp", kk)

                def head():
                    S.op("sp", lambda e: e.dma_start(out=xf[:], in_=src_ap), writes=[xkey], dma=True)
                    S.op("act", lambda e: e.activation(out=sq[:], in_=xf[:], func=AF.Square), reads=[xkey], writes=sqk)
                    S.op("dve", lambda e: e.reduce_sum(out=SSp[:, kk:kk + 1], in_=sq[:], axis=AX.X), reads=sqk, writes=[ssk])
                    S.op("act", lambda e: e.activation(out=RSp[:, kk:kk + 1], in_=SSp[:, kk:kk + 1], func=AF.Sqrt, bias=EPS, scale=1.0 / 1024), reads=[ssk], writes=[rsk])
                    S.op("dve", lambda e: e.reciprocal(out=RSp[:, kk:kk + 1], in_=RSp[:, kk:kk + 1]), reads=[rsk], writes=[rsk])
                    S.op("dve", lambda e: e.tensor_scalar(out=xb[:], in0=xf[:], scalar1=RSp[:, kk:kk + 1], scalar2=None, op0=ALU.mult), reads=[xkey, rsk], writes=xbk)

                def tail():
                    for c in range(8):
                        S.op("pe", lambda e, c=c: e.transpose(out=TR[:, c * 128:(c + 1) * 128], in_=xb[:, c * 128:(c + 1) * 128], identity=ident[:]), reads=xbk + ["ident"], writes=["TR"])
                    S.op("act", lambda e: e.activation(out=dst_tile[:, :, col0:col0 + 128], in_=TR[:].rearrange("p (c t) -> p c t", c=8), func=AF.Copy), reads=["TR"], writes=[dst_key])

                return head, tail

            if LEVEL >= 2:
                pro = []
                for t in range(NB):
                    pro.append(prologue(x_ext[t * 128:(t + 1) * 128, :], xT_halo, ("xTh", t), t * 128, len(pro)))
                for t in range(NB):
                    pro.append(prologue(x_ext[NOWN + t * 128:NOWN + (t + 1) * 128, :], xT_own, ("xTo", t), t * 128, len(pro)))
                pro.append(prologue(x_s, xT_s, "xTs", 0, len(pro), keep_f32=Xs_f))
                pro[0][0]()
                for i, (hd, tl) in enumerate(pro):
                    if i + 1 < len(pro):
                        pro[i + 1][0]()
                    tl()
            XTH_ALL = [("xTh", t) for t in range(NB)]
            XTO_ALL = [("xTo", t) for t in range(NB)]

            wcnt = {"n": 0}
            ETK = [("Et_h", h) for h in range(8)]
            WK = {}
            WKR = {}

            def load_w(dst, col_ranges, key):
                WK[key] = []
                off = 0
                for (c0, n) in col_ranges:
                    for c in range(8):
                        k = wcnt["n"] % 4
                        wcnt["n"] += 1
                        st = Wst[k]
                        S.op("sp", lambda e, c=c, c0=c0, n=n, st=st: e.dma_start(out=st[:, 0:n], in_=w_in[c * 128:(c + 1) * 128, c0:c0 + n]), writes=[("Wst", k)], dma=True)
                        if wcnt["n"] % 2:
                            S.op("act", lambda e, c=c, n=n, st=st, off=off: e.activation(out=dst[:, c, off:off + n], in_=st[:, 0:n], func=AF.Copy, scale=NG[:, c:c + 1]), reads=[("Wst", k), "NG"] + [("Wrd", key)], writes=[(key, c, off)])
                        else:
                            S.op("dve", lambda e, c=c, n=n, st=st, off=off: e.tensor_scalar(out=dst[:, c, off:off + n], in0=st[:, 0:n], scalar1=NG[:, c:c + 1], scalar2=None, op0=ALU.mult), reads=[("Wst", k), "NG"] + [("Wrd", key)], writes=[(key, c, off)])
                        WK[key].append((key, c, off))
                    off += n

            def inproj(lhs_fn, xkeys, ncols, wtile=None, wkey="W", wcol0=0):
                wt = Wsb if wtile is None else wtile
                ng_ = (ncols + 511) // 512
                for g in range(ng_):
                    n = min(512, ncols - g * 512)
                    for c in range(8):
                        S.op("pe", lambda e, g=g, c=c, n=n: e.matmul(PJ[:, g * 512:g * 512 + n], lhsT=lhs_fn(c), rhs=wt[:, c, wcol0 + g * 512:wcol0 + g * 512 + n], start=(c == 0), stop=(c == 7)),
                             reads=list(xkeys) + WK.get(wkey, [wkey]), writes=[("PJ", g), ("Wrd", wkey)])

            def build_E(gname):
                G = GROUPS[gname]
                for h in range(8):
                    kk = h % 2
                    eh, ehb = Eh2[kk], Ehb2[kk]
                    src = bass.AP(tensor=EFscr, offset=(G["di"] * 32 + G["hb"] + h) * 384, ap=[[1, 128], [128, 2], [1, 128]])
                    S.op("sp", lambda e, src=src, eh=eh: e.dma_start(out=eh[:].rearrange("p (a b) -> p a b", a=2), in_=src), reads=[("EFscr", G["di"])], writes=[("Eh", kk), "EFs"] if kk == 1 else [("Eh", kk)], dma=True)
                    S.op("act", lambda e, eh=eh, ehb=ehb: e.activation(out=ehb[:], in_=eh[:], func=AF.Copy), reads=[("Eh", kk)], writes=[("Ehb", kk)])
                    S.op("pe", lambda e, ehb=ehb, kk=kk: e.matmul(SPp[:, kk * 512:kk * 512 + 256], lhsT=Jm[:], rhs=ehb[:], start=True, stop=True), reads=["Jm", ("Ehb", kk)], writes=[("SP", kk)])
                    S.op("dve", lambda e, h=h, kk=kk: e.tensor_copy(out=Et[:, h, :], in_=SPp[:, kk * 512:kk * 512 + 256]), reads=[("SP", kk)], writes=[("Et_h", h)])

            def qkv_block(gname, lhs_fn, xkeys, par, want_q, kv_out=None, nrows=128):
                isA = gname == "a"
                nq = 512
                nk = 128 if isA else 512
                nkh = nk // 64
                if want_q:
                    ncols = nq + 2 * nk
                    ko, vo = nq, nq + nk
                    wc0 = 0
                else:
                    ncols = 2 * nk
                    ko, vo = 0, nk
                    wc0 = nq
                ngrp = (ncols + 511) // 512
                pjk = [("PJ", g) for g in range(ngrp)]
                nn = (nq + nk) if want_q else nk
                nh = nn // 64
                if isA:
                    gq, gk = GQA[:, :], GKA[:, :]
                else:
                    gi = {"b1": 0, "b2": 1, "b3": 2}[gname]
                    gq, gk = GQB[:, gi * 64:(gi + 1) * 64], GKB[:, gi * 64:(gi + 1) * 64]
                nkt = nk // 128

                def proj():
                    inproj(lhs_fn, xkeys, ncols, wcol0=wc0)

                def norm():
                    S.op("act", lambda e: e.activation(out=SQ[:, 0:nn], in_=PJ[:, 0:nn], func=AF.Square), reads=pjk, writes=["SQ"])
                    S.op("dve", lambda e: e.reduce_sum(out=SS[:, 0:nh], in_=SQ[:, 0:nn].rearrange("p (h d) -> p h d", d=64), axis=AX.X), reads=["SQ"], writes=["SS"])
                    S.op("act", lambda e: e.activation(out=RS[:, 0:nh], in_=SS[:, 0:nh], func=AF.Ln, bias=EPS, scale=1.0 / 64), reads=["SS"], writes=["RS"])
                    S.op("act", lambda e: e.activation(out=RS[:, 0:nh], in_=RS[:, 0:nh], func=AF.Exp, scale=-0.5), reads=["RS"], writes=["RS"])
                    S.op("dve", lambda e: e.tensor_tensor(out=QN[:, 0:nn].rearrange("p (h d) -> p h d", d=64), in0=PJ[:, 0:nn].rearrange("p (h d) -> p h d", d=64),
                                                           in1=RS[:, 0:nh].unsqueeze(2).broadcast_to([128, nh, 64]), op=ALU.mult), reads=pjk + ["RS"], writes=["QN"])
                    S.op("dve", lambda e: e.tensor_copy(out=V1[par][:, 0:nkh, 0:64], in_=PJ[:, vo:vo + nk].rearrange("p (h d) -> p h d", d=64)), reads=pjk, writes=[("V1", par)])
                    if kv_out is not None:
                        S.op("act", lambda e: e.activation(out=VF[:, 0:nk], in_=PJ[:, vo:vo + nk], func=AF.Copy), reads=pjk, writes=["VF"])
                        S.op("sp", lambda e: e.dma_start(out=kv_out[1], in_=VF[0:nrows, 0:nk]), reads=["VF"], dma=True)

                def rest():
                    if want_q:
                        S.op("dve", lambda e: e.tensor_tensor(out=QNb[:].rearrange("p (h d) -> p h d", d=64), in0=QN[:, 0:512].rearrange("p (h d) -> p h d", d=64),
                                                                in1=gq.unsqueeze(1).broadcast_to([128, 8, 64]), op=ALU.mult), reads=["QN", "GQA", "GQB"], writes=["QNb"])
                    S.op("dve", lambda e: e.tensor_tensor(out=KN[:, 0:nk].rearrange("p (h d) -> p h d", d=64), in0=QN[:, ko:ko + nk].rearrange("p (h d) -> p h d", d=64),
                                                            in1=gk.unsqueeze(1).broadcast_to([128, nkh, 64]), op=ALU.mult), reads=["QN", "GKA", "GKB"], writes=["KN"])
                    S.op("act", lambda e: e.activation(out=KNb[:, 0:nk], in_=KN[:, 0:nk], func=AF.Copy), reads=["KN"], writes=["KNb"])
                    if kv_out is not None:
                        S.op("sp", lambda e: e.dma_start(out=kv_out[0], in_=KN[0:nrows, 0:nk]), reads=["KN"], dma=True)
                    if want_q:
                        for t in range(4):
                            src = QNb[:, t * 128:(t + 1) * 128]
                            S.op("pe", lambda e, t=t, src=src: e.transpose(out=TR[:, t * 128:(t + 1) * 128], in_=src, identity=ident[:]), reads=["QNb", "ident"], writes=["TR"])
                    for t in range(nkt):
                        S.op("pe", lambda e, t=t: e.transpose(out=TR[:, (4 + t) * 128:(5 + t) * 128], in_=KNb[:, t * 128:(t + 1) * 128], identity=ident[:]), reads=["KNb", "ident"], writes=["TR"])
                    if want_q:
                        S.op("dve", lambda e: e.tensor_copy(out=QT[:].rearrange("p t k -> p (t k)"), in_=TR[:, 0:512]), reads=["TR"], writes=["QT"])
                    trk = TR[:, 512:512 + nkt * 128].rearrange("p (t k) -> p t k", t=nkt)
                    kp = Kpad[par][:, 0:2 * nkt, :].rearrange("p (t s) k -> p t s k", s=2)
                    S.op("dve", lambda e: e.tensor_scalar(out=kp[:, :, 0, :], in0=trk, scalar1=MASK[:, 0:1], scalar2=None, op0=ALU.mult), reads=["TR", "MASK"], writes=[("Kpad", par)])
                    S.op("act", lambda e: e.activation(out=kp[:, :, 1, :], in_=trk, func=AF.Copy, scale=MASK[:, 1:2]), reads=["TR", "MASK"], writes=[("Kpad", par)])

                return proj, norm, rest

            def attend(gname, cur, prv, first, out_rows):
                isA = gname == "a"

                def st(g):
                    bank = g % 2
                    for hl in range(2):
                        h = 2 * g + hl
                        kidx = (h // 4) if isA else h
                        qt = (h % 4) if isA else (h // 2)
                        for blk in range(2):
                            pp = cur if blk == 0 else prv
                            c0 = bank * 512 + hl * 256 + blk * 128
                            S.op("pe", lambda e, c0=c0, pp=pp, kidx=kidx, qt=qt: e.matmul(SPp[:, c0:c0 + 128], lhsT=Kpad[pp][:, kidx, :], rhs=QT[:, qt, :], start=True, stop=True),
                                 reads=[("Kpad", pp), "QT"], writes=[("SP", bank)])

                def ex(g):
                    bank = g % 2
                    S.op("act", lambda e: e.activation(out=PEx[:, bank * 512:(bank + 1) * 512], in_=SPp[:, bank * 512:(bank + 1) * 512], func=AF.Exp), reads=[("SP", bank)], writes=[("PEx", bank)])
                    pt = Pt[g // 2][:, (g % 2) * 512:(g % 2 + 1) * 512]
                    S.op("dve", lambda e: e.tensor_tensor(out=pt, in0=PEx[:, bank * 512:(bank + 1) * 512], in1=Et[:, 2 * g:2 * g + 2, :].rearrange("p h k -> p (h k)"), op=ALU.mult), reads=[("PEx", bank)] + ETK, writes=[("Pt", g)])
                    if first:
                        pv_ = pt.rearrange("p (h b q) -> p h b q", h=2, b=2)[:, :, 1, :]
                        S.op("dve", lambda e: e.tensor_scalar(out=pv_, in0=pv_, scalar1=HV[:, 0:1], scalar2=None, op0=ALU.mult), reads=[("Pt", g), "HV"], writes=[("Pt", g)])

                def pv(g):
                    pt = Pt[g // 2][:, (g % 2) * 512:(g % 2 + 1) * 512]
                    for hl in range(2):
                        h = 2 * g + hl
                        kv = (h // 4) if isA else h
                        for blk in range(2):
                            pp = cur if blk == 0 else prv
                            S.op("pe", lambda e, h=h, hl=hl, blk=blk, pp=pp, kv=kv: e.matmul(OPp[:, h * 128:h * 128 + 65], lhsT=pt[:, hl * 256 + blk * 128: hl * 256 + (blk + 1) * 128], rhs=V1[pp][:, kv, :], start=(blk == 0), stop=(blk == 1)),
                                 reads=[("Pt", g), ("V1", pp)], writes=[("OP", h // 4)])

                st(0)
                st(1)
                for g in range(4):
                    ex(g)
                    pv(g)
                    if g + 2 < 4:
                        st(g + 2)
                opk = [("OP", 0), ("OP", 1)]
                opv = OPp[:].rearrange("p (h c) -> p h c", h=8)
                k = rr["ev"] % 2
                rr["ev"] += 1
                if isA:
                    S.op("dve", lambda e: e.tensor_tensor(out=LL[:], in0=opv[:, :, 64], in1=SNK[:], op=ALU.add), reads=opk + ["SNK"], writes=["LL"])
                    S.op("dve", lambda e: e.reciprocal(out=LL[:], in_=LL[:]), reads=["LL"], writes=["LL"])
                    S.op("dve", lambda e: e.tensor_tensor(out=OaT[k][:].rearrange("p (h d) -> p h d", d=64), in0=opv[:, :, 0:64], in1=LL[:].unsqueeze(2).broadcast_to([128, 8, 64]), op=ALU.mult), reads=opk + ["LL"], writes=[("OaT", k)])
                    S.op("sp", lambda e: e.dma_start(out=out_rows, in_=OaT[k][:]), reads=[("OaT", k)], writes=["OSCR_a"], dma=True)
                else:
                    S.op("act", lambda e: e.activation(out=Ost[k][:], in_=opv[:, :, 0:65], func=AF.Copy), reads=opk, writes=[("Ost", k)])
                    S.op("sp", lambda e: e.dma_start(out=out_rows, in_=Ost[k][:].rearrange("p h c -> p (h c)")), reads=[("Ost", k)], writes=["OSCR_" + gname], dma=True)

            def sample_attn(gname):
                G = GROUPS[gname]
                d = G["d"]
                isA = gname == "a"
                nk = 128 if isA else 512
                kvw = 2 * nk
                nkt = nk // 128
                nkh = nk // 64
                cache = {"a": c_a_in, "b1": c_b1_in, "b2": c_b2_in, "b3": c_b3_in}[gname]
                nsk = nskv_out[gname]
                for stg in qkv_block(gname, lambda c: xT_s[:, c, 0:128], ["xTs"], 0, want_q=True, kv_out=(nsk[:, 0, :], nsk[:, 1, :]), nrows=64):
                    stg()
                if NEXTG[gname] is not None and LEVEL >= 99:
                    load_phase_w(NEXTG[gname])
                if isA:
                    qbdA = Qbd[:].rearrange("p a b c -> p (a b c)").rearrange("p (t g s) -> p g t s", g=4, s=2)
                    S.op("dve", lambda e: e.tensor_scalar(out=qbdA[:, :, :, 0], in0=QT[:], scalar1=MASK[:, 0:1], scalar2=None, op0=ALU.mult), reads=["QT", "MASK"], writes=["Qbd"])
                    S.op("dve", lambda e: e.tensor_scalar(out=qbdA[:, :, :, 1], in0=QT[:], scalar1=MASK[:, 1:2], scalar2=None, op0=ALU.mult), reads=["QT", "MASK"], writes=["Qbd"])
                else:
                    S.op("dve", lambda e: e.tensor_scalar(out=Qbd[:, :, :, 0], in0=QT[:], scalar1=MASK[:, 0:1], scalar2=None, op0=ALU.mult), reads=["QT", "MASK"], writes=["Qbd"])
                    S.op("dve", lambda e: e.tensor_scalar(out=Qbd[:, :, :, 1], in0=QT[:], scalar1=MASK[:, 1:2], scalar2=None, op0=ALU.mult), reads=["QT", "MASK"], writes=["Qbd"])
                qflat = Qbd[:].rearrange("p a b c -> p (a b c)")
                if isA:
                    es_v = Et[:].rearrange("p (k g) c -> p g k c", k=2)[:, :, :, 127]
                else:
                    es_v = Et[:, :, 127]
                S.op("pool", lambda e: e.memset(OsAcc[:], 0.0), writes=["OsAcc"])
                for n in range(16):
                    for t in range(4):
                        tok = 4 * n + t
                        tokc = t
                        slot = tok % 2
                        kslot = tok % (NKV + 4)
                        if kslot < NKV:
                            KVt = Xf[kslot]
                            kvk = ("Xf", kslot)
                        else:
                            KVt = Wst[kslot - NKV]
                            kvk = ("Wst", kslot - NKV)
                        dq = "sp"
                        if d == 1:
                            npc = 127 - t
                            r0_, r1_ = 4 * n, 4 * n + t + 1
                            S.op(dq, lambda e, n=n, t=t, npc=npc, KVt=KVt: e.dma_start(out=KVt[0:112, 0:kvw], in_=cache[n, t + 1:t + 113, :]), writes=[(kvk, 0)], dma=True)
                            S.op(dq, lambda e, n=n, t=t, npc=npc, KVt=KVt: e.dma_start(out=KVt[112:npc, 0:kvw], in_=cache[n, t + 113:128, :]), writes=[(kvk, 1)], dma=True)
                        else:
                            npc = 127
                            r0_, r1_ = tok, tok + 1
                            S.op(dq, lambda e, n=n, t=t, KVt=KVt: e.dma_start(out=KVt[0:112, 0:kvw], in_=cache[n, t + d:t + d + 111 * d + 1:d, :]), writes=[(kvk, 0)], dma=True)
                            S.op(dq, lambda e, n=n, t=t, KVt=KVt: e.dma_start(out=KVt[112:127, 0:kvw], in_=cache[n, t + 113 * d:t + 113 * d + 14 * d + 1:d, :]), writes=[(kvk, 1)], dma=True)
                        if isA:
                            src_new = KNV[r0_:r1_, :].rearrange("p (a b) -> p a b", a=2)[:, :, 0:128]
                            dst_new = KVt[npc:128, 0:256].rearrange("p (a b) -> p a b", a=2)
                        else:
                            src_new = KNV[r0_:r1_, :]
                            dst_new = KVt[npc:128, 0:1024]
                        S.op(dq, lambda e, src_new=src_new, dst_new=dst_new: e.dma_start(out=dst_new, in_=src_new), reads=["KN", "VF"], writes=[(kvk, 2)], dma=True)
                        vs = tok % 8
                        S.op("act", lambda e, KVt=KVt, slot=slot: e.activation(out=Ksb[slot][:, 0:nk], in_=KVt[:, 0:nk], func=AF.Copy), reads=[(kvk, 0), (kvk, 1), (kvk, 2)], writes=[("Ksb", slot)])
                        S.op("dve", lambda e, KVt=KVt, vs=vs: e.tensor_copy(out=V1s[vs][:, 0:nkh, 0:64], in_=KVt[:, nk:kvw].rearrange("p (h d) -> p h d", d=64)), reads=[(kvk, 0), (kvk, 1), (kvk, 2)], writes=[("V1s", vs)])
                        for tt in range(nkt):
                            S.op("pe", lambda e, tt=tt, slot=slot: e.transpose(out=TR[:, tt * 128:(tt + 1) * 128], in_=Ksb[slot][:, tt * 128:(tt + 1) * 128], identity=ident[:]), reads=[("Ksb", slot), "ident"], writes=["TR"])
                        S.op("dve", lambda e, slot=slot: e.tensor_copy(out=KTs[slot][:, 0:nk], in_=TR[:, 0:nk]), reads=["TR"], writes=[("KTs", slot)])
                        if isA:
                            S.op("pe", lambda e, slot=slot, tok=tok, tokc=tokc: e.matmul(SPp[:, tokc * 8:tokc * 8 + 8], lhsT=KTs[slot][:, 0:128], rhs=qflat[:, tok * 8:tok * 8 + 8], start=True, stop=True),
                                 reads=[("KTs", slot), "Qbd"], writes=[("SP", 0)])
                        else:
                            for tp in range(4):
                                S.op("pe", lambda e, tp=tp, slot=slot, tok=tok, tokc=tokc: e.matmul(SPp[:, tokc * 8 + tp * 2: tokc * 8 + tp * 2 + 2], lhsT=KTs[slot][:, tp * 128:(tp + 1) * 128], rhs=Qbd[:, tp, tok, :], start=True, stop=True),
                                     reads=[("KTs", slot), "Qbd"], writes=[("SP", 0)])
                    S.op("act", lambda e: e.activation(out=PEs[:], in_=SPp[:, 0:32], func=AF.Exp), reads=[("SP", 0)], writes=["PEs"])
                    if isA:
                        S.op("dve", lambda e: e.tensor_tensor(out=Zb[:, :, 63].rearrange("p (t g k) -> p t g k", t=4, g=4), in0=PEs[:].rearrange("p (t g k) -> p t g k", t=4, g=4),
                                                               in1=es_v.unsqueeze(1).broadcast_to([128, 4, 4, 2]), op=ALU.mult), reads=["PEs"] + ETK, writes=["Zb"])
                    else:
                        S.op("dve", lambda e: e.tensor_tensor(out=Zb[:, :, 63].rearrange("p (t h) -> p t h", t=4), in0=PEs[:].rearrange("p (t h) -> p t h", t=4),
                                                               in1=es_v.unsqueeze(1).broadcast_to([128, 4, 8]), op=ALU.mult), reads=["PEs"] + ETK, writes=["Zb"])
                    for h in range(8):
                        if isA:
                            col = (h % 4) * 2 + h // 4
                            kv = h // 4
                        else:
                            col = h
                            kv = h
                        for t in range(4):
                            tok = 4 * n + t
                            vs = tok % 8
                            S.op("pe", lambda e, h=h, col=col, kv=kv, t=t, vs=vs, tok=tok: e.matmul(OPp[:, h * 128:h * 128 + 65], lhsT=Zb[:, t * 8 + col, 63 - tok:191 - tok], rhs=V1s[vs][:, kv, :], start=(t == 0), stop=(t == 3)),
                                 reads=["Zb", ("V1s", vs)], writes=[("OP", h // 4)])
                    S.op("dve", lambda e: e.tensor_tensor(out=OsAcc[:], in0=OsAcc[:], in1=OPp[:].rearrange("p (h c) -> p h c", h=8)[:, :, 0:65], op=ALU.add), reads=["OsAcc", ("OP", 0), ("OP", 1)], writes=["OsAcc"])
                k = rr["ev"] % 2
                rr["ev"] += 1
                if isA:
                    S.op("dve", lambda e: e.tensor_tensor(out=LL[:], in0=OsAcc[:, :, 64], in1=SNK[:], op=ALU.add), reads=["OsAcc", "SNK"], writes=["LL"])
                    S.op("dve", lambda e: e.reciprocal(out=LL[:], in_=LL[:]), reads=["LL"], writes=["LL"])
                    S.op("dve", lambda e: e.tensor_tensor(out=OaT[k][:].rearrange("p (h d) -> p h d", d=64), in0=OsAcc[:, :, 0:64], in1=LL[:].unsqueeze(2).broadcast_to([128, 8, 64]), op=ALU.mult), reads=["OsAcc", "LL"], writes=[("OaT", k)])
                    S.op("sp", lambda e: e.dma_start(out=Oa_scr.ap()[NOWN:NOWN + 128, :], in_=OaT[k][:]), reads=[("OaT", k)], writes=["OSCR_a"], dma=True)
                else:
                    S.op("sp", lambda e: e.dma_start(out=Oscr[gname].ap()[NOWN:NOWN + 128, :], in_=OsAcc[:].rearrange("p h c -> p (h c)")), reads=["OsAcc"], writes=["OSCR_" + gname], dma=True)

            WLOADED = {}
            NEXTG = {"b3": "b2", "b2": "b1", "b1": "a", "a": None}

            def load_phase_w(gname):
                G = GROUPS[gname]
                isA = gname == "a"
                nk = 128 if isA else 512
                if isA:
                    qr = [(C_QA + kvh * 256 + g * 64, 64) for g in range(4) for kvh in range(2)]
                else:
                    qr = [(G["cq"], 512)]
                load_w(Wsb, qr + [(G["ck"], nk), (G["cv"], nk)], "W")
                WLOADED[gname] = True

            def attn_phase(gname):
                G = GROUPS[gname]
                d = G["d"]
                isA = gname == "a"
                nk = 128 if isA else 512
                if not WLOADED.get(gname):
                    load_phase_w(gname)
                if SUB >= 2:
                    build_E(gname)
                oscr = Oa_scr.ap() if isA else Oscr[gname].ap()
                nkv = nkv_out[gname]
                ncb = NB // d
                win = {"a": 128, "b1": 128, "b2": 512, "b3": 2048}[gname]
                blocks = []
                cnt = 0
                for r in range(min(d, NCLS)):
                    hs = NOWN - 128 * d + r
                    lhs_h = (lambda c, hs=hs: xT_halo[:, c, hs:hs + 127 * d + 1:d]) if d > 1 else (lambda c, hs=hs: xT_halo[:, c, hs:hs + 128])
                    blocks.append(dict(stages=qkv_block(gname, lhs_h, XTH_ALL, cnt % 3, want_q=False), att=None))
                    cnt += 1
                    for cb in range(ncb):
                        st = r + d * 128 * cb
                        kv_out = None
                        lo = NOWN - win
                        if st >= lo:
                            r0 = st - lo
                            if d > 1:
                                kv_out = (nkv[r0:r0 + 127 * d + 1:d, 0, :], nkv[r0:r0 + 127 * d + 1:d, 1, :])
                            else:
                                kv_out = (nkv[r0:r0 + 128, 0, :], nkv[r0:r0 + 128, 1, :])
                        lhs_o = (lambda c, st=st: xT_own[:, c, st:st + 127 * d + 1:d]) if d > 1 else (lambda c, st=st: xT_own[:, c, st:st + 128])
                        rows = oscr[st:st + 127 * d + 1:d, :] if d > 1 else oscr[st:st + 128, :]
                        blocks.append(dict(stages=qkv_block(gname, lhs_o, XTO_ALL, cnt % 3, want_q=True, kv_out=kv_out),
                                           att=(cnt % 3, (cnt - 1) % 3, cb == 0, rows)))
                        cnt += 1
                if blocks:
                    blocks[0]["stages"][0]()
                for i, b in enumerate(blocks):
                    b["stages"][1]()
                    if i + 1 < len(blocks):
                        blocks[i + 1]["stages"][0]()
                    b["stages"][2]()
                    if b["att"] is not None:
                        cur, prv, first, rows = b["att"]
                        attend(gname, cur, prv, first, rows)
                if WITH_CACHE and SUB >= 6:
                    sample_attn(gname)

            for gi_, gname in enumerate(("b3", "b2", "b1", "a")):
                if LEVEL >= 3 + gi_:
                    attn_phase(gname)

        with ExitStack() as esF:
            def sbF(name, shape, dt):
                return esF.enter_context(nc.sbuf_tensor(name, shape, dt))

            Wg = sbF("Wg", [128, 8, 3072], BF16)
            WuA = sbF("WuA", [128, 4, 1024], BF16)
            WuB = sbF("WuB", [128, 4, 1024], BF16)
            Wo = sbF("Wo", [128, 8, 1024], BF16)
            Wst = [sbF("WstF%d" % i, [128, 1536], F32) for i in range(4)]
            O1 = sbF("O1", [128, 520], F32)
            O2 = sbF("O2", [128, 520], F32)
            O3 = sbF("O3", [128, 520], F32)
            OA = sbF("OA", [128, 512], F32)
            XR = sbF("XR", [128, 1024], F32)
            SG = sbF("SG", [128, 1024], F32)
            SM2 = [sbF("SM%d" % i, [128, 2048], F32) for i in range(2)]
            OB = sbF("OB", [128, 512], F32)
            SGs = sbF("SGs", [128, 1024], F32)
            LB = sbF("LB", [128, 8], F32)
            U = sbF("U", [128, 1024], BF16)
            UT = sbF("UT", [128, 8, 128], BF16)
            M1 = sbF("M1", [128, 1024], F32)
            M2 = sbF("M2", [128, 1024], F32)
            MG = sbF("MG", [128, 1024], BF16)
            MT = sbF("MT", [128, 8, 128], BF16)
            Y = sbF("Y", [128, 1024], F32)

            wc = {"n": 0}
            BAR = S.last_all()

            WKF = {}

            def load_wF(dst_fn, src_ap_fn, ncols_list, key, scale_gain):
                WKF[key] = []
                for (c0, n, off) in ncols_list:
                    for c in range(dst_fn("nchunk")):
                        k = wc["n"] % 4
                        wc["n"] += 1
                        st = Wst[k]
                        S.op("sp", lambda e, c=c, c0=c0, n=n, st=st: e.dma_start(out=st[:, 0:n], in_=src_ap_fn(c, c0, n)), writes=[("WstF", k)], dma=True, extra=BAR)
                        useact = bool(wc["n"] % 2)
                        WKF[key].append((key, c, off))
                        if scale_gain:
                            if useact:
                                S.op("act", lambda e, c=c, n=n, st=st, off=off: e.activation(out=dst_fn(c)[:, off:off + n], in_=st[:, 0:n], func=AF.Copy, scale=NG[:, c:c + 1]), reads=[("WstF", k), "NG"], writes=[(key, c, off)])
                            else:
                                S.op("dve", lambda e, c=c, n=n, st=st, off=off: e.tensor_scalar(out=dst_fn(c)[:, off:off + n], in0=st[:, 0:n], scalar1=NG[:, c:c + 1], scalar2=None, op0=ALU.mult), reads=[("WstF", k), "NG"], writes=[(key, c, off)])
                        else:
                            if useact:
                                S.op("act", lambda e, c=c, n=n, st=st, off=off: e.activation(out=dst_fn(c)[:, off:off + n], in_=st[:, 0:n], func=AF.Copy), reads=[("WstF", k)], writes=[(key, c, off)])
                            else:
                                S.op("dve", lambda e, c=c, n=n, st=st, off=off: e.tensor_copy(out=dst_fn(c)[:, off:off + n], in_=st[:, 0:n]), reads=[("WstF", k)], writes=[(key, c, off)])

            load_wF(lambda c: 8 if c == "nchunk" else Wg[:, c, :], lambda c, c0, n: w_in[c * 128:(c + 1) * 128, c0:c0 + n],
                    [(C_GA, 512, 0), (C_GB, 512, 512), (C_MA, 1024, 1024), (C_MB, 1024, 2048)], "Wg", True)
            load_wF(lambda c: 4 if c == "nchunk" else WuA[:, c, :], lambda c, c0, n: wup_a_in[c * 128:(c + 1) * 128, c0:c0 + n], [(0, 1024, 0)], "WuA", False)
            load_wF(lambda c: 4 if c == "nchunk" else WuB[:, c, :], lambda c, c0, n: wup_b_in[c * 128:(c + 1) * 128, c0:c0 + n], [(0, 1024, 0)], "WuB", False)
            load_wF(lambda c: 8 if c == "nchunk" else Wo[:, c, :], lambda c, c0, n: wout_in[c * 128:(c + 1) * 128, c0:c0 + n], [(0, 1024, 0)], "Wo", False)

            def final_block(lhs_fn, xkeys, row0, x_src, y_dst, k, nrows=128, xres=None):
                SMk = SM2[k]
                smk = ("SM", k)
                pjk = [("PJ", g) for g in range(3)]

                def loads():
                    S.op("sp", lambda e: e.dma_start(out=O1[:], in_=Oscr["b1"].ap()[row0:row0 + 128, :]), reads=["OSCR_b1"], writes=["O1"], dma=True)
                    S.op("sp", lambda e: e.dma_start(out=O2[:], in_=Oscr["b2"].ap()[row0:row0 + 128, :]), reads=["OSCR_b2"], writes=["O2"], dma=True)
                    S.op("sp", lambda e: e.dma_start(out=O3[:], in_=Oscr["b3"].ap()[row0:row0 + 128, :]), reads=["OSCR_b3"], writes=["O3"], dma=True)
                    S.op("sp", lambda e: e.dma_start(out=OA[:], in_=Oa_scr.ap()[row0:row0 + 128, :]), reads=["OSCR_a"], writes=["OA"], dma=True)

                def gates(rnd):
                    for g in range(3):
                        for c in range(8):
                            S.op("pe", lambda e, g=g, c=c: e.matmul(PJ[:, g * 512:(g + 1) * 512], lhsT=lhs_fn(c), rhs=Wg[:, c, rnd * 1536 + g * 512: rnd * 1536 + (g + 1) * 512], start=(c == 0), stop=(c == 7)),
                                 reads=list(xkeys) + WKF["Wg"], writes=[("PJ", g)])
                    if rnd == 0:
                        S.op("act", lambda e: e.activation(out=SGs[:], in_=PJ[:, 0:1024], func=AF.Sigmoid), reads=pjk, writes=["SGs"])
                        S.op("act", lambda e: e.activation(out=SMk[:, 0:512], in_=PJ[:, 1024:1536], func=AF.Sigmoid), reads=pjk, writes=[smk])
                        S.op("dve", lambda e: e.tensor_tensor(out=SG[:], in0=PJ[:, 0:1024], in1=SGs[:], op=ALU.mult), reads=pjk + ["SGs"], writes=["SG"])
                    else:
                        S.op("act", lambda e: e.activation(out=SMk[:, 512:2048], in_=PJ[:, 0:1536], func=AF.Sigmoid), reads=pjk, writes=[smk])

                def gating():
                    S.op("dve", lambda e: e.tensor_tensor(out=O1[:], in0=O1[:], in1=O2[:], op=ALU.add), reads=["O1", "O2"], writes=["O1"])
                    S.op("dve", lambda e: e.tensor_tensor(out=O1[:], in0=O1[:], in1=O3[:], op=ALU.add), reads=["O1", "O3"], writes=["O1"])
                    o1v = O1[:].rearrange("p (h c) -> p h c", c=65)
                    S.op("dve", lambda e: e.reciprocal(out=LB[:], in_=o1v[:, :, 64]), reads=["O1"], writes=["LB"])
                    S.op("dve", lambda e: e.tensor_tensor(out=OB[:].rearrange("p (h d) -> p h d", d=64), in0=o1v[:, :, 0:64], in1=LB[:].unsqueeze(2).broadcast_to([128, 8, 64]), op=ALU.mult), reads=["O1", "LB"], writes=["OB"])
                    S.op("dve", lambda e: e.tensor_tensor(out=U[:, 0:512], in0=OA[:], in1=SG[:, 0:512], op=ALU.mult), reads=["OA", "SG"], writes=["U"])
                    S.op("dve", lambda e: e.tensor_tensor(out=U[:, 512:1024], in0=OB[:], in1=SG[:, 512:1024], op=ALU.mult), reads=["OB", "SG"], writes=["U"])

                def up():
                    for c in range(8):
                        S.op("pe", lambda e, c=c: e.transpose(out=TR[:, c * 128:(c + 1) * 128], in_=U[:, c * 128:(c + 1) * 128], identity=ident[:]), reads=["U", "ident"], writes=["TR"])
                    S.op("act", lambda e: e.activation(out=UT[:], in_=TR[:].rearrange("p (c t) -> p c t", c=8), func=AF.Copy), reads=["TR"], writes=["UT"])
                    for n in range(2):
                        for c in range(4):
                            S.op("pe", lambda e, n=n, c=c: e.matmul(SPp[:, n * 512:(n + 1) * 512], lhsT=UT[:, c, :], rhs=WuA[:, c, n * 512:(n + 1) * 512], start=(c == 0), stop=(c == 3)), reads=["UT"] + WKF["WuA"], writes=[("SP", n)])
                    for n in range(2):
                        for c in range(4):
                            S.op("pe", lambda e, n=n, c=c: e.matmul(OPp[:, n * 512:(n + 1) * 512], lhsT=UT[:, 4 + c, :], rhs=WuB[:, c, n * 512:(n + 1) * 512], start=(c == 0), stop=(c == 3)), reads=["UT"] + WKF["WuB"], writes=[("OP", n)])

                def merge():
                    S.op("dve", lambda e: e.tensor_tensor(out=M1[:], in0=SPp[:], in1=SMk[:, 0:1024], op=ALU.mult), reads=[("SP", 0), ("SP", 1), smk], writes=["M1"])
                    S.op("dve", lambda e: e.tensor_tensor(out=M2[:], in0=OPp[:], in1=SMk[:, 1024:2048], op=ALU.mult), reads=[("OP", 0), ("OP", 1), smk], writes=["M2"])
                    S.op("dve", lambda e: e.tensor_tensor(out=MG[:], in0=M1[:], in1=M2[:], op=ALU.add), reads=["M1", "M2"], writes=["MG"])

                def outp():
                    for c in range(8):
                        S.op("pe", lambda e, c=c: e.transpose(out=TR[:, c * 128:(c + 1) * 128], in_=MG[:, c * 128:(c + 1) * 128], identity=ident[:]), reads=["MG", "ident"], writes=["TR"])
                    S.op("act", lambda e: e.activation(out=MT[:], in_=TR[:].rearrange("p (c t) -> p c t", c=8), func=AF.Copy), reads=["TR"], writes=["MT"])
                    for n in range(2):
                        for c in range(8):
                            S.op("pe", lambda e, n=n, c=c: e.matmul(SPp[:, n * 512:(n + 1) * 512], lhsT=MT[:, c, :], rhs=Wo[:, c, n * 512:(n + 1) * 512], start=(c == 0), stop=(c == 7)), reads=["MT"] + WKF["Wo"], writes=[("SP", n)])

                def resid():
                    if xres is None:
                        S.op("sp", lambda e: e.dma_start(out=XR[:], in_=x_src), writes=["XR"], dma=True)
                        xr, xrk = XR, "XR"
                    else:
                        xr, xrk = xres, "Xs_f"
                    S.op("dve", lambda e: e.tensor_tensor(out=Y[:], in0=SPp[:], in1=xr[:], op=ALU.add), reads=[("SP", 0), ("SP", 1), xrk], writes=["Y"])
                    S.op("sp", lambda e: e.dma_start(out=y_dst, in_=Y[0:nrows, :]), reads=["Y"], dma=True)

                return dict(loads=loads, gates=gates, gating=gating, up=up, merge=merge, outp=outp, resid=resid)

            fblocks = []
            for t in range(NB if LEVEL >= 7 else 0):
                fblocks.append(final_block(lambda c, t=t: xT_own[:, c, t * 128:(t + 1) * 128], [("xTo", t)], t * 128, x_ext[NOWN + t * 128:NOWN + (t + 1) * 128, :], y_out[t * 128:(t + 1) * 128, :], len(fblocks) % 2))
            if WITH_CACHE and LEVEL >= 8:
                fblocks.append(final_block(lambda c: xT_s[:, c, 0:128], ["xTs"], NOWN, None, ys_out, len(fblocks) % 2, nrows=64, xres=Xs_f))
            if fblocks:
                fblocks[0]["loads"]()
                fblocks[0]["gates"](0)
                fblocks[0]["gates"](1)
            for i, fb in enumerate(fblocks):
                nxt = fblocks[i + 1] if i + 1 < len(fblocks) else None
                fb["gating"]()
                if i > 0:
                    fblocks[i - 1]["resid"]()
                if nxt is not None:
                    nxt["loads"]()
                    nxt["gates"](0)
                fb["up"]()
                if nxt is not None:
                    nxt["gates"](1)
                fb["merge"]()
                fb["outp"]()
            if fblocks:
                fblocks[-1]["resid"]()

        S.emit(sems, dsems, block)
    return nc


def shared_inputs(rel_bias, norm_gain, w_in, q_gain_a, k_gain_a, sinks_a, q_gain_b, k_gain_b, w_up_a, w_up_b, w_out):
    relb = np.zeros((128, 128), np.float32)
    relb[:32, :32] = rel_bias
    oh = onehot_tables()
    mask2 = np.zeros((128, 2), np.float32)
    mask2[:64, 0] = 1.0
    mask2[64:, 1] = 1.0
    return {
        "mask2": mask2,
        "w_in": w_in[0], "ng": np.ascontiguousarray(norm_gain[0].reshape(8, 128).T), "relb": relb, "oh": oh,
        "gq_a": np.ascontiguousarray(np.broadcast_to(q_gain_a[0][None, :], (128, 64))),
        "gk_a": np.ascontiguousarray(np.broadcast_to(k_gain_a[0][None, :], (128, 64))),
        "gq_b": np.ascontiguousarray(np.broadcast_to(q_gain_b[0].reshape(1, 192), (128, 192))),
        "gk_b": np.ascontiguousarray(np.broadcast_to(k_gain_b[0].reshape(1, 192), (128, 192))),
        "sinks": np.ascontiguousarray(np.broadcast_to(sinks_a[0][None, :], (128, 8))),
        "wup_a": w_up_a[0], "wup_b": w_up_b[0], "wout": w_out[0],
    }


_CACHE = {}


def kernel(x_prompt, x_sample, cache_a_kv, cache_b1_kv, cache_b2_kv, cache_b3_kv, rel_bias, norm_gain, w_in,
           q_gain_a, k_gain_a, sinks_a, q_gain_b, k_gain_b, w_up_a, w_up_b, w_out):
    f = lambda a: np.ascontiguousarray(np.asarray(a, dtype=np.float32))
    x_prompt = f(x_prompt); x_sample = f(x_sample)
    cache_a_kv = f(cache_a_kv); cache_b1_kv = f(cache_b1_kv); cache_b2_kv = f(cache_b2_kv); cache_b3_kv = f(cache_b3_kv)
    rel_bias = f(rel_bias); norm_gain = f(norm_gain); w_in = f(w_in)
    q_gain_a = f(q_gain_a); k_gain_a = f(k_gain_a); sinks_a = f(sinks_a); q_gain_b = f(q_gain_b); k_gain_b = f(k_gain_b)
    w_up_a = f(w_up_a); w_up_b = f(w_up_b); w_out = f(w_out)

    nc = build_program()
    shared = shared_inputs(rel_bias, norm_gain, w_in, q_gain_a, k_gain_a, sinks_a, q_gain_b, k_gain_b, w_up_a, w_up_b, w_out)

    in_maps = []
    for c in range(8):
        b, h = c // 2, c % 2
        x_ext = np.zeros((4096, 1024), np.float32)
        if h == 1:
            x_ext[:] = x_prompt[b]
        else:
            x_ext[2048:] = x_prompt[b, :2048]
        xs = np.zeros((128, 1024), np.float32)
        xs[:64] = x_sample[16 * c:16 * c + 16].reshape(64, 1024)
        m = dict(shared)
        m["x_ext"] = x_ext
        m["x_s"] = xs
        m["hv"] = np.full((128, 1), float(h), np.float32)
        if WITH_CACHE:
          m["c_a"] = np.ascontiguousarray(cache_a_kv[0, 16 * c:16 * c + 16].reshape(16, 128, 256))
          m["c_b1"] = np.ascontiguousarray(cache_b1_kv[0, 16 * c:16 * c + 16].reshape(16, 128, 1024))
          m["c_b2"] = np.ascontiguousarray(cache_b2_kv[0, 16 * c:16 * c + 16].reshape(16, 512, 1024))
          m["c_b3"] = np.ascontiguousarray(cache_b3_kv[0, 16 * c:16 * c + 16].reshape(16, 2048, 1024))
        in_maps.append(m)
    res = run_bass_kernel_spmd(nc, in_maps, core_ids=list(range(8)))
    R = res.results
    y = np.zeros((4, 4096, 1024), np.float32)
    ys = np.zeros((128, 4, 1024), np.float32)
    for c in range(8):
        b, h = c // 2, c % 2
        y[b, h * 2048:(h + 1) * 2048] = R[c]["y"]
        ys[16 * c:16 * c + 16] = R[c]["ys"].reshape(16, 4, 1024)
    np_a = np.stack([R[2 * b + 1]["nkv_a"].reshape(128, 2, 2, 64) for b in range(4)])[None]
    np_b1 = np.stack([R[2 * b + 1]["nkv_b1"].reshape(128, 2, 8, 64) for b in range(4)])[None]
    np_b2 = np.stack([R[2 * b + 1]["nkv_b2"].reshape(512, 2, 8, 64) for b in range(4)])[None]
    np_b3 = np.stack([R[2 * b + 1]["nkv_b3"].reshape(2048, 2, 8, 64) for b in range(4)])[None]
    ns_a = np.concatenate([R[c]["ns_a"].reshape(16, 4, 2, 2, 64) for c in range(8)])[None]
    ns_b1 = np.concatenate([R[c]["ns_b1"].reshape(16, 4, 2, 8, 64) for c in range(8)])[None]
    ns_b2 = np.concatenate([R[c]["ns_b2"].reshape(16, 4, 2, 8, 64) for c in range(8)])[None]
    ns_b3 = np.concatenate([R[c]["ns_b3"].reshape(16, 4, 2, 8, 64) for c in range(8)])[None]
    return (y, ys, np_a, np_b1, np_b2, np_b3, ns_a, ns_b1, ns_b2, ns_b3)
```
